# Optimizing a Trainium2 kernel written in Bass

```python
import math
import jax, jax.numpy as jnp
from jax import lax
import numpy as np

D_MODEL = 1024
BATCH = 2
SEQ = 8192
DEPTH = 4
DEC_BATCH = 128
DEC_SEQ = 8
PAST_LEN = 8192
PAGE_SIZE = 128

N_META = 16
POOL_WINDOWS = (2, 4, 8, 16)
POOL_GROUPS = 4
POOL_WIDTH = D_MODEL // 2
POOL_GW = POOL_WIDTH // POOL_GROUPS
POOL_OUT_GW = D_MODEL // POOL_GROUPS
POOL_HIST = 15
CONV_DIM = D_MODEL // 2
CONV_WIDTH = 31
CONV_HIST = CONV_WIDTH - 1
HEAD_DIM = 64
N_HEADS = D_MODEL // HEAD_DIM
N_KV_HEADS = 4
GQA_GROUP = N_HEADS // N_KV_HEADS
WINDOW = 128
ATTN_BLOCK = 128
N_BUCKETS = 32
MAX_DISTANCE = 128
D_FF = 4 * D_MODEL
N_BRANCHES = 3
NORM_EPS = 1e-6
D_IN = POOL_WIDTH + 2 * CONV_DIM + (N_HEADS + 2 * N_KV_HEADS) * HEAD_DIM + N_BRANCHES * D_MODEL

kernel_name = 'hybrid_pool_conformer_swa_gated_decoder_step'


def _rmsnorm(x, g):
    x32 = x.astype(jnp.float32)
    y = x32 * lax.rsqrt(jnp.mean(x32 * x32, axis=-1, keepdims=True) + NORM_EPS)
    return (y * g.astype(jnp.float32)).astype(x.dtype)


def _t5_bucket(dist):
    n = np.maximum(dist, 0)
    exact = N_BUCKETS // 2
    large = exact + (np.log(np.maximum(n, 1) / exact) / np.log(MAX_DISTANCE / exact)
                     * (N_BUCKETS - exact)).astype(np.int32)
    large = np.minimum(large, N_BUCKETS - 1)
    return np.where(n < exact, n, large).astype(np.int32)


def _pool_mixer(a, past, start, w, scale):
    B, T, _ = a.shape
    ext = jnp.concatenate([past.astype(a.dtype), a], axis=1)
    csum = jnp.pad(jnp.cumsum(ext.astype(jnp.float32), axis=1), ((0, 0), (1, 0), (0, 0)))
    pos = start + jnp.arange(T)
    a32 = a.astype(jnp.float32)
    parts = []
    for gi, win in enumerate(POOL_WINDOWS):
        c0, c1 = gi * POOL_GW, (gi + 1) * POOL_GW
        hi = csum[:, POOL_HIST + 1:POOL_HIST + 1 + T, c0:c1]
        lo = csum[:, POOL_HIST + 1 - win:POOL_HIST + 1 - win + T, c0:c1]
        cnt = jnp.minimum(pos + 1, win).astype(jnp.float32)[None, :, None]
        parts.append((hi - lo) / cnt - a32[..., c0:c1])
    r = jnp.stack(parts, axis=2).astype(a.dtype)
    y = jnp.einsum('btgc,gcd->btgd', r, w).reshape(B, T, D_MODEL) * scale
    return y, ext[:, -POOL_HIST:]


def _conv_mixer(c_in, past, w, b, ng, nb, w_out):
    glu = c_in[..., :CONV_DIM] * jax.nn.sigmoid(c_in[..., CONV_DIM:])
    ext = jnp.concatenate([past.astype(glu.dtype), glu], axis=1)
    y = lax.conv_general_dilated(ext, w[:, None, :].astype(ext.dtype), (1,), 'VALID',
                                 dimension_numbers=('NWC', 'WIO', 'NWC'),
                                 feature_group_count=CONV_DIM) + b.astype(ext.dtype)
    y32 = y.astype(jnp.float32)
    mu = jnp.mean(y32, axis=-1, keepdims=True)
    var = jnp.mean(jnp.square(y32 - mu), axis=-1, keepdims=True)
    yn = (y32 - mu) * lax.rsqrt(var + NORM_EPS) * ng.astype(jnp.float32) + nb.astype(jnp.float32)
    s = jax.nn.silu(yn).astype(glu.dtype)
    return s @ w_out, ext[:, -CONV_HIST:]


def _attn_core(qb, kb, vb, dist, key_valid, sinks, rel_bias):
    B, N, Q = qb.shape[:3]
    S = kb.shape[2]
    mask = ((dist >= 0) & (dist <= WINDOW))[None]
    if key_valid is not None:
        mask = mask & key_valid[:, None, :]
    bias = rel_bias[_t5_bucket(dist)].astype(jnp.float32)
    bias = bias.transpose(2, 0, 1).reshape(N_KV_HEADS, GQA_GROUP, Q, S)
    q = qb.reshape(B, N, Q, N_KV_HEADS, GQA_GROUP, HEAD_DIM)
    logits = jnp.einsum('bnqkgd,bnskd->bnkgqs', q, kb,
                        preferred_element_type=jnp.float32) * (HEAD_DIM ** -0.5) + bias
    logits = jnp.where(mask[:, None, None], logits, -jnp.inf)
    sink = sinks.astype(jnp.float32).reshape(N_KV_HEADS, GQA_GROUP, 1)
    m = jnp.maximum(logits.max(axis=-1), sink)
    p = jnp.exp(logits - m[..., None])
    denom = p.sum(axis=-1) + jnp.exp(sink - m)
    out = jnp.einsum('bnkgqs,bnskd->bnqkgd', (p / denom[..., None]).astype(vb.dtype), vb)
    return out.reshape(B, N, Q, N_HEADS * HEAD_DIM)


def _attention_prompt(q, k, v, sinks, rel_bias):
    B, T = q.shape[:2]
    pad = (-T) % ATTN_BLOCK
    nb = (T + pad) // ATTN_BLOCK
    padf = lambda x: jnp.pad(x, ((0, 0), (pad, 0), (0, 0), (0, 0)))
    qb = padf(q).reshape(B, nb, ATTN_BLOCK, N_HEADS, HEAD_DIM)
    kc = padf(k).reshape(B, nb, ATTN_BLOCK, N_KV_HEADS, HEAD_DIM)
    vc = padf(v).reshape(B, nb, ATTN_BLOCK, N_KV_HEADS, HEAD_DIM)
    band = lambda x: jnp.concatenate(
        [jnp.concatenate([jnp.zeros_like(x[:, :1]), x[:, :-1]], axis=1), x], axis=2)
    kb, vb = band(kc), band(vc)
    dist = ATTN_BLOCK + np.arange(ATTN_BLOCK)[:, None] - np.arange(2 * ATTN_BLOCK)[None, :]
    key_idx = (np.arange(nb)[:, None] - 1) * ATTN_BLOCK + np.arange(2 * ATTN_BLOCK)[None, :]
    key_valid = key_idx >= pad
    out = _attn_core(qb, kb, vb, dist, key_valid, sinks, rel_bias)
    out = out.reshape(B, nb * ATTN_BLOCK, N_HEADS * HEAD_DIM)[:, pad:]
    return out, k[:, -WINDOW:], v[:, -WINDOW:]


def _attention_sample(q, k, v, past_k, past_v, sinks, rel_bias):
    T = q.shape[1]
    W = past_k.shape[1]
    kk = jnp.concatenate([past_k.astype(k.dtype), k], axis=1)
    vv = jnp.concatenate([past_v.astype(v.dtype), v], axis=1)
    dist = W + np.arange(T)[:, None] - np.arange(W + T)[None, :]
    out = _attn_core(q[:, None], kk[:, None], vv[:, None], dist, None, sinks, rel_bias)[:, 0]
    return out, kk[:, -WINDOW:], vv[:, -WINDOW:]


def _layer(h, start, pool_past, conv_past, kv_past, rel_bias, norm_mix, w_in, b_in, pool_w,
           pool_scale, conv_w, conv_b, conv_norm_g, conv_norm_b, w_conv_out, attn_sinks,
           w_attn_out, w_out, norm_mlp, w_up, w_down):
    B, T, _ = h.shape
    u = _rmsnorm(h, norm_mix)
    z = u @ w_in + b_in
    sizes = [POOL_WIDTH, 2 * CONV_DIM, N_HEADS * HEAD_DIM, N_KV_HEADS * HEAD_DIM,
             N_KV_HEADS * HEAD_DIM, N_BRANCHES * D_MODEL]
    a_in, c_in, q, k, v, g = jnp.split(z, np.cumsum(sizes)[:-1].tolist(), axis=-1)
    y_a, pool_state = _pool_mixer(a_in, pool_past, start, pool_w, pool_scale)
    y_b, conv_state = _conv_mixer(c_in, conv_past, conv_w, conv_b, conv_norm_g, conv_norm_b,
                                  w_conv_out)
    q = q.reshape(B, T, N_HEADS, HEAD_DIM)
    k = k.reshape(B, T, N_KV_HEADS, HEAD_DIM)
    v = v.reshape(B, T, N_KV_HEADS, HEAD_DIM)
    if kv_past is None:
        o, k_state, v_state = _attention_prompt(q, k, v, attn_sinks, rel_bias)
    else:
        o, k_state, v_state = _attention_sample(q, k, v, kv_past[0], kv_past[1], attn_sinks,
                                                rel_bias)
    y_c = o @ w_attn_out
    gates = jax.nn.sigmoid(g.astype(jnp.float32)).reshape(B, T, N_BRANCHES, D_MODEL)
    merged = gates[:, :, 0] * y_a + gates[:, :, 1] * y_b + gates[:, :, 2] * y_c
    h = h + merged.astype(h.dtype) @ w_out
    u2 = _rmsnorm(h, norm_mlp)
    h = h + jnp.square(jax.nn.relu(u2 @ w_up)) @ w_down
    return h, pool_state, conv_state, k_state, v_state


def setup_inputs(seed: int = 0) -> dict:
    key = jax.random.key(seed)
    ks = jax.random.split(key, 25)
    nrm = lambda k, shape, s: jax.random.normal(k, shape, jnp.float32) * s
    return {
        'x_prompt': nrm(ks[0], (BATCH, SEQ, D_MODEL), 1.0),
        'x_sample': nrm(ks[1], (DEC_BATCH, DEC_SEQ, D_MODEL), 1.0),
        'state_pool': nrm(ks[2], (DEPTH, DEC_BATCH, POOL_HIST, POOL_WIDTH), 1.0),
        'state_conv': nrm(ks[3], (DEPTH, DEC_BATCH, CONV_HIST, CONV_DIM), 0.5),
        'cache_k': nrm(ks[4], (DEPTH, DEC_BATCH, WINDOW, N_KV_HEADS, HEAD_DIM), 1.0),
        'cache_v': nrm(ks[5], (DEPTH, DEC_BATCH, WINDOW, N_KV_HEADS, HEAD_DIM), 1.0),
        'meta_tokens': nrm(ks[6], (N_META, D_MODEL), 1.0),
        'rel_bias': nrm(ks[7], (N_BUCKETS, N_HEADS), 0.5),
        'norm_mix': 1.0 + nrm(ks[8], (DEPTH, D_MODEL), 0.02),
        'w_in': nrm(ks[9], (DEPTH, D_MODEL, D_IN), D_MODEL ** -0.5),
        'b_in': nrm(ks[10], (DEPTH, D_IN), 0.01),
        'pool_w': nrm(ks[11], (DEPTH, POOL_GROUPS, POOL_GW, POOL_OUT_GW), POOL_GW ** -0.5),
        'pool_scale': 1.0 + nrm(ks[12], (DEPTH, D_MODEL), 0.02),
        'conv_w': nrm(ks[13], (DEPTH, CONV_WIDTH, CONV_DIM), CONV_WIDTH ** -0.5),
        'conv_b': nrm(ks[14], (DEPTH, CONV_DIM), 0.01),
        'conv_norm_g': 1.0 + nrm(ks[15], (DEPTH, CONV_DIM), 0.02),
        'conv_norm_b': nrm(ks[16], (DEPTH, CONV_DIM), 0.01),
        'w_conv_out': nrm(ks[17], (DEPTH, CONV_DIM, D_MODEL), CONV_DIM ** -0.5),
        'attn_sinks': nrm(ks[18], (DEPTH, N_HEADS), 1.0),
        'w_attn_out': nrm(ks[19], (DEPTH, N_HEADS * HEAD_DIM, D_MODEL), (N_HEADS * HEAD_DIM) ** -0.5),
        'w_out': nrm(ks[20], (DEPTH, D_MODEL, D_MODEL), D_MODEL ** -0.5),
        'norm_mlp': 1.0 + nrm(ks[21], (DEPTH, D_MODEL), 0.02),
        'w_up': nrm(ks[22], (DEPTH, D_MODEL, D_FF), D_MODEL ** -0.5),
        'w_down': nrm(ks[23], (DEPTH, D_FF, D_MODEL), D_FF ** -0.5),
        'norm_final': 1.0 + nrm(ks[24], (D_MODEL,), 0.02),
    }


def reference(x_prompt, x_sample, state_pool, state_conv, cache_k, cache_v, meta_tokens,
              rel_bias, norm_mix, w_in, b_in, pool_w, pool_scale, conv_w, conv_b, conv_norm_g,
              conv_norm_b, w_conv_out, attn_sinks, w_attn_out, w_out, norm_mlp, w_up, w_down,
              norm_final):
    bp = x_prompt.shape[0]
    meta = jnp.broadcast_to(meta_tokens[None].astype(x_prompt.dtype), (bp, N_META, D_MODEL))
    hp = jnp.concatenate([meta, x_prompt], axis=1)
    hs = x_sample
    zero_pool = jnp.zeros((bp, POOL_HIST, POOL_WIDTH), x_prompt.dtype)
    zero_conv = jnp.zeros((bp, CONV_HIST, CONV_DIM), x_prompt.dtype)
    pool_p, pool_s, conv_p, conv_s, k_p, k_s, v_p, v_s = [], [], [], [], [], [], [], []
    for l in range(DEPTH):
        wl = (norm_mix[l], w_in[l], b_in[l], pool_w[l], pool_scale[l], conv_w[l], conv_b[l],
              conv_norm_g[l], conv_norm_b[l], w_conv_out[l], attn_sinks[l], w_attn_out[l],
              w_out[l], norm_mlp[l], w_up[l], w_down[l])
        hp, ps, cs, kst, vst = _layer(hp, 0, zero_pool, zero_conv, None, rel_bias, *wl)
        pool_p.append(ps); conv_p.append(cs); k_p.append(kst); v_p.append(vst)
        hs, ps, cs, kst, vst = _layer(hs, PAST_LEN, state_pool[l], state_conv[l],
                                      (cache_k[l], cache_v[l]), rel_bias, *wl)
        pool_s.append(ps); conv_s.append(cs); k_s.append(kst); v_s.append(vst)
    y_prompt = _rmsnorm(hp, norm_final)[:, N_META:]
    y_sample = _rmsnorm(hs, norm_final)
    return (y_prompt, y_sample, jnp.stack(pool_p), jnp.stack(pool_s), jnp.stack(conv_p),
            jnp.stack(conv_s), jnp.stack(k_p), jnp.stack(k_s), jnp.stack(v_p), jnp.stack(v_s))
```

```python
import contextlib
import numpy as np
import concourse.bass as bass
import concourse.mybir as mybir
from concourse.bass_utils import run_bass_kernel_spmd

F32 = mybir.dt.float32
BF16 = mybir.dt.bfloat16
AF = mybir.ActivationFunctionType
ALU = mybir.AluOpType

D = 1024
NL = 4
DIN = 6144
TT = 512
NSEQ = 16
DT = 8
NS = NSEQ * DT
NEG = -30000.0
EPS = 1e-6
WINS = (2, 4, 8, 16)

PB_IN = 0
PB_NMIX = 48
PB_PSC = 56
PB_NMLP = 64
PB_CW = 72
PB_CB = 196
PB_NG = 200
PB_NB = 204
PB_SINK = 208
PL = 216
PB_NF = NL * PL
NPP = PB_NF + 8

SAME_ENG_SYNC = True
NWS = 4
NDIAG = 16
TRI_HALO = True
SKIP = set()


class Op:
    __slots__ = ("eng", "fn", "deps", "sig", "sigval", "dma")

    def __init__(self, eng, fn, dma):
        self.eng = eng
        self.fn = fn
        self.deps = []
        self.sig = False
        self.sigval = None
        self.dma = dma


class Prog:
    ENGS = ("pe", "act", "dve", "pool", "sp")

    def __init__(self, nc):
        self.nc = nc
        self.ops = []
        self.writer = {}
        self.readers = {}
        self.dma_count = {}
        self.nbank = 0
        self.reserved = set()

    def eng_obj(self, eng):
        nc = self.nc
        return {"pe": nc.tensor, "act": nc.scalar, "dve": nc.vector,
                "pool": nc.gpsimd, "sp": nc.sync}[eng]

    def add(self, eng, fn, reads=(), writes=(), dma=None):
        o = Op(eng, fn, dma)
        deps = {}
        for t in reads:
            w = self.writer.get(t)
            if w is not None:
                deps[id(w)] = w
        for t in writes:
            w = self.writer.get(t)
            if w is not None:
                deps[id(w)] = w
            last = {}
            for r in self.readers.get(t, ()):
                if r.dma is not None:
                    deps[id(r)] = r
                else:
                    last[r.eng] = r
            for r in last.values():
                deps[id(r)] = r
        o.deps = list(deps.values())
        for t in reads:
            self.readers.setdefault(t, []).append(o)
        for t in writes:
            self.writer[t] = o
            self.readers[t] = []
        if dma is not None:
            c = self.dma_count.get(dma, 0) + 1
            self.dma_count[dma] = c
            o.sigval = 16 * c
        self.ops.append(o)
        return o

    def bank(self):
        while True:
            b = self.nbank % 8
            self.nbank += 1
            if b not in self.reserved:
                return b

    @staticmethod
    def _needs(o, d):
        if d.dma is not None:
            return True
        if o.dma is not None:
            return True
        if d.eng != o.eng:
            return True
        return SAME_ENG_SYNC and d.eng != "pe"

    def emit(self):
        nc = self.nc
        for o in self.ops:
            for d in o.deps:
                if d.dma is None and self._needs(o, d):
                    d.sig = True
        cnt = {e: 0 for e in self.ENGS}
        for o in self.ops:
            if o.dma is None and o.sig:
                cnt[o.eng] += 1
                o.sigval = cnt[o.eng]
        self.sig_counts = dict(cnt)
        per_eng = {e: [o for o in self.ops if o.eng == e] for e in self.ENGS}
        with contextlib.ExitStack() as st:
            esem = {e: st.enter_context(nc.semaphore("c_" + e)) for e in self.ENGS}
            dsem = {n: st.enter_context(nc.semaphore("d_%d" % i))
                    for i, n in enumerate(self.dma_count)}
            block = st.enter_context(nc.Block())

            def run(eng):
                E = self.eng_obj(eng)
                seen = {}
                for o in per_eng[eng]:
                    waits = {}
                    for d in o.deps:
                        if not self._needs(o, d):
                            continue
                        s = dsem[d.dma] if d.dma is not None else esem[d.eng]
                        k = id(s)
                        if seen.get(k, 0) >= d.sigval:
                            continue
                        if k not in waits or waits[k][1] < d.sigval:
                            waits[k] = (s, d.sigval)
                    for k, (s, v) in waits.items():
                        E.wait_ge(s, v)
                        seen[k] = v
                    ins = o.fn()
                    if o.dma is not None:
                        ins.then_inc(dsem[o.dma], 16)
                    elif o.sig:
                        ins.then_inc(esem[o.eng], 1)
                if eng == "sp":
                    for n, c in self.dma_count.items():
                        E.wait_ge(dsem[n], 16 * c)

            block.tensor(lambda e: run("pe"))
            block.scalar(lambda e: run("act"))
            block.vector(lambda e: run("dve"))
            block.gpsimd(lambda e: run("pool"))
            block.sync(lambda e: run("sp"))


class Builder:
    def __init__(self, tiles=("halo", "m0", "m1", "m2", "m3", "samp"), nl=NL, dbg=False):
        self.tiles = tiles
        self.nl = nl
        self.dbg = dbg
        self.nc = bass.Bass("TRN2", target_bir_lowering=False)
        self.P = Prog(self.nc)
        self.st = contextlib.ExitStack()

    def din(self, name, shape):
        return self.nc.dram_tensor(name, list(shape), F32, kind="ExternalInput").ap()

    def dout(self, name, shape):
        return self.nc.dram_tensor(name, list(shape), F32, kind="ExternalOutput").ap()

    def sb(self, name, shape, dt):
        return self.st.enter_context(self.nc.sbuf_tensor(name, list(shape), dt))

    def add(self, *a, **k):
        return self.P.add(*a, **k)

    def build(self):
        nc = self.nc
        with self.st:
            self.declare()
            self.prologue()
            for tname in self.tiles:
                self.run_tile(tname)
            self.P.emit()
        return nc

    def declare(self):
        nc = self.nc
        sb = self.sb
        self.xh = self.din("xh", [512, D])
        self.xm = self.din("xm", [2048, D])
        self.xs = self.din("xs", [NS, D])
        self.tokmask_d = self.din("tokmask", [128, 512])
        self.kmask_d = self.din("kmask", [128, 8])
        self.invc_d = self.din("invc", [128, 64])
        self.spool = self.din("spool", [NL, NSEQ, 15, 512])
        self.sconv = self.din("sconv", [NL, NSEQ, 30, 512])
        self.ck = self.din("ck", [NL, NSEQ, 128, 256])
        self.cv = self.din("cv", [NL, NSEQ, 128, 256])
        self.w_in = self.din("w_in", [NL, D, DIN])
        self.w_pool = self.din("w_pool", [NL, 4, 128, 256])
        self.w_co = self.din("w_co", [NL, 512, D])
        self.w_ao = self.din("w_ao", [NL, D, D])
        self.w_o = self.din("w_o", [NL, D, D])
        self.w_up = self.din("w_up", [NL, D, 4096])
        self.w_dn = self.din("w_dn", [NL, 4096, D])
        self.pp_d = self.din("pp", [128, NPP])
        self.bkv_d = self.din("bkv", [NL, 128, 512])
        self.biasT_d = self.din("biasT", [128, 2 * 16 * 128])
        self.ident_d = self.din("identm", [128, 128])
        self.biasS_d = self.din("biasS", [128, 256])
        self.ym = self.dout("ym", [2048, D])
        self.ys = self.dout("ys", [NS, D])
        self.pool_p = self.dout("pool_p", [NL, 15, 512])
        self.conv_p = self.dout("conv_p", [NL, 30, 512])
        self.k_p = self.dout("k_p", [NL, 128, 256])
        self.v_p = self.dout("v_p", [NL, 128, 256])
        self.pool_s = self.dout("pool_s", [NL, NSEQ, 15, 512])
        self.conv_s = self.dout("conv_s", [NL, NSEQ, 30, 512])
        self.k_s = self.dout("k_s", [NL, NSEQ, 128, 256])
        self.v_s = self.dout("v_s", [NL, NSEQ, 128, 256])
        if self.dbg:
            self.dbg_d = self.dout("dbg", [128, 8 * TT])
        self.h = sb("h", [128, 8, TT], F32)
        self.u = sb("u", [128, 8, TT], BF16)
        self.sq = sb("sq", [128, 2, TT], BF16)
        self.rs = sb("rs", [128, TT], F32)
        self.aext = sb("aext", [128, 4, 16 + TT], F32)
        self.pscr = sb("pscr", [128, 2, 16 + TT], F32)
        self.gext = sb("gext", [128, 4, 640], F32)
        self.sig = sb("sig", [128, 2, TT], F32)
        self.cy = sb("cy", [128, 4, TT], F32)
        self.big = sb("big", [128, 32 * TT], BF16)
        self.kT = sb("kT", [128, 2, 128 + TT], BF16)
        self.v = sb("v", [128, 5, 256], BF16)
        self.biasT = sb("biasT_s", [128, 2, 16, 128], BF16)
        self.atmp = sb("atmp", [128, 2, TT], F32)
        self.pT = sb("pT", [128, 4, TT], BF16)
        self.rD = sb("rD", [128, 2, TT], F32)
        self.sinkB = sb("sinkB", [128, 2, TT], F32)
        self.sinkE = sb("sinkE", [128, 8], F32)
        self.bq8 = sb("bq8", [128, 8], F32)
        self.mt = sb("mt", [128, 2, TT], F32)
        self.relu = sb("relu", [128, 2, TT], BF16)
        self.diag = sb("diag", [128, NDIAG, 128], BF16)
        self.ws = [sb("ws%d" % i, [128, 4096], BF16) for i in range(NWS)]
        self.wpl = sb("wpl", [128, 4, 256], BF16)
        self.stA = sb("stA", [128, NL, 4, 16], F32)
        self.stG = sb("stG", [128, NL, 4, 32], F32)
        self.stb = sb("stb", [128, 2048], BF16)
        self.stK = self.stb[:, 0:1024].rearrange("p (l j s) -> p l j s", l=NL, j=2)
        self.stV = self.stb[:, 1024:2048].rearrange("p (l f) -> p l f", l=NL)
        self.kcT = self.stb[:, 0:1024].rearrange("p (j b s) -> p j b s", j=2, b=4)
        self.vc = self.stb[:, 1024:2048].rearrange("p (b f) -> p b f", b=4)
        self.ident = sb("ident", [128, 128], F32)
        self.identb = sb("identb", [128, 128], BF16)
        self.onesb = sb("onesb", [128, 128], BF16)
        self.pp = sb("pp_s", [128, NPP], F32)
        self.bkv = sb("bkv_s", [128, 512], F32)
        self.tokmask = sb("tokmask_s", [128, 512], F32)
        self.kmask = sb("kmask_s", [128, 8], F32)
        self.invc = sb("invc_s", [128, 4, 16], F32)
        self.kf = sb("kf", [128, 2, 128], F32)
        self.vnew = sb("vnew", [8, NSEQ, 256], BF16)
        self.biasS = sb("biasS_s", [128, 2, 128], F32)
        self.ps = [self.st.enter_context(nc.psum_tensor("ps%d" % i, [128, 512], F32))
                   for i in range(8)]
        self.ost = self.pscr[:, :, 0:512]
        self.kst = self.mt[:, :, :].rearrange("p a b -> p (a b)").rearrange("p (b f) -> p b f", f=256)
        self.anew = self.sinkB[:, 0, :].rearrange("p (g t) -> p g t", t=128)
        self.gnew = self.sinkB[:, 1, :].rearrange("p (g t) -> p g t", t=128)
        self.vnewf = self.rD[0:8, 1, :].rearrange("p (a f) -> p a f", f=256)
        bigv = self.big[:, :].rearrange("p (c t) -> p c t", t=TT)
        self.hid = bigv
        self.q = bigv[:, 0:8, :]
        self.merged = bigv[:, 8:16, :]
        self.gbf = self.big[:, 16 * TT:16 * TT + 4 * 640].rearrange("p (c t) -> p c t", t=640)
        self.r = bigv[:, 21:25, :]
        self.s = bigv[:, 25:29, :]
        self.wq = []
        self.wi = 0
        self.wissued = 0
        self.wcur = {}

    def tq(self, j):
        return ("big", j)

    def tmg(self, m):
        return ("big", 8 + m)

    def tgbf(self, c):
        return [("big", 16 + c), ("big", 17 + c)]

    def tr(self, g):
        return ("big", 21 + g)

    def ts_(self, c):
        return ("big", 25 + c)

    def thid(self, f):
        return ("big", f)

    def macc(self, m, N):
        if m < 4:
            return self.cy[:, m, 0:N], ("cy", m)
        return self.aext[:, m - 4, 0:N], ("aext", m - 4)

    def layer_items(self, l):
        it = []
        for j in range(6):
            it.append((("win", j), self.w_in[l, :, j * 512:(j + 1) * 512].rearrange("(k p) c -> p k c", p=128), 8, 512))
        for j in (6, 7):
            it.append((("win", j), self.w_in[l, :, j * 512:(j + 1) * 512].rearrange("(k p) c -> p k c", p=128), 8, 512))
        for hh in range(2):
            it.append((("wco", hh), self.w_co[l, :, hh * 512:(hh + 1) * 512].rearrange("(k p) c -> p k c", p=128), 4, 512))
            it.append((("win", 8 + hh), self.w_in[l, :, (8 + hh) * 512:(9 + hh) * 512].rearrange("(k p) c -> p k c", p=128), 8, 512))
        for hh in range(2):
            it.append((("wao", hh), self.w_ao[l, :, hh * 512:(hh + 1) * 512].rearrange("(k p) c -> p k c", p=128), 8, 512))
            it.append((("win", 10 + hh), self.w_in[l, :, (10 + hh) * 512:(11 + hh) * 512].rearrange("(k p) c -> p k c", p=128), 8, 512))
        for hh in range(2):
            it.append((("wo", hh), self.w_o[l, :, hh * 512:(hh + 1) * 512].rearrange("(k p) c -> p k c", p=128), 8, 512))
        for j in range(8):
            it.append((("wup", j), self.w_up[l, :, j * 512:(j + 1) * 512].rearrange("(k p) c -> p k c", p=128), 8, 512))
        for m in range(8):
            it.append((("wdn", m), self.w_dn[l, :, m * 128:(m + 1) * 128].rearrange("(k p) c -> p k c", p=128), 32, 128))
        return it

    def wissue(self):
        nc = self.nc
        while self.wissued < len(self.wq):
            i = self.wissued
            if i >= NWS and not self.wreleased[i - NWS]:
                break
            key, src, d1, d2 = self.wq[i]
            slot = i % NWS
            dst = self.ws[slot][:, 0:d1 * d2].rearrange("p (a b) -> p a b", b=d2)
            self.add("pool", lambda dst=dst, src=src: nc.gpsimd.dma_start(out=dst, in_=src),
                     writes=[("ws", slot)], dma=("ws", slot))
            self.wissued += 1

    def wget(self, key):
        i = self.wi
        k, src, d1, d2 = self.wq[i]
        assert k == key, (k, key)
        self.wissue()
        assert self.wissued > i, ("weight stream stuck", i, key)
        self.wi += 1
        slot = i % NWS
        self.wcur[key] = i
        return self.ws[slot][:, 0:d1 * d2].rearrange("p (a b) -> p a b", b=d2), ("ws", slot)

    def wrel(self, key):
        self.wreleased[self.wcur.pop(key)] = True
        self.wissue()

    def prologue(self):
        nc = self.nc
        add = self.add
        for t in self.tiles:
            for l in range(self.nl):
                self.wq.extend(self.layer_items(l))
        self.wreleased = [False] * len(self.wq)
        add("sp", lambda: nc.sync.dma_start(out=self.pp[:, :], in_=self.pp_d), writes=["pp"], dma="c0")
        add("pool", lambda: nc.gpsimd.dma_start(out=self.biasT[:, :, :, :].rearrange("p a b c -> p (a b c)"), in_=self.biasT_d),
            writes=["biasT"], dma="c1")
        add("sp", lambda: nc.sync.dma_start(out=self.biasS[:, :, :].rearrange("p a b -> p (a b)"), in_=self.biasS_d),
            writes=["biasS"], dma="c6")
        add("sp", lambda: nc.sync.dma_start(out=self.tokmask[:, :], in_=self.tokmask_d), writes=["tokmask"], dma="c2")
        add("sp", lambda: nc.sync.dma_start(out=self.kmask[:, :], in_=self.kmask_d), writes=["kmask"], dma="c3")
        add("sp", lambda: nc.sync.dma_start(out=self.invc[:, :, :].rearrange("p a b -> p (a b)"), in_=self.invc_d),
            writes=["invc"], dma="c4")
        add("pool", lambda: nc.gpsimd.memset(self.onesb[:, :], 1.0), writes=["onesb"])
        add("sp", lambda: nc.sync.dma_start(out=self.ident[:, :], in_=self.ident_d), writes=["ident"], dma="c5")
        add("pool", lambda: nc.gpsimd.memset(self.aext[:, :, :].rearrange("p a b -> p (a b)"), 0.0),
            writes=[("aext", g) for g in range(4)])
        add("pool", lambda: nc.gpsimd.memset(self.gext[:, :, :].rearrange("p a b -> p (a b)"), 0.0),
            writes=[("gext", g) for g in range(4)])
        add("pool", lambda: nc.gpsimd.memset(self.pscr[:, :, :].rearrange("p a b -> p (a b)"), 0.0),
            writes=[("pscr", 0), ("pscr", 1)])
        add("dve", lambda: nc.vector.tensor_copy(out=self.identb[:, :], in_=self.ident[:, :]),
            reads=["ident"], writes=["identb"])
        add("pool", lambda: nc.gpsimd.memset(self.stA[:, :, :, :].rearrange("p a b c -> p (a b c)"), 0.0), writes=["stA"])
        add("pool", lambda: nc.gpsimd.memset(self.stG[:, :, :, :].rearrange("p a b c -> p (a b c)"), 0.0), writes=["stG"])
        add("pool", lambda: nc.gpsimd.memset(self.stb[:, :], 0.0), writes=["stK", "stV"])

    def run_tile(self, tname):
        if tname == "samp":
            N, kind = NS, "samp"
            xsrc, ydst = self.xs, self.ys
        elif tname == "halo":
            N, kind = TT, "halo"
            xsrc, ydst = self.xh, None
        else:
            i = int(tname[1])
            N, kind = TT, "main"
            xsrc, ydst = self.xm[i * TT:(i + 1) * TT, :], self.ym[i * TT:(i + 1) * TT, :]
        self.N = N
        self.kind = kind
        self.tname = tname
        self.last_main = (tname == "m3")
        self.c0 = 0
        if getattr(self, "stats_ready", False):
            self.P.reserved.discard(self.sbank)
            self.stats_ready = False
        self.load_x(xsrc, N)
        for l in range(self.nl):
            if kind == "halo" and TRI_HALO:
                self.c0 = 128 * l
                self.N = TT - 128 * l
            self.layer(l)
        if self.dbg:
            nc = self.nc
            self.add("sp", lambda: nc.sync.dma_start(out=self.dbg_d, in_=self.h[:, :, :].rearrange("p a b -> p (a b)")),
                     reads=[("h", k) for k in range(8)], dma="dbg")
        if ydst is not None:
            self.final_norm(ydst, N)

    def load_x(self, xsrc, N):
        nc = self.nc
        add = self.add
        for tb in range(N // 128):
            xin = self.cy[:, 2 * (tb % 2):2 * (tb % 2) + 2, :].rearrange("p a b -> p (a b)")
            tk = [("cy", 2 * (tb % 2)), ("cy", 2 * (tb % 2) + 1)]
            add("sp", lambda xin=xin, tb=tb: nc.sync.dma_start(out=xin, in_=xsrc[tb * 128:(tb + 1) * 128, :]),
                writes=tk, dma=("xin", tb % 2))
            for hh in range(2):
                b = self.P.bank()
                for kk in range(4):
                    k = hh * 4 + kk
                    add("pe", lambda b=b, kk=kk, k=k, xin=xin: nc.tensor.transpose(
                        self.ps[b][:, kk * 128:(kk + 1) * 128], xin[:, k * 128:(k + 1) * 128], self.ident[:, :]),
                        reads=tk + ["ident"], writes=[("ps", b)])
                add("act", lambda b=b, hh=hh, tb=tb: nc.scalar.copy(
                    out=self.h[:, hh * 4:(hh + 1) * 4, tb * 128:(tb + 1) * 128],
                    in_=self.ps[b][:, :].rearrange("p (a b) -> p a b", b=128)),
                    reads=[("ps", b)], writes=[("h", hh * 4 + kk) for kk in range(4)])

    def stat_begin(self):
        self.sbank = self.P.bank()
        self.P.reserved.add(self.sbank)
        self.stat_n = 0
        self.stat_q = []
        self.stat_N = self.N

    def stat_chunk(self, k):
        nc = self.nc
        N = self.N
        i = self.stat_n
        self.stat_n += 1
        hk = self.h[:, k, self.c0:self.c0 + N]
        b = self.sbank
        self.add("act", lambda: nc.scalar.activation(out=self.sq[:, i % 2, 0:N], in_=hk, func=AF.Square),
                 reads=[("h", k)], writes=[("sq", i % 2)])
        self.stat_q.append(lambda: self.add(
            "pe", lambda: nc.tensor.matmul(self.ps[b][:, 0:N], self.onesb[:, :], self.sq[:, i % 2, 0:N],
                                           start=(i == 0), stop=(i == 7)),
            reads=[("sq", i % 2), "onesb"], writes=[("ps", b)]))

    def stat_pe(self, all_=False):
        while self.stat_q:
            self.stat_q.pop(0)()
            if not all_:
                break

    def stat_end(self):
        self.stat_pe(all_=True)
        self.stats_ready = True

    def rmsnorm_stats(self, N):
        nc = self.nc
        add = self.add
        if getattr(self, "stats_ready", False):
            self.stats_ready = False
            b = self.sbank
            off = self.stat_N - N
        else:
            off = 0
            b = self.P.bank()
            for k in range(8):
                hk = self.h[:, k, self.c0:self.c0 + N]
                add("act", lambda k=k, hk=hk: nc.scalar.activation(out=self.sq[:, k % 2, 0:N], in_=hk, func=AF.Square),
                    reads=[("h", k)], writes=[("sq", k % 2)])
                add("pe", lambda k=k, b=b: nc.tensor.matmul(self.ps[b][:, 0:N], self.onesb[:, :], self.sq[:, k % 2, 0:N],
                                                            start=(k == 0), stop=(k == 7)),
                    reads=[("sq", k % 2), "onesb"], writes=[("ps", b)])
        add("act", lambda b=b: nc.scalar.activation(out=self.rs[:, 0:N], in_=self.ps[b][:, off:off + N], func=AF.Ln,
                                                    scale=1.0 / D, bias=EPS),
            reads=[("ps", b)], writes=["rs"])
        add("act", lambda: nc.scalar.activation(out=self.rs[:, 0:N], in_=self.rs[:, 0:N], func=AF.Exp, scale=-0.5),
            reads=["rs"], writes=["rs"])
        self.P.reserved.discard(b)

    def rmsnorm(self, N, gcol):
        nc = self.nc
        self.rmsnorm_stats(N)
        for k in range(8):
            hk = self.h[:, k, self.c0:self.c0 + N]
            self.add("dve", lambda k=k, hk=hk: nc.vector.scalar_tensor_tensor(
                out=self.u[:, k, 0:N], in0=hk, scalar=self.pp[:, gcol + k:gcol + k + 1],
                in1=self.rs[:, 0:N], op0=ALU.mult, op1=ALU.mult),
                reads=[("h", k), "rs", "pp"], writes=[("u", k)])

    def final_norm(self, ydst, N):
        nc = self.nc
        add = self.add
        self.rmsnorm_stats(N)
        for k in range(8):
            add("dve", lambda k=k: nc.vector.scalar_tensor_tensor(
                out=self.h[:, k, 0:N], in0=self.h[:, k, 0:N], scalar=self.pp[:, PB_NF + k:PB_NF + k + 1],
                in1=self.rs[:, 0:N], op0=ALU.mult, op1=ALU.mult),
                reads=[("h", k), "rs", "pp"], writes=[("h", k)])
        for tb in range(N // 128):
            yo = self.cy[:, 2 * (tb % 2):2 * (tb % 2) + 2, :].rearrange("p a b -> p (a b)")
            tk = [("cy", 2 * (tb % 2)), ("cy", 2 * (tb % 2) + 1)]
            for hh in range(2):
                b = self.P.bank()
                for kk in range(4):
                    k = hh * 4 + kk
                    add("pe", lambda b=b, kk=kk, k=k, tb=tb: nc.tensor.transpose(
                        self.ps[b][:, kk * 128:(kk + 1) * 128], self.h[:, k, tb * 128:(tb + 1) * 128], self.ident[:, :]),
                        reads=[("h", k), "ident"], writes=[("ps", b)])
                add("act", lambda b=b, hh=hh, yo=yo: nc.scalar.copy(out=yo[:, hh * 512:(hh + 1) * 512], in_=self.ps[b][:, :]),
                    reads=[("ps", b)], writes=[tk[hh]])
            add("sp", lambda yo=yo, tb=tb: nc.sync.dma_start(out=ydst[tb * 128:(tb + 1) * 128, :], in_=yo),
                reads=tk, dma=("yout", tb % 2))

    def a_tok(self, ap2d):
        if self.kind == "samp":
            return ap2d[:, 0:NSEQ * 24].rearrange("p (b i) -> p b i", i=24)[:, :, 16:24]
        return ap2d[:, 16:16 + self.N]

    def g_tok(self, ap2d, off):
        if self.kind == "samp":
            return ap2d[:, 0:NSEQ * 40].rearrange("p (b i) -> p b i", i=40)[:, :, off:off + 8]
        return ap2d[:, off:off + self.N]

    def tokv(self, ap2d):
        if self.kind == "samp":
            return ap2d[:, 0:NS].rearrange("p (b t) -> p b t", t=8)
        return ap2d[:, 0:self.N]

    def layer(self, l):
        nc = self.nc
        add = self.add
        N = self.N
        kind = self.kind
        pb = l * PL
        samp = kind == "samp"
        c0 = self.c0
        tmask = self.tokmask[:, c0:c0 + N]
        WA = NSEQ * 24 if samp else 16 + N
        WG = NSEQ * 40 if samp else 32 + N

        def a_tok(ap2d):
            if samp:
                return ap2d[:, 0:NSEQ * 24].rearrange("p (b i) -> p b i", i=24)[:, :, 16:24]
            return ap2d[:, 16:16 + N]

        def g_tok(ap2d, off):
            if samp:
                return ap2d[:, 0:NSEQ * 40].rearrange("p (b i) -> p b i", i=40)[:, :, off:off + 8]
            return ap2d[:, off:off + N]

        def tokv(ap2d):
            if samp:
                return ap2d[:, 0:NS].rearrange("p (b t) -> p b t", t=8)
            return ap2d[:, 0:N]

        add("sp", lambda: nc.sync.dma_start(out=self.bkv[:, :], in_=self.bkv_d[l]), writes=["bkv"], dma="bkv")
        add("pool", lambda: nc.gpsimd.dma_start(out=self.wpl[:, :, :], in_=self.w_pool[l].rearrange("g c d -> c g d")),
            writes=["wpl"], dma="wpl")
        add("dve", lambda: nc.vector.tensor_scalar(self.bq8[:, :], self.pp[:, pb + PB_IN + 12:pb + PB_IN + 20], 0.125, None, ALU.mult),
            reads=["pp"], writes=["bq8"])
        add("act", lambda: nc.scalar.activation(out=self.sinkE[:, :], in_=self.pp[:, pb + PB_SINK:pb + PB_SINK + 8], func=AF.Exp),
            reads=["pp"], writes=["sinkE"])
        if not samp:
            for J in range(2):
                add("dve", lambda J=J: nc.vector.tensor_copy(
                    out=self.sinkB[:, J, :].rearrange("p (g q) -> p g q", q=128),
                    in_=self.sinkE[:, J * 4:(J + 1) * 4].unsqueeze(2).broadcast_to([128, 4, 128])),
                    reads=["sinkE"], writes=[("sinkB", J)])
            add("pool", lambda: nc.gpsimd.tensor_copy(out=self.aext[:, :, 0:16], in_=self.stA[:, l, :, :]),
                reads=["stA"], writes=[("aext", g) for g in range(4)])
            add("pool", lambda: nc.gpsimd.tensor_copy(out=self.gext[:, :, 0:32], in_=self.stG[:, l, :, :]),
                reads=["stG"], writes=[("gext", c) for c in range(4)])
            add("pool", lambda: nc.gpsimd.tensor_copy(out=self.kT[:, :, 0:128], in_=self.stK[:, l, :, :]),
                reads=["stK"], writes=[("kT", 0), ("kT", 1)])
            add("pool", lambda: nc.gpsimd.tensor_copy(out=self.v[:, 0, :], in_=self.stV[:, l, :]),
                reads=["stV"], writes=[("v", 0)])
        elif "load" not in SKIP:
            self.samp_load_states(l)

        self.rmsnorm(N, pb + PB_NMIX)

        def proj_chunk(W, wtok, c0, evac):
            b = self.P.bank()
            for k in range(8):
                add("pe", lambda k=k, b=b: nc.tensor.matmul(self.ps[b][:, 0:N], W[:, k, c0:c0 + 128], self.u[:, k, 0:N],
                                                            start=(k == 0), stop=(k == 7)),
                    reads=[wtok, ("u", k)], writes=[("ps", b)])
            evac(b)

        W, wt = self.wget(("win", 0))
        for g in range(4):
            def ev(b, g=g):
                add("act", lambda: nc.scalar.activation(out=a_tok(self.aext[:, g, :]), in_=tokv(self.ps[b][:, :]),
                                                        func=AF.Identity, bias=self.pp[:, pb + PB_IN + g:pb + PB_IN + g + 1]),
                    reads=[("ps", b), "pp"], writes=[("aext", g)])
                if kind == "halo":
                    add("pool", lambda: nc.gpsimd.tensor_tensor(out=self.aext[:, g, 16:16 + N], in0=self.aext[:, g, 16:16 + N],
                                                                in1=tmask, op=ALU.mult),
                        reads=[("aext", g), "tokmask"], writes=[("aext", g)])
                if samp:
                    add("act", lambda: nc.scalar.activation(out=self.anew[:, g, 0:N], in_=self.ps[b][:, 0:N], func=AF.Identity,
                                                            bias=self.pp[:, pb + PB_IN + g:pb + PB_IN + g + 1]),
                        reads=[("ps", b), "pp"], writes=[("sinkB", 0)])
            proj_chunk(W, wt, g * 128, ev)
        self.wrel(("win", 0))
        Wv, wvt = self.wget(("win", 1))
        Wgt, wgtt = self.wget(("win", 2))
        for c in range(4):
            def evg(b, c=c):
                add("act", lambda: nc.scalar.activation(out=self.sig[:, c % 2, 0:N], in_=self.ps[b][:, 0:N], func=AF.Sigmoid,
                                                        bias=self.pp[:, pb + PB_IN + 8 + c:pb + PB_IN + 9 + c]),
                    reads=[("ps", b), "pp"], writes=[("sig", c % 2)])
            proj_chunk(Wgt, wgtt, c * 128, evg)

            def evv(b, c=c):
                add("dve", lambda: nc.vector.scalar_tensor_tensor(
                    out=g_tok(self.gext[:, c, :], 32), in0=tokv(self.ps[b][:, :]),
                    scalar=self.pp[:, pb + PB_IN + 4 + c:pb + PB_IN + 5 + c], in1=tokv(self.sig[:, c % 2, :]),
                    op0=ALU.add, op1=ALU.mult),
                    reads=[("ps", b), ("sig", c % 2), "pp"], writes=[("gext", c)])
                if kind == "halo":
                    add("pool", lambda: nc.gpsimd.tensor_tensor(out=self.gext[:, c, 32:32 + N], in0=self.gext[:, c, 32:32 + N],
                                                                in1=tmask, op=ALU.mult),
                        reads=[("gext", c), "tokmask"], writes=[("gext", c)])
                if samp:
                    add("pool", lambda: nc.gpsimd.tensor_copy(out=tokv(self.gnew[:, c, :]), in_=g_tok(self.gext[:, c, :], 32)),
                        reads=[("gext", c)], writes=[("sinkB", 1)])
                add("act", lambda: nc.scalar.copy(out=self.gbf[:, c, 0:WG], in_=self.gext[:, c, 0:WG]),
                    reads=[("gext", c)], writes=self.tgbf(c))
            proj_chunk(Wv, wvt, c * 128, evv)
        self.wrel(("win", 1))
        self.wrel(("win", 2))
        for hh in range(2):
            W, wt = self.wget(("win", 3 + hh))
            for jj in range(4):
                j = hh * 4 + jj

                def ev(b, j=j):
                    add("act", lambda: nc.scalar.activation(out=self.q[:, j, 0:N], in_=self.ps[b][:, 0:N], func=AF.Identity,
                                                            scale=0.125, bias=self.bq8[:, j:j + 1]),
                        reads=[("ps", b), "bq8"], writes=[self.tq(j)])
                proj_chunk(W, wt, jj * 128, ev)
            self.wrel(("win", 3 + hh))
        W, wt = self.wget(("win", 5))
        koff = 0 if samp else 128
        for J in range(2):
            def ev(b, J=J):
                add("act", lambda: nc.scalar.activation(out=self.kT[:, J, koff:koff + N], in_=self.ps[b][:, 0:N], func=AF.Identity,
                                                        bias=self.pp[:, pb + PB_IN + 20 + J:pb + PB_IN + 21 + J]),
                    reads=[("ps", b), "pp"], writes=[("kT", J)])
                if samp or self.last_main:
                    add("act", lambda: nc.scalar.activation(out=self.kf[:, J, :], in_=self.ps[b][:, N - 128:N], func=AF.Identity,
                                                            bias=self.pp[:, pb + PB_IN + 20 + J:pb + PB_IN + 21 + J]),
                        reads=[("ps", b), "pp"], writes=[("kf", J)])
            proj_chunk(W, wt, J * 128, ev)
        if not samp:
            for tb in range(N // 128):
                b = self.P.bank()
                for k in range(8):
                    add("pe", lambda k=k, b=b, tb=tb: nc.tensor.matmul(self.ps[b][:, 0:256], self.u[:, k, tb * 128:(tb + 1) * 128],
                                                                       W[:, k, 256:512], start=(k == 0), stop=(k == 7)),
                        reads=[wt, ("u", k)], writes=[("ps", b)])
                add("dve", lambda b=b, tb=tb: nc.vector.tensor_tensor(out=self.v[:, 1 + tb, :], in0=self.ps[b][:, 0:256],
                                                                      in1=self.bkv[:, 256:512], op=ALU.add),
                    reads=[("ps", b), "bkv"], writes=[("v", 1 + tb)])
                if self.last_main and tb == 3:
                    add("dve", lambda b=b: nc.vector.tensor_tensor(out=self.ost[:, 1, 0:256], in0=self.ps[b][:, 0:256],
                                                                   in1=self.bkv[:, 256:512], op=ALU.add),
                        reads=[("ps", b), "bkv"], writes=[("pscr", 1)])
                    add("sp", lambda: nc.sync.dma_start(out=self.v_p[l], in_=self.ost[:, 1, 0:256]),
                        reads=[("pscr", 1)], dma=("ostd", 1))
        elif "sv" not in SKIP:
            for bp in range(NSEQ // 2):
                b = self.P.bank()
                for bb in range(2):
                    sq_ = bp * 2 + bb
                    for k in range(8):
                        add("pe", lambda k=k, b=b, bb=bb, sq_=sq_: nc.tensor.matmul(
                            self.ps[b][0:8, bb * 256:(bb + 1) * 256], self.u[:, k, sq_ * 8:(sq_ + 1) * 8], W[:, k, 256:512],
                            start=(k == 0), stop=(k == 7)),
                            reads=[wt, ("u", k)], writes=[("ps", b)])
                add("dve", lambda b=b: nc.vector.tensor_tensor(
                    out=self.vnewf[:, :, :], in0=self.ps[b][0:8, :].rearrange("p (a f) -> p a f", f=256),
                    in1=self.bkv[0:8, 256:512].unsqueeze(1).broadcast_to([8, 2, 256]), op=ALU.add),
                    reads=[("ps", b), "bkv"], writes=[("rD", 1)])
                add("act", lambda bp=bp: nc.scalar.copy(out=self.vnew[:, bp * 2:bp * 2 + 2, :], in_=self.vnewf[:, :, :]),
                    reads=[("rD", 1)], writes=["vnew"])
                add("sp", lambda bp=bp: nc.sync.dma_start(
                    out=self.v_s[l, bp * 2:bp * 2 + 2, 120:128, :].rearrange("b t f -> t b f"), in_=self.vnewf[:, :, :]),
                    reads=[("rD", 1)], dma="vs")
        self.wrel(("win", 5))

        if not samp:
            add("pool", lambda: nc.gpsimd.tensor_copy(out=self.stA[:, l, :, :], in_=self.aext[:, :, N:N + 16]),
                reads=[("aext", g) for g in range(4)], writes=["stA"])
            add("pool", lambda: nc.gpsimd.tensor_copy(out=self.stG[:, l, :, :], in_=self.gext[:, :, N:N + 32]),
                reads=[("gext", c) for c in range(4)], writes=["stG"])
            if self.last_main:
                self.prompt_state_out(l)
        elif "out" not in SKIP:
            self.samp_state_out(l)

        nd = [0]
        cb = []
        for c in range(4):
            b = self.P.bank()
            cb.append(b)
            for j in range(31):
                ds = nd[0] % NDIAG
                nd[0] += 1
                if j % 2 == 0:
                    add("pool", lambda ds=ds, c=c, j=j: nc.gpsimd.tensor_scalar(
                        self.diag[:, ds, :], self.identb[:, :], self.pp[:, pb + PB_CW + c * 31 + j:pb + PB_CW + c * 31 + j + 1],
                        1.0, ALU.mult, ALU.mult),
                        reads=["identb", "pp"], writes=[("diag", ds)])
                else:
                    add("act", lambda ds=ds, c=c, j=j: nc.scalar.activation(
                        out=self.diag[:, ds, :], in_=self.identb[:, :], func=AF.Identity,
                        scale=self.pp[:, pb + PB_CW + c * 31 + j:pb + PB_CW + c * 31 + j + 1]),
                        reads=["identb", "pp"], writes=[("diag", ds)])
                add("pe", lambda ds=ds, c=c, j=j, b=b: nc.tensor.matmul(
                    tokv(self.ps[b][:, :]), self.diag[:, ds, :], g_tok(self.gbf[:, c, :], 2 + j),
                    start=(j == 0), stop=(j == 30)),
                    reads=[("diag", ds)] + self.tgbf(c), writes=[("ps", b)])
        bm = self.P.bank()
        bv = self.P.bank()
        for c in range(4):
            b = cb[c]
            add("act", lambda b=b, c=c: nc.scalar.activation(out=self.cy[:, c, 0:N], in_=self.ps[b][:, 0:N], func=AF.Identity,
                                                             bias=self.pp[:, pb + PB_CB + c:pb + PB_CB + c + 1]),
                reads=[("ps", b), "pp"], writes=[("cy", c)])
            add("act", lambda c=c: nc.scalar.activation(out=self.sq[:, 0, 0:N], in_=self.cy[:, c, 0:N], func=AF.Square),
                reads=[("cy", c)], writes=[("sq", 0)])
            add("pool", lambda c=c: nc.gpsimd.tensor_copy(out=self.sq[:, 1, 0:N], in_=self.cy[:, c, 0:N]),
                reads=[("cy", c)], writes=[("sq", 1)])
            add("pe", lambda c=c: nc.tensor.matmul(self.ps[bm][:, 0:N], self.onesb[:, :], self.sq[:, 1, 0:N],
                                                   start=(c == 0), stop=(c == 3)),
                reads=[("sq", 1), "onesb"], writes=[("ps", bm)])
            add("pe", lambda c=c: nc.tensor.matmul(self.ps[bv][:, 0:N], self.onesb[:, :], self.sq[:, 0, 0:N],
                                                   start=(c == 0), stop=(c == 3)),
                reads=[("sq", 0), "onesb"], writes=[("ps", bv)])
        mu = self.mt[:, 0, 0:N]
        var = self.mt[:, 1, 0:N]
        add("dve", lambda: nc.vector.tensor_scalar(mu, self.ps[bm][:, 0:N], 1.0 / 512, None, ALU.mult),
            reads=[("ps", bm)], writes=[("mt", 0)])
        add("dve", lambda: nc.vector.tensor_tensor(out=self.rs[:, 0:N], in0=mu, in1=mu, op=ALU.mult),
            reads=[("mt", 0)], writes=["rs"])
        add("dve", lambda: nc.vector.scalar_tensor_tensor(out=var, in0=self.ps[bv][:, 0:N], scalar=1.0 / 512, in1=self.rs[:, 0:N],
                                                          op0=ALU.mult, op1=ALU.subtract),
            reads=[("ps", bv), "rs"], writes=[("mt", 1)])
        add("dve", lambda: nc.vector.tensor_scalar(var, var, 0.0, None, ALU.max), reads=[("mt", 1)], writes=[("mt", 1)])
        add("act", lambda: nc.scalar.activation(out=var, in_=var, func=AF.Ln, bias=EPS), reads=[("mt", 1)], writes=[("mt", 1)])
        add("act", lambda: nc.scalar.activation(out=var, in_=var, func=AF.Exp, scale=-0.5), reads=[("mt", 1)], writes=[("mt", 1)])
        for c in range(4):
            add("pool", lambda c=c: nc.gpsimd.tensor_tensor(out=self.cy[:, c, 0:N], in0=self.cy[:, c, 0:N], in1=mu, op=ALU.subtract),
                reads=[("cy", c), ("mt", 0)], writes=[("cy", c)])
            add("dve", lambda c=c: nc.vector.tensor_tensor(out=self.cy[:, c, 0:N], in0=self.cy[:, c, 0:N], in1=var, op=ALU.mult),
                reads=[("cy", c), ("mt", 1)], writes=[("cy", c)])
            add("act", lambda c=c: nc.scalar.activation(out=self.s[:, c, 0:N], in_=self.cy[:, c, 0:N], func=AF.Silu,
                                                        scale=self.pp[:, pb + PB_NG + c:pb + PB_NG + c + 1],
                                                        bias=self.pp[:, pb + PB_NB + c:pb + PB_NB + c + 1]),
                reads=[("cy", c), "pp"], writes=[self.ts_(c)])

        for g in range(4):
            ext = self.aext[:, g, :]
            src = ext
            src_tok = [("aext", g)]
            sh = 1
            i = 0
            while sh < WINS[g]:
                dst = self.pscr[:, i % 2, :]
                add("pool", lambda dst=dst, src=src, sh=sh: nc.gpsimd.tensor_tensor(
                    out=dst[:, sh:WA], in0=src[:, sh:WA], in1=src[:, 0:WA - sh], op=ALU.add),
                    reads=src_tok, writes=[("pscr", i % 2)])
                src = dst
                src_tok = [("pscr", i % 2)]
                sh *= 2
                i += 1
            add("dve", lambda src=src, ext=ext, g=g: nc.vector.scalar_tensor_tensor(
                out=tokv(self.r[:, g, :]), in0=a_tok(src), scalar=1.0 / WINS[g], in1=a_tok(ext),
                op0=ALU.mult, op1=ALU.subtract),
                reads=src_tok + [("aext", g)], writes=[self.tr(g)])
            if kind == "halo":
                add("dve", lambda src=src, g=g: nc.vector.tensor_tensor(out=self.rs[:, 0:16], in0=src[:, N:N + 16],
                                                                        in1=self.invc[:, g, :], op=ALU.mult),
                    reads=src_tok + ["invc"], writes=["rs"])
                add("dve", lambda ext=ext, g=g: nc.vector.tensor_tensor(out=self.r[:, g, N - 16:N], in0=self.rs[:, 0:16],
                                                                        in1=ext[:, N:N + 16], op=ALU.subtract),
                    reads=["rs", ("aext", g)], writes=[self.tr(g)])

        if samp:
            if "attn" not in SKIP:
                self.attn_sample(l)
        else:
            self.attn_prompt(l)
            add("pool", lambda: nc.gpsimd.tensor_copy(out=self.stK[:, l, :, :], in_=self.kT[:, :, N:N + 128]),
                reads=[("kT", 0), ("kT", 1)], writes=["stK"])
            add("pool", lambda: nc.gpsimd.tensor_copy(out=self.stV[:, l, :], in_=self.v[:, N // 128, :]),
                reads=[("v", N // 128)], writes=["stV"])

        def gate_chunk(Wg, wgt, m, bi):
            bg = self.P.bank()
            for k in range(8):
                add("pe", lambda k=k, bg=bg: nc.tensor.matmul(self.ps[bg][:, 0:N], Wg[:, k, (m % 4) * 128:(m % 4 + 1) * 128],
                                                              self.u[:, k, 0:N], start=(k == 0), stop=(k == 7)),
                    reads=[wgt, ("u", k)], writes=[("ps", bg)])
            col = pb + PB_IN + 24 + bi * 8 + m
            add("act", lambda bg=bg: nc.scalar.activation(out=self.sig[:, m % 2, 0:N], in_=self.ps[bg][:, 0:N], func=AF.Sigmoid,
                                                          bias=self.pp[:, col:col + 1]),
                reads=[("ps", bg), "pp"], writes=[("sig", m % 2)])

        Wp, wpt = self.wpl, "wpl"
        for hh in range(2):
            Wg, wgt = self.wget(("win", 6 + hh))
            for mm in range(4):
                m = hh * 4 + mm
                gate_chunk(Wg, wgt, m, 0)
                by = self.P.bank()
                add("pe", lambda by=by, m=m: nc.tensor.matmul(self.ps[by][:, 0:N], Wp[:, m // 2, (m % 2) * 128:(m % 2 + 1) * 128],
                                                              self.r[:, m // 2, 0:N], start=True, stop=True),
                    reads=[wpt, self.tr(m // 2)], writes=[("ps", by)])
                ma, mtok = self.macc(m, N)
                add("dve", lambda by=by, m=m, ma=ma: nc.vector.scalar_tensor_tensor(
                    out=ma, in0=self.ps[by][:, 0:N], scalar=self.pp[:, pb + PB_PSC + m:pb + PB_PSC + m + 1],
                    in1=self.sig[:, m % 2, 0:N], op0=ALU.mult, op1=ALU.mult),
                    reads=[("ps", by), ("sig", m % 2), "pp"], writes=[mtok])
            self.wrel(("win", 6 + hh))
        for hh in range(2):
            Wc, wct = self.wget(("wco", hh))
            Wg, wgt = self.wget(("win", 8 + hh))
            for mm in range(4):
                m = hh * 4 + mm
                gate_chunk(Wg, wgt, m, 1)
                by = self.P.bank()
                for c in range(4):
                    add("pe", lambda by=by, mm=mm, c=c, Wc=Wc: nc.tensor.matmul(self.ps[by][:, 0:N], Wc[:, c, mm * 128:(mm + 1) * 128],
                                                                       self.s[:, c, 0:N], start=(c == 0), stop=(c == 3)),
                        reads=[wct, self.ts_(c)], writes=[("ps", by)])
                ma, mtok = self.macc(m, N)
                add("dve", lambda by=by, m=m: nc.vector.tensor_tensor(out=self.mt[:, m % 2, 0:N], in0=self.ps[by][:, 0:N],
                                                                      in1=self.sig[:, m % 2, 0:N], op=ALU.mult),
                    reads=[("ps", by), ("sig", m % 2)], writes=[("mt", m % 2)])
                add("pool", lambda m=m, ma=ma: nc.gpsimd.tensor_tensor(out=ma, in0=ma, in1=self.mt[:, m % 2, 0:N], op=ALU.add),
                    reads=[mtok, ("mt", m % 2)], writes=[mtok])
            self.wrel(("win", 8 + hh))
            self.wrel(("wco", hh))
        for hh in range(2):
            Wa, wat = self.wget(("wao", hh))
            Wg, wgt = self.wget(("win", 10 + hh))
            for mm in range(4):
                m = hh * 4 + mm
                gate_chunk(Wg, wgt, m, 2)
                by = self.P.bank()
                for k in range(8):
                    add("pe", lambda by=by, mm=mm, k=k, Wa=Wa: nc.tensor.matmul(self.ps[by][:, 0:N], Wa[:, k, mm * 128:(mm + 1) * 128],
                                                                         self.q[:, k, 0:N], start=(k == 0), stop=(k == 7)),
                        reads=[wat, self.tq(k)], writes=[("ps", by)])
                ma, mtok = self.macc(m, N)
                add("dve", lambda by=by, m=m: nc.vector.tensor_tensor(out=self.mt[:, m % 2, 0:N], in0=self.ps[by][:, 0:N],
                                                                      in1=self.sig[:, m % 2, 0:N], op=ALU.mult),
                    reads=[("ps", by), ("sig", m % 2)], writes=[("mt", m % 2)])
                add("pool", lambda m=m, ma=ma: nc.gpsimd.tensor_tensor(out=self.merged[:, m, 0:N], in0=ma, in1=self.mt[:, m % 2, 0:N],
                                                                       op=ALU.add),
                    reads=[mtok, ("mt", m % 2)], writes=[self.tmg(m)])
            self.wrel(("wao", hh))
            self.wrel(("win", 10 + hh))
        self.stat_begin()
        for hh in range(2):
            Wo, wot = self.wget(("wo", hh))
            for mm in range(4):
                m = hh * 4 + mm
                b = self.P.bank()
                for k in range(8):
                    add("pe", lambda b=b, mm=mm, k=k, Wo=Wo: nc.tensor.matmul(self.ps[b][:, 0:N], Wo[:, k, mm * 128:(mm + 1) * 128],
                                                                       self.merged[:, k, 0:N], start=(k == 0), stop=(k == 7)),
                        reads=[wot, self.tmg(k)], writes=[("ps", b)])
                self.stat_pe()
                hm = self.h[:, m, self.c0:self.c0 + N]
                add("dve", lambda b=b, hm=hm: nc.vector.tensor_tensor(out=hm, in0=self.ps[b][:, 0:N], in1=hm, op=ALU.add),
                    reads=[("ps", b), ("h", m)], writes=[("h", m)])
                self.stat_chunk(m)
            self.wrel(("wo", hh))
        self.stat_end()
        self.rmsnorm(N, pb + PB_NMLP)
        for j in range(8):
            Wu, wut = self.wget(("wup", j))
            for ff in range(4):
                f = j * 4 + ff
                b = self.P.bank()
                for k in range(8):
                    add("pe", lambda b=b, ff=ff, k=k, Wu=Wu: nc.tensor.matmul(self.ps[b][:, 0:N], Wu[:, k, ff * 128:(ff + 1) * 128],
                                                                       self.u[:, k, 0:N], start=(k == 0), stop=(k == 7)),
                        reads=[wut, ("u", k)], writes=[("ps", b)])
                add("act", lambda b=b, f=f: nc.scalar.activation(out=self.relu[:, f % 2, 0:N], in_=self.ps[b][:, 0:N], func=AF.Relu),
                    reads=[("ps", b)], writes=[("relu", f % 2)])
                add("dve", lambda b=b, f=f: nc.vector.tensor_tensor(out=self.hid[:, f, 0:N], in0=self.ps[b][:, 0:N],
                                                                    in1=self.relu[:, f % 2, 0:N], op=ALU.mult),
                    reads=[("ps", b), ("relu", f % 2)], writes=[self.thid(f)])
            self.wrel(("wup", j))
        self.stat_begin()
        for m in range(8):
            Wd, wdt = self.wget(("wdn", m))
            b = self.P.bank()
            for f in range(32):
                add("pe", lambda b=b, f=f, Wd=Wd: nc.tensor.matmul(self.ps[b][:, 0:N], Wd[:, f, :], self.hid[:, f, 0:N],
                                                            start=(f == 0), stop=(f == 31)),
                    reads=[wdt, self.thid(f)], writes=[("ps", b)])
            self.stat_pe()
            hm = self.h[:, m, self.c0:self.c0 + N]
            add("dve", lambda b=b, hm=hm: nc.vector.tensor_tensor(out=hm, in0=self.ps[b][:, 0:N], in1=hm, op=ALU.add),
                reads=[("ps", b), ("h", m)], writes=[("h", m)])
            self.stat_chunk(m)
            self.wrel(("wdn", m))
        self.stat_end()

    def attn_prompt(self, l):
        nc = self.nc
        add = self.add
        N = self.N
        LA = 3
        halo_off = self.c0 // 128
        units = [(qb, J, hf, c) for qb in range(N // 128) for J in range(2) for c in range(2) for hf in range(2)]
        info = {}
        grp = {}
        for gi, (qb, J) in enumerate([(qb, J) for qb in range(N // 128) for J in range(2)]):
            grp[(qb, J)] = (4, 5) if gi % 2 == 0 else (6, 7)

        def emit_qk(i):
            qb, J, hf, c = units[i]
            kv = 2 * J + hf
            qs = slice(qb * 128, (qb + 1) * 128)
            ps_ = slice(hf * 64, (hf + 1) * 64)
            kb = qb + c
            bl = i % 4
            add("pe", lambda: nc.tensor.matmul(
                self.ps[bl][:, :], self.identb[:, :],
                self.biasT[:, c, kv * 4:(kv + 1) * 4, :].rearrange("p g q -> p (g q)"), start=True, stop=False),
                reads=["identb", "biasT"], writes=[("ps", bl)])
            add("pe", lambda: nc.tensor.matmul(
                self.ps[bl][:, :].rearrange("p (g q) -> p g q", q=128),
                self.kT[ps_, J, kb * 128:(kb + 1) * 128], self.q[ps_, J * 4:(J + 1) * 4, qs], start=False, stop=True),
                reads=[("kT", J)] + [self.tq(J * 4 + g) for g in range(4)], writes=[("ps", bl)])
            info[i] = (bl, kb)

        def emit_soft(i):
            bl, kb = info[i]
            ai = i % 4
            mcol = None
            if self.kind == "halo":
                mcol = 4 if kb == 0 else kb - 1 + halo_off
            elif self.tname == "m0" and kb == 0:
                mcol = 3
            if mcol is None:
                add("act", lambda: nc.scalar.activation(out=self.pT[:, ai, :], in_=self.ps[bl][:, :], func=AF.Exp),
                    reads=[("ps", bl)], writes=[("pT", ai)])
            else:
                add("act", lambda: nc.scalar.activation(
                    out=self.pT[:, ai, :], in_=self.ps[bl][:, :], func=AF.Exp, bias=self.kmask[:, mcol:mcol + 1]),
                    reads=[("ps", bl), "kmask"], writes=[("pT", ai)])

        def emit_pv(i):
            qb, J, hf, c = units[i]
            kv = 2 * J + hf
            ps_ = slice(hf * 64, (hf + 1) * 64)
            bo, bd = grp[(qb, J)]
            bl, kb = info[i]
            ai = i % 4
            add("pe", lambda: nc.tensor.matmul(
                self.ps[bo][ps_, :], self.v[:, kb, kv * 64:(kv + 1) * 64], self.pT[:, ai, :], start=(c == 0), stop=(c == 1)),
                reads=[("v", kb), ("pT", ai)], writes=[("ps", bo)])
            add("pe", lambda: nc.tensor.matmul(
                self.ps[bd][ps_, :], self.onesb[:, 0:64], self.pT[:, ai, :], start=(c == 0), stop=(c == 1)),
                reads=["onesb", ("pT", ai)], writes=[("ps", bd)])

        def emit_norm(qb, J):
            bo, bd = grp[(qb, J)]
            qs = slice(qb * 128, (qb + 1) * 128)
            add("dve", lambda: nc.vector.tensor_tensor(out=self.rD[:, J, :], in0=self.ps[bd][:, :], in1=self.sinkB[:, J, :], op=ALU.add),
                reads=[("ps", bd), ("sinkB", J)], writes=[("rD", J)])
            add("act", lambda: nc.scalar.activation(out=self.rD[:, J, :], in_=self.rD[:, J, :], func=AF.Ln),
                reads=[("rD", J)], writes=[("rD", J)])
            add("act", lambda: nc.scalar.activation(out=self.rD[:, J, :], in_=self.rD[:, J, :], func=AF.Exp, scale=-1.0),
                reads=[("rD", J)], writes=[("rD", J)])
            add("dve", lambda: nc.vector.tensor_tensor(
                out=self.q[:, J * 4:(J + 1) * 4, qs], in0=self.ps[bo][:, :].rearrange("p (g q) -> p g q", q=128),
                in1=self.rD[:, J, :].rearrange("p (g q) -> p g q", q=128), op=ALU.mult),
                reads=[("ps", bo), ("rD", J)], writes=[self.tq(J * 4 + g) for g in range(4)])

        n = len(units)
        pending = None
        for i in range(min(LA, n)):
            emit_qk(i)
        for i in range(n):
            if i + LA < n:
                emit_qk(i + LA)
            emit_soft(i)
            if pending is not None:
                emit_norm(*pending)
                pending = None
            emit_pv(i)
            qb, J, hf, c = units[i]
            if hf == 1 and c == 1:
                pending = (qb, J)
        emit_norm(*pending)

    def prompt_state_out(self, l):
        nc = self.nc
        add = self.add
        N = self.N
        for (src, tokname, lo, n, dst, slot) in ((self.aext, "aext", N + 1, 15, self.pool_p, 0),
                                                 (self.gext, "gext", N + 2, 30, self.conv_p, 0)):
            b = self.P.bank()
            for g in range(4):
                add("pe", lambda b=b, g=g, src=src, lo=lo, n=n: nc.tensor.transpose(
                    self.ps[b][0:n, g * 128:(g + 1) * 128], src[:, g, lo:lo + n], self.ident[:, :]),
                    reads=[(tokname, g), "ident"], writes=[("ps", b)])
            add("act", lambda b=b, n=n: nc.scalar.copy(out=self.ost[0:n, 0, :], in_=self.ps[b][0:n, :]),
                reads=[("ps", b)], writes=[("pscr", 0)])
            add("sp", lambda dst=dst, n=n: nc.sync.dma_start(out=dst[l], in_=self.ost[0:n, 0, :]),
                reads=[("pscr", 0)], dma=("ostd", 0))
        b = self.P.bank()
        for J in range(2):
            add("pe", lambda b=b, J=J: nc.tensor.transpose(self.ps[b][:, J * 128:(J + 1) * 128], self.kf[:, J, :], self.ident[:, :]),
                reads=[("kf", J), "ident"], writes=[("ps", b)])
        add("act", lambda b=b: nc.scalar.copy(out=self.ost[:, 0, 0:256], in_=self.ps[b][:, 0:256]),
            reads=[("ps", b)], writes=[("pscr", 0)])
        add("sp", lambda: nc.sync.dma_start(out=self.k_p[l], in_=self.ost[:, 0, 0:256]), reads=[("pscr", 0)], dma=("ostd", 0))

    def samp_load_states(self, l):
        nc = self.nc
        add = self.add
        for rb in range(2):
            stg = self.ost[0:120, rb, :]
            add("sp", lambda rb=rb, stg=stg: nc.sync.dma_start(
                out=stg, in_=self.spool[l, rb * 8:(rb + 1) * 8].rearrange("b i f -> (b i) f")),
                writes=[("pscr", rb)], dma=("ostd", rb))
            b = self.P.bank()
            for g in range(4):
                add("pe", lambda b=b, g=g, stg=stg: nc.tensor.transpose(
                    self.ps[b][:, g * 128:g * 128 + 120], stg[:, g * 128:(g + 1) * 128], self.ident[0:120, 0:120]),
                    reads=[("pscr", rb), "ident"], writes=[("ps", b)])
            add("act", lambda b=b, rb=rb: nc.scalar.copy(
                out=self.aext[:, :, rb * 8 * 24:(rb + 1) * 8 * 24].rearrange("p g (b i) -> p g b i", i=24)[:, :, :, 1:16],
                in_=self.ps[b][:, :].rearrange("p (g x) -> p g x", x=128)[:, :, 0:120].rearrange("p g (b i) -> p g b i", i=15)),
                reads=[("ps", b)], writes=[("aext", g) for g in range(4)])
        for rb in range(4):
            stg = self.ost[0:120, rb % 2, :]
            add("sp", lambda rb=rb, stg=stg: nc.sync.dma_start(
                out=stg, in_=self.sconv[l, rb * 4:(rb + 1) * 4].rearrange("b i f -> (b i) f")),
                writes=[("pscr", rb % 2)], dma=("ostd", rb % 2))
            b = self.P.bank()
            for g in range(4):
                add("pe", lambda b=b, g=g, stg=stg: nc.tensor.transpose(
                    self.ps[b][:, g * 128:g * 128 + 120], stg[:, g * 128:(g + 1) * 128], self.ident[0:120, 0:120]),
                    reads=[("pscr", rb % 2), "ident"], writes=[("ps", b)])
            add("act", lambda b=b, rb=rb: nc.scalar.copy(
                out=self.gext[:, :, rb * 4 * 40:(rb + 1) * 4 * 40].rearrange("p g (b i) -> p g b i", i=40)[:, :, :, 2:32],
                in_=self.ps[b][:, :].rearrange("p (g x) -> p g x", x=128)[:, :, 0:120].rearrange("p g (b i) -> p g b i", i=30)),
                reads=[("ps", b)], writes=[("gext", g) for g in range(4)])

    def samp_state_out(self, l):
        nc = self.nc
        add = self.add
        add("sp", lambda: nc.sync.dma_start(out=self.pool_s[l, :, 0:7, :], in_=self.spool[l, :, 8:15, :]), dma="h2h0")
        add("sp", lambda: nc.sync.dma_start(out=self.conv_s[l, :, 0:22, :], in_=self.sconv[l, :, 8:30, :]), dma="h2h1")
        add("sp", lambda: nc.sync.dma_start(out=self.k_s[l, :, 0:120, :], in_=self.ck[l, :, 8:128, :]), dma="h2h2")
        add("sp", lambda: nc.sync.dma_start(out=self.v_s[l, :, 0:120, :], in_=self.cv[l, :, 8:128, :]), dma="h2h3")
        for (src, tokname, dst, r0, sidx) in ((self.anew, ("sinkB", 0), self.pool_s, 7, 0), (self.gnew, ("sinkB", 1), self.conv_s, 22, 1)):
            b = self.P.bank()
            for g in range(4):
                add("pe", lambda b=b, g=g, src=src: nc.tensor.transpose(self.ps[b][:, g * 128:(g + 1) * 128], src[:, g, :], self.ident[:, :]),
                    reads=[tokname, "ident"], writes=[("ps", b)])
            add("act", lambda b=b, sidx=sidx: nc.scalar.copy(out=self.ost[:, sidx, :], in_=self.ps[b][:, :]),
                reads=[("ps", b)], writes=[("pscr", sidx)])
            for sq_ in range(NSEQ):
                add("sp", lambda dst=dst, r0=r0, sq_=sq_, sidx=sidx: nc.sync.dma_start(
                    out=dst[l, sq_, r0:r0 + 8, :], in_=self.ost[sq_ * 8:(sq_ + 1) * 8, sidx, :]),
                    reads=[("pscr", sidx)], dma=("osts", sidx * 4 + sq_ % 4))
        b = self.P.bank()
        for J in range(2):
            add("pe", lambda b=b, J=J: nc.tensor.transpose(self.ps[b][:, J * 128:(J + 1) * 128], self.kf[:, J, :], self.ident[:, :]),
                reads=[("kf", J), "ident"], writes=[("ps", b)])
        add("act", lambda b=b: nc.scalar.copy(out=self.rs[:, 0:256], in_=self.ps[b][:, 0:256]), reads=[("ps", b)], writes=["rs"])
        for sq_ in range(NSEQ):
            add("sp", lambda sq_=sq_: nc.sync.dma_start(out=self.k_s[l, sq_, 120:128, :], in_=self.rs[sq_ * 8:(sq_ + 1) * 8, 0:256]),
                reads=["rs"], dma=("osts", 8 + sq_ % 4))

    def attn_sample(self, l):
        nc = self.nc
        add = self.add
        for grp in range(4):
            buf = grp % 2
            s0 = grp * 4
            add("sp", lambda s0=s0: nc.sync.dma_start(out=self.kst[:, :, :], in_=self.ck[l, s0:s0 + 4].rearrange("b s f -> s b f")),
                writes=[("mt", 0), ("mt", 1)], dma="kst")
            for J in range(2):
                b = self.P.bank()
                for bb in range(4):
                    add("pe", lambda b=b, bb=bb, J=J: nc.tensor.transpose(
                        self.ps[b][:, bb * 128:(bb + 1) * 128], self.kst[:, bb, J * 128:(J + 1) * 128], self.ident[:, :]),
                        reads=[("mt", 0), ("mt", 1), "ident"], writes=[("ps", b)])
                add("act", lambda b=b, J=J: nc.scalar.copy(out=self.kcT[:, J, :, :].rearrange("p b s -> p (b s)"),
                                                          in_=self.ps[b][:, :]),
                    reads=[("ps", b)], writes=["stK"])
            add("pool", lambda s0=s0: nc.gpsimd.dma_start(out=self.vc[:, :, :],
                                                          in_=self.cv[l, s0:s0 + 4].rearrange("b s f -> s b f")),
                writes=["stV"], dma="vc")
            blc = self.P.bank()
            blo = self.P.bank()
            for bb in range(4):
                sq_ = s0 + bb
                cs = slice(sq_ * 8, (sq_ + 1) * 8)
                for kv in range(4):
                    J, hf = kv // 2, kv % 2
                    ps_ = slice(hf * 64, (hf + 1) * 64)
                    oc = self.ps[blc][:, kv * 128:(kv + 1) * 128].rearrange("p (g b q) -> p g b q", g=4, b=4)[:, :, bb, :]
                    add("pe", lambda oc=oc, J=J, ps_=ps_, cs=cs, buf=buf, bb=bb: nc.tensor.matmul(
                        oc, self.kcT[ps_, J, bb, :], self.q[ps_, J * 4:(J + 1) * 4, cs], start=True, stop=True),
                        reads=["stK"] + [self.tq(J * 4 + g) for g in range(4)], writes=[("ps", blc)])
                    oo = self.ps[blo][0:8, kv * 128:(kv + 1) * 128].rearrange("p (g b q) -> p g b q", g=4, b=4)[:, :, bb, :]
                    add("pe", lambda oo=oo, J=J, ps_=ps_, cs=cs: nc.tensor.matmul(
                        oo, self.kT[ps_, J, cs], self.q[ps_, J * 4:(J + 1) * 4, cs], start=True, stop=True),
                        reads=[("kT", J)] + [self.tq(J * 4 + g) for g in range(4)], writes=[("ps", blo)])
            add("dve", lambda blc=blc: nc.vector.tensor_tensor(
                out=self.atmp[:, 0, :].rearrange("p (h b q) -> p h b q", h=16, b=4), in0=self.ps[blc][:, :].rearrange("p (h b q) -> p h b q", h=16, b=4),
                in1=self.biasS[:, 0, :].rearrange("p (h q) -> p h q", q=8).unsqueeze(2).broadcast_to([128, 16, 4, 8]), op=ALU.add),
                reads=[("ps", blc), "biasS"], writes=[("atmp", 0)])
            add("dve", lambda blo=blo: nc.vector.tensor_tensor(
                out=self.atmp[0:8, 1, :].rearrange("p (h b q) -> p h b q", h=16, b=4), in0=self.ps[blo][0:8, :].rearrange("p (h b q) -> p h b q", h=16, b=4),
                in1=self.biasS[0:8, 1, :].rearrange("p (h q) -> p h q", q=8).unsqueeze(2).broadcast_to([8, 16, 4, 8]), op=ALU.add),
                reads=[("ps", blo), "biasS"], writes=[("atmp", 1)])
            add("act", lambda: nc.scalar.activation(out=self.pT[:, 0, :], in_=self.atmp[:, 0, :], func=AF.Exp),
                reads=[("atmp", 0)], writes=[("pT", 0)])
            add("act", lambda: nc.scalar.activation(out=self.pT[0:8, 1, :], in_=self.atmp[0:8, 1, :], func=AF.Exp),
                reads=[("atmp", 1)], writes=[("pT", 1)])
            bo = self.P.bank()
            bd = self.P.bank()
            for bb in range(4):
                sq_ = s0 + bb
                for kv in range(4):
                    J, hf = kv // 2, kv % 2
                    ps_ = slice(hf * 64, (hf + 1) * 64)
                    pc = self.pT[:, 0, kv * 128:(kv + 1) * 128].rearrange("p (g b q) -> p g b q", g=4, b=4)[:, :, bb, :]
                    po = self.pT[0:8, 1, kv * 128:(kv + 1) * 128].rearrange("p (g b q) -> p g b q", g=4, b=4)[:, :, bb, :]
                    for (bk, wa, wb) in ((bo, self.vc[:, bb, kv * 64:(kv + 1) * 64], self.vnew[0:8, sq_, kv * 64:(kv + 1) * 64]),
                                         (bd, self.onesb[:, 0:64], self.onesb[0:8, 0:64])):
                        oo = self.ps[bk][ps_, J * 128:(J + 1) * 128].rearrange("p (g b q) -> p g b q", g=4, b=4)[:, :, bb, :]
                        add("pe", lambda oo=oo, wa=wa, pc=pc: nc.tensor.matmul(oo, wa, pc, start=True, stop=False),
                            reads=["stV", ("pT", 0), "onesb"], writes=[("ps", bk)])
                        add("pe", lambda oo=oo, wb=wb, po=po: nc.tensor.matmul(oo, wb, po, start=False, stop=True),
                            reads=["vnew", ("pT", 1), "onesb"], writes=[("ps", bk)])
            add("dve", lambda bd=bd: nc.vector.tensor_tensor(
                out=self.rD[:, 0, 0:256].rearrange("p (h x) -> p h x", x=32), in0=self.ps[bd][:, 0:256].rearrange("p (h x) -> p h x", x=32),
                in1=self.sinkE[:, :].unsqueeze(2).broadcast_to([128, 8, 32]), op=ALU.add),
                reads=[("ps", bd), "sinkE"], writes=[("rD", 0)])
            add("act", lambda: nc.scalar.activation(out=self.rD[:, 0, 0:256], in_=self.rD[:, 0, 0:256], func=AF.Ln),
                reads=[("rD", 0)], writes=[("rD", 0)])
            add("act", lambda: nc.scalar.activation(out=self.rD[:, 0, 0:256], in_=self.rD[:, 0, 0:256], func=AF.Exp, scale=-1.0),
                reads=[("rD", 0)], writes=[("rD", 0)])
            add("dve", lambda bo=bo, s0=s0: nc.vector.tensor_tensor(
                out=self.q[:, 0:8, s0 * 8:(s0 + 4) * 8], in0=self.ps[bo][:, 0:256].rearrange("p (h x) -> p h x", x=32),
                in1=self.rD[:, 0, 0:256].rearrange("p (h x) -> p h x", x=32), op=ALU.mult),
                reads=[("ps", bo), ("rD", 0)], writes=[self.tq(j) for j in range(8)])


def _t5_bucket(dist):
    n = np.maximum(dist, 0)
    exact = 16
    large = exact + (np.log(np.maximum(n, 1) / exact) / np.log(128 / exact) * (32 - exact)).astype(np.int32)
    large = np.minimum(large, 31)
    return np.where(n < exact, n, large).astype(np.int32)


def _qperm():
    idx = []
    for j in range(8):
        for hf in range(2):
            kv = 2 * (j // 4) + hf
            g = j % 4
            head = kv * 4 + g
            idx.extend(range(head * 64, head * 64 + 64))
    return np.array(idx)


def _fm(vec, nch):
    return np.ascontiguousarray(np.asarray(vec, np.float32).reshape(nch, 128).T)


_NC_CACHE = {}


def prepare(inputs):
    f = lambda k: np.asarray(inputs[k], np.float32)
    x_prompt, x_sample = f("x_prompt"), f("x_sample")
    qp = _qperm()
    cols = np.concatenate([np.arange(0, 1536), 1536 + qp, np.arange(2560, 6144)])
    w_in = np.ascontiguousarray(f("w_in")[:, :, cols])
    b_in = f("b_in")[:, cols]
    w_ao = np.ascontiguousarray(f("w_attn_out")[:, qp, :])
    pp = np.zeros((128, NPP), np.float32)
    sinks = f("attn_sinks")
    for l in range(NL):
        o = l * PL
        pp[:, o + PB_IN:o + PB_IN + 48] = _fm(b_in[l], 48)
        pp[:, o + PB_NMIX:o + PB_NMIX + 8] = _fm(f("norm_mix")[l], 8)
        pp[:, o + PB_PSC:o + PB_PSC + 8] = _fm(f("pool_scale")[l], 8)
        pp[:, o + PB_NMLP:o + PB_NMLP + 8] = _fm(f("norm_mlp")[l], 8)
        cw = f("conv_w")[l]
        pp[:, o + PB_CW:o + PB_CW + 124] = cw.T.reshape(4, 128, 31).transpose(1, 0, 2).reshape(128, 124)
        pp[:, o + PB_CB:o + PB_CB + 4] = _fm(f("conv_b")[l], 4)
        pp[:, o + PB_NG:o + PB_NG + 4] = _fm(f("conv_norm_g")[l], 4)
        pp[:, o + PB_NB:o + PB_NB + 4] = _fm(f("conv_norm_b")[l], 4)
        for J in range(2):
            for g in range(4):
                for hf in range(2):
                    pp[hf * 64:(hf + 1) * 64, o + PB_SINK + J * 4 + g] = sinks[l, (2 * J + hf) * 4 + g]
    pp[:, PB_NF:PB_NF + 8] = _fm(f("norm_final"), 8)
    bkv = np.ascontiguousarray(np.broadcast_to(b_in[:, None, 2560:3072], (NL, 128, 512)))
    ext = np.concatenate([f("rel_bias"), np.full((1, 16), NEG, np.float32)], axis=0)
    s = np.arange(128)[:, None]
    q = np.arange(128)[None, :]
    d_prev = 128 + q - s
    d_own = q - s
    tabs = []
    for dmat in (d_prev, d_own):
        idx = np.where((dmat >= 0) & (dmat <= 128), _t5_bucket(dmat), 32)
        tabs.append(ext[idx])
    biasT = np.stack(tabs, axis=1).transpose(0, 1, 3, 2)
    biasT = np.ascontiguousarray(biasT).reshape(128, 2 * 16 * 128)
    biasS = np.ascontiguousarray(biasT.reshape(128, 2, 16, 128)[:, :, :, 0:8]).reshape(128, 256)
    common = dict(identm=np.eye(128, dtype=np.float32), biasS=biasS, w_in=w_in, w_pool=f("pool_w"), w_co=f("w_conv_out"), w_ao=w_ao, w_o=f("w_out"),
                  w_up=f("w_up"), w_dn=f("w_down"), pp=pp, bkv=bkv, biasT=biasT)
    meta = f("meta_tokens")
    in_maps = []
    for c in range(8):
        b, cc = c // 4, c % 4
        m = dict(common)
        m["xm"] = np.ascontiguousarray(x_prompt[b, cc * 2048:(cc + 1) * 2048])
        tokmask = np.ones((128, 512), np.float32)
        kmask = np.zeros((128, 8), np.float32)
        kmask[:, 4] = NEG
        invc = np.zeros((128, 4, 16), np.float32)
        for g, w in enumerate(WINS):
            invc[:, g, :] = 1.0 / w
        if cc == 0:
            xh = np.zeros((512, D), np.float32)
            xh[496:] = meta
            tokmask[:, :496] = 0.0
            kmask[:, 0:3] = NEG
            kmask[:112, 3] = NEG
            for g, w in enumerate(WINS):
                invc[:, g, :] = 1.0 / np.minimum(np.arange(16) + 1, w)
        else:
            xh = np.ascontiguousarray(x_prompt[b, cc * 2048 - 512:cc * 2048])
        m["xh"] = xh
        m["tokmask"] = tokmask
        m["kmask"] = kmask
        m["invc"] = invc.reshape(128, 64)
        sl = slice(c * NSEQ, (c + 1) * NSEQ)
        m["xs"] = np.ascontiguousarray(x_sample[sl].reshape(NS, D))
        m["spool"] = np.ascontiguousarray(f("state_pool")[:, sl])
        m["sconv"] = np.ascontiguousarray(f("state_conv")[:, sl])
        m["ck"] = np.ascontiguousarray(f("cache_k")[:, sl].reshape(NL, NSEQ, 128, 256))
        m["cv"] = np.ascontiguousarray(f("cache_v")[:, sl].reshape(NL, NSEQ, 128, 256))
        in_maps.append(m)
    return in_maps


def assemble(res):
    R = res.results
    y_prompt = np.zeros((2, 8192, D), np.float32)
    for c in range(8):
        y_prompt[c // 4, (c % 4) * 2048:(c % 4 + 1) * 2048] = R[c]["ym"]
    y_sample = np.concatenate([R[c]["ys"].reshape(NSEQ, DT, D) for c in range(8)], axis=0)
    pool_p = np.stack([R[3]["pool_p"], R[7]["pool_p"]], axis=1)
    conv_p = np.stack([R[3]["conv_p"], R[7]["conv_p"]], axis=1)
    k_p = np.stack([R[3]["k_p"], R[7]["k_p"]], axis=1).reshape(NL, 2, 128, 4, 64)
    v_p = np.stack([R[3]["v_p"], R[7]["v_p"]], axis=1).reshape(NL, 2, 128, 4, 64)
    pool_s = np.concatenate([R[c]["pool_s"] for c in range(8)], axis=1)
    conv_s = np.concatenate([R[c]["conv_s"] for c in range(8)], axis=1)
    k_s = np.concatenate([R[c]["k_s"] for c in range(8)], axis=1).reshape(NL, 128, 128, 4, 64)
    v_s = np.concatenate([R[c]["v_s"] for c in range(8)], axis=1).reshape(NL, 128, 128, 4, 64)
    outs = (y_prompt, y_sample, pool_p, pool_s, conv_p, conv_s, k_p, k_s, v_p, v_s)
    return tuple(np.ascontiguousarray(o, dtype=np.float32) for o in outs)


def kernel(**inputs):
    in_maps = prepare(inputs)
    if "nc" not in _NC_CACHE:
        _NC_CACHE["nc"] = Builder().build()
    nc = _NC_CACHE["nc"]
    res = run_bass_kernel_spmd(nc, in_maps, core_ids=list(range(8)))
    return assemble(res)
```

```python
import contextlib
import numpy as np
import concourse.bass as bass
import concourse.mybir as mybir
from concourse.bass_utils import run_bass_kernel_spmd

F32 = mybir.dt.float32
BF16 = mybir.dt.bfloat16
AF = mybir.ActivationFunctionType
ALU = mybir.AluOpType

D = 1024
NL = 4
DIN = 6144
TT = 512
NSEQ = 16
DT = 8
NS = NSEQ * DT
NEG = -30000.0
EPS = 1e-6
WINS = (2, 4, 8, 16)

PB_IN = 0
PB_NMIX = 48
PB_PSC = 56
PB_NMLP = 64
PB_CW = 72
PB_CB = 196
PB_NG = 200
PB_NB = 204
PB_SINK = 208
PL = 216
PB_NF = NL * PL
NPP = PB_NF + 8

SAME_ENG_SYNC = True
NWS = 4
NDIAG = 16
TRI_HALO = True
SKIP = set()


class Op:
    __slots__ = ("eng", "fn", "deps", "sig", "sigval", "dma")

    def __init__(self, eng, fn, dma):
        self.eng = eng
        self.fn = fn
        self.deps = []
        self.sig = False
        self.sigval = None
        self.dma = dma


class Prog:
    ENGS = ("pe", "act", "dve", "pool", "sp")

    def __init__(self, nc):
        self.nc = nc
        self.ops = []
        self.writer = {}
        self.readers = {}
        self.dma_count = {}
        self.nbank = 0
        self.reserved = set()

    def eng_obj(self, eng):
        nc = self.nc
        return {"pe": nc.tensor, "act": nc.scalar, "dve": nc.vector,
                "pool": nc.gpsimd, "sp": nc.sync}[eng]

    def add(self, eng, fn, reads=(), writes=(), dma=None):
        o = Op(eng, fn, dma)
        deps = {}
        for t in reads:
            w = self.writer.get(t)
            if w is not None:
                deps[id(w)] = w
        for t in writes:
            w = self.writer.get(t)
            if w is not None:
                deps[id(w)] = w
            last = {}
            for r in self.readers.get(t, ()):
                if r.dma is not None:
                    deps[id(r)] = r
                else:
                    last[r.eng] = r
            for r in last.values():
                deps[id(r)] = r
        o.deps = list(deps.values())
        for t in reads:
            self.readers.setdefault(t, []).append(o)
        for t in writes:
            self.writer[t] = o
            self.readers[t] = []
        if dma is not None:
            c = self.dma_count.get(dma, 0) + 1
            self.dma_count[dma] = c
            o.sigval = 16 * c
        self.ops.append(o)
        return o

    def bank(self):
        while True:
            b = self.nbank % 8
            self.nbank += 1
            if b not in self.reserved:
                return b

    @staticmethod
    def _needs(o, d):
        if d.dma is not None:
            return True
        if o.dma is not None:
            return True
        if d.eng != o.eng:
            return True
        return SAME_ENG_SYNC and d.eng != "pe"

    def emit(self):
        nc = self.nc
        for o in self.ops:
            for d in o.deps:
                if d.dma is None and self._needs(o, d):
                    d.sig = True
        cnt = {e: 0 for e in self.ENGS}
        for o in self.ops:
            if o.dma is None and o.sig:
                cnt[o.eng] += 1
                o.sigval = cnt[o.eng]
        self.sig_counts = dict(cnt)
        per_eng = {e: [o for o in self.ops if o.eng == e] for e in self.ENGS}
        with contextlib.ExitStack() as st:
            esem = {e: st.enter_context(nc.semaphore("c_" + e)) for e in self.ENGS}
            dsem = {n: st.enter_context(nc.semaphore("d_%d" % i))
                    for i, n in enumerate(self.dma_count)}
            block = st.enter_context(nc.Block())

            def run(eng):
                E = self.eng_obj(eng)
                seen = {}
                for o in per_eng[eng]:
                    waits = {}
                    for d in o.deps:
                        if not self._needs(o, d):
                            continue
                        s = dsem[d.dma] if d.dma is not None else esem[d.eng]
                        k = id(s)
                        if seen.get(k, 0) >= d.sigval:
                            continue
                        if k not in waits or waits[k][1] < d.sigval:
                            waits[k] = (s, d.sigval)
                    for k, (s, v) in waits.items():
                        E.wait_ge(s, v)
                        seen[k] = v
                    ins = o.fn()
                    if o.dma is not None:
                        ins.then_inc(dsem[o.dma], 16)
                    elif o.sig:
                        ins.then_inc(esem[o.eng], 1)
                if eng == "sp":
                    for n, c in self.dma_count.items():
                        E.wait_ge(dsem[n], 16 * c)

            block.tensor(lambda e: run("pe"))
            block.scalar(lambda e: run("act"))
            block.vector(lambda e: run("dve"))
            block.gpsimd(lambda e: run("pool"))
            block.sync(lambda e: run("sp"))


class Builder:
    def __init__(self, tiles=("halo", "m0", "m1", "m2", "m3", "samp"), nl=NL, dbg=False):
        self.tiles = tiles
        self.nl = nl
        self.dbg = dbg
        self.nc = bass.Bass("TRN2", target_bir_lowering=False)
        self.P = Prog(self.nc)
        self.st = contextlib.ExitStack()

    def din(self, name, shape):
        return self.nc.dram_tensor(name, list(shape), F32, kind="ExternalInput").ap()

    def dout(self, name, shape):
        return self.nc.dram_tensor(name, list(shape), F32, kind="ExternalOutput").ap()

    def sb(self, name, shape, dt):
        return self.st.enter_context(self.nc.sbuf_tensor(name, list(shape), dt))

    def add(self, *a, **k):
        return self.P.add(*a, **k)

    def build(self):
        nc = self.nc
        with self.st:
            self.declare()
            self.prologue()
            for tname in self.tiles:
                self.run_tile(tname)
            self.P.emit()
        return nc

    def declare(self):
        nc = self.nc
        sb = self.sb
        self.xh = self.din("xh", [512, D])
        self.xm = self.din("xm", [2048, D])
        self.xs = self.din("xs", [NS, D])
        self.tokmask_d = self.din("tokmask", [128, 512])
        self.kmask_d = self.din("kmask", [128, 8])
        self.invc_d = self.din("invc", [128, 64])
        self.spool = self.din("spool", [NL, NSEQ, 15, 512])
        self.sconv = self.din("sconv", [NL, NSEQ, 30, 512])
        self.ck = self.din("ck", [NL, NSEQ, 128, 256])
        self.cv = self.din("cv", [NL, NSEQ, 128, 256])
        self.w_in = self.din("w_in", [NL, D, DIN])
        self.w_pool = self.din("w_pool", [NL, 4, 128, 256])
        self.w_co = self.din("w_co", [NL, 512, D])
        self.w_ao = self.din("w_ao", [NL, D, D])
        self.w_o = self.din("w_o", [NL, D, D])
        self.w_up = self.din("w_up", [NL, D, 4096])
        self.w_dn = self.din("w_dn", [NL, 4096, D])
        self.pp_d = self.din("pp", [128, NPP])
        self.bkv_d = self.din("bkv", [NL, 128, 512])
        self.biasT_d = self.din("biasT", [128, 2 * 16 * 128])
        self.ident_d = self.din("identm", [128, 128])
        self.biasS_d = self.din("biasS", [128, 256])
        self.ym = self.dout("ym", [2048, D])
        self.ys = self.dout("ys", [NS, D])
        self.pool_p = self.dout("pool_p", [NL, 15, 512])
        self.conv_p = self.dout("conv_p", [NL, 30, 512])
        self.k_p = self.dout("k_p", [NL, 128, 256])
        self.v_p = self.dout("v_p", [NL, 128, 256])
        self.pool_s = self.dout("pool_s", [NL, NSEQ, 15, 512])
        self.conv_s = self.dout("conv_s", [NL, NSEQ, 30, 512])
        self.k_s = self.dout("k_s", [NL, NSEQ, 128, 256])
        self.v_s = self.dout("v_s", [NL, NSEQ, 128, 256])
        if self.dbg:
            self.dbg_d = self.dout("dbg", [128, 8 * TT])
        self.h = sb("h", [128, 8, TT], F32)
        self.u = sb("u", [128, 8, TT], BF16)
        self.sq = sb("sq", [128, 2, TT], BF16)
        self.rs = sb("rs", [128, TT], F32)
        self.aext = sb("aext", [128, 4, 16 + TT], F32)
        self.pscr = sb("pscr", [128, 2, 16 + TT], F32)
        self.gext = sb("gext", [128, 4, 640], F32)
        self.sig = sb("sig", [128, 2, TT], F32)
        self.cy = sb("cy", [128, 4, TT], F32)
        self.big = sb("big", [128, 32 * TT], BF16)
        self.kTz = sb("kTz", [128, 2, 2, 128 + TT], BF16)
        self.vz = sb("vz", [128, 5, 4, 128], BF16)
        self.onesz = sb("onesz", [128, 2, 128], BF16)
        self.biasT = sb("biasT_s", [128, 2, 16, 128], BF16)
        self.atmp = sb("atmp", [128, 2, TT], F32)
        self.pT = sb("pT", [128, 4, TT], BF16)
        self.rD = sb("rD", [128, 2, TT], F32)
        self.sinkB = sb("sinkB", [128, 2, TT], F32)
        self.sinkE = sb("sinkE", [128, 8], F32)
        self.bq8 = sb("bq8", [128, 8], F32)
        self.mt = sb("mt", [128, 2, TT], F32)
        self.relu = sb("relu", [128, 2, TT], BF16)
        self.diag = sb("diag", [128, NDIAG, 128], BF16)
        self.ws = [sb("ws%d" % i, [128, 4096], BF16) for i in range(NWS)]
        self.wpl = sb("wpl", [128, 4, 256], BF16)
        self.stA = sb("stA", [128, NL, 4, 16], F32)
        self.stG = sb("stG", [128, NL, 4, 32], F32)
        self.stb = sb("stb", [128, 2048], BF16)
        self.stK = self.stb[:, 0:1024].rearrange("p (l j s) -> p l j s", l=NL, j=2)
        self.stV = self.stb[:, 1024:2048].rearrange("p (l f) -> p l f", l=NL)
        self.kcT = self.stb[:, 0:1024].rearrange("p (j b s) -> p j b s", j=2, b=4)
        self.vc = self.stb[:, 1024:2048].rearrange("p (b f) -> p b f", b=4)
        self.ident = sb("ident", [128, 128], F32)
        self.identb = sb("identb", [128, 128], BF16)
        self.onesb = sb("onesb", [128, 128], BF16)
        self.pp = sb("pp_s", [128, NPP], F32)
        self.bkv = sb("bkv_s", [128, 512], F32)
        self.tokmask = sb("tokmask_s", [128, 512], F32)
        self.kmask = sb("kmask_s", [128, 8], F32)
        self.invc = sb("invc_s", [128, 4, 16], F32)
        self.kf = sb("kf", [128, 2, 128], F32)
        self.vnew = sb("vnew", [8, NSEQ, 256], BF16)
        self.biasS = sb("biasS_s", [128, 2, 128], F32)
        self.ps = [self.st.enter_context(nc.psum_tensor("ps%d" % i, [128, 512], F32))
                   for i in range(8)]
        self.ost = self.pscr[:, :, 0:512]
        self.kst = self.mt[:, :, :].rearrange("p a b -> p (a b)").rearrange("p (b f) -> p b f", f=256)
        self.anew = self.sinkB[:, 0, :].rearrange("p (g t) -> p g t", t=128)
        self.gnew = self.sinkB[:, 1, :].rearrange("p (g t) -> p g t", t=128)
        self.vnewf = self.rD[0:8, 1, :].rearrange("p (a f) -> p a f", f=256)
        bigv = self.big[:, :].rearrange("p (c t) -> p c t", t=TT)
        self.hid = bigv
        self.q = bigv[:, 0:8, :]
        self.merged = bigv[:, 8:16, :]
        self.gbf = self.big[:, 16 * TT:16 * TT + 4 * 640].rearrange("p (c t) -> p c t", t=640)
        self.r = bigv[:, 21:25, :]
        self.s = bigv[:, 25:29, :]
        self.wq = []
        self.wi = 0
        self.wissued = 0
        self.wcur = {}

    def tq(self, j):
        return ("big", j)

    def tmg(self, m):
        return ("big", 8 + m)

    def tgbf(self, c):
        return [("big", 16 + c), ("big", 17 + c)]

    def tr(self, g):
        return ("big", 21 + g)

    def ts_(self, c):
        return ("big", 25 + c)

    def thid(self, f):
        return ("big", f)

    def macc(self, m, N):
        if m < 4:
            return self.cy[:, m, 0:N], ("cy", m)
        return self.aext[:, m - 4, 0:N], ("aext", m - 4)

    def layer_items(self, l):
        it = []
        for j in range(6):
            it.append((("win", j), self.w_in[l, :, j * 512:(j + 1) * 512].rearrange("(k p) c -> p k c", p=128), 8, 512))
        for j in (6, 7):
            it.append((("win", j), self.w_in[l, :, j * 512:(j + 1) * 512].rearrange("(k p) c -> p k c", p=128), 8, 512))
        for hh in range(2):
            it.append((("wco", hh), self.w_co[l, :, hh * 512:(hh + 1) * 512].rearrange("(k p) c -> p k c", p=128), 4, 512))
            it.append((("win", 8 + hh), self.w_in[l, :, (8 + hh) * 512:(9 + hh) * 512].rearrange("(k p) c -> p k c", p=128), 8, 512))
        for hh in range(2):
            it.append((("wao", hh), self.w_ao[l, :, hh * 512:(hh + 1) * 512].rearrange("(k p) c -> p k c", p=128), 8, 512))
            it.append((("win", 10 + hh), self.w_in[l, :, (10 + hh) * 512:(11 + hh) * 512].rearrange("(k p) c -> p k c", p=128), 8, 512))
        for hh in range(2):
            it.append((("wo", hh), self.w_o[l, :, hh * 512:(hh + 1) * 512].rearrange("(k p) c -> p k c", p=128), 8, 512))
        for j in range(8):
            it.append((("wup", j), self.w_up[l, :, j * 512:(j + 1) * 512].rearrange("(k p) c -> p k c", p=128), 8, 512))
        for m in range(8):
            it.append((("wdn", m), self.w_dn[l, :, m * 128:(m + 1) * 128].rearrange("(k p) c -> p k c", p=128), 32, 128))
        return it

    def wissue(self):
        nc = self.nc
        while self.wissued < len(self.wq):
            i = self.wissued
            if i >= NWS and not self.wreleased[i - NWS]:
                break
            key, src, d1, d2 = self.wq[i]
            slot = i % NWS
            dst = self.ws[slot][:, 0:d1 * d2].rearrange("p (a b) -> p a b", b=d2)
            self.add("pool", lambda dst=dst, src=src: nc.gpsimd.dma_start(out=dst, in_=src),
                     writes=[("ws", slot)], dma=("ws", slot))
            self.wissued += 1

    def wget(self, key):
        i = self.wi
        k, src, d1, d2 = self.wq[i]
        assert k == key, (k, key)
        self.wissue()
        assert self.wissued > i, ("weight stream stuck", i, key)
        self.wi += 1
        slot = i % NWS
        self.wcur[key] = i
        return self.ws[slot][:, 0:d1 * d2].rearrange("p (a b) -> p a b", b=d2), ("ws", slot)

    def wrel(self, key):
        self.wreleased[self.wcur.pop(key)] = True
        self.wissue()

    def prologue(self):
        nc = self.nc
        add = self.add
        for t in self.tiles:
            for l in range(self.nl):
                self.wq.extend(self.layer_items(l))
        self.wreleased = [False] * len(self.wq)
        add("sp", lambda: nc.sync.dma_start(out=self.pp[:, :], in_=self.pp_d), writes=["pp"], dma="c0")
        add("pool", lambda: nc.gpsimd.dma_start(out=self.biasT[:, :, :, :].rearrange("p a b c -> p (a b c)"), in_=self.biasT_d),
            writes=["biasT"], dma="c1")
        add("sp", lambda: nc.sync.dma_start(out=self.biasS[:, :, :].rearrange("p a b -> p (a b)"), in_=self.biasS_d),
            writes=["biasS"], dma="c6")
        add("sp", lambda: nc.sync.dma_start(out=self.tokmask[:, :], in_=self.tokmask_d), writes=["tokmask"], dma="c2")
        add("sp", lambda: nc.sync.dma_start(out=self.kmask[:, :], in_=self.kmask_d), writes=["kmask"], dma="c3")
        add("sp", lambda: nc.sync.dma_start(out=self.invc[:, :, :].rearrange("p a b -> p (a b)"), in_=self.invc_d),
            writes=["invc"], dma="c4")
        add("pool", lambda: nc.gpsimd.memset(self.onesb[:, :], 1.0), writes=["onesb"])
        add("pool", lambda: nc.gpsimd.memset(self.onesz[:, :, :].rearrange("p a b -> p (a b)"), 0.0), writes=["onesz"])
        add("pool", lambda: nc.gpsimd.memset(self.onesz[:, 0, 0:64], 1.0), writes=["onesz"])
        add("pool", lambda: nc.gpsimd.memset(self.onesz[:, 1, 64:128], 1.0), writes=["onesz"])
        add("pool", lambda: nc.gpsimd.memset(self.kTz[:, :, :, :].rearrange("p a b c -> p (a b c)"), 0.0),
            writes=[("kT", 0), ("kT", 1)])
        add("pool", lambda: nc.gpsimd.memset(self.vz[:, :, :, :].rearrange("p a b c -> p (a b c)"), 0.0),
            writes=[("v", i) for i in range(5)])
        add("sp", lambda: nc.sync.dma_start(out=self.ident[:, :], in_=self.ident_d), writes=["ident"], dma="c5")
        add("pool", lambda: nc.gpsimd.memset(self.aext[:, :, :].rearrange("p a b -> p (a b)"), 0.0),
            writes=[("aext", g) for g in range(4)])
        add("pool", lambda: nc.gpsimd.memset(self.gext[:, :, :].rearrange("p a b -> p (a b)"), 0.0),
            writes=[("gext", g) for g in range(4)])
        add("pool", lambda: nc.gpsimd.memset(self.pscr[:, :, :].rearrange("p a b -> p (a b)"), 0.0),
            writes=[("pscr", 0), ("pscr", 1)])
        add("dve", lambda: nc.vector.tensor_copy(out=self.identb[:, :], in_=self.ident[:, :]),
            reads=["ident"], writes=["identb"])
        add("pool", lambda: nc.gpsimd.memset(self.stA[:, :, :, :].rearrange("p a b c -> p (a b c)"), 0.0), writes=["stA"])
        add("pool", lambda: nc.gpsimd.memset(self.stG[:, :, :, :].rearrange("p a b c -> p (a b c)"), 0.0), writes=["stG"])
        add("pool", lambda: nc.gpsimd.memset(self.stb[:, :], 0.0), writes=["stK", "stV"])

    def run_tile(self, tname):
        if tname == "samp":
            N, kind = NS, "samp"
            xsrc, ydst = self.xs, self.ys
        elif tname == "halo":
            N, kind = TT, "halo"
            xsrc, ydst = self.xh, None
        else:
            i = int(tname[1])
            N, kind = TT, "main"
            xsrc, ydst = self.xm[i * TT:(i + 1) * TT, :], self.ym[i * TT:(i + 1) * TT, :]
        self.N = N
        self.kind = kind
        self.tname = tname
        self.last_main = (tname == "m3")
        self.c0 = 0
        if getattr(self, "stats_ready", False):
            self.P.reserved.discard(self.sbank)
            self.stats_ready = False
        self.load_x(xsrc, N)
        for l in range(self.nl):
            if kind == "halo" and TRI_HALO:
                self.c0 = 128 * l
                self.N = TT - 128 * l
            self.layer(l)
        if self.dbg:
            nc = self.nc
            self.add("sp", lambda: nc.sync.dma_start(out=self.dbg_d, in_=self.h[:, :, :].rearrange("p a b -> p (a b)")),
                     reads=[("h", k) for k in range(8)], dma="dbg")
        if ydst is not None:
            self.final_norm(ydst, N)

    def load_x(self, xsrc, N):
        nc = self.nc
        add = self.add
        for tb in range(N // 128):
            xin = self.cy[:, 2 * (tb % 2):2 * (tb % 2) + 2, :].rearrange("p a b -> p (a b)")
            tk = [("cy", 2 * (tb % 2)), ("cy", 2 * (tb % 2) + 1)]
            add("sp", lambda xin=xin, tb=tb: nc.sync.dma_start(out=xin, in_=xsrc[tb * 128:(tb + 1) * 128, :]),
                writes=tk, dma=("xin", tb % 2))
            for hh in range(2):
                b = self.P.bank()
                for kk in range(4):
                    k = hh * 4 + kk
                    add("pe", lambda b=b, kk=kk, k=k, xin=xin: nc.tensor.transpose(
                        self.ps[b][:, kk * 128:(kk + 1) * 128], xin[:, k * 128:(k + 1) * 128], self.ident[:, :]),
                        reads=tk + ["ident"], writes=[("ps", b)])
                add("act", lambda b=b, hh=hh, tb=tb: nc.scalar.copy(
                    out=self.h[:, hh * 4:(hh + 1) * 4, tb * 128:(tb + 1) * 128],
                    in_=self.ps[b][:, :].rearrange("p (a b) -> p a b", b=128)),
                    reads=[("ps", b)], writes=[("h", hh * 4 + kk) for kk in range(4)])

    def stat_begin(self):
        self.sbank = self.P.bank()
        self.P.reserved.add(self.sbank)
        self.stat_n = 0
        self.stat_q = []
        self.stat_N = self.N

    def stat_chunk(self, k):
        nc = self.nc
        N = self.N
        i = self.stat_n
        self.stat_n += 1
        hk = self.h[:, k, self.c0:self.c0 + N]
        b = self.sbank
        self.add("act", lambda: nc.scalar.activation(out=self.sq[:, i % 2, 0:N], in_=hk, func=AF.Square),
                 reads=[("h", k)], writes=[("sq", i % 2)])
        self.stat_q.append(lambda: self.add(
            "pe", lambda: nc.tensor.matmul(self.ps[b][:, 0:N], self.onesb[:, :], self.sq[:, i % 2, 0:N],
                                           start=(i == 0), stop=(i == 7)),
            reads=[("sq", i % 2), "onesb"], writes=[("ps", b)]))

    def stat_pe(self, all_=False):
        while self.stat_q:
            self.stat_q.pop(0)()
            if not all_:
                break

    def stat_end(self):
        self.stat_pe(all_=True)
        self.stats_ready = True

    def rmsnorm_stats(self, N):
        nc = self.nc
        add = self.add
        if getattr(self, "stats_ready", False):
            self.stats_ready = False
            b = self.sbank
            off = self.stat_N - N
        else:
            off = 0
            b = self.P.bank()
            for k in range(8):
                hk = self.h[:, k, self.c0:self.c0 + N]
                add("act", lambda k=k, hk=hk: nc.scalar.activation(out=self.sq[:, k % 2, 0:N], in_=hk, func=AF.Square),
                    reads=[("h", k)], writes=[("sq", k % 2)])
                add("pe", lambda k=k, b=b: nc.tensor.matmul(self.ps[b][:, 0:N], self.onesb[:, :], self.sq[:, k % 2, 0:N],
                                                            start=(k == 0), stop=(k == 7)),
                    reads=[("sq", k % 2), "onesb"], writes=[("ps", b)])
        add("act", lambda b=b: nc.scalar.activation(out=self.rs[:, 0:N], in_=self.ps[b][:, off:off + N], func=AF.Ln,
                                                    scale=1.0 / D, bias=EPS),
            reads=[("ps", b)], writes=["rs"])
        add("act", lambda: nc.scalar.activation(out=self.rs[:, 0:N], in_=self.rs[:, 0:N], func=AF.Exp, scale=-0.5),
            reads=["rs"], writes=["rs"])
        self.P.reserved.discard(b)

    def rmsnorm(self, N, gcol):
        nc = self.nc
        self.rmsnorm_stats(N)
        for k in range(8):
            hk = self.h[:, k, self.c0:self.c0 + N]
            self.add("dve", lambda k=k, hk=hk: nc.vector.scalar_tensor_tensor(
                out=self.u[:, k, 0:N], in0=hk, scalar=self.pp[:, gcol + k:gcol + k + 1],
                in1=self.rs[:, 0:N], op0=ALU.mult, op1=ALU.mult),
                reads=[("h", k), "rs", "pp"], writes=[("u", k)])

    def final_norm(self, ydst, N):
        nc = self.nc
        add = self.add
        self.rmsnorm_stats(N)
        for k in range(8):
            add("dve", lambda k=k: nc.vector.scalar_tensor_tensor(
                out=self.h[:, k, 0:N], in0=self.h[:, k, 0:N], scalar=self.pp[:, PB_NF + k:PB_NF + k + 1],
                in1=self.rs[:, 0:N], op0=ALU.mult, op1=ALU.mult),
                reads=[("h", k), "rs", "pp"], writes=[("h", k)])
        for tb in range(N // 128):
            yo = self.cy[:, 2 * (tb % 2):2 * (tb % 2) + 2, :].rearrange("p a b -> p (a b)")
            tk = [("cy", 2 * (tb % 2)), ("cy", 2 * (tb % 2) + 1)]
            for hh in range(2):
                b = self.P.bank()
                for kk in range(4):
                    k = hh * 4 + kk
                    add("pe", lambda b=b, kk=kk, k=k, tb=tb: nc.tensor.transpose(
                        self.ps[b][:, kk * 128:(kk + 1) * 128], self.h[:, k, tb * 128:(tb + 1) * 128], self.ident[:, :]),
                        reads=[("h", k), "ident"], writes=[("ps", b)])
                add("act", lambda b=b, hh=hh, yo=yo: nc.scalar.copy(out=yo[:, hh * 512:(hh + 1) * 512], in_=self.ps[b][:, :]),
                    reads=[("ps", b)], writes=[tk[hh]])
            add("sp", lambda yo=yo, tb=tb: nc.sync.dma_start(out=ydst[tb * 128:(tb + 1) * 128, :], in_=yo),
                reads=tk, dma=("yout", tb % 2))

    def a_tok(self, ap2d):
        if self.kind == "samp":
            return ap2d[:, 0:NSEQ * 24].rearrange("p (b i) -> p b i", i=24)[:, :, 16:24]
        return ap2d[:, 16:16 + self.N]

    def g_tok(self, ap2d, off):
        if self.kind == "samp":
            return ap2d[:, 0:NSEQ * 40].rearrange("p (b i) -> p b i", i=40)[:, :, off:off + 8]
        return ap2d[:, off:off + self.N]

    def tokv(self, ap2d):
        if self.kind == "samp":
            return ap2d[:, 0:NS].rearrange("p (b t) -> p b t", t=8)
        return ap2d[:, 0:self.N]

    def layer(self, l):
        nc = self.nc
        add = self.add
        N = self.N
        kind = self.kind
        pb = l * PL
        samp = kind == "samp"
        c0 = self.c0
        tmask = self.tokmask[:, c0:c0 + N]
        WA = NSEQ * 24 if samp else 16 + N
        WG = NSEQ * 40 if samp else 32 + N

        def a_tok(ap2d):
            if samp:
                return ap2d[:, 0:NSEQ * 24].rearrange("p (b i) -> p b i", i=24)[:, :, 16:24]
            return ap2d[:, 16:16 + N]

        def g_tok(ap2d, off):
            if samp:
                return ap2d[:, 0:NSEQ * 40].rearrange("p (b i) -> p b i", i=40)[:, :, off:off + 8]
            return ap2d[:, off:off + N]

        def tokv(ap2d):
            if samp:
                return ap2d[:, 0:NS].rearrange("p (b t) -> p b t", t=8)
            return ap2d[:, 0:N]

        add("sp", lambda: nc.sync.dma_start(out=self.bkv[:, :], in_=self.bkv_d[l]), writes=["bkv"], dma="bkv")
        add("pool", lambda: nc.gpsimd.dma_start(out=self.wpl[:, :, :], in_=self.w_pool[l].rearrange("g c d -> c g d")),
            writes=["wpl"], dma="wpl")
        add("dve", lambda: nc.vector.tensor_scalar(self.bq8[:, :], self.pp[:, pb + PB_IN + 12:pb + PB_IN + 20], 0.125, None, ALU.mult),
            reads=["pp"], writes=["bq8"])
        add("act", lambda: nc.scalar.activation(out=self.sinkE[:, :], in_=self.pp[:, pb + PB_SINK:pb + PB_SINK + 8], func=AF.Exp),
            reads=["pp"], writes=["sinkE"])
        if not samp:
            for J in range(2):
                add("dve", lambda J=J: nc.vector.tensor_copy(
                    out=self.sinkB[:, J, :].rearrange("p (g q) -> p g q", q=128),
                    in_=self.sinkE[:, J * 4:(J + 1) * 4].unsqueeze(2).broadcast_to([128, 4, 128])),
                    reads=["sinkE"], writes=[("sinkB", J)])
            add("pool", lambda: nc.gpsimd.tensor_copy(out=self.aext[:, :, 0:16], in_=self.stA[:, l, :, :]),
                reads=["stA"], writes=[("aext", g) for g in range(4)])
            add("pool", lambda: nc.gpsimd.tensor_copy(out=self.gext[:, :, 0:32], in_=self.stG[:, l, :, :]),
                reads=["stG"], writes=[("gext", c) for c in range(4)])
            for hf in range(2):
                hs = slice(hf * 64, (hf + 1) * 64)
                add("pool", lambda hf=hf, hs=hs: nc.gpsimd.tensor_copy(out=self.kTz[hs, :, hf, 0:128], in_=self.stK[hs, l, :, :]),
                    reads=["stK"], writes=[("kT", 0), ("kT", 1)])
                add("pool", lambda hf=hf, hs=hs: nc.gpsimd.tensor_copy(
                    out=self.vz[:, 0, :, :].rearrange("p (j h) c -> p j h c", h=2)[:, :, hf, hs],
                    in_=self.stV[:, l, :].rearrange("p (j h d) -> p j h d", j=2, h=2)[:, :, hf, :]),
                    reads=["stV"], writes=[("v", 0)])
        elif "load" not in SKIP:
            self.samp_load_states(l)

        self.rmsnorm(N, pb + PB_NMIX)

        def proj_chunk(W, wtok, c0, evac):
            b = self.P.bank()
            for k in range(8):
                add("pe", lambda k=k, b=b: nc.tensor.matmul(self.ps[b][:, 0:N], W[:, k, c0:c0 + 128], self.u[:, k, 0:N],
                                                            start=(k == 0), stop=(k == 7)),
                    reads=[wtok, ("u", k)], writes=[("ps", b)])
            evac(b)

        W, wt = self.wget(("win", 0))
        for g in range(4):
            def ev(b, g=g):
                add("act", lambda: nc.scalar.activation(out=a_tok(self.aext[:, g, :]), in_=tokv(self.ps[b][:, :]),
                                                        func=AF.Identity, bias=self.pp[:, pb + PB_IN + g:pb + PB_IN + g + 1]),
                    reads=[("ps", b), "pp"], writes=[("aext", g)])
                if kind == "halo":
                    add("pool", lambda: nc.gpsimd.tensor_tensor(out=self.aext[:, g, 16:16 + N], in0=self.aext[:, g, 16:16 + N],
                                                                in1=tmask, op=ALU.mult),
                        reads=[("aext", g), "tokmask"], writes=[("aext", g)])
                if samp:
                    add("act", lambda: nc.scalar.activation(out=self.anew[:, g, 0:N], in_=self.ps[b][:, 0:N], func=AF.Identity,
                                                            bias=self.pp[:, pb + PB_IN + g:pb + PB_IN + g + 1]),
                        reads=[("ps", b), "pp"], writes=[("sinkB", 0)])
            proj_chunk(W, wt, g * 128, ev)
        self.wrel(("win", 0))
        Wv, wvt = self.wget(("win", 1))
        Wgt, wgtt = self.wget(("win", 2))
        for c in range(4):
            def evg(b, c=c):
                add("act", lambda: nc.scalar.activation(out=self.sig[:, c % 2, 0:N], in_=self.ps[b][:, 0:N], func=AF.Sigmoid,
                                                        bias=self.pp[:, pb + PB_IN + 8 + c:pb + PB_IN + 9 + c]),
                    reads=[("ps", b), "pp"], writes=[("sig", c % 2)])
            proj_chunk(Wgt, wgtt, c * 128, evg)

            def evv(b, c=c):
                add("dve", lambda: nc.vector.scalar_tensor_tensor(
                    out=g_tok(self.gext[:, c, :], 32), in0=tokv(self.ps[b][:, :]),
                    scalar=self.pp[:, pb + PB_IN + 4 + c:pb + PB_IN + 5 + c], in1=tokv(self.sig[:, c % 2, :]),
                    op0=ALU.add, op1=ALU.mult),
                    reads=[("ps", b), ("sig", c % 2), "pp"], writes=[("gext", c)])
                if kind == "halo":
                    add("pool", lambda: nc.gpsimd.tensor_tensor(out=self.gext[:, c, 32:32 + N], in0=self.gext[:, c, 32:32 + N],
                                                                in1=tmask, op=ALU.mult),
                        reads=[("gext", c), "tokmask"], writes=[("gext", c)])
                if samp:
                    add("pool", lambda: nc.gpsimd.tensor_copy(out=tokv(self.gnew[:, c, :]), in_=g_tok(self.gext[:, c, :], 32)),
                        reads=[("gext", c)], writes=[("sinkB", 1)])
                add("act", lambda: nc.scalar.copy(out=self.gbf[:, c, 0:WG], in_=self.gext[:, c, 0:WG]),
                    reads=[("gext", c)], writes=self.tgbf(c))
            proj_chunk(Wv, wvt, c * 128, evv)
        self.wrel(("win", 1))
        self.wrel(("win", 2))
        for hh in range(2):
            W, wt = self.wget(("win", 3 + hh))
            for jj in range(4):
                j = hh * 4 + jj

                def ev(b, j=j):
                    add("act", lambda: nc.scalar.activation(out=self.q[:, j, 0:N], in_=self.ps[b][:, 0:N], func=AF.Identity,
                                                            scale=0.125, bias=self.bq8[:, j:j + 1]),
                        reads=[("ps", b), "bq8"], writes=[self.tq(j)])
                proj_chunk(W, wt, jj * 128, ev)
            self.wrel(("win", 3 + hh))
        W, wt = self.wget(("win", 5))
        koff = 0 if samp else 128
        for J in range(2):
            def ev(b, J=J):
                for hf in range(2):
                    hs = slice(hf * 64, (hf + 1) * 64)
                    add("act", lambda hf=hf, hs=hs: nc.scalar.activation(
                        out=self.kTz[hs, J, hf, koff:koff + N], in_=self.ps[b][hs, 0:N], func=AF.Identity,
                        bias=self.pp[hs, pb + PB_IN + 20 + J:pb + PB_IN + 21 + J]),
                        reads=[("ps", b), "pp"], writes=[("kT", J)])
                if samp or self.last_main:
                    add("act", lambda: nc.scalar.activation(out=self.kf[:, J, :], in_=self.ps[b][:, N - 128:N], func=AF.Identity,
                                                            bias=self.pp[:, pb + PB_IN + 20 + J:pb + PB_IN + 21 + J]),
                        reads=[("ps", b), "pp"], writes=[("kf", J)])
            proj_chunk(W, wt, J * 128, ev)
        if not samp:
            for tb in range(N // 128):
                b = self.P.bank()
                for k in range(8):
                    add("pe", lambda k=k, b=b, tb=tb: nc.tensor.matmul(self.ps[b][:, 0:256], self.u[:, k, tb * 128:(tb + 1) * 128],
                                                                       W[:, k, 256:512], start=(k == 0), stop=(k == 7)),
                        reads=[wt, ("u", k)], writes=[("ps", b)])
                for hf in range(2):
                    hs = slice(hf * 64, (hf + 1) * 64)
                    add("dve", lambda b=b, tb=tb, hf=hf, hs=hs: nc.vector.tensor_tensor(
                        out=self.vz[:, 1 + tb, :, :].rearrange("p (j h) c -> p j h c", h=2)[:, :, hf, hs],
                        in0=self.ps[b][:, 0:256].rearrange("p (j h d) -> p j h d", j=2, h=2)[:, :, hf, :],
                        in1=self.bkv[:, 256:512].rearrange("p (j h d) -> p j h d", j=2, h=2)[:, :, hf, :], op=ALU.add),
                        reads=[("ps", b), "bkv"], writes=[("v", 1 + tb)])
                if self.last_main and tb == 3:
                    add("dve", lambda b=b: nc.vector.tensor_tensor(out=self.ost[:, 1, 0:256], in0=self.ps[b][:, 0:256],
                                                                   in1=self.bkv[:, 256:512], op=ALU.add),
                        reads=[("ps", b), "bkv"], writes=[("pscr", 1)])
                    add("sp", lambda: nc.sync.dma_start(out=self.v_p[l], in_=self.ost[:, 1, 0:256]),
                        reads=[("pscr", 1)], dma=("ostd", 1))
        elif "sv" not in SKIP:
            for bp in range(NSEQ // 2):
                b = self.P.bank()
                for bb in range(2):
                    sq_ = bp * 2 + bb
                    for k in range(8):
                        add("pe", lambda k=k, b=b, bb=bb, sq_=sq_: nc.tensor.matmul(
                            self.ps[b][0:8, bb * 256:(bb + 1) * 256], self.u[:, k, sq_ * 8:(sq_ + 1) * 8], W[:, k, 256:512],
                            start=(k == 0), stop=(k == 7)),
                            reads=[wt, ("u", k)], writes=[("ps", b)])
                add("dve", lambda b=b: nc.vector.tensor_tensor(
                    out=self.vnewf[:, :, :], in0=self.ps[b][0:8, :].rearrange("p (a f) -> p a f", f=256),
                    in1=self.bkv[0:8, 256:512].unsqueeze(1).broadcast_to([8, 2, 256]), op=ALU.add),
                    reads=[("ps", b), "bkv"], writes=[("rD", 1)])
                add("act", lambda bp=bp: nc.scalar.copy(out=self.vnew[:, bp * 2:bp * 2 + 2, :], in_=self.vnewf[:, :, :]),
                    reads=[("rD", 1)], writes=["vnew"])
                add("sp", lambda bp=bp: nc.sync.dma_start(
                    out=self.v_s[l, bp * 2:bp * 2 + 2, 120:128, :].rearrange("b t f -> t b f"), in_=self.vnewf[:, :, :]),
                    reads=[("rD", 1)], dma="vs")
        self.wrel(("win", 5))

        if not samp:
            add("pool", lambda: nc.gpsimd.tensor_copy(out=self.stA[:, l, :, :], in_=self.aext[:, :, N:N + 16]),
                reads=[("aext", g) for g in range(4)], writes=["stA"])
            add("pool", lambda: nc.gpsimd.tensor_copy(out=self.stG[:, l, :, :], in_=self.gext[:, :, N:N + 32]),
                reads=[("gext", c) for c in range(4)], writes=["stG"])
            if self.last_main:
                self.prompt_state_out(l)
        elif "out" not in SKIP:
            self.samp_state_out(l)

        nd = [0]
        cb = []
        for c in range(4):
            b = self.P.bank()
            cb.append(b)
            for j in range(31):
                ds = nd[0] % NDIAG
                nd[0] += 1
                if j % 2 == 0:
                    add("pool", lambda ds=ds, c=c, j=j: nc.gpsimd.tensor_scalar(
                        self.diag[:, ds, :], self.identb[:, :], self.pp[:, pb + PB_CW + c * 31 + j:pb + PB_CW + c * 31 + j + 1],
                        1.0, ALU.mult, ALU.mult),
                        reads=["identb", "pp"], writes=[("diag", ds)])
                else:
                    add("act", lambda ds=ds, c=c, j=j: nc.scalar.activation(
                        out=self.diag[:, ds, :], in_=self.identb[:, :], func=AF.Identity,
                        scale=self.pp[:, pb + PB_CW + c * 31 + j:pb + PB_CW + c * 31 + j + 1]),
                        reads=["identb", "pp"], writes=[("diag", ds)])
                add("pe", lambda ds=ds, c=c, j=j, b=b: nc.tensor.matmul(
                    tokv(self.ps[b][:, :]), self.diag[:, ds, :], g_tok(self.gbf[:, c, :], 2 + j),
                    start=(j == 0), stop=(j == 30)),
                    reads=[("diag", ds)] + self.tgbf(c), writes=[("ps", b)])
        bm = self.P.bank()
        bv = self.P.bank()
        for c in range(4):
            b = cb[c]
            add("act", lambda b=b, c=c: nc.scalar.activation(out=self.cy[:, c, 0:N], in_=self.ps[b][:, 0:N], func=AF.Identity,
                                                             bias=self.pp[:, pb + PB_CB + c:pb + PB_CB + c + 1]),
                reads=[("ps", b), "pp"], writes=[("cy", c)])
            add("act", lambda c=c: nc.scalar.activation(out=self.sq[:, 0, 0:N], in_=self.cy[:, c, 0:N], func=AF.Square),
                reads=[("cy", c)], writes=[("sq", 0)])
            add("pool", lambda c=c: nc.gpsimd.tensor_copy(out=self.sq[:, 1, 0:N], in_=self.cy[:, c, 0:N]),
                reads=[("cy", c)], writes=[("sq", 1)])
            add("pe", lambda c=c: nc.tensor.matmul(self.ps[bm][:, 0:N], self.onesb[:, :], self.sq[:, 1, 0:N],
                                                   start=(c == 0), stop=(c == 3)),
                reads=[("sq", 1), "onesb"], writes=[("ps", bm)])
            add("pe", lambda c=c: nc.tensor.matmul(self.ps[bv][:, 0:N], self.onesb[:, :], self.sq[:, 0, 0:N],
                                                   start=(c == 0), stop=(c == 3)),
                reads=[("sq", 0), "onesb"], writes=[("ps", bv)])
        mu = self.mt[:, 0, 0:N]
        var = self.mt[:, 1, 0:N]
        add("dve", lambda: nc.vector.tensor_scalar(mu, self.ps[bm][:, 0:N], 1.0 / 512, None, ALU.mult),
            reads=[("ps", bm)], writes=[("mt", 0)])
        add("dve", lambda: nc.vector.tensor_tensor(out=self.rs[:, 0:N], in0=mu, in1=mu, op=ALU.mult),
            reads=[("mt", 0)], writes=["rs"])
        add("dve", lambda: nc.vector.scalar_tensor_tensor(out=var, in0=self.ps[bv][:, 0:N], scalar=1.0 / 512, in1=self.rs[:, 0:N],
                                                          op0=ALU.mult, op1=ALU.subtract),
            reads=[("ps", bv), "rs"], writes=[("mt", 1)])
        add("dve", lambda: nc.vector.tensor_scalar(var, var, 0.0, None, ALU.max), reads=[("mt", 1)], writes=[("mt", 1)])
        add("act", lambda: nc.scalar.activation(out=var, in_=var, func=AF.Ln, bias=EPS), reads=[("mt", 1)], writes=[("mt", 1)])
        add("act", lambda: nc.scalar.activation(out=var, in_=var, func=AF.Exp, scale=-0.5), reads=[("mt", 1)], writes=[("mt", 1)])
        for c in range(4):
            add("pool", lambda c=c: nc.gpsimd.tensor_tensor(out=self.cy[:, c, 0:N], in0=self.cy[:, c, 0:N], in1=mu, op=ALU.subtract),
                reads=[("cy", c), ("mt", 0)], writes=[("cy", c)])
            add("dve", lambda c=c: nc.vector.tensor_tensor(out=self.cy[:, c, 0:N], in0=self.cy[:, c, 0:N], in1=var, op=ALU.mult),
                reads=[("cy", c), ("mt", 1)], writes=[("cy", c)])
            add("act", lambda c=c: nc.scalar.activation(out=self.s[:, c, 0:N], in_=self.cy[:, c, 0:N], func=AF.Silu,
                                                        scale=self.pp[:, pb + PB_NG + c:pb + PB_NG + c + 1],
                                                        bias=self.pp[:, pb + PB_NB + c:pb + PB_NB + c + 1]),
                reads=[("cy", c), "pp"], writes=[self.ts_(c)])

        for g in range(4):
            ext = self.aext[:, g, :]
            src = ext
            src_tok = [("aext", g)]
            sh = 1
            i = 0
            while sh < WINS[g]:
                dst = self.pscr[:, i % 2, :]
                add("pool", lambda dst=dst, src=src, sh=sh: nc.gpsimd.tensor_tensor(
                    out=dst[:, sh:WA], in0=src[:, sh:WA], in1=src[:, 0:WA - sh], op=ALU.add),
                    reads=src_tok, writes=[("pscr", i % 2)])
                src = dst
                src_tok = [("pscr", i % 2)]
                sh *= 2
                i += 1
            add("dve", lambda src=src, ext=ext, g=g: nc.vector.scalar_tensor_tensor(
                out=tokv(self.r[:, g, :]), in0=a_tok(src), scalar=1.0 / WINS[g], in1=a_tok(ext),
                op0=ALU.mult, op1=ALU.subtract),
                reads=src_tok + [("aext", g)], writes=[self.tr(g)])
            if kind == "halo":
                add("dve", lambda src=src, g=g: nc.vector.tensor_tensor(out=self.rs[:, 0:16], in0=src[:, N:N + 16],
                                                                        in1=self.invc[:, g, :], op=ALU.mult),
                    reads=src_tok + ["invc"], writes=["rs"])
                add("dve", lambda ext=ext, g=g: nc.vector.tensor_tensor(out=self.r[:, g, N - 16:N], in0=self.rs[:, 0:16],
                                                                        in1=ext[:, N:N + 16], op=ALU.subtract),
                    reads=["rs", ("aext", g)], writes=[self.tr(g)])

        if samp:
            if "attn" not in SKIP:
                self.attn_sample(l)
        else:
            self.attn_prompt(l)
            for hf in range(2):
                hs = slice(hf * 64, (hf + 1) * 64)
                add("pool", lambda hf=hf, hs=hs: nc.gpsimd.tensor_copy(out=self.stK[hs, l, :, :], in_=self.kTz[hs, :, hf, N:N + 128]),
                    reads=[("kT", 0), ("kT", 1)], writes=["stK"])
                add("pool", lambda hf=hf, hs=hs: nc.gpsimd.tensor_copy(
                    out=self.stV[:, l, :].rearrange("p (j h d) -> p j h d", j=2, h=2)[:, :, hf, :],
                    in_=self.vz[:, N // 128, :, :].rearrange("p (j h) c -> p j h c", h=2)[:, :, hf, hs]),
                    reads=[("v", N // 128)], writes=["stV"])

        def gate_chunk(Wg, wgt, m, bi):
            bg = self.P.bank()
            for k in range(8):
                add("pe", lambda k=k, bg=bg: nc.tensor.matmul(self.ps[bg][:, 0:N], Wg[:, k, (m % 4) * 128:(m % 4 + 1) * 128],
                                                              self.u[:, k, 0:N], start=(k == 0), stop=(k == 7)),
                    reads=[wgt, ("u", k)], writes=[("ps", bg)])
            col = pb + PB_IN + 24 + bi * 8 + m
            add("act", lambda bg=bg: nc.scalar.activation(out=self.sig[:, m % 2, 0:N], in_=self.ps[bg][:, 0:N], func=AF.Sigmoid,
                                                          bias=self.pp[:, col:col + 1]),
                reads=[("ps", bg), "pp"], writes=[("sig", m % 2)])

        Wp, wpt = self.wpl, "wpl"
        for hh in range(2):
            Wg, wgt = self.wget(("win", 6 + hh))
            for mm in range(4):
                m = hh * 4 + mm
                gate_chunk(Wg, wgt, m, 0)
                by = self.P.bank()
                add("pe", lambda by=by, m=m: nc.tensor.matmul(self.ps[by][:, 0:N], Wp[:, m // 2, (m % 2) * 128:(m % 2 + 1) * 128],
                                                              self.r[:, m // 2, 0:N], start=True, stop=True),
                    reads=[wpt, self.tr(m // 2)], writes=[("ps", by)])
                ma, mtok = self.macc(m, N)
                add("dve", lambda by=by, m=m, ma=ma: nc.vector.scalar_tensor_tensor(
                    out=ma, in0=self.ps[by][:, 0:N], scalar=self.pp[:, pb + PB_PSC + m:pb + PB_PSC + m + 1],
                    in1=self.sig[:, m % 2, 0:N], op0=ALU.mult, op1=ALU.mult),
                    reads=[("ps", by), ("sig", m % 2), "pp"], writes=[mtok])
            self.wrel(("win", 6 + hh))
        for hh in range(2):
            Wc, wct = self.wget(("wco", hh))
            Wg, wgt = self.wget(("win", 8 + hh))
            for mm in range(4):
                m = hh * 4 + mm
                gate_chunk(Wg, wgt, m, 1)
                by = self.P.bank()
                for c in range(4):
                    add("pe", lambda by=by, mm=mm, c=c, Wc=Wc: nc.tensor.matmul(self.ps[by][:, 0:N], Wc[:, c, mm * 128:(mm + 1) * 128],
                                                                       self.s[:, c, 0:N], start=(c == 0), stop=(c == 3)),
                        reads=[wct, self.ts_(c)], writes=[("ps", by)])
                ma, mtok = self.macc(m, N)
                add("dve", lambda by=by, m=m: nc.vector.tensor_tensor(out=self.mt[:, m % 2, 0:N], in0=self.ps[by][:, 0:N],
                                                                      in1=self.sig[:, m % 2, 0:N], op=ALU.mult),
                    reads=[("ps", by), ("sig", m % 2)], writes=[("mt", m % 2)])
                add("pool", lambda m=m, ma=ma: nc.gpsimd.tensor_tensor(out=ma, in0=ma, in1=self.mt[:, m % 2, 0:N], op=ALU.add),
                    reads=[mtok, ("mt", m % 2)], writes=[mtok])
            self.wrel(("win", 8 + hh))
            self.wrel(("wco", hh))
        for hh in range(2):
            Wa, wat = self.wget(("wao", hh))
            Wg, wgt = self.wget(("win", 10 + hh))
            for mm in range(4):
                m = hh * 4 + mm
                gate_chunk(Wg, wgt, m, 2)
                by = self.P.bank()
                for k in range(8):
                    add("pe", lambda by=by, mm=mm, k=k, Wa=Wa: nc.tensor.matmul(self.ps[by][:, 0:N], Wa[:, k, mm * 128:(mm + 1) * 128],
                                                                         self.q[:, k, 0:N], start=(k == 0), stop=(k == 7)),
                        reads=[wat, self.tq(k)], writes=[("ps", by)])
                ma, mtok = self.macc(m, N)
                add("dve", lambda by=by, m=m: nc.vector.tensor_tensor(out=self.mt[:, m % 2, 0:N], in0=self.ps[by][:, 0:N],
                                                                      in1=self.sig[:, m % 2, 0:N], op=ALU.mult),
                    reads=[("ps", by), ("sig", m % 2)], writes=[("mt", m % 2)])
                add("pool", lambda m=m, ma=ma: nc.gpsimd.tensor_tensor(out=self.merged[:, m, 0:N], in0=ma, in1=self.mt[:, m % 2, 0:N],
                                                                       op=ALU.add),
                    reads=[mtok, ("mt", m % 2)], writes=[self.tmg(m)])
            self.wrel(("wao", hh))
            self.wrel(("win", 10 + hh))
        self.stat_begin()
        for hh in range(2):
            Wo, wot = self.wget(("wo", hh))
            for mm in range(4):
                m = hh * 4 + mm
                b = self.P.bank()
                for k in range(8):
                    add("pe", lambda b=b, mm=mm, k=k, Wo=Wo: nc.tensor.matmul(self.ps[b][:, 0:N], Wo[:, k, mm * 128:(mm + 1) * 128],
                                                                       self.merged[:, k, 0:N], start=(k == 0), stop=(k == 7)),
                        reads=[wot, self.tmg(k)], writes=[("ps", b)])
                self.stat_pe()
                hm = self.h[:, m, self.c0:self.c0 + N]
                add("dve", lambda b=b, hm=hm: nc.vector.tensor_tensor(out=hm, in0=self.ps[b][:, 0:N], in1=hm, op=ALU.add),
                    reads=[("ps", b), ("h", m)], writes=[("h", m)])
                self.stat_chunk(m)
            self.wrel(("wo", hh))
        self.stat_end()
        self.rmsnorm(N, pb + PB_NMLP)
        for j in range(8):
            Wu, wut = self.wget(("wup", j))
            for ff in range(4):
                f = j * 4 + ff
                b = self.P.bank()
                for k in range(8):
                    add("pe", lambda b=b, ff=ff, k=k, Wu=Wu: nc.tensor.matmul(self.ps[b][:, 0:N], Wu[:, k, ff * 128:(ff + 1) * 128],
                                                                       self.u[:, k, 0:N], start=(k == 0), stop=(k == 7)),
                        reads=[wut, ("u", k)], writes=[("ps", b)])
                add("act", lambda b=b, f=f: nc.scalar.activation(out=self.relu[:, f % 2, 0:N], in_=self.ps[b][:, 0:N], func=AF.Relu),
                    reads=[("ps", b)], writes=[("relu", f % 2)])
                add("dve", lambda b=b, f=f: nc.vector.tensor_tensor(out=self.hid[:, f, 0:N], in0=self.ps[b][:, 0:N],
                                                                    in1=self.relu[:, f % 2, 0:N], op=ALU.mult),
                    reads=[("ps", b), ("relu", f % 2)], writes=[self.thid(f)])
            self.wrel(("wup", j))
        self.stat_begin()
        for m in range(8):
            Wd, wdt = self.wget(("wdn", m))
            b = self.P.bank()
            for f in range(32):
                add("pe", lambda b=b, f=f, Wd=Wd: nc.tensor.matmul(self.ps[b][:, 0:N], Wd[:, f, :], self.hid[:, f, 0:N],
                                                            start=(f == 0), stop=(f == 31)),
                    reads=[wdt, self.thid(f)], writes=[("ps", b)])
            self.stat_pe()
            hm = self.h[:, m, self.c0:self.c0 + N]
            add("dve", lambda b=b, hm=hm: nc.vector.tensor_tensor(out=hm, in0=self.ps[b][:, 0:N], in1=hm, op=ALU.add),
                reads=[("ps", b), ("h", m)], writes=[("h", m)])
            self.stat_chunk(m)
            self.wrel(("wdn", m))
        self.stat_end()

    def attn_prompt(self, l):
        nc = self.nc
        add = self.add
        N = self.N
        LA = 3
        halo_off = self.c0 // 128
        units = [(qb, J, hf, c) for qb in range(N // 128) for J in range(2) for hf in range(2) for c in range(2)]
        info = {}
        grp = {}
        for gi, (qb, J) in enumerate([(qb, J) for qb in range(N // 128) for J in range(2)]):
            grp[(qb, J)] = (4, 5) if gi % 2 == 0 else (6, 7)

        def emit_qk(i):
            qb, J, hf, c = units[i]
            kv = 2 * J + hf
            qs = slice(qb * 128, (qb + 1) * 128)
            ps_ = slice(hf * 64, (hf + 1) * 64)
            kb = qb + c
            bl = i % 4
            add("pe", lambda: nc.tensor.matmul(
                self.ps[bl][:, :], self.identb[:, :],
                self.biasT[:, c, kv * 4:(kv + 1) * 4, :].rearrange("p g q -> p (g q)"), start=True, stop=False),
                reads=["identb", "biasT"], writes=[("ps", bl)])
            add("pe", lambda: nc.tensor.matmul(
                self.ps[bl][:, :].rearrange("p (g q) -> p g q", q=128),
                self.kTz[:, J, hf, kb * 128:(kb + 1) * 128], self.q[:, J * 4:(J + 1) * 4, qs], start=False, stop=True),
                reads=[("kT", J)] + [self.tq(J * 4 + g) for g in range(4)], writes=[("ps", bl)])
            info[i] = (bl, kb)

        def emit_soft(i):
            bl, kb = info[i]
            ai = i % 4
            mcol = None
            if self.kind == "halo":
                mcol = 4 if kb == 0 else kb - 1 + halo_off
            elif self.tname == "m0" and kb == 0:
                mcol = 3
            if mcol is None:
                add("act", lambda: nc.scalar.activation(out=self.pT[:, ai, :], in_=self.ps[bl][:, :], func=AF.Exp),
                    reads=[("ps", bl)], writes=[("pT", ai)])
            else:
                add("act", lambda: nc.scalar.activation(
                    out=self.pT[:, ai, :], in_=self.ps[bl][:, :], func=AF.Exp, bias=self.kmask[:, mcol:mcol + 1]),
                    reads=[("ps", bl), "kmask"], writes=[("pT", ai)])

        def emit_pv(i):
            qb, J, hf, c = units[i]
            kv = 2 * J + hf
            ps_ = slice(hf * 64, (hf + 1) * 64)
            bo, bd = grp[(qb, J)]
            bl, kb = info[i]
            ai = i % 4
            first = (hf == 0 and c == 0)
            last = (hf == 1 and c == 1)
            add("pe", lambda: nc.tensor.matmul(
                self.ps[bo][:, :], self.vz[:, kb, kv, :], self.pT[:, ai, :], start=first, stop=last),
                reads=[("v", kb), ("pT", ai)], writes=[("ps", bo)])
            add("pe", lambda: nc.tensor.matmul(
                self.ps[bd][:, :], self.onesz[:, hf, :], self.pT[:, ai, :], start=first, stop=last),
                reads=["onesz", ("pT", ai)], writes=[("ps", bd)])

        def emit_norm(qb, J):
            bo, bd = grp[(qb, J)]
            qs = slice(qb * 128, (qb + 1) * 128)
            add("dve", lambda: nc.vector.tensor_tensor(out=self.rD[:, J, :], in0=self.ps[bd][:, :], in1=self.sinkB[:, J, :], op=ALU.add),
                reads=[("ps", bd), ("sinkB", J)], writes=[("rD", J)])
            add("act", lambda: nc.scalar.activation(out=self.rD[:, J, :], in_=self.rD[:, J, :], func=AF.Ln),
                reads=[("rD", J)], writes=[("rD", J)])
            add("act", lambda: nc.scalar.activation(out=self.rD[:, J, :], in_=self.rD[:, J, :], func=AF.Exp, scale=-1.0),
                reads=[("rD", J)], writes=[("rD", J)])
            add("dve", lambda: nc.vector.tensor_tensor(
                out=self.q[:, J * 4:(J + 1) * 4, qs], in0=self.ps[bo][:, :].rearrange("p (g q) -> p g q", q=128),
                in1=self.rD[:, J, :].rearrange("p (g q) -> p g q", q=128), op=ALU.mult),
                reads=[("ps", bo), ("rD", J)], writes=[self.tq(J * 4 + g) for g in range(4)])

        n = len(units)
        pending = None
        for i in range(min(LA, n)):
            emit_qk(i)
        for i in range(n):
            if i + LA < n:
                emit_qk(i + LA)
            emit_soft(i)
            if pending is not None:
                emit_norm(*pending)
                pending = None
            emit_pv(i)
            qb, J, hf, c = units[i]
            if hf == 1 and c == 1:
                pending = (qb, J)
        emit_norm(*pending)

    def prompt_state_out(self, l):
        nc = self.nc
        add = self.add
        N = self.N
        for (src, tokname, lo, n, dst, slot) in ((self.aext, "aext", N + 1, 15, self.pool_p, 0),
                                                 (self.gext, "gext", N + 2, 30, self.conv_p, 0)):
            b = self.P.bank()
            for g in range(4):
                add("pe", lambda b=b, g=g, src=src, lo=lo, n=n: nc.tensor.transpose(
                    self.ps[b][0:n, g * 128:(g + 1) * 128], src[:, g, lo:lo + n], self.ident[:, :]),
                    reads=[(tokname, g), "ident"], writes=[("ps", b)])
            add("act", lambda b=b, n=n: nc.scalar.copy(out=self.ost[0:n, 0, :], in_=self.ps[b][0:n, :]),
                reads=[("ps", b)], writes=[("pscr", 0)])
            add("sp", lambda dst=dst, n=n: nc.sync.dma_start(out=dst[l], in_=self.ost[0:n, 0, :]),
                reads=[("pscr", 0)], dma=("ostd", 0))
        b = self.P.bank()
        for J in range(2):
            add("pe", lambda b=b, J=J: nc.tensor.transpose(self.ps[b][:, J * 128:(J + 1) * 128], self.kf[:, J, :], self.ident[:, :]),
                reads=[("kf", J), "ident"], writes=[("ps", b)])
        add("act", lambda b=b: nc.scalar.copy(out=self.ost[:, 0, 0:256], in_=self.ps[b][:, 0:256]),
            reads=[("ps", b)], writes=[("pscr", 0)])
        add("sp", lambda: nc.sync.dma_start(out=self.k_p[l], in_=self.ost[:, 0, 0:256]), reads=[("pscr", 0)], dma=("ostd", 0))

    def samp_load_states(self, l):
        nc = self.nc
        add = self.add
        for rb in range(2):
            stg = self.ost[0:120, rb, :]
            add("sp", lambda rb=rb, stg=stg: nc.sync.dma_start(
                out=stg, in_=self.spool[l, rb * 8:(rb + 1) * 8].rearrange("b i f -> (b i) f")),
                writes=[("pscr", rb)], dma=("ostd", rb))
            b = self.P.bank()
            for g in range(4):
                add("pe", lambda b=b, g=g, stg=stg: nc.tensor.transpose(
                    self.ps[b][:, g * 128:g * 128 + 120], stg[:, g * 128:(g + 1) * 128], self.ident[0:120, 0:120]),
                    reads=[("pscr", rb), "ident"], writes=[("ps", b)])
            add("act", lambda b=b, rb=rb: nc.scalar.copy(
                out=self.aext[:, :, rb * 8 * 24:(rb + 1) * 8 * 24].rearrange("p g (b i) -> p g b i", i=24)[:, :, :, 1:16],
                in_=self.ps[b][:, :].rearrange("p (g x) -> p g x", x=128)[:, :, 0:120].rearrange("p g (b i) -> p g b i", i=15)),
                reads=[("ps", b)], writes=[("aext", g) for g in range(4)])
        for rb in range(4):
            stg = self.ost[0:120, rb % 2, :]
            add("sp", lambda rb=rb, stg=stg: nc.sync.dma_start(
                out=stg, in_=self.sconv[l, rb * 4:(rb + 1) * 4].rearrange("b i f -> (b i) f")),
                writes=[("pscr", rb % 2)], dma=("ostd", rb % 2))
            b = self.P.bank()
            for g in range(4):
                add("pe", lambda b=b, g=g, stg=stg: nc.tensor.transpose(
                    self.ps[b][:, g * 128:g * 128 + 120], stg[:, g * 128:(g + 1) * 128], self.ident[0:120, 0:120]),
                    reads=[("pscr", rb % 2), "ident"], writes=[("ps", b)])
            add("act", lambda b=b, rb=rb: nc.scalar.copy(
                out=self.gext[:, :, rb * 4 * 40:(rb + 1) * 4 * 40].rearrange("p g (b i) -> p g b i", i=40)[:, :, :, 2:32],
                in_=self.ps[b][:, :].rearrange("p (g x) -> p g x", x=128)[:, :, 0:120].rearrange("p g (b i) -> p g b i", i=30)),
                reads=[("ps", b)], writes=[("gext", g) for g in range(4)])

    def samp_state_out(self, l):
        nc = self.nc
        add = self.add
        add("sp", lambda: nc.sync.dma_start(out=self.pool_s[l, :, 0:7, :], in_=self.spool[l, :, 8:15, :]), dma="h2h0")
        add("sp", lambda: nc.sync.dma_start(out=self.conv_s[l, :, 0:22, :], in_=self.sconv[l, :, 8:30, :]), dma="h2h1")
        add("sp", lambda: nc.sync.dma_start(out=self.k_s[l, :, 0:120, :], in_=self.ck[l, :, 8:128, :]), dma="h2h2")
        add("sp", lambda: nc.sync.dma_start(out=self.v_s[l, :, 0:120, :], in_=self.cv[l, :, 8:128, :]), dma="h2h3")
        for (src, tokname, dst, r0, sidx) in ((self.anew, ("sinkB", 0), self.pool_s, 7, 0), (self.gnew, ("sinkB", 1), self.conv_s, 22, 1)):
            b = self.P.bank()
            for g in range(4):
                add("pe", lambda b=b, g=g, src=src: nc.tensor.transpose(self.ps[b][:, g * 128:(g + 1) * 128], src[:, g, :], self.ident[:, :]),
                    reads=[tokname, "ident"], writes=[("ps", b)])
            add("act", lambda b=b, sidx=sidx: nc.scalar.copy(out=self.ost[:, sidx, :], in_=self.ps[b][:, :]),
                reads=[("ps", b)], writes=[("pscr", sidx)])
            for sq_ in range(NSEQ):
                add("sp", lambda dst=dst, r0=r0, sq_=sq_, sidx=sidx: nc.sync.dma_start(
                    out=dst[l, sq_, r0:r0 + 8, :], in_=self.ost[sq_ * 8:(sq_ + 1) * 8, sidx, :]),
                    reads=[("pscr", sidx)], dma=("osts", sidx * 4 + sq_ % 4))
        b = self.P.bank()
        for J in range(2):
            add("pe", lambda b=b, J=J: nc.tensor.transpose(self.ps[b][:, J * 128:(J + 1) * 128], self.kf[:, J, :], self.ident[:, :]),
                reads=[("kf", J), "ident"], writes=[("ps", b)])
        add("act", lambda b=b: nc.scalar.copy(out=self.rs[:, 0:256], in_=self.ps[b][:, 0:256]), reads=[("ps", b)], writes=["rs"])
        for sq_ in range(NSEQ):
            add("sp", lambda sq_=sq_: nc.sync.dma_start(out=self.k_s[l, sq_, 120:128, :], in_=self.rs[sq_ * 8:(sq_ + 1) * 8, 0:256]),
                reads=["rs"], dma=("osts", 8 + sq_ % 4))

    def attn_sample(self, l):
        nc = self.nc
        add = self.add
        for grp in range(4):
            buf = grp % 2
            s0 = grp * 4
            add("sp", lambda s0=s0: nc.sync.dma_start(out=self.kst[:, :, :], in_=self.ck[l, s0:s0 + 4].rearrange("b s f -> s b f")),
                writes=[("mt", 0), ("mt", 1)], dma="kst")
            for J in range(2):
                b = self.P.bank()
                for bb in range(4):
                    add("pe", lambda b=b, bb=bb, J=J: nc.tensor.transpose(
                        self.ps[b][:, bb * 128:(bb + 1) * 128], self.kst[:, bb, J * 128:(J + 1) * 128], self.ident[:, :]),
                        reads=[("mt", 0), ("mt", 1), "ident"], writes=[("ps", b)])
                add("act", lambda b=b, J=J: nc.scalar.copy(out=self.kcT[:, J, :, :].rearrange("p b s -> p (b s)"),
                                                          in_=self.ps[b][:, :]),
                    reads=[("ps", b)], writes=["stK"])
            add("pool", lambda s0=s0: nc.gpsimd.dma_start(out=self.vc[:, :, :],
                                                          in_=self.cv[l, s0:s0 + 4].rearrange("b s f -> s b f")),
                writes=["stV"], dma="vc")
            blc = self.P.bank()
            blo = self.P.bank()
            for bb in range(4):
                sq_ = s0 + bb
                cs = slice(sq_ * 8, (sq_ + 1) * 8)
                for kv in range(4):
                    J, hf = kv // 2, kv % 2
                    ps_ = slice(hf * 64, (hf + 1) * 64)
                    oc = self.ps[blc][:, kv * 128:(kv + 1) * 128].rearrange("p (g b q) -> p g b q", g=4, b=4)[:, :, bb, :]
                    add("pe", lambda oc=oc, J=J, ps_=ps_, cs=cs, buf=buf, bb=bb: nc.tensor.matmul(
                        oc, self.kcT[ps_, J, bb, :], self.q[ps_, J * 4:(J + 1) * 4, cs], start=True, stop=True),
                        reads=["stK"] + [self.tq(J * 4 + g) for g in range(4)], writes=[("ps", blc)])
                    oo = self.ps[blo][0:8, kv * 128:(kv + 1) * 128].rearrange("p (g b q) -> p g b q", g=4, b=4)[:, :, bb, :]
                    add("pe", lambda oo=oo, J=J, ps_=ps_, cs=cs, hf=hf: nc.tensor.matmul(
                        oo, self.kTz[ps_, J, hf, cs], self.q[ps_, J * 4:(J + 1) * 4, cs], start=True, stop=True),
                        reads=[("kT", J)] + [self.tq(J * 4 + g) for g in range(4)], writes=[("ps", blo)])
            add("dve", lambda blc=blc: nc.vector.tensor_tensor(
                out=self.atmp[:, 0, :].rearrange("p (h b q) -> p h b q", h=16, b=4), in0=self.ps[blc][:, :].rearrange("p (h b q) -> p h b q", h=16, b=4),
                in1=self.biasS[:, 0, :].rearrange("p (h q) -> p h q", q=8).unsqueeze(2).broadcast_to([128, 16, 4, 8]), op=ALU.add),
                reads=[("ps", blc), "biasS"], writes=[("atmp", 0)])
            add("dve", lambda blo=blo: nc.vector.tensor_tensor(
                out=self.atmp[0:8, 1, :].rearrange("p (h b q) -> p h b q", h=16, b=4), in0=self.ps[blo][0:8, :].rearrange("p (h b q) -> p h b q", h=16, b=4),
                in1=self.biasS[0:8, 1, :].rearrange("p (h q) -> p h q", q=8).unsqueeze(2).broadcast_to([8, 16, 4, 8]), op=ALU.add),
                reads=[("ps", blo), "biasS"], writes=[("atmp", 1)])
            add("act", lambda: nc.scalar.activation(out=self.pT[:, 0, :], in_=self.atmp[:, 0, :], func=AF.Exp),
                reads=[("atmp", 0)], writes=[("pT", 0)])
            add("act", lambda: nc.scalar.activation(out=self.pT[0:8, 1, :], in_=self.atmp[0:8, 1, :], func=AF.Exp),
                reads=[("atmp", 1)], writes=[("pT", 1)])
            bo = self.P.bank()
            bd = self.P.bank()
            for bb in range(4):
                sq_ = s0 + bb
                for kv in range(4):
                    J, hf = kv // 2, kv % 2
                    ps_ = slice(hf * 64, (hf + 1) * 64)
                    pc = self.pT[:, 0, kv * 128:(kv + 1) * 128].rearrange("p (g b q) -> p g b q", g=4, b=4)[:, :, bb, :]
                    po = self.pT[0:8, 1, kv * 128:(kv + 1) * 128].rearrange("p (g b q) -> p g b q", g=4, b=4)[:, :, bb, :]
                    for (bk, wa, wb) in ((bo, self.vc[:, bb, kv * 64:(kv + 1) * 64], self.vnew[0:8, sq_, kv * 64:(kv + 1) * 64]),
                                         (bd, self.onesb[:, 0:64], self.onesb[0:8, 0:64])):
                        oo = self.ps[bk][ps_, J * 128:(J + 1) * 128].rearrange("p (g b q) -> p g b q", g=4, b=4)[:, :, bb, :]
                        add("pe", lambda oo=oo, wa=wa, pc=pc: nc.tensor.matmul(oo, wa, pc, start=True, stop=False),
                            reads=["stV", ("pT", 0), "onesb"], writes=[("ps", bk)])
                        add("pe", lambda oo=oo, wb=wb, po=po: nc.tensor.matmul(oo, wb, po, start=False, stop=True),
                            reads=["vnew", ("pT", 1), "onesb"], writes=[("ps", bk)])
            add("dve", lambda bd=bd: nc.vector.tensor_tensor(
                out=self.rD[:, 0, 0:256].rearrange("p (h x) -> p h x", x=32), in0=self.ps[bd][:, 0:256].rearrange("p (h x) -> p h x", x=32),
                in1=self.sinkE[:, :].unsqueeze(2).broadcast_to([128, 8, 32]), op=ALU.add),
                reads=[("ps", bd), "sinkE"], writes=[("rD", 0)])
            add("act", lambda: nc.scalar.activation(out=self.rD[:, 0, 0:256], in_=self.rD[:, 0, 0:256], func=AF.Ln),
                reads=[("rD", 0)], writes=[("rD", 0)])
            add("act", lambda: nc.scalar.activation(out=self.rD[:, 0, 0:256], in_=self.rD[:, 0, 0:256], func=AF.Exp, scale=-1.0),
                reads=[("rD", 0)], writes=[("rD", 0)])
            add("dve", lambda bo=bo, s0=s0: nc.vector.tensor_tensor(
                out=self.q[:, 0:8, s0 * 8:(s0 + 4) * 8], in0=self.ps[bo][:, 0:256].rearrange("p (h x) -> p h x", x=32),
                in1=self.rD[:, 0, 0:256].rearrange("p (h x) -> p h x", x=32), op=ALU.mult),
                reads=[("ps", bo), ("rD", 0)], writes=[self.tq(j) for j in range(8)])


def _t5_bucket(dist):
    n = np.maximum(dist, 0)
    exact = 16
    large = exact + (np.log(np.maximum(n, 1) / exact) / np.log(128 / exact) * (32 - exact)).astype(np.int32)
    large = np.minimum(large, 31)
    return np.where(n < exact, n, large).astype(np.int32)


def _qperm():
    idx = []
    for j in range(8):
        for hf in range(2):
            kv = 2 * (j // 4) + hf
            g = j % 4
            head = kv * 4 + g
            idx.extend(range(head * 64, head * 64 + 64))
    return np.array(idx)


def _fm(vec, nch):
    return np.ascontiguousarray(np.asarray(vec, np.float32).reshape(nch, 128).T)


_NC_CACHE = {}


def prepare(inputs):
    f = lambda k: np.asarray(inputs[k], np.float32)
    x_prompt, x_sample = f("x_prompt"), f("x_sample")
    qp = _qperm()
    cols = np.concatenate([np.arange(0, 1536), 1536 + qp, np.arange(2560, 6144)])
    w_in = np.ascontiguousarray(f("w_in")[:, :, cols])
    b_in = f("b_in")[:, cols]
    w_ao = np.ascontiguousarray(f("w_attn_out")[:, qp, :])
    pp = np.zeros((128, NPP), np.float32)
    sinks = f("attn_sinks")
    for l in range(NL):
        o = l * PL
        pp[:, o + PB_IN:o + PB_IN + 48] = _fm(b_in[l], 48)
        pp[:, o + PB_NMIX:o + PB_NMIX + 8] = _fm(f("norm_mix")[l], 8)
        pp[:, o + PB_PSC:o + PB_PSC + 8] = _fm(f("pool_scale")[l], 8)
        pp[:, o + PB_NMLP:o + PB_NMLP + 8] = _fm(f("norm_mlp")[l], 8)
        cw = f("conv_w")[l]
        pp[:, o + PB_CW:o + PB_CW + 124] = cw.T.reshape(4, 128, 31).transpose(1, 0, 2).reshape(128, 124)
        pp[:, o + PB_CB:o + PB_CB + 4] = _fm(f("conv_b")[l], 4)
        pp[:, o + PB_NG:o + PB_NG + 4] = _fm(f("conv_norm_g")[l], 4)
        pp[:, o + PB_NB:o + PB_NB + 4] = _fm(f("conv_norm_b")[l], 4)
        for J in range(2):
            for g in range(4):
                for hf in range(2):
                    pp[hf * 64:(hf + 1) * 64, o + PB_SINK + J * 4 + g] = sinks[l, (2 * J + hf) * 4 + g]
    pp[:, PB_NF:PB_NF + 8] = _fm(f("norm_final"), 8)
    bkv = np.ascontiguousarray(np.broadcast_to(b_in[:, None, 2560:3072], (NL, 128, 512)))
    ext = np.concatenate([f("rel_bias"), np.full((1, 16), NEG, np.float32)], axis=0)
    s = np.arange(128)[:, None]
    q = np.arange(128)[None, :]
    d_prev = 128 + q - s
    d_own = q - s
    tabs = []
    for dmat in (d_prev, d_own):
        idx = np.where((dmat >= 0) & (dmat <= 128), _t5_bucket(dmat), 32)
        tabs.append(ext[idx])
    biasT = np.stack(tabs, axis=1).transpose(0, 1, 3, 2)
    biasT = np.ascontiguousarray(biasT).reshape(128, 2 * 16 * 128)
    biasS = np.ascontiguousarray(biasT.reshape(128, 2, 16, 128)[:, :, :, 0:8]).reshape(128, 256)
    common = dict(identm=np.eye(128, dtype=np.float32), biasS=biasS, w_in=w_in, w_pool=f("pool_w"), w_co=f("w_conv_out"), w_ao=w_ao, w_o=f("w_out"),
                  w_up=f("w_up"), w_dn=f("w_down"), pp=pp, bkv=bkv, biasT=biasT)
    meta = f("meta_tokens")
    in_maps = []
    for c in range(8):
        b, cc = c // 4, c % 4
        m = dict(common)
        m["xm"] = np.ascontiguousarray(x_prompt[b, cc * 2048:(cc + 1) * 2048])
        tokmask = np.ones((128, 512), np.float32)
        kmask = np.zeros((128, 8), np.float32)
        kmask[:, 4] = NEG
        invc = np.zeros((128, 4, 16), np.float32)
        for g, w in enumerate(WINS):
            invc[:, g, :] = 1.0 / w
        if cc == 0:
            xh = np.zeros((512, D), np.float32)
            xh[496:] = meta
            tokmask[:, :496] = 0.0
            kmask[:, 0:3] = NEG
            kmask[:112, 3] = NEG
            for g, w in enumerate(WINS):
                invc[:, g, :] = 1.0 / np.minimum(np.arange(16) + 1, w)
        else:
            xh = np.ascontiguousarray(x_prompt[b, cc * 2048 - 512:cc * 2048])
        m["xh"] = xh
        m["tokmask"] = tokmask
        m["kmask"] = kmask
        m["invc"] = invc.reshape(128, 64)
        sl = slice(c * NSEQ, (c + 1) * NSEQ)
        m["xs"] = np.ascontiguousarray(x_sample[sl].reshape(NS, D))
        m["spool"] = np.ascontiguousarray(f("state_pool")[:, sl])
        m["sconv"] = np.ascontiguousarray(f("state_conv")[:, sl])
        m["ck"] = np.ascontiguousarray(f("cache_k")[:, sl].reshape(NL, NSEQ, 128, 256))
        m["cv"] = np.ascontiguousarray(f("cache_v")[:, sl].reshape(NL, NSEQ, 128, 256))
        in_maps.append(m)
    return in_maps


def assemble(res):
    R = res.results
    y_prompt = np.zeros((2, 8192, D), np.float32)
    for c in range(8):
        y_prompt[c // 4, (c % 4) * 2048:(c % 4 + 1) * 2048] = R[c]["ym"]
    y_sample = np.concatenate([R[c]["ys"].reshape(NSEQ, DT, D) for c in range(8)], axis=0)
    pool_p = np.stack([R[3]["pool_p"], R[7]["pool_p"]], axis=1)
    conv_p = np.stack([R[3]["conv_p"], R[7]["conv_p"]], axis=1)
    k_p = np.stack([R[3]["k_p"], R[7]["k_p"]], axis=1).reshape(NL, 2, 128, 4, 64)
    v_p = np.stack([R[3]["v_p"], R[7]["v_p"]], axis=1).reshape(NL, 2, 128, 4, 64)
    pool_s = np.concatenate([R[c]["pool_s"] for c in range(8)], axis=1)
    conv_s = np.concatenate([R[c]["conv_s"] for c in range(8)], axis=1)
    k_s = np.concatenate([R[c]["k_s"] for c in range(8)], axis=1).reshape(NL, 128, 128, 4, 64)
    v_s = np.concatenate([R[c]["v_s"] for c in range(8)], axis=1).reshape(NL, 128, 128, 4, 64)
    outs = (y_prompt, y_sample, pool_p, pool_s, conv_p, conv_s, k_p, k_s, v_p, v_s)
    return tuple(np.ascontiguousarray(o, dtype=np.float32) for o in outs)


def kernel(**inputs):
    in_maps = prepare(inputs)
    if "nc" not in _NC_CACHE:
        _NC_CACHE["nc"] = Builder().build()
    nc = _NC_CACHE["nc"]
    res = run_bass_kernel_spmd(nc, in_maps, core_ids=list(range(8)))
    return assemble(res)
```

```python
import contextlib
import numpy as np
import concourse.bass as bass
import concourse.mybir as mybir
from concourse.bass_utils import run_bass_kernel_spmd

F32 = mybir.dt.float32
BF16 = mybir.dt.bfloat16
AF = mybir.ActivationFunctionType
ALU = mybir.AluOpType

D = 1024
NL = 4
DIN = 6144
TT = 512
NSEQ = 16
DT = 8
NS = NSEQ * DT
NEG = -30000.0
EPS = 1e-6
WINS = (2, 4, 8, 16)

PB_IN = 0
PB_NMIX = 48
PB_PSC = 56
PB_NMLP = 64
PB_CW = 72
PB_CB = 196
PB_NG = 200
PB_NB = 204
PB_SINK = 208
PL = 216
PB_NF = NL * PL
NPP = PB_NF + 8

SAME_ENG_SYNC = True
NWS = 4
NDIAG = 16
TRI_HALO = True
SKIP = set()


class Op:
    __slots__ = ("eng", "fn", "deps", "sig", "sigval", "dma")

    def __init__(self, eng, fn, dma):
        self.eng = eng
        self.fn = fn
        self.deps = []
        self.sig = False
        self.sigval = None
        self.dma = dma


class Prog:
    ENGS = ("pe", "act", "dve", "pool", "sp")

    def __init__(self, nc):
        self.nc = nc
        self.ops = []
        self.writer = {}
        self.readers = {}
        self.dma_count = {}
        self.nbank = 0
        self.reserved = set()

    def eng_obj(self, eng):
        nc = self.nc
        return {"pe": nc.tensor, "act": nc.scalar, "dve": nc.vector,
                "pool": nc.gpsimd, "sp": nc.sync}[eng]

    def add(self, eng, fn, reads=(), writes=(), dma=None):
        o = Op(eng, fn, dma)
        deps = {}
        for t in reads:
            w = self.writer.get(t)
            if w is not None:
                deps[id(w)] = w
        for t in writes:
            w = self.writer.get(t)
            if w is not None:
                deps[id(w)] = w
            last = {}
            for r in self.readers.get(t, ()):
                if r.dma is not None:
                    deps[id(r)] = r
                else:
                    last[r.eng] = r
            for r in last.values():
                deps[id(r)] = r
        o.deps = list(deps.values())
        for t in reads:
            self.readers.setdefault(t, []).append(o)
        for t in writes:
            self.writer[t] = o
            self.readers[t] = []
        if dma is not None:
            c = self.dma_count.get(dma, 0) + 1
            self.dma_count[dma] = c
            o.sigval = 16 * c
        self.ops.append(o)
        return o

    def bank(self):
        while True:
            b = self.nbank % 8
            self.nbank += 1
            if b not in self.reserved:
                return b

    @staticmethod
    def _needs(o, d):
        if d.dma is not None:
            return True
        if o.dma is not None:
            return True
        if d.eng != o.eng:
            return True
        return SAME_ENG_SYNC and d.eng != "pe"

    def emit(self):
        nc = self.nc
        for o in self.ops:
            for d in o.deps:
                if d.dma is None and self._needs(o, d):
                    d.sig = True
        cnt = {e: 0 for e in self.ENGS}
        for o in self.ops:
            if o.dma is None and o.sig:
                cnt[o.eng] += 1
                o.sigval = cnt[o.eng]
        self.sig_counts = dict(cnt)
        per_eng = {e: [o for o in self.ops if o.eng == e] for e in self.ENGS}
        with contextlib.ExitStack() as st:
            esem = {e: st.enter_context(nc.semaphore("c_" + e)) for e in self.ENGS}
            dsem = {n: st.enter_context(nc.semaphore("d_%d" % i))
                    for i, n in enumerate(self.dma_count)}
            block = st.enter_context(nc.Block())

            def run(eng):
                E = self.eng_obj(eng)
                seen = {}
                for o in per_eng[eng]:
                    waits = {}
                    for d in o.deps:
                        if not self._needs(o, d):
                            continue
                        s = dsem[d.dma] if d.dma is not None else esem[d.eng]
                        k = id(s)
                        if seen.get(k, 0) >= d.sigval:
                            continue
                        if k not in waits or waits[k][1] < d.sigval:
                            waits[k] = (s, d.sigval)
                    for k, (s, v) in waits.items():
                        E.wait_ge(s, v)
                        seen[k] = v
                    ins = o.fn()
                    if o.dma is not None:
                        ins.then_inc(dsem[o.dma], 16)
                    elif o.sig:
                        ins.then_inc(esem[o.eng], 1)
                if eng == "sp":
                    for n, c in self.dma_count.items():
                        E.wait_ge(dsem[n], 16 * c)

            block.tensor(lambda e: run("pe"))
            block.scalar(lambda e: run("act"))
            block.vector(lambda e: run("dve"))
            block.gpsimd(lambda e: run("pool"))
            block.sync(lambda e: run("sp"))


class Builder:
    def __init__(self, tiles=("halo", "m0", "m1", "m2", "m3", "samp"), nl=NL, dbg=False):
        self.tiles = tiles
        self.nl = nl
        self.dbg = dbg
        self.nc = bass.Bass("TRN2", target_bir_lowering=False)
        self.P = Prog(self.nc)
        self.st = contextlib.ExitStack()

    def din(self, name, shape):
        return self.nc.dram_tensor(name, list(shape), F32, kind="ExternalInput").ap()

    def dout(self, name, shape):
        return self.nc.dram_tensor(name, list(shape), F32, kind="ExternalOutput").ap()

    def sb(self, name, shape, dt):
        return self.st.enter_context(self.nc.sbuf_tensor(name, list(shape), dt))

    def add(self, *a, **k):
        return self.P.add(*a, **k)

    def build(self):
        nc = self.nc
        with self.st:
            self.declare()
            self.prologue()
            for tname in self.tiles:
                self.run_tile(tname)
            self.P.emit()
        return nc

    def declare(self):
        nc = self.nc
        sb = self.sb
        self.xh = self.din("xh", [512, D])
        self.xm = self.din("xm", [2048, D])
        self.xs = self.din("xs", [NS, D])
        self.tokmask_d = self.din("tokmask", [128, 512])
        self.kmask_d = self.din("kmask", [128, 8])
        self.invc_d = self.din("invc", [128, 64])
        self.spool = self.din("spool", [NL, NSEQ, 15, 512])
        self.sconv = self.din("sconv", [NL, NSEQ, 30, 512])
        self.ck = self.din("ck", [NL, NSEQ, 128, 256])
        self.cv = self.din("cv", [NL, NSEQ, 128, 256])
        self.w_in = self.din("w_in", [NL, D, DIN])
        self.w_pool = self.din("w_pool", [NL, 4, 128, 256])
        self.w_co = self.din("w_co", [NL, 512, D])
        self.w_ao = self.din("w_ao", [NL, D, D])
        self.w_o = self.din("w_o", [NL, D, D])
        self.w_up = self.din("w_up", [NL, D, 4096])
        self.w_dn = self.din("w_dn", [NL, 4096, D])
        self.pp_d = self.din("pp", [128, NPP])
        self.bkv_d = self.din("bkv", [NL, 128, 512])
        self.biasT_d = self.din("biasT", [128, 2 * 16 * 128])
        self.ident_d = self.din("identm", [128, 128])
        self.biasS_d = self.din("biasS", [128, 256])
        self.ym = self.dout("ym", [2048, D])
        self.ys = self.dout("ys", [NS, D])
        self.pool_p = self.dout("pool_p", [NL, 15, 512])
        self.conv_p = self.dout("conv_p", [NL, 30, 512])
        self.k_p = self.dout("k_p", [NL, 128, 256])
        self.v_p = self.dout("v_p", [NL, 128, 256])
        self.pool_s = self.dout("pool_s", [NL, NSEQ, 15, 512])
        self.conv_s = self.dout("conv_s", [NL, NSEQ, 30, 512])
        self.k_s = self.dout("k_s", [NL, NSEQ, 128, 256])
        self.v_s = self.dout("v_s", [NL, NSEQ, 128, 256])
        if self.dbg:
            self.dbg_d = self.dout("dbg", [128, 8 * TT])
        self.h = sb("h", [128, 8, TT], F32)
        self.u = sb("u", [128, 8, TT], BF16)
        self.sq = sb("sq", [128, 2, TT], BF16)
        self.rs = sb("rs", [128, TT], F32)
        self.aext = sb("aext", [128, 4, 16 + TT], F32)
        self.pscr = sb("pscr", [128, 2, 16 + TT], F32)
        self.gext = sb("gext", [128, 4, 640], F32)
        self.sig = sb("sig", [128, 2, TT], F32)
        self.cy = sb("cy", [128, 4, TT], F32)
        self.big = sb("big", [128, 32 * TT], BF16)
        self.kTz = sb("kTz", [128, 2, 2, 128 + TT], BF16)
        self.vz = sb("vz", [128, 5, 4, 128], BF16)
        self.onesz = sb("onesz", [128, 2, 128], BF16)
        self.biasT = sb("biasT_s", [128, 2, 16, 128], BF16)
        self.atmp = sb("atmp", [128, 2, TT], F32)
        self.pT = sb("pT", [128, 4, TT], BF16)
        self.rD = sb("rD", [128, 2, TT], F32)
        self.sinkB = sb("sinkB", [128, 2, TT], F32)
        self.sinkE = sb("sinkE", [128, 8], F32)
        self.bq8 = sb("bq8", [128, 8], F32)
        self.mt = sb("mt", [128, 2, TT], F32)
        self.relu = sb("relu", [128, 2, TT], BF16)
        self.diag = sb("diag", [128, NDIAG, 128], BF16)
        self.ws = [sb("ws%d" % i, [128, 4096], BF16) for i in range(NWS)]
        self.wpl = sb("wpl", [128, 4, 256], BF16)
        self.stA = sb("stA", [128, NL, 4, 16], F32)
        self.stG = sb("stG", [128, NL, 4, 32], F32)
        self.stb = sb("stb", [128, 2048], BF16)
        self.stK = self.stb[:, 0:1024].rearrange("p (l j s) -> p l j s", l=NL, j=2)
        self.stV = self.stb[:, 1024:2048].rearrange("p (l f) -> p l f", l=NL)
        self.kcT = self.stb[:, 0:1024].rearrange("p (j b s) -> p j b s", j=2, b=4)
        self.vc = self.stb[:, 1024:2048].rearrange("p (b f) -> p b f", b=4)
        self.ident = sb("ident", [128, 128], F32)
        self.identb = sb("identb", [128, 128], BF16)
        self.onesb = sb("onesb", [128, 128], BF16)
        self.pp = sb("pp_s", [128, NPP], F32)
        self.bkv = sb("bkv_s", [128, 512], F32)
        self.tokmask = sb("tokmask_s", [128, 512], F32)
        self.kmask = sb("kmask_s", [128, 8], F32)
        self.invc = sb("invc_s", [128, 4, 16], F32)
        self.kf = sb("kf", [128, 2, 128], F32)
        self.vnew = sb("vnew", [8, NSEQ, 256], BF16)
        self.biasS = sb("biasS_s", [128, 2, 128], F32)
        self.ps = [self.st.enter_context(nc.psum_tensor("ps%d" % i, [128, 512], F32))
                   for i in range(8)]
        self.ost = self.pscr[:, :, 0:512]
        self.kst = self.mt[:, :, :].rearrange("p a b -> p (a b)").rearrange("p (b f) -> p b f", f=256)
        self.anew = self.sinkB[:, 0, :].rearrange("p (g t) -> p g t", t=128)
        self.gnew = self.sinkB[:, 1, :].rearrange("p (g t) -> p g t", t=128)
        self.vnewf = self.rD[0:8, 1, :].rearrange("p (a f) -> p a f", f=256)
        bigv = self.big[:, :].rearrange("p (c t) -> p c t", t=TT)
        self.hid = bigv
        self.q = bigv[:, 0:8, :]
        self.merged = bigv[:, 8:16, :]
        self.gbf = self.big[:, 16 * TT:16 * TT + 4 * 640].rearrange("p (c t) -> p c t", t=640)
        self.r = bigv[:, 21:25, :]
        self.s = bigv[:, 25:29, :]
        self.wq = []
        self.wi = 0
        self.wissued = 0
        self.wcur = {}

    def tq(self, j):
        return ("big", j)

    def tmg(self, m):
        return ("big", 8 + m)

    def tgbf(self, c):
        return [("big", 16 + c), ("big", 17 + c)]

    def tr(self, g):
        return ("big", 21 + g)

    def ts_(self, c):
        return ("big", 25 + c)

    def thid(self, f):
        return ("big", f)

    def macc(self, m, N):
        if m < 4:
            return self.cy[:, m, 0:N], ("cy", m)
        return self.aext[:, m - 4, 0:N], ("aext", m - 4)

    def layer_items(self, l):
        it = []
        for j in range(6):
            it.append((("win", j), self.w_in[l, :, j * 512:(j + 1) * 512].rearrange("(k p) c -> p k c", p=128), 8, 512))
        for j in (6, 7):
            it.append((("win", j), self.w_in[l, :, j * 512:(j + 1) * 512].rearrange("(k p) c -> p k c", p=128), 8, 512))
        for hh in range(2):
            it.append((("wco", hh), self.w_co[l, :, hh * 512:(hh + 1) * 512].rearrange("(k p) c -> p k c", p=128), 4, 512))
            it.append((("win", 8 + hh), self.w_in[l, :, (8 + hh) * 512:(9 + hh) * 512].rearrange("(k p) c -> p k c", p=128), 8, 512))
        for hh in range(2):
            it.append((("wao", hh), self.w_ao[l, :, hh * 512:(hh + 1) * 512].rearrange("(k p) c -> p k c", p=128), 8, 512))
            it.append((("win", 10 + hh), self.w_in[l, :, (10 + hh) * 512:(11 + hh) * 512].rearrange("(k p) c -> p k c", p=128), 8, 512))
        for hh in range(2):
            it.append((("wo", hh), self.w_o[l, :, hh * 512:(hh + 1) * 512].rearrange("(k p) c -> p k c", p=128), 8, 512))
        for j in range(8):
            it.append((("wup", j), self.w_up[l, :, j * 512:(j + 1) * 512].rearrange("(k p) c -> p k c", p=128), 8, 512))
        for m in range(8):
            it.append((("wdn", m), self.w_dn[l, :, m * 128:(m + 1) * 128].rearrange("(k p) c -> p k c", p=128), 32, 128))
        return it

    def wissue(self):
        nc = self.nc
        while self.wissued < len(self.wq):
            i = self.wissued
            if i >= NWS and not self.wreleased[i - NWS]:
                break
            key, src, d1, d2 = self.wq[i]
            slot = i % NWS
            dst = self.ws[slot][:, 0:d1 * d2].rearrange("p (a b) -> p a b", b=d2)
            self.add("pool", lambda dst=dst, src=src: nc.gpsimd.dma_start(out=dst, in_=src),
                     writes=[("ws", slot)], dma=("ws", slot))
            self.wissued += 1

    def wget(self, key):
        i = self.wi
        k, src, d1, d2 = self.wq[i]
        assert k == key, (k, key)
        self.wissue()
        assert self.wissued > i, ("weight stream stuck", i, key)
        self.wi += 1
        slot = i % NWS
        self.wcur[key] = i
        return self.ws[slot][:, 0:d1 * d2].rearrange("p (a b) -> p a b", b=d2), ("ws", slot)

    def wrel(self, key):
        self.wreleased[self.wcur.pop(key)] = True
        self.wissue()

    def prologue(self):
        nc = self.nc
        add = self.add
        for t in self.tiles:
            for l in range(self.nl):
                self.wq.extend(self.layer_items(l))
        self.wreleased = [False] * len(self.wq)
        add("sp", lambda: nc.sync.dma_start(out=self.pp[:, :], in_=self.pp_d), writes=["pp"], dma="c0")
        add("pool", lambda: nc.gpsimd.dma_start(out=self.biasT[:, :, :, :].rearrange("p a b c -> p (a b c)"), in_=self.biasT_d),
            writes=["biasT"], dma="c1")
        add("sp", lambda: nc.sync.dma_start(out=self.biasS[:, :, :].rearrange("p a b -> p (a b)"), in_=self.biasS_d),
            writes=["biasS"], dma="c6")
        add("sp", lambda: nc.sync.dma_start(out=self.tokmask[:, :], in_=self.tokmask_d), writes=["tokmask"], dma="c2")
        add("sp", lambda: nc.sync.dma_start(out=self.kmask[:, :], in_=self.kmask_d), writes=["kmask"], dma="c3")
        add("sp", lambda: nc.sync.dma_start(out=self.invc[:, :, :].rearrange("p a b -> p (a b)"), in_=self.invc_d),
            writes=["invc"], dma="c4")
        add("pool", lambda: nc.gpsimd.memset(self.onesb[:, :], 1.0), writes=["onesb"])
        add("pool", lambda: nc.gpsimd.memset(self.onesz[:, :, :].rearrange("p a b -> p (a b)"), 0.0), writes=["onesz"])
        add("pool", lambda: nc.gpsimd.memset(self.onesz[:, 0, 0:64], 1.0), writes=["onesz"])
        add("pool", lambda: nc.gpsimd.memset(self.onesz[:, 1, 64:128], 1.0), writes=["onesz"])
        add("pool", lambda: nc.gpsimd.memset(self.kTz[:, :, :, :].rearrange("p a b c -> p (a b c)"), 0.0),
            writes=[("kT", 0), ("kT", 1)])
        add("pool", lambda: nc.gpsimd.memset(self.vz[:, :, :, :].rearrange("p a b c -> p (a b c)"), 0.0),
            writes=[("v", i) for i in range(5)])
        add("sp", lambda: nc.sync.dma_start(out=self.ident[:, :], in_=self.ident_d), writes=["ident"], dma="c5")
        add("pool", lambda: nc.gpsimd.memset(self.aext[:, :, :].rearrange("p a b -> p (a b)"), 0.0),
            writes=[("aext", g) for g in range(4)])
        add("pool", lambda: nc.gpsimd.memset(self.gext[:, :, :].rearrange("p a b -> p (a b)"), 0.0),
            writes=[("gext", g) for g in range(4)])
        add("pool", lambda: nc.gpsimd.memset(self.pscr[:, :, :].rearrange("p a b -> p (a b)"), 0.0),
            writes=[("pscr", 0), ("pscr", 1)])
        add("dve", lambda: nc.vector.tensor_copy(out=self.identb[:, :], in_=self.ident[:, :]),
            reads=["ident"], writes=["identb"])
        add("pool", lambda: nc.gpsimd.memset(self.stA[:, :, :, :].rearrange("p a b c -> p (a b c)"), 0.0), writes=["stA"])
        add("pool", lambda: nc.gpsimd.memset(self.stG[:, :, :, :].rearrange("p a b c -> p (a b c)"), 0.0), writes=["stG"])
        add("pool", lambda: nc.gpsimd.memset(self.stb[:, :], 0.0), writes=["stK", "stV"])

    def run_tile(self, tname):
        if tname == "samp":
            N, kind = NS, "samp"
            xsrc, ydst = self.xs, self.ys
        elif tname == "halo":
            N, kind = TT, "halo"
            xsrc, ydst = self.xh, None
        else:
            i = int(tname[1])
            N, kind = TT, "main"
            xsrc, ydst = self.xm[i * TT:(i + 1) * TT, :], self.ym[i * TT:(i + 1) * TT, :]
        self.N = N
        self.kind = kind
        self.tname = tname
        self.last_main = (tname == "m3")
        self.c0 = 0
        if getattr(self, "stats_ready", False):
            self.P.reserved.discard(self.sbank)
            self.stats_ready = False
        self.load_x(xsrc, N)
        for l in range(self.nl):
            if kind == "halo" and TRI_HALO:
                self.c0 = 128 * l
                self.N = TT - 128 * l
            self.layer(l)
        if self.dbg:
            nc = self.nc
            self.add("sp", lambda: nc.sync.dma_start(out=self.dbg_d, in_=self.h[:, :, :].rearrange("p a b -> p (a b)")),
                     reads=[("h", k) for k in range(8)], dma="dbg")
        if ydst is not None:
            self.final_norm(ydst, N)

    def load_x(self, xsrc, N):
        nc = self.nc
        add = self.add
        for tb in range(N // 128):
            xin = self.cy[:, 2 * (tb % 2):2 * (tb % 2) + 2, :].rearrange("p a b -> p (a b)")
            tk = [("cy", 2 * (tb % 2)), ("cy", 2 * (tb % 2) + 1)]
            add("sp", lambda xin=xin, tb=tb: nc.sync.dma_start(out=xin, in_=xsrc[tb * 128:(tb + 1) * 128, :]),
                writes=tk, dma=("xin", tb % 2))
            for hh in range(2):
                b = self.P.bank()
                for kk in range(4):
                    k = hh * 4 + kk
                    add("pe", lambda b=b, kk=kk, k=k, xin=xin: nc.tensor.transpose(
                        self.ps[b][:, kk * 128:(kk + 1) * 128], xin[:, k * 128:(k + 1) * 128], self.ident[:, :]),
                        reads=tk + ["ident"], writes=[("ps", b)])
                add("act", lambda b=b, hh=hh, tb=tb: nc.scalar.copy(
                    out=self.h[:, hh * 4:(hh + 1) * 4, tb * 128:(tb + 1) * 128],
                    in_=self.ps[b][:, :].rearrange("p (a b) -> p a b", b=128)),
                    reads=[("ps", b)], writes=[("h", hh * 4 + kk) for kk in range(4)])

    def stat_begin(self):
        self.sbank = self.P.bank()
        self.P.reserved.add(self.sbank)
        self.stat_n = 0
        self.stat_q = []
        self.stat_N = self.N

    def stat_chunk(self, k):
        nc = self.nc
        N = self.N
        i = self.stat_n
        self.stat_n += 1
        hk = self.h[:, k, self.c0:self.c0 + N]
        b = self.sbank
        self.add("act", lambda: nc.scalar.activation(out=self.sq[:, i % 2, 0:N], in_=hk, func=AF.Square),
                 reads=[("h", k)], writes=[("sq", i % 2)])
        self.stat_q.append(lambda: self.add(
            "pe", lambda: nc.tensor.matmul(self.ps[b][:, 0:N], self.onesb[:, :], self.sq[:, i % 2, 0:N],
                                           start=(i == 0), stop=(i == 7)),
            reads=[("sq", i % 2), "onesb"], writes=[("ps", b)]))

    def stat_pe(self, all_=False):
        while self.stat_q:
            self.stat_q.pop(0)()
            if not all_:
                break

    def stat_end(self):
        self.stat_pe(all_=True)
        self.stats_ready = True

    def rmsnorm_stats(self, N):
        nc = self.nc
        add = self.add
        if getattr(self, "stats_ready", False):
            self.stats_ready = False
            b = self.sbank
            off = self.stat_N - N
        else:
            off = 0
            b = self.P.bank()
            for k in range(8):
                hk = self.h[:, k, self.c0:self.c0 + N]
                add("act", lambda k=k, hk=hk: nc.scalar.activation(out=self.sq[:, k % 2, 0:N], in_=hk, func=AF.Square),
                    reads=[("h", k)], writes=[("sq", k % 2)])
                add("pe", lambda k=k, b=b: nc.tensor.matmul(self.ps[b][:, 0:N], self.onesb[:, :], self.sq[:, k % 2, 0:N],
                                                            start=(k == 0), stop=(k == 7)),
                    reads=[("sq", k % 2), "onesb"], writes=[("ps", b)])
        add("act", lambda b=b: nc.scalar.activation(out=self.rs[:, 0:N], in_=self.ps[b][:, off:off + N], func=AF.Ln,
                                                    scale=1.0 / D, bias=EPS),
            reads=[("ps", b)], writes=["rs"])
        add("act", lambda: nc.scalar.activation(out=self.rs[:, 0:N], in_=self.rs[:, 0:N], func=AF.Exp, scale=-0.5),
            reads=["rs"], writes=["rs"])
        self.P.reserved.discard(b)

    def rmsnorm(self, N, gcol):
        nc = self.nc
        self.rmsnorm_stats(N)
        for k in range(8):
            hk = self.h[:, k, self.c0:self.c0 + N]
            self.add("dve", lambda k=k, hk=hk: nc.vector.scalar_tensor_tensor(
                out=self.u[:, k, 0:N], in0=hk, scalar=self.pp[:, gcol + k:gcol + k + 1],
                in1=self.rs[:, 0:N], op0=ALU.mult, op1=ALU.mult),
                reads=[("h", k), "rs", "pp"], writes=[("u", k)])

    def final_norm(self, ydst, N):
        nc = self.nc
        add = self.add
        self.rmsnorm_stats(N)
        for k in range(8):
            add("dve", lambda k=k: nc.vector.scalar_tensor_tensor(
                out=self.h[:, k, 0:N], in0=self.h[:, k, 0:N], scalar=self.pp[:, PB_NF + k:PB_NF + k + 1],
                in1=self.rs[:, 0:N], op0=ALU.mult, op1=ALU.mult),
                reads=[("h", k), "rs", "pp"], writes=[("h", k)])
        for tb in range(N // 128):
            yo = self.cy[:, 2 * (tb % 2):2 * (tb % 2) + 2, :].rearrange("p a b -> p (a b)")
            tk = [("cy", 2 * (tb % 2)), ("cy", 2 * (tb % 2) + 1)]
            for hh in range(2):
                b = self.P.bank()
                for kk in range(4):
                    k = hh * 4 + kk
                    add("pe", lambda b=b, kk=kk, k=k, tb=tb: nc.tensor.transpose(
                        self.ps[b][:, kk * 128:(kk + 1) * 128], self.h[:, k, tb * 128:(tb + 1) * 128], self.ident[:, :]),
                        reads=[("h", k), "ident"], writes=[("ps", b)])
                add("act", lambda b=b, hh=hh, yo=yo: nc.scalar.copy(out=yo[:, hh * 512:(hh + 1) * 512], in_=self.ps[b][:, :]),
                    reads=[("ps", b)], writes=[tk[hh]])
            add("sp", lambda yo=yo, tb=tb: nc.sync.dma_start(out=ydst[tb * 128:(tb + 1) * 128, :], in_=yo),
                reads=tk, dma=("yout", tb % 2))

    def a_tok(self, ap2d):
        if self.kind == "samp":
            return ap2d[:, 0:NSEQ * 24].rearrange("p (b i) -> p b i", i=24)[:, :, 16:24]
        return ap2d[:, 16:16 + self.N]

    def g_tok(self, ap2d, off):
        if self.kind == "samp":
            return ap2d[:, 0:NSEQ * 40].rearrange("p (b i) -> p b i", i=40)[:, :, off:off + 8]
        return ap2d[:, off:off + self.N]

    def tokv(self, ap2d):
        if self.kind == "samp":
            return ap2d[:, 0:NS].rearrange("p (b t) -> p b t", t=8)
        return ap2d[:, 0:self.N]

    def layer(self, l):
        nc = self.nc
        add = self.add
        N = self.N
        kind = self.kind
        pb = l * PL
        samp = kind == "samp"
        c0 = self.c0
        tmask = self.tokmask[:, c0:c0 + N]
        WA = NSEQ * 24 if samp else 16 + N
        WG = NSEQ * 40 if samp else 32 + N

        def a_tok(ap2d):
            if samp:
                return ap2d[:, 0:NSEQ * 24].rearrange("p (b i) -> p b i", i=24)[:, :, 16:24]
            return ap2d[:, 16:16 + N]

        def g_tok(ap2d, off):
            if samp:
                return ap2d[:, 0:NSEQ * 40].rearrange("p (b i) -> p b i", i=40)[:, :, off:off + 8]
            return ap2d[:, off:off + N]

        def tokv(ap2d):
            if samp:
                return ap2d[:, 0:NS].rearrange("p (b t) -> p b t", t=8)
            return ap2d[:, 0:N]

        add("sp", lambda: nc.sync.dma_start(out=self.bkv[:, :], in_=self.bkv_d[l]), writes=["bkv"], dma="bkv")
        add("pool", lambda: nc.gpsimd.dma_start(out=self.wpl[:, :, :], in_=self.w_pool[l].rearrange("g c d -> c g d")),
            writes=["wpl"], dma="wpl")
        add("dve", lambda: nc.vector.tensor_scalar(self.bq8[:, :], self.pp[:, pb + PB_IN + 12:pb + PB_IN + 20], 0.125, None, ALU.mult),
            reads=["pp"], writes=["bq8"])
        add("act", lambda: nc.scalar.activation(out=self.sinkE[:, :], in_=self.pp[:, pb + PB_SINK:pb + PB_SINK + 8], func=AF.Exp),
            reads=["pp"], writes=["sinkE"])
        if not samp:
            for J in range(2):
                add("dve", lambda J=J: nc.vector.tensor_copy(
                    out=self.sinkB[:, J, :].rearrange("p (g q) -> p g q", q=128),
                    in_=self.sinkE[:, J * 4:(J + 1) * 4].unsqueeze(2).broadcast_to([128, 4, 128])),
                    reads=["sinkE"], writes=[("sinkB", J)])
            add("pool", lambda: nc.gpsimd.tensor_copy(out=self.aext[:, :, 0:16], in_=self.stA[:, l, :, :]),
                reads=["stA"], writes=[("aext", g) for g in range(4)])
            add("pool", lambda: nc.gpsimd.tensor_copy(out=self.gext[:, :, 0:32], in_=self.stG[:, l, :, :]),
                reads=["stG"], writes=[("gext", c) for c in range(4)])
            for hf in range(2):
                hs = slice(hf * 64, (hf + 1) * 64)
                add("pool", lambda hf=hf, hs=hs: nc.gpsimd.tensor_copy(out=self.kTz[hs, :, hf, 0:128], in_=self.stK[hs, l, :, :]),
                    reads=["stK"], writes=[("kT", 0), ("kT", 1)])
                add("pool", lambda hf=hf, hs=hs: nc.gpsimd.tensor_copy(
                    out=self.vz[:, 0, :, :].rearrange("p (j h) c -> p j h c", h=2)[:, :, hf, hs],
                    in_=self.stV[:, l, :].rearrange("p (j h d) -> p j h d", j=2, h=2)[:, :, hf, :]),
                    reads=["stV"], writes=[("v", 0)])
        elif "load" not in SKIP:
            self.samp_load_states(l)

        self.rmsnorm(N, pb + PB_NMIX)

        def proj_chunk(W, wtok, c0, evac):
            b = self.P.bank()
            for k in range(8):
                add("pe", lambda k=k, b=b: nc.tensor.matmul(self.ps[b][:, 0:N], W[:, k, c0:c0 + 128], self.u[:, k, 0:N],
                                                            start=(k == 0), stop=(k == 7)),
                    reads=[wtok, ("u", k)], writes=[("ps", b)])
            evac(b)

        W, wt = self.wget(("win", 0))
        for g in range(4):
            def ev(b, g=g):
                add("act", lambda: nc.scalar.activation(out=a_tok(self.aext[:, g, :]), in_=tokv(self.ps[b][:, :]),
                                                        func=AF.Identity, bias=self.pp[:, pb + PB_IN + g:pb + PB_IN + g + 1]),
                    reads=[("ps", b), "pp"], writes=[("aext", g)])
                if kind == "halo":
                    add("pool", lambda: nc.gpsimd.tensor_tensor(out=self.aext[:, g, 16:16 + N], in0=self.aext[:, g, 16:16 + N],
                                                                in1=tmask, op=ALU.mult),
                        reads=[("aext", g), "tokmask"], writes=[("aext", g)])
                if samp:
                    add("act", lambda: nc.scalar.activation(out=self.anew[:, g, 0:N], in_=self.ps[b][:, 0:N], func=AF.Identity,
                                                            bias=self.pp[:, pb + PB_IN + g:pb + PB_IN + g + 1]),
                        reads=[("ps", b), "pp"], writes=[("sinkB", 0)])
            proj_chunk(W, wt, g * 128, ev)
        self.wrel(("win", 0))
        Wv, wvt = self.wget(("win", 1))
        Wgt, wgtt = self.wget(("win", 2))
        for c in range(4):
            def evg(b, c=c):
                add("act", lambda: nc.scalar.activation(out=self.sig[:, c % 2, 0:N], in_=self.ps[b][:, 0:N], func=AF.Sigmoid,
                                                        bias=self.pp[:, pb + PB_IN + 8 + c:pb + PB_IN + 9 + c]),
                    reads=[("ps", b), "pp"], writes=[("sig", c % 2)])
            proj_chunk(Wgt, wgtt, c * 128, evg)

            def evv(b, c=c):
                add("dve", lambda: nc.vector.scalar_tensor_tensor(
                    out=g_tok(self.gext[:, c, :], 32), in0=tokv(self.ps[b][:, :]),
                    scalar=self.pp[:, pb + PB_IN + 4 + c:pb + PB_IN + 5 + c], in1=tokv(self.sig[:, c % 2, :]),
                    op0=ALU.add, op1=ALU.mult),
                    reads=[("ps", b), ("sig", c % 2), "pp"], writes=[("gext", c)])
                if kind == "halo":
                    add("pool", lambda: nc.gpsimd.tensor_tensor(out=self.gext[:, c, 32:32 + N], in0=self.gext[:, c, 32:32 + N],
                                                                in1=tmask, op=ALU.mult),
                        reads=[("gext", c), "tokmask"], writes=[("gext", c)])
                if samp:
                    add("pool", lambda: nc.gpsimd.tensor_copy(out=tokv(self.gnew[:, c, :]), in_=g_tok(self.gext[:, c, :], 32)),
                        reads=[("gext", c)], writes=[("sinkB", 1)])
                add("act", lambda: nc.scalar.copy(out=self.gbf[:, c, 0:WG], in_=self.gext[:, c, 0:WG]),
                    reads=[("gext", c)], writes=self.tgbf(c))
            proj_chunk(Wv, wvt, c * 128, evv)
        self.wrel(("win", 1))
        self.wrel(("win", 2))
        for hh in range(2):
            W, wt = self.wget(("win", 3 + hh))
            for jj in range(4):
                j = hh * 4 + jj

                def ev(b, j=j):
                    add("act", lambda: nc.scalar.activation(out=self.q[:, j, 0:N], in_=self.ps[b][:, 0:N], func=AF.Identity,
                                                            scale=0.125, bias=self.bq8[:, j:j + 1]),
                        reads=[("ps", b), "bq8"], writes=[self.tq(j)])
                proj_chunk(W, wt, jj * 128, ev)
            self.wrel(("win", 3 + hh))
        W, wt = self.wget(("win", 5))
        koff = 0 if samp else 128
        for J in range(2):
            def ev(b, J=J):
                for hf in range(2):
                    hs = slice(hf * 64, (hf + 1) * 64)
                    add("act", lambda hf=hf, hs=hs: nc.scalar.activation(
                        out=self.kTz[hs, J, hf, koff:koff + N], in_=self.ps[b][hs, 0:N], func=AF.Identity,
                        bias=self.pp[hs, pb + PB_IN + 20 + J:pb + PB_IN + 21 + J]),
                        reads=[("ps", b), "pp"], writes=[("kT", J)])
                if samp or self.last_main:
                    add("act", lambda: nc.scalar.activation(out=self.kf[:, J, :], in_=self.ps[b][:, N - 128:N], func=AF.Identity,
                                                            bias=self.pp[:, pb + PB_IN + 20 + J:pb + PB_IN + 21 + J]),
                        reads=[("ps", b), "pp"], writes=[("kf", J)])
            proj_chunk(W, wt, J * 128, ev)
        if not samp:
            for tb in range(N // 128):
                b = self.P.bank()
                for k in range(8):
                    add("pe", lambda k=k, b=b, tb=tb: nc.tensor.matmul(self.ps[b][:, 0:256], self.u[:, k, tb * 128:(tb + 1) * 128],
                                                                       W[:, k, 256:512], start=(k == 0), stop=(k == 7)),
                        reads=[wt, ("u", k)], writes=[("ps", b)])
                for hf in range(2):
                    hs = slice(hf * 64, (hf + 1) * 64)
                    add("dve", lambda b=b, tb=tb, hf=hf, hs=hs: nc.vector.tensor_tensor(
                        out=self.vz[:, 1 + tb, :, :].rearrange("p (j h) c -> p j h c", h=2)[:, :, hf, hs],
                        in0=self.ps[b][:, 0:256].rearrange("p (j h d) -> p j h d", j=2, h=2)[:, :, hf, :],
                        in1=self.bkv[:, 256:512].rearrange("p (j h d) -> p j h d", j=2, h=2)[:, :, hf, :], op=ALU.add),
                        reads=[("ps", b), "bkv"], writes=[("v", 1 + tb)])
                if self.last_main and tb == 3:
                    add("dve", lambda b=b: nc.vector.tensor_tensor(out=self.ost[:, 1, 0:256], in0=self.ps[b][:, 0:256],
                                                                   in1=self.bkv[:, 256:512], op=ALU.add),
                        reads=[("ps", b), "bkv"], writes=[("pscr", 1)])
                    add("sp", lambda: nc.sync.dma_start(out=self.v_p[l], in_=self.ost[:, 1, 0:256]),
                        reads=[("pscr", 1)], dma=("ostd", 1))
        elif "sv" not in SKIP:
            for bp in range(NSEQ // 2):
                b = self.P.bank()
                for bb in range(2):
                    sq_ = bp * 2 + bb
                    for k in range(8):
                        add("pe", lambda k=k, b=b, bb=bb, sq_=sq_: nc.tensor.matmul(
                            self.ps[b][0:8, bb * 256:(bb + 1) * 256], self.u[:, k, sq_ * 8:(sq_ + 1) * 8], W[:, k, 256:512],
                            start=(k == 0), stop=(k == 7)),
                            reads=[wt, ("u", k)], writes=[("ps", b)])
                add("dve", lambda b=b: nc.vector.tensor_tensor(
                    out=self.vnewf[:, :, :], in0=self.ps[b][0:8, :].rearrange("p (a f) -> p a f", f=256),
                    in1=self.bkv[0:8, 256:512].unsqueeze(1).broadcast_to([8, 2, 256]), op=ALU.add),
                    reads=[("ps", b), "bkv"], writes=[("rD", 1)])
                add("act", lambda bp=bp: nc.scalar.copy(out=self.vnew[:, bp * 2:bp * 2 + 2, :], in_=self.vnewf[:, :, :]),
                    reads=[("rD", 1)], writes=["vnew"])
                add("sp", lambda bp=bp: nc.sync.dma_start(
                    out=self.v_s[l, bp * 2:bp * 2 + 2, 120:128, :].rearrange("b t f -> t b f"), in_=self.vnewf[:, :, :]),
                    reads=[("rD", 1)], dma="vs")
        self.wrel(("win", 5))

        if not samp:
            add("pool", lambda: nc.gpsimd.tensor_copy(out=self.stA[:, l, :, :], in_=self.aext[:, :, N:N + 16]),
                reads=[("aext", g) for g in range(4)], writes=["stA"])
            add("pool", lambda: nc.gpsimd.tensor_copy(out=self.stG[:, l, :, :], in_=self.gext[:, :, N:N + 32]),
                reads=[("gext", c) for c in range(4)], writes=["stG"])
            if self.last_main:
                self.prompt_state_out(l)
        elif "out" not in SKIP:
            self.samp_state_out(l)

        nd = [0]
        cb = []
        for c in range(4):
            b = self.P.bank()
            cb.append(b)
            for j in range(31):
                ds = nd[0] % NDIAG
                nd[0] += 1
                if j % 2 == 0:
                    add("pool", lambda ds=ds, c=c, j=j: nc.gpsimd.tensor_scalar(
                        self.diag[:, ds, :], self.identb[:, :], self.pp[:, pb + PB_CW + c * 31 + j:pb + PB_CW + c * 31 + j + 1],
                        1.0, ALU.mult, ALU.mult),
                        reads=["identb", "pp"], writes=[("diag", ds)])
                else:
                    add("act", lambda ds=ds, c=c, j=j: nc.scalar.activation(
                        out=self.diag[:, ds, :], in_=self.identb[:, :], func=AF.Identity,
                        scale=self.pp[:, pb + PB_CW + c * 31 + j:pb + PB_CW + c * 31 + j + 1]),
                        reads=["identb", "pp"], writes=[("diag", ds)])
                add("pe", lambda ds=ds, c=c, j=j, b=b: nc.tensor.matmul(
                    tokv(self.ps[b][:, :]), self.diag[:, ds, :], g_tok(self.gbf[:, c, :], 2 + j),
                    start=(j == 0), stop=(j == 30)),
                    reads=[("diag", ds)] + self.tgbf(c), writes=[("ps", b)])
        bm = self.P.bank()
        bv = self.P.bank()
        for c in range(4):
            b = cb[c]
            add("act", lambda b=b, c=c: nc.scalar.activation(out=self.cy[:, c, 0:N], in_=self.ps[b][:, 0:N], func=AF.Identity,
                                                             bias=self.pp[:, pb + PB_CB + c:pb + PB_CB + c + 1]),
                reads=[("ps", b), "pp"], writes=[("cy", c)])
            add("act", lambda c=c: nc.scalar.activation(out=self.sq[:, 0, 0:N], in_=self.cy[:, c, 0:N], func=AF.Square),
                reads=[("cy", c)], writes=[("sq", 0)])
            add("pool", lambda c=c: nc.gpsimd.tensor_copy(out=self.sq[:, 1, 0:N], in_=self.cy[:, c, 0:N]),
                reads=[("cy", c)], writes=[("sq", 1)])
            add("pe", lambda c=c: nc.tensor.matmul(self.ps[bm][:, 0:N], self.onesb[:, :], self.sq[:, 1, 0:N],
                                                   start=(c == 0), stop=(c == 3)),
                reads=[("sq", 1), "onesb"], writes=[("ps", bm)])
            add("pe", lambda c=c: nc.tensor.matmul(self.ps[bv][:, 0:N], self.onesb[:, :], self.sq[:, 0, 0:N],
                                                   start=(c == 0), stop=(c == 3)),
                reads=[("sq", 0), "onesb"], writes=[("ps", bv)])
        mu = self.mt[:, 0, 0:N]
        var = self.mt[:, 1, 0:N]
        add("dve", lambda: nc.vector.tensor_scalar(mu, self.ps[bm][:, 0:N], 1.0 / 512, None, ALU.mult),
            reads=[("ps", bm)], writes=[("mt", 0)])
        add("dve", lambda: nc.vector.tensor_tensor(out=self.rs[:, 0:N], in0=mu, in1=mu, op=ALU.mult),
            reads=[("mt", 0)], writes=["rs"])
        add("dve", lambda: nc.vector.scalar_tensor_tensor(out=var, in0=self.ps[bv][:, 0:N], scalar=1.0 / 512, in1=self.rs[:, 0:N],
                                                          op0=ALU.mult, op1=ALU.subtract),
            reads=[("ps", bv), "rs"], writes=[("mt", 1)])
        add("dve", lambda: nc.vector.tensor_scalar(var, var, 0.0, None, ALU.max), reads=[("mt", 1)], writes=[("mt", 1)])
        add("act", lambda: nc.scalar.activation(out=var, in_=var, func=AF.Ln, bias=EPS), reads=[("mt", 1)], writes=[("mt", 1)])
        add("act", lambda: nc.scalar.activation(out=var, in_=var, func=AF.Exp, scale=-0.5), reads=[("mt", 1)], writes=[("mt", 1)])
        for c in range(4):
            add("pool", lambda c=c: nc.gpsimd.tensor_tensor(out=self.cy[:, c, 0:N], in0=self.cy[:, c, 0:N], in1=mu, op=ALU.subtract),
                reads=[("cy", c), ("mt", 0)], writes=[("cy", c)])
            add("dve", lambda c=c: nc.vector.tensor_tensor(out=self.cy[:, c, 0:N], in0=self.cy[:, c, 0:N], in1=var, op=ALU.mult),
                reads=[("cy", c), ("mt", 1)], writes=[("cy", c)])
            add("act", lambda c=c: nc.scalar.activation(out=self.s[:, c, 0:N], in_=self.cy[:, c, 0:N], func=AF.Silu,
                                                        scale=self.pp[:, pb + PB_NG + c:pb + PB_NG + c + 1],
                                                        bias=self.pp[:, pb + PB_NB + c:pb + PB_NB + c + 1]),
                reads=[("cy", c), "pp"], writes=[self.ts_(c)])

        for g in range(4):
            ext = self.aext[:, g, :]
            src = ext
            src_tok = [("aext", g)]
            sh = 1
            i = 0
            while sh < WINS[g]:
                dst = self.pscr[:, i % 2, :]
                add("pool", lambda dst=dst, src=src, sh=sh: nc.gpsimd.tensor_tensor(
                    out=dst[:, sh:WA], in0=src[:, sh:WA], in1=src[:, 0:WA - sh], op=ALU.add),
                    reads=src_tok, writes=[("pscr", i % 2)])
                src = dst
                src_tok = [("pscr", i % 2)]
                sh *= 2
                i += 1
            add("dve", lambda src=src, ext=ext, g=g: nc.vector.scalar_tensor_tensor(
                out=tokv(self.r[:, g, :]), in0=a_tok(src), scalar=1.0 / WINS[g], in1=a_tok(ext),
                op0=ALU.mult, op1=ALU.subtract),
                reads=src_tok + [("aext", g)], writes=[self.tr(g)])
            if kind == "halo":
                add("dve", lambda src=src, g=g: nc.vector.tensor_tensor(out=self.rs[:, 0:16], in0=src[:, N:N + 16],
                                                                        in1=self.invc[:, g, :], op=ALU.mult),
                    reads=src_tok + ["invc"], writes=["rs"])
                add("dve", lambda ext=ext, g=g: nc.vector.tensor_tensor(out=self.r[:, g, N - 16:N], in0=self.rs[:, 0:16],
                                                                        in1=ext[:, N:N + 16], op=ALU.subtract),
                    reads=["rs", ("aext", g)], writes=[self.tr(g)])

        if samp:
            if "attn" not in SKIP:
                self.attn_sample(l)
        else:
            self.attn_prompt(l)
            for hf in range(2):
                hs = slice(hf * 64, (hf + 1) * 64)
                add("pool", lambda hf=hf, hs=hs: nc.gpsimd.tensor_copy(out=self.stK[hs, l, :, :], in_=self.kTz[hs, :, hf, N:N + 128]),
                    reads=[("kT", 0), ("kT", 1)], writes=["stK"])
                add("pool", lambda hf=hf, hs=hs: nc.gpsimd.tensor_copy(
                    out=self.stV[:, l, :].rearrange("p (j h d) -> p j h d", j=2, h=2)[:, :, hf, :],
                    in_=self.vz[:, N // 128, :, :].rearrange("p (j h) c -> p j h c", h=2)[:, :, hf, hs]),
                    reads=[("v", N // 128)], writes=["stV"])

        def gate_chunk(Wg, wgt, m, bi):
            bg = self.P.bank()
            for k in range(8):
                add("pe", lambda k=k, bg=bg: nc.tensor.matmul(self.ps[bg][:, 0:N], Wg[:, k, (m % 4) * 128:(m % 4 + 1) * 128],
                                                              self.u[:, k, 0:N], start=(k == 0), stop=(k == 7)),
                    reads=[wgt, ("u", k)], writes=[("ps", bg)])
            col = pb + PB_IN + 24 + bi * 8 + m
            add("act", lambda bg=bg: nc.scalar.activation(out=self.sig[:, m % 2, 0:N], in_=self.ps[bg][:, 0:N], func=AF.Sigmoid,
                                                          bias=self.pp[:, col:col + 1]),
                reads=[("ps", bg), "pp"], writes=[("sig", m % 2)])

        Wp, wpt = self.wpl, "wpl"
        for hh in range(2):
            Wg, wgt = self.wget(("win", 6 + hh))
            for mm in range(4):
                m = hh * 4 + mm
                gate_chunk(Wg, wgt, m, 0)
                by = self.P.bank()
                add("pe", lambda by=by, m=m: nc.tensor.matmul(self.ps[by][:, 0:N], Wp[:, m // 2, (m % 2) * 128:(m % 2 + 1) * 128],
                                                              self.r[:, m // 2, 0:N], start=True, stop=True),
                    reads=[wpt, self.tr(m // 2)], writes=[("ps", by)])
                ma, mtok = self.macc(m, N)
                add("dve", lambda by=by, m=m, ma=ma: nc.vector.scalar_tensor_tensor(
                    out=ma, in0=self.ps[by][:, 0:N], scalar=self.pp[:, pb + PB_PSC + m:pb + PB_PSC + m + 1],
                    in1=self.sig[:, m % 2, 0:N], op0=ALU.mult, op1=ALU.mult),
                    reads=[("ps", by), ("sig", m % 2), "pp"], writes=[mtok])
            self.wrel(("win", 6 + hh))
        for hh in range(2):
            Wc, wct = self.wget(("wco", hh))
            Wg, wgt = self.wget(("win", 8 + hh))
            for mm in range(4):
                m = hh * 4 + mm
                gate_chunk(Wg, wgt, m, 1)
                by = self.P.bank()
                for c in range(4):
                    add("pe", lambda by=by, mm=mm, c=c, Wc=Wc: nc.tensor.matmul(self.ps[by][:, 0:N], Wc[:, c, mm * 128:(mm + 1) * 128],
                                                                       self.s[:, c, 0:N], start=(c == 0), stop=(c == 3)),
                        reads=[wct, self.ts_(c)], writes=[("ps", by)])
                ma, mtok = self.macc(m, N)
                add("dve", lambda by=by, m=m: nc.vector.tensor_tensor(out=self.mt[:, m % 2, 0:N], in0=self.ps[by][:, 0:N],
                                                                      in1=self.sig[:, m % 2, 0:N], op=ALU.mult),
                    reads=[("ps", by), ("sig", m % 2)], writes=[("mt", m % 2)])
                add("pool", lambda m=m, ma=ma: nc.gpsimd.tensor_tensor(out=ma, in0=ma, in1=self.mt[:, m % 2, 0:N], op=ALU.add),
                    reads=[mtok, ("mt", m % 2)], writes=[mtok])
            self.wrel(("win", 8 + hh))
            self.wrel(("wco", hh))
        for hh in range(2):
            Wa, wat = self.wget(("wao", hh))
            Wg, wgt = self.wget(("win", 10 + hh))
            for mm in range(4):
                m = hh * 4 + mm
                gate_chunk(Wg, wgt, m, 2)
                by = self.P.bank()
                for k in range(8):
                    add("pe", lambda by=by, mm=mm, k=k, Wa=Wa: nc.tensor.matmul(self.ps[by][:, 0:N], Wa[:, k, mm * 128:(mm + 1) * 128],
                                                                         self.q[:, k, 0:N], start=(k == 0), stop=(k == 7)),
                        reads=[wat, self.tq(k)], writes=[("ps", by)])
                ma, mtok = self.macc(m, N)
                add("dve", lambda by=by, m=m: nc.vector.tensor_tensor(out=self.mt[:, m % 2, 0:N], in0=self.ps[by][:, 0:N],
                                                                      in1=self.sig[:, m % 2, 0:N], op=ALU.mult),
                    reads=[("ps", by), ("sig", m % 2)], writes=[("mt", m % 2)])
                add("pool", lambda m=m, ma=ma: nc.gpsimd.tensor_tensor(out=self.merged[:, m, 0:N], in0=ma, in1=self.mt[:, m % 2, 0:N],
                                                                       op=ALU.add),
                    reads=[mtok, ("mt", m % 2)], writes=[self.tmg(m)])
            self.wrel(("wao", hh))
            self.wrel(("win", 10 + hh))
        self.stat_begin()
        for hh in range(2):
            Wo, wot = self.wget(("wo", hh))
            for mm in range(4):
                m = hh * 4 + mm
                b = self.P.bank()
                for k in range(8):
                    add("pe", lambda b=b, mm=mm, k=k, Wo=Wo: nc.tensor.matmul(self.ps[b][:, 0:N], Wo[:, k, mm * 128:(mm + 1) * 128],
                                                                       self.merged[:, k, 0:N], start=(k == 0), stop=(k == 7)),
                        reads=[wot, self.tmg(k)], writes=[("ps", b)])
                self.stat_pe()
                hm = self.h[:, m, self.c0:self.c0 + N]
                add("dve", lambda b=b, hm=hm: nc.vector.tensor_tensor(out=hm, in0=self.ps[b][:, 0:N], in1=hm, op=ALU.add),
                    reads=[("ps", b), ("h", m)], writes=[("h", m)])
                self.stat_chunk(m)
            self.wrel(("wo", hh))
        self.stat_end()
        self.rmsnorm(N, pb + PB_NMLP)
        for j in range(8):
            Wu, wut = self.wget(("wup", j))
            for ff in range(4):
                f = j * 4 + ff
                b = self.P.bank()
                for k in range(8):
                    add("pe", lambda b=b, ff=ff, k=k, Wu=Wu: nc.tensor.matmul(self.ps[b][:, 0:N], Wu[:, k, ff * 128:(ff + 1) * 128],
                                                                       self.u[:, k, 0:N], start=(k == 0), stop=(k == 7)),
                        reads=[wut, ("u", k)], writes=[("ps", b)])
                add("act", lambda b=b, f=f: nc.scalar.activation(out=self.relu[:, f % 2, 0:N], in_=self.ps[b][:, 0:N], func=AF.Relu),
                    reads=[("ps", b)], writes=[("relu", f % 2)])
                add("dve", lambda b=b, f=f: nc.vector.tensor_tensor(out=self.hid[:, f, 0:N], in0=self.ps[b][:, 0:N],
                                                                    in1=self.relu[:, f % 2, 0:N], op=ALU.mult),
                    reads=[("ps", b), ("relu", f % 2)], writes=[self.thid(f)])
            self.wrel(("wup", j))
        self.stat_begin()
        for m in range(8):
            Wd, wdt = self.wget(("wdn", m))
            b = self.P.bank()
            for f in range(32):
                add("pe", lambda b=b, f=f, Wd=Wd: nc.tensor.matmul(self.ps[b][:, 0:N], Wd[:, f, :], self.hid[:, f, 0:N],
                                                            start=(f == 0), stop=(f == 31)),
                    reads=[wdt, self.thid(f)], writes=[("ps", b)])
            self.stat_pe()
            hm = self.h[:, m, self.c0:self.c0 + N]
            add("dve", lambda b=b, hm=hm: nc.vector.tensor_tensor(out=hm, in0=self.ps[b][:, 0:N], in1=hm, op=ALU.add),
                reads=[("ps", b), ("h", m)], writes=[("h", m)])
            self.stat_chunk(m)
            self.wrel(("wdn", m))
        self.stat_end()

    def attn_prompt(self, l):
        nc = self.nc
        add = self.add
        N = self.N
        LA = 3
        halo_off = self.c0 // 128
        units = [(qb, J, hf, c) for qb in range(N // 128) for J in range(2) for hf in range(2) for c in range(2)]
        info = {}
        grp = {}
        for gi, (qb, J) in enumerate([(qb, J) for qb in range(N // 128) for J in range(2)]):
            grp[(qb, J)] = (4, 5) if gi % 2 == 0 else (6, 7)

        def emit_qk(i):
            qb, J, hf, c = units[i]
            kv = 2 * J + hf
            qs = slice(qb * 128, (qb + 1) * 128)
            ps_ = slice(hf * 64, (hf + 1) * 64)
            kb = qb + c
            bl = i % 4
            add("pe", lambda: nc.tensor.matmul(
                self.ps[bl][:, :], self.identb[:, :],
                self.biasT[:, c, kv * 4:(kv + 1) * 4, :].rearrange("p g q -> p (g q)"), start=True, stop=False),
                reads=["identb", "biasT"], writes=[("ps", bl)])
            add("pe", lambda: nc.tensor.matmul(
                self.ps[bl][:, :].rearrange("p (g q) -> p g q", q=128),
                self.kTz[:, J, hf, kb * 128:(kb + 1) * 128], self.q[:, J * 4:(J + 1) * 4, qs], start=False, stop=True),
                reads=[("kT", J)] + [self.tq(J * 4 + g) for g in range(4)], writes=[("ps", bl)])
            info[i] = (bl, kb)

        def emit_soft(i):
            bl, kb = info[i]
            ai = i % 4
            mcol = None
            if self.kind == "halo":
                mcol = 4 if kb == 0 else kb - 1 + halo_off
            elif self.tname == "m0" and kb == 0:
                mcol = 3
            if mcol is None:
                add("act", lambda: nc.scalar.activation(out=self.pT[:, ai, :], in_=self.ps[bl][:, :], func=AF.Exp),
                    reads=[("ps", bl)], writes=[("pT", ai)])
            else:
                add("act", lambda: nc.scalar.activation(
                    out=self.pT[:, ai, :], in_=self.ps[bl][:, :], func=AF.Exp, bias=self.kmask[:, mcol:mcol + 1]),
                    reads=[("ps", bl), "kmask"], writes=[("pT", ai)])

        def emit_pv(i):
            qb, J, hf, c = units[i]
            kv = 2 * J + hf
            ps_ = slice(hf * 64, (hf + 1) * 64)
            bo, bd = grp[(qb, J)]
            bl, kb = info[i]
            ai = i % 4
            first = (hf == 0 and c == 0)
            last = (hf == 1 and c == 1)
            add("pe", lambda: nc.tensor.matmul(
                self.ps[bo][:, :], self.vz[:, kb, kv, :], self.pT[:, ai, :], start=first, stop=last),
                reads=[("v", kb), ("pT", ai)], writes=[("ps", bo)])
            add("pe", lambda: nc.tensor.matmul(
                self.ps[bd][:, :], self.onesz[:, hf, :], self.pT[:, ai, :], start=first, stop=last),
                reads=["onesz", ("pT", ai)], writes=[("ps", bd)])

        def emit_norm(qb, J):
            bo, bd = grp[(qb, J)]
            qs = slice(qb * 128, (qb + 1) * 128)
            add("dve", lambda: nc.vector.tensor_tensor(out=self.rD[:, J, :], in0=self.ps[bd][:, :], in1=self.sinkB[:, J, :], op=ALU.add),
                reads=[("ps", bd), ("sinkB", J)], writes=[("rD", J)])
            add("act", lambda: nc.scalar.activation(out=self.rD[:, J, :], in_=self.rD[:, J, :], func=AF.Ln),
                reads=[("rD", J)], writes=[("rD", J)])
            add("act", lambda: nc.scalar.activation(out=self.rD[:, J, :], in_=self.rD[:, J, :], func=AF.Exp, scale=-1.0),
                reads=[("rD", J)], writes=[("rD", J)])
            add("dve", lambda: nc.vector.tensor_tensor(
                out=self.q[:, J * 4:(J + 1) * 4, qs], in0=self.ps[bo][:, :].rearrange("p (g q) -> p g q", q=128),
                in1=self.rD[:, J, :].rearrange("p (g q) -> p g q", q=128), op=ALU.mult),
                reads=[("ps", bo), ("rD", J)], writes=[self.tq(J * 4 + g) for g in range(4)])

        n = len(units)
        NDEF = 3
        pending = []
        for i in range(min(LA, n)):
            emit_qk(i)
        for i in range(n):
            if i + LA < n:
                emit_qk(i + LA)
            emit_soft(i)
            while pending and pending[0][0] <= i:
                _, pq, pj = pending.pop(0)
                emit_norm(pq, pj)
            emit_pv(i)
            qb, J, hf, c = units[i]
            if hf == 1 and c == 1:
                pending.append((i + NDEF, qb, J))
        for _, pq, pj in pending:
            emit_norm(pq, pj)

    def prompt_state_out(self, l):
        nc = self.nc
        add = self.add
        N = self.N
        for (src, tokname, lo, n, dst, slot) in ((self.aext, "aext", N + 1, 15, self.pool_p, 0),
                                                 (self.gext, "gext", N + 2, 30, self.conv_p, 0)):
            b = self.P.bank()
            for g in range(4):
                add("pe", lambda b=b, g=g, src=src, lo=lo, n=n: nc.tensor.transpose(
                    self.ps[b][0:n, g * 128:(g + 1) * 128], src[:, g, lo:lo + n], self.ident[:, :]),
                    reads=[(tokname, g), "ident"], writes=[("ps", b)])
            add("act", lambda b=b, n=n: nc.scalar.copy(out=self.ost[0:n, 0, :], in_=self.ps[b][0:n, :]),
                reads=[("ps", b)], writes=[("pscr", 0)])
            add("sp", lambda dst=dst, n=n: nc.sync.dma_start(out=dst[l], in_=self.ost[0:n, 0, :]),
                reads=[("pscr", 0)], dma=("ostd", 0))
        b = self.P.bank()
        for J in range(2):
            add("pe", lambda b=b, J=J: nc.tensor.transpose(self.ps[b][:, J * 128:(J + 1) * 128], self.kf[:, J, :], self.ident[:, :]),
                reads=[("kf", J), "ident"], writes=[("ps", b)])
        add("act", lambda b=b: nc.scalar.copy(out=self.ost[:, 0, 0:256], in_=self.ps[b][:, 0:256]),
            reads=[("ps", b)], writes=[("pscr", 0)])
        add("sp", lambda: nc.sync.dma_start(out=self.k_p[l], in_=self.ost[:, 0, 0:256]), reads=[("pscr", 0)], dma=("ostd", 0))

    def samp_load_states(self, l):
        nc = self.nc
        add = self.add
        for rb in range(2):
            stg = self.ost[0:120, rb, :]
            add("sp", lambda rb=rb, stg=stg: nc.sync.dma_start(
                out=stg, in_=self.spool[l, rb * 8:(rb + 1) * 8].rearrange("b i f -> (b i) f")),
                writes=[("pscr", rb)], dma=("ostd", rb))
            b = self.P.bank()
            for g in range(4):
                add("pe", lambda b=b, g=g, stg=stg: nc.tensor.transpose(
                    self.ps[b][:, g * 128:g * 128 + 120], stg[:, g * 128:(g + 1) * 128], self.ident[0:120, 0:120]),
                    reads=[("pscr", rb), "ident"], writes=[("ps", b)])
            add("act", lambda b=b, rb=rb: nc.scalar.copy(
                out=self.aext[:, :, rb * 8 * 24:(rb + 1) * 8 * 24].rearrange("p g (b i) -> p g b i", i=24)[:, :, :, 1:16],
                in_=self.ps[b][:, :].rearrange("p (g x) -> p g x", x=128)[:, :, 0:120].rearrange("p g (b i) -> p g b i", i=15)),
                reads=[("ps", b)], writes=[("aext", g) for g in range(4)])
        for rb in range(4):
            stg = self.ost[0:120, rb % 2, :]
            add("sp", lambda rb=rb, stg=stg: nc.sync.dma_start(
                out=stg, in_=self.sconv[l, rb * 4:(rb + 1) * 4].rearrange("b i f -> (b i) f")),
                writes=[("pscr", rb % 2)], dma=("ostd", rb % 2))
            b = self.P.bank()
            for g in range(4):
                add("pe", lambda b=b, g=g, stg=stg: nc.tensor.transpose(
                    self.ps[b][:, g * 128:g * 128 + 120], stg[:, g * 128:(g + 1) * 128], self.ident[0:120, 0:120]),
                    reads=[("pscr", rb % 2), "ident"], writes=[("ps", b)])
            add("act", lambda b=b, rb=rb: nc.scalar.copy(
                out=self.gext[:, :, rb * 4 * 40:(rb + 1) * 4 * 40].rearrange("p g (b i) -> p g b i", i=40)[:, :, :, 2:32],
                in_=self.ps[b][:, :].rearrange("p (g x) -> p g x", x=128)[:, :, 0:120].rearrange("p g (b i) -> p g b i", i=30)),
                reads=[("ps", b)], writes=[("gext", g) for g in range(4)])

    def samp_state_out(self, l):
        nc = self.nc
        add = self.add
        add("sp", lambda: nc.sync.dma_start(out=self.pool_s[l, :, 0:7, :], in_=self.spool[l, :, 8:15, :]), dma="h2h0")
        add("sp", lambda: nc.sync.dma_start(out=self.conv_s[l, :, 0:22, :], in_=self.sconv[l, :, 8:30, :]), dma="h2h1")
        add("sp", lambda: nc.sync.dma_start(out=self.k_s[l, :, 0:120, :], in_=self.ck[l, :, 8:128, :]), dma="h2h2")
        add("sp", lambda: nc.sync.dma_start(out=self.v_s[l, :, 0:120, :], in_=self.cv[l, :, 8:128, :]), dma="h2h3")
        for (src, tokname, dst, r0, sidx) in ((self.anew, ("sinkB", 0), self.pool_s, 7, 0), (self.gnew, ("sinkB", 1), self.conv_s, 22, 1)):
            b = self.P.bank()
            for g in range(4):
                add("pe", lambda b=b, g=g, src=src: nc.tensor.transpose(self.ps[b][:, g * 128:(g + 1) * 128], src[:, g, :], self.ident[:, :]),
                    reads=[tokname, "ident"], writes=[("ps", b)])
            add("act", lambda b=b, sidx=sidx: nc.scalar.copy(out=self.ost[:, sidx, :], in_=self.ps[b][:, :]),
                reads=[("ps", b)], writes=[("pscr", sidx)])
            for sq_ in range(NSEQ):
                add("sp", lambda dst=dst, r0=r0, sq_=sq_, sidx=sidx: nc.sync.dma_start(
                    out=dst[l, sq_, r0:r0 + 8, :], in_=self.ost[sq_ * 8:(sq_ + 1) * 8, sidx, :]),
                    reads=[("pscr", sidx)], dma=("osts", sidx * 4 + sq_ % 4))
        b = self.P.bank()
        for J in range(2):
            add("pe", lambda b=b, J=J: nc.tensor.transpose(self.ps[b][:, J * 128:(J + 1) * 128], self.kf[:, J, :], self.ident[:, :]),
                reads=[("kf", J), "ident"], writes=[("ps", b)])
        add("act", lambda b=b: nc.scalar.copy(out=self.rs[:, 0:256], in_=self.ps[b][:, 0:256]), reads=[("ps", b)], writes=["rs"])
        for sq_ in range(NSEQ):
            add("sp", lambda sq_=sq_: nc.sync.dma_start(out=self.k_s[l, sq_, 120:128, :], in_=self.rs[sq_ * 8:(sq_ + 1) * 8, 0:256]),
                reads=["rs"], dma=("osts", 8 + sq_ % 4))

    def attn_sample(self, l):
        nc = self.nc
        add = self.add
        for grp in range(4):
            buf = grp % 2
            s0 = grp * 4
            add("sp", lambda s0=s0: nc.sync.dma_start(out=self.kst[:, :, :], in_=self.ck[l, s0:s0 + 4].rearrange("b s f -> s b f")),
                writes=[("mt", 0), ("mt", 1)], dma="kst")
            for J in range(2):
                b = self.P.bank()
                for bb in range(4):
                    add("pe", lambda b=b, bb=bb, J=J: nc.tensor.transpose(
                        self.ps[b][:, bb * 128:(bb + 1) * 128], self.kst[:, bb, J * 128:(J + 1) * 128], self.ident[:, :]),
                        reads=[("mt", 0), ("mt", 1), "ident"], writes=[("ps", b)])
                add("act", lambda b=b, J=J: nc.scalar.copy(out=self.kcT[:, J, :, :].rearrange("p b s -> p (b s)"),
                                                          in_=self.ps[b][:, :]),
                    reads=[("ps", b)], writes=["stK"])
            add("pool", lambda s0=s0: nc.gpsimd.dma_start(out=self.vc[:, :, :],
                                                          in_=self.cv[l, s0:s0 + 4].rearrange("b s f -> s b f")),
                writes=["stV"], dma="vc")
            blc = self.P.bank()
            blo = self.P.bank()
            for bb in range(4):
                sq_ = s0 + bb
                cs = slice(sq_ * 8, (sq_ + 1) * 8)
                for kv in range(4):
                    J, hf = kv // 2, kv % 2
                    ps_ = slice(hf * 64, (hf + 1) * 64)
                    oc = self.ps[blc][:, kv * 128:(kv + 1) * 128].rearrange("p (g b q) -> p g b q", g=4, b=4)[:, :, bb, :]
                    add("pe", lambda oc=oc, J=J, ps_=ps_, cs=cs, buf=buf, bb=bb: nc.tensor.matmul(
                        oc, self.kcT[ps_, J, bb, :], self.q[ps_, J * 4:(J + 1) * 4, cs], start=True, stop=True),
                        reads=["stK"] + [self.tq(J * 4 + g) for g in range(4)], writes=[("ps", blc)])
                    oo = self.ps[blo][0:8, kv * 128:(kv + 1) * 128].rearrange("p (g b q) -> p g b q", g=4, b=4)[:, :, bb, :]
                    add("pe", lambda oo=oo, J=J, ps_=ps_, cs=cs, hf=hf: nc.tensor.matmul(
                        oo, self.kTz[ps_, J, hf, cs], self.q[ps_, J * 4:(J + 1) * 4, cs], start=True, stop=True),
                        reads=[("kT", J)] + [self.tq(J * 4 + g) for g in range(4)], writes=[("ps", blo)])
            add("dve", lambda blc=blc: nc.vector.tensor_tensor(
                out=self.atmp[:, 0, :].rearrange("p (h b q) -> p h b q", h=16, b=4), in0=self.ps[blc][:, :].rearrange("p (h b q) -> p h b q", h=16, b=4),
                in1=self.biasS[:, 0, :].rearrange("p (h q) -> p h q", q=8).unsqueeze(2).broadcast_to([128, 16, 4, 8]), op=ALU.add),
                reads=[("ps", blc), "biasS"], writes=[("atmp", 0)])
            add("dve", lambda blo=blo: nc.vector.tensor_tensor(
                out=self.atmp[0:8, 1, :].rearrange("p (h b q) -> p h b q", h=16, b=4), in0=self.ps[blo][0:8, :].rearrange("p (h b q) -> p h b q", h=16, b=4),
                in1=self.biasS[0:8, 1, :].rearrange("p (h q) -> p h q", q=8).unsqueeze(2).broadcast_to([8, 16, 4, 8]), op=ALU.add),
                reads=[("ps", blo), "biasS"], writes=[("atmp", 1)])
            add("act", lambda: nc.scalar.activation(out=self.pT[:, 0, :], in_=self.atmp[:, 0, :], func=AF.Exp),
                reads=[("atmp", 0)], writes=[("pT", 0)])
            add("act", lambda: nc.scalar.activation(out=self.pT[0:8, 1, :], in_=self.atmp[0:8, 1, :], func=AF.Exp),
                reads=[("atmp", 1)], writes=[("pT", 1)])
            bo = self.P.bank()
            bd = self.P.bank()
            for bb in range(4):
                sq_ = s0 + bb
                for kv in range(4):
                    J, hf = kv // 2, kv % 2
                    ps_ = slice(hf * 64, (hf + 1) * 64)
                    pc = self.pT[:, 0, kv * 128:(kv + 1) * 128].rearrange("p (g b q) -> p g b q", g=4, b=4)[:, :, bb, :]
                    po = self.pT[0:8, 1, kv * 128:(kv + 1) * 128].rearrange("p (g b q) -> p g b q", g=4, b=4)[:, :, bb, :]
                    for (bk, wa, wb) in ((bo, self.vc[:, bb, kv * 64:(kv + 1) * 64], self.vnew[0:8, sq_, kv * 64:(kv + 1) * 64]),
                                         (bd, self.onesb[:, 0:64], self.onesb[0:8, 0:64])):
                        oo = self.ps[bk][ps_, J * 128:(J + 1) * 128].rearrange("p (g b q) -> p g b q", g=4, b=4)[:, :, bb, :]
                        add("pe", lambda oo=oo, wa=wa, pc=pc: nc.tensor.matmul(oo, wa, pc, start=True, stop=False),
                            reads=["stV", ("pT", 0), "onesb"], writes=[("ps", bk)])
                        add("pe", lambda oo=oo, wb=wb, po=po: nc.tensor.matmul(oo, wb, po, start=False, stop=True),
                            reads=["vnew", ("pT", 1), "onesb"], writes=[("ps", bk)])
            add("dve", lambda bd=bd: nc.vector.tensor_tensor(
                out=self.rD[:, 0, 0:256].rearrange("p (h x) -> p h x", x=32), in0=self.ps[bd][:, 0:256].rearrange("p (h x) -> p h x", x=32),
                in1=self.sinkE[:, :].unsqueeze(2).broadcast_to([128, 8, 32]), op=ALU.add),
                reads=[("ps", bd), "sinkE"], writes=[("rD", 0)])
            add("act", lambda: nc.scalar.activation(out=self.rD[:, 0, 0:256], in_=self.rD[:, 0, 0:256], func=AF.Ln),
                reads=[("rD", 0)], writes=[("rD", 0)])
            add("act", lambda: nc.scalar.activation(out=self.rD[:, 0, 0:256], in_=self.rD[:, 0, 0:256], func=AF.Exp, scale=-1.0),
                reads=[("rD", 0)], writes=[("rD", 0)])
            add("dve", lambda bo=bo, s0=s0: nc.vector.tensor_tensor(
                out=self.q[:, 0:8, s0 * 8:(s0 + 4) * 8], in0=self.ps[bo][:, 0:256].rearrange("p (h x) -> p h x", x=32),
                in1=self.rD[:, 0, 0:256].rearrange("p (h x) -> p h x", x=32), op=ALU.mult),
                reads=[("ps", bo), ("rD", 0)], writes=[self.tq(j) for j in range(8)])


def _t5_bucket(dist):
    n = np.maximum(dist, 0)
    exact = 16
    large = exact + (np.log(np.maximum(n, 1) / exact) / np.log(128 / exact) * (32 - exact)).astype(np.int32)
    large = np.minimum(large, 31)
    return np.where(n < exact, n, large).astype(np.int32)


def _qperm():
    idx = []
    for j in range(8):
        for hf in range(2):
            kv = 2 * (j // 4) + hf
            g = j % 4
            head = kv * 4 + g
            idx.extend(range(head * 64, head * 64 + 64))
    return np.array(idx)


def _fm(vec, nch):
    return np.ascontiguousarray(np.asarray(vec, np.float32).reshape(nch, 128).T)


_NC_CACHE = {}


def prepare(inputs):
    f = lambda k: np.asarray(inputs[k], np.float32)
    x_prompt, x_sample = f("x_prompt"), f("x_sample")
    qp = _qperm()
    cols = np.concatenate([np.arange(0, 1536), 1536 + qp, np.arange(2560, 6144)])
    w_in = np.ascontiguousarray(f("w_in")[:, :, cols])
    b_in = f("b_in")[:, cols]
    w_ao = np.ascontiguousarray(f("w_attn_out")[:, qp, :])
    pp = np.zeros((128, NPP), np.float32)
    sinks = f("attn_sinks")
    for l in range(NL):
        o = l * PL
        pp[:, o + PB_IN:o + PB_IN + 48] = _fm(b_in[l], 48)
        pp[:, o + PB_NMIX:o + PB_NMIX + 8] = _fm(f("norm_mix")[l], 8)
        pp[:, o + PB_PSC:o + PB_PSC + 8] = _fm(f("pool_scale")[l], 8)
        pp[:, o + PB_NMLP:o + PB_NMLP + 8] = _fm(f("norm_mlp")[l], 8)
        cw = f("conv_w")[l]
        pp[:, o + PB_CW:o + PB_CW + 124] = cw.T.reshape(4, 128, 31).transpose(1, 0, 2).reshape(128, 124)
        pp[:, o + PB_CB:o + PB_CB + 4] = _fm(f("conv_b")[l], 4)
        pp[:, o + PB_NG:o + PB_NG + 4] = _fm(f("conv_norm_g")[l], 4)
        pp[:, o + PB_NB:o + PB_NB + 4] = _fm(f("conv_norm_b")[l], 4)
        for J in range(2):
            for g in range(4):
                for hf in range(2):
                    pp[hf * 64:(hf + 1) * 64, o + PB_SINK + J * 4 + g] = sinks[l, (2 * J + hf) * 4 + g]
    pp[:, PB_NF:PB_NF + 8] = _fm(f("norm_final"), 8)
    bkv = np.ascontiguousarray(np.broadcast_to(b_in[:, None, 2560:3072], (NL, 128, 512)))
    ext = np.concatenate([f("rel_bias"), np.full((1, 16), NEG, np.float32)], axis=0)
    s = np.arange(128)[:, None]
    q = np.arange(128)[None, :]
    d_prev = 128 + q - s
    d_own = q - s
    tabs = []
    for dmat in (d_prev, d_own):
        idx = np.where((dmat >= 0) & (dmat <= 128), _t5_bucket(dmat), 32)
        tabs.append(ext[idx])
    biasT = np.stack(tabs, axis=1).transpose(0, 1, 3, 2)
    biasT = np.ascontiguousarray(biasT).reshape(128, 2 * 16 * 128)
    biasS = np.ascontiguousarray(biasT.reshape(128, 2, 16, 128)[:, :, :, 0:8]).reshape(128, 256)
    common = dict(identm=np.eye(128, dtype=np.float32), biasS=biasS, w_in=w_in, w_pool=f("pool_w"), w_co=f("w_conv_out"), w_ao=w_ao, w_o=f("w_out"),
                  w_up=f("w_up"), w_dn=f("w_down"), pp=pp, bkv=bkv, biasT=biasT)
    meta = f("meta_tokens")
    in_maps = []
    for c in range(8):
        b, cc = c // 4, c % 4
        m = dict(common)
        m["xm"] = np.ascontiguousarray(x_prompt[b, cc * 2048:(cc + 1) * 2048])
        tokmask = np.ones((128, 512), np.float32)
        kmask = np.zeros((128, 8), np.float32)
        kmask[:, 4] = NEG
        invc = np.zeros((128, 4, 16), np.float32)
        for g, w in enumerate(WINS):
            invc[:, g, :] = 1.0 / w
        if cc == 0:
            xh = np.zeros((512, D), np.float32)
            xh[496:] = meta
            tokmask[:, :496] = 0.0
            kmask[:, 0:3] = NEG
            kmask[:112, 3] = NEG
            for g, w in enumerate(WINS):
                invc[:, g, :] = 1.0 / np.minimum(np.arange(16) + 1, w)
        else:
            xh = np.ascontiguousarray(x_prompt[b, cc * 2048 - 512:cc * 2048])
        m["xh"] = xh
        m["tokmask"] = tokmask
        m["kmask"] = kmask
        m["invc"] = invc.reshape(128, 64)
        sl = slice(c * NSEQ, (c + 1) * NSEQ)
        m["xs"] = np.ascontiguousarray(x_sample[sl].reshape(NS, D))
        m["spool"] = np.ascontiguousarray(f("state_pool")[:, sl])
        m["sconv"] = np.ascontiguousarray(f("state_conv")[:, sl])
        m["ck"] = np.ascontiguousarray(f("cache_k")[:, sl].reshape(NL, NSEQ, 128, 256))
        m["cv"] = np.ascontiguousarray(f("cache_v")[:, sl].reshape(NL, NSEQ, 128, 256))
        in_maps.append(m)
    return in_maps


def assemble(res):
    R = res.results
    y_prompt = np.zeros((2, 8192, D), np.float32)
    for c in range(8):
        y_prompt[c // 4, (c % 4) * 2048:(c % 4 + 1) * 2048] = R[c]["ym"]
    y_sample = np.concatenate([R[c]["ys"].reshape(NSEQ, DT, D) for c in range(8)], axis=0)
    pool_p = np.stack([R[3]["pool_p"], R[7]["pool_p"]], axis=1)
    conv_p = np.stack([R[3]["conv_p"], R[7]["conv_p"]], axis=1)
    k_p = np.stack([R[3]["k_p"], R[7]["k_p"]], axis=1).reshape(NL, 2, 128, 4, 64)
    v_p = np.stack([R[3]["v_p"], R[7]["v_p"]], axis=1).reshape(NL, 2, 128, 4, 64)
    pool_s = np.concatenate([R[c]["pool_s"] for c in range(8)], axis=1)
    conv_s = np.concatenate([R[c]["conv_s"] for c in range(8)], axis=1)
    k_s = np.concatenate([R[c]["k_s"] for c in range(8)], axis=1).reshape(NL, 128, 128, 4, 64)
    v_s = np.concatenate([R[c]["v_s"] for c in range(8)], axis=1).reshape(NL, 128, 128, 4, 64)
    outs = (y_prompt, y_sample, pool_p, pool_s, conv_p, conv_s, k_p, k_s, v_p, v_s)
    return tuple(np.ascontiguousarray(o, dtype=np.float32) for o in outs)


def kernel(**inputs):
    in_maps = prepare(inputs)
    if "nc" not in _NC_CACHE:
        _NC_CACHE["nc"] = Builder().build()
    nc = _NC_CACHE["nc"]
    res = run_bass_kernel_spmd(nc, in_maps, core_ids=list(range(8)))
    return assemble(res)
```

```python
import contextlib
import numpy as np
import concourse.bass as bass
import concourse.mybir as mybir
from concourse.bass_utils import run_bass_kernel_spmd

F32 = mybir.dt.float32
BF16 = mybir.dt.bfloat16
AF = mybir.ActivationFunctionType
ALU = mybir.AluOpType

D = 1024
NL = 4
DIN = 6144
TT = 512
NSEQ = 16
DT = 8
NS = NSEQ * DT
NEG = -30000.0
EPS = 1e-6
WINS = (2, 4, 8, 16)

PB_IN = 0
PB_NMIX = 48
PB_PSC = 56
PB_NMLP = 64
PB_CW = 72
PB_CB = 196
PB_NG = 200
PB_NB = 204
PB_SINK = 208
PL = 216
PB_NF = NL * PL
NPP = PB_NF + 8

SAME_ENG_SYNC = True
NWS = 4
NDIAG = 16
TRI_HALO = True
HALO_EARLY_EXIT = True
SKIP = set()


class Op:
    __slots__ = ("eng", "fn", "deps", "sig", "sigval", "dma")

    def __init__(self, eng, fn, dma):
        self.eng = eng
        self.fn = fn
        self.deps = []
        self.sig = False
        self.sigval = None
        self.dma = dma


class Prog:
    ENGS = ("pe", "act", "dve", "pool", "sp")

    def __init__(self, nc):
        self.nc = nc
        self.ops = []
        self.writer = {}
        self.readers = {}
        self.dma_count = {}
        self.nbank = 0
        self.reserved = set()

    def eng_obj(self, eng):
        nc = self.nc
        return {"pe": nc.tensor, "act": nc.scalar, "dve": nc.vector,
                "pool": nc.gpsimd, "sp": nc.sync}[eng]

    def add(self, eng, fn, reads=(), writes=(), dma=None):
        o = Op(eng, fn, dma)
        deps = {}
        for t in reads:
            w = self.writer.get(t)
            if w is not None:
                deps[id(w)] = w
        for t in writes:
            w = self.writer.get(t)
            if w is not None:
                deps[id(w)] = w
            last = {}
            for r in self.readers.get(t, ()):
                if r.dma is not None:
                    deps[id(r)] = r
                else:
                    last[r.eng] = r
            for r in last.values():
                deps[id(r)] = r
        o.deps = list(deps.values())
        for t in reads:
            self.readers.setdefault(t, []).append(o)
        for t in writes:
            self.writer[t] = o
            self.readers[t] = []
        if dma is not None:
            c = self.dma_count.get(dma, 0) + 1
            self.dma_count[dma] = c
            o.sigval = 16 * c
        self.ops.append(o)
        return o

    def bank(self):
        while True:
            b = self.nbank % 8
            self.nbank += 1
            if b not in self.reserved:
                return b

    @staticmethod
    def _needs(o, d):
        if d.dma is not None:
            return True
        if o.dma is not None:
            return True
        if d.eng != o.eng:
            return True
        return SAME_ENG_SYNC and d.eng != "pe"

    def emit(self):
        nc = self.nc
        for o in self.ops:
            for d in o.deps:
                if d.dma is None and self._needs(o, d):
                    d.sig = True
        cnt = {e: 0 for e in self.ENGS}
        for o in self.ops:
            if o.dma is None and o.sig:
                cnt[o.eng] += 1
                o.sigval = cnt[o.eng]
        self.sig_counts = dict(cnt)
        per_eng = {e: [o for o in self.ops if o.eng == e] for e in self.ENGS}
        with contextlib.ExitStack() as st:
            esem = {e: st.enter_context(nc.semaphore("c_" + e)) for e in self.ENGS}
            dsem = {n: st.enter_context(nc.semaphore("d_%d" % i))
                    for i, n in enumerate(self.dma_count)}
            block = st.enter_context(nc.Block())

            def run(eng):
                E = self.eng_obj(eng)
                seen = {}
                for o in per_eng[eng]:
                    waits = {}
                    for d in o.deps:
                        if not self._needs(o, d):
                            continue
                        s = dsem[d.dma] if d.dma is not None else esem[d.eng]
                        k = id(s)
                        if seen.get(k, 0) >= d.sigval:
                            continue
                        if k not in waits or waits[k][1] < d.sigval:
                            waits[k] = (s, d.sigval)
                    for k, (s, v) in waits.items():
                        E.wait_ge(s, v)
                        seen[k] = v
                    ins = o.fn()
                    if o.dma is not None:
                        ins.then_inc(dsem[o.dma], 16)
                    elif o.sig:
                        ins.then_inc(esem[o.eng], 1)
                if eng == "sp":
                    for n, c in self.dma_count.items():
                        E.wait_ge(dsem[n], 16 * c)

            block.tensor(lambda e: run("pe"))
            block.scalar(lambda e: run("act"))
            block.vector(lambda e: run("dve"))
            block.gpsimd(lambda e: run("pool"))
            block.sync(lambda e: run("sp"))


class Builder:
    def __init__(self, tiles=("halo", "m0", "m1", "m2", "m3", "samp"), nl=NL, dbg=False):
        self.tiles = tiles
        self.nl = nl
        self.dbg = dbg
        self.nc = bass.Bass("TRN2", target_bir_lowering=False)
        self.P = Prog(self.nc)
        self.st = contextlib.ExitStack()

    def din(self, name, shape):
        return self.nc.dram_tensor(name, list(shape), F32, kind="ExternalInput").ap()

    def dout(self, name, shape):
        return self.nc.dram_tensor(name, list(shape), F32, kind="ExternalOutput").ap()

    def sb(self, name, shape, dt):
        return self.st.enter_context(self.nc.sbuf_tensor(name, list(shape), dt))

    def add(self, *a, **k):
        return self.P.add(*a, **k)

    def build(self):
        nc = self.nc
        with self.st:
            self.declare()
            self.prologue()
            for tname in self.tiles:
                self.run_tile(tname)
            self.P.emit()
        return nc

    def declare(self):
        nc = self.nc
        sb = self.sb
        self.xh = self.din("xh", [512, D])
        self.xm = self.din("xm", [2048, D])
        self.xs = self.din("xs", [NS, D])
        self.tokmask_d = self.din("tokmask", [128, 512])
        self.kmask_d = self.din("kmask", [128, 8])
        self.invc_d = self.din("invc", [128, 64])
        self.spool = self.din("spool", [NL, NSEQ, 15, 512])
        self.sconv = self.din("sconv", [NL, NSEQ, 30, 512])
        self.ck = self.din("ck", [NL, NSEQ, 128, 256])
        self.cv = self.din("cv", [NL, NSEQ, 128, 256])
        self.w_in = self.din("w_in", [NL, D, DIN])
        self.w_pool = self.din("w_pool", [NL, 4, 128, 256])
        self.w_co = self.din("w_co", [NL, 512, D])
        self.w_ao = self.din("w_ao", [NL, D, D])
        self.w_o = self.din("w_o", [NL, D, D])
        self.w_up = self.din("w_up", [NL, D, 4096])
        self.w_dn = self.din("w_dn", [NL, 4096, D])
        self.pp_d = self.din("pp", [128, NPP])
        self.bkv_d = self.din("bkv", [NL, 128, 512])
        self.biasT_d = self.din("biasT", [128, 2 * 16 * 128])
        self.ident_d = self.din("identm", [128, 128])
        self.biasS_d = self.din("biasS", [128, 256])
        self.ym = self.dout("ym", [2048, D])
        self.ys = self.dout("ys", [NS, D])
        self.pool_p = self.dout("pool_p", [NL, 15, 512])
        self.conv_p = self.dout("conv_p", [NL, 30, 512])
        self.k_p = self.dout("k_p", [NL, 128, 256])
        self.v_p = self.dout("v_p", [NL, 128, 256])
        self.pool_s = self.dout("pool_s", [NL, NSEQ, 15, 512])
        self.conv_s = self.dout("conv_s", [NL, NSEQ, 30, 512])
        self.k_s = self.dout("k_s", [NL, NSEQ, 128, 256])
        self.v_s = self.dout("v_s", [NL, NSEQ, 128, 256])
        if self.dbg:
            self.dbg_d = self.dout("dbg", [128, 8 * TT])
        self.h = sb("h", [128, 8, TT], F32)
        self.u = sb("u", [128, 8, TT], BF16)
        self.sq = sb("sq", [128, 2, TT], BF16)
        self.rs = sb("rs", [128, TT], F32)
        self.aext = sb("aext", [128, 4, 16 + TT], F32)
        self.pscr = sb("pscr", [128, 2, 16 + TT], F32)
        self.gext = sb("gext", [128, 4, 640], F32)
        self.sig = sb("sig", [128, 2, TT], F32)
        self.cy = sb("cy", [128, 4, TT], F32)
        self.big = sb("big", [128, 32 * TT], BF16)
        self.kTz = sb("kTz", [128, 2, 2, 128 + TT], BF16)
        self.vz = sb("vz", [128, 5, 4, 128], BF16)
        self.onesz = sb("onesz", [128, 2, 128], BF16)
        self.biasT = sb("biasT_s", [128, 2, 16, 128], BF16)
        self.atmp = sb("atmp", [128, 2, TT], F32)
        self.pT = sb("pT", [128, 4, TT], BF16)
        self.rD = sb("rD", [128, 2, TT], F32)
        self.sinkB = sb("sinkB", [128, 2, TT], F32)
        self.sinkE = sb("sinkE", [128, 8], F32)
        self.bq8 = sb("bq8", [128, 8], F32)
        self.mt = sb("mt", [128, 2, TT], F32)
        self.relu = sb("relu", [128, 2, TT], BF16)
        self.diag = sb("diag", [128, NDIAG, 128], BF16)
        self.ws = [sb("ws%d" % i, [128, 4096], BF16) for i in range(NWS)]
        self.wpl = sb("wpl", [128, 4, 256], BF16)
        self.stA = sb("stA", [128, NL, 4, 16], F32)
        self.stG = sb("stG", [128, NL, 4, 32], F32)
        self.stb = sb("stb", [128, 2048], BF16)
        self.stK = self.stb[:, 0:1024].rearrange("p (l j s) -> p l j s", l=NL, j=2)
        self.stV = self.stb[:, 1024:2048].rearrange("p (l f) -> p l f", l=NL)
        self.kcT = self.stb[:, 0:1024].rearrange("p (j b s) -> p j b s", j=2, b=4)
        self.vc = self.stb[:, 1024:2048].rearrange("p (b f) -> p b f", b=4)
        self.ident = sb("ident", [128, 128], F32)
        self.identb = sb("identb", [128, 128], BF16)
        self.onesb = sb("onesb", [128, 128], BF16)
        self.pp = sb("pp_s", [128, NPP], F32)
        self.bkv = sb("bkv_s", [128, 512], F32)
        self.tokmask = sb("tokmask_s", [128, 512], F32)
        self.kmask = sb("kmask_s", [128, 8], F32)
        self.invc = sb("invc_s", [128, 4, 16], F32)
        self.kf = sb("kf", [128, 2, 128], F32)
        self.vnew = sb("vnew", [8, NSEQ, 256], BF16)
        self.biasS = sb("biasS_s", [128, 2, 128], F32)
        self.ps = [self.st.enter_context(nc.psum_tensor("ps%d" % i, [128, 512], F32))
                   for i in range(8)]
        self.ost = self.pscr[:, :, 0:512]
        self.kst = self.mt[:, :, :].rearrange("p a b -> p (a b)").rearrange("p (b f) -> p b f", f=256)
        self.anew = self.sinkB[:, 0, :].rearrange("p (g t) -> p g t", t=128)
        self.gnew = self.sinkB[:, 1, :].rearrange("p (g t) -> p g t", t=128)
        self.vnewf = self.rD[0:8, 1, :].rearrange("p (a f) -> p a f", f=256)
        bigv = self.big[:, :].rearrange("p (c t) -> p c t", t=TT)
        self.hid = bigv
        self.q = bigv[:, 0:8, :]
        self.merged = bigv[:, 8:16, :]
        self.gbf = self.big[:, 16 * TT:16 * TT + 4 * 640].rearrange("p (c t) -> p c t", t=640)
        self.r = bigv[:, 21:25, :]
        self.s = bigv[:, 25:29, :]
        self.wq = []
        self.wi = 0
        self.wissued = 0
        self.wcur = {}

    def tq(self, j):
        return ("big", j)

    def tmg(self, m):
        return ("big", 8 + m)

    def tgbf(self, c):
        return [("big", 16 + c), ("big", 17 + c)]

    def tr(self, g):
        return ("big", 21 + g)

    def ts_(self, c):
        return ("big", 25 + c)

    def thid(self, f):
        return ("big", f)

    def macc(self, m, N):
        if m < 4:
            return self.cy[:, m, 0:N], ("cy", m)
        return self.aext[:, m - 4, 0:N], ("aext", m - 4)

    def layer_items(self, l):
        it = []
        for j in range(6):
            it.append((("win", j), self.w_in[l, :, j * 512:(j + 1) * 512].rearrange("(k p) c -> p k c", p=128), 8, 512))
        for j in (6, 7):
            it.append((("win", j), self.w_in[l, :, j * 512:(j + 1) * 512].rearrange("(k p) c -> p k c", p=128), 8, 512))
        for hh in range(2):
            it.append((("wco", hh), self.w_co[l, :, hh * 512:(hh + 1) * 512].rearrange("(k p) c -> p k c", p=128), 4, 512))
            it.append((("win", 8 + hh), self.w_in[l, :, (8 + hh) * 512:(9 + hh) * 512].rearrange("(k p) c -> p k c", p=128), 8, 512))
        for hh in range(2):
            it.append((("wao", hh), self.w_ao[l, :, hh * 512:(hh + 1) * 512].rearrange("(k p) c -> p k c", p=128), 8, 512))
            it.append((("win", 10 + hh), self.w_in[l, :, (10 + hh) * 512:(11 + hh) * 512].rearrange("(k p) c -> p k c", p=128), 8, 512))
        for hh in range(2):
            it.append((("wo", hh), self.w_o[l, :, hh * 512:(hh + 1) * 512].rearrange("(k p) c -> p k c", p=128), 8, 512))
        for j in range(8):
            it.append((("wup", j), self.w_up[l, :, j * 512:(j + 1) * 512].rearrange("(k p) c -> p k c", p=128), 8, 512))
        for m in range(8):
            it.append((("wdn", m), self.w_dn[l, :, m * 128:(m + 1) * 128].rearrange("(k p) c -> p k c", p=128), 32, 128))
        return it

    def wissue(self):
        nc = self.nc
        while self.wissued < len(self.wq):
            i = self.wissued
            if i >= NWS and not self.wreleased[i - NWS]:
                break
            key, src, d1, d2 = self.wq[i]
            slot = i % NWS
            dst = self.ws[slot][:, 0:d1 * d2].rearrange("p (a b) -> p a b", b=d2)
            self.add("pool", lambda dst=dst, src=src: nc.gpsimd.dma_start(out=dst, in_=src),
                     writes=[("ws", slot)], dma=("ws", slot))
            self.wissued += 1

    def wget(self, key):
        i = self.wi
        k, src, d1, d2 = self.wq[i]
        assert k == key, (k, key)
        self.wissue()
        assert self.wissued > i, ("weight stream stuck", i, key)
        self.wi += 1
        slot = i % NWS
        self.wcur[key] = i
        return self.ws[slot][:, 0:d1 * d2].rearrange("p (a b) -> p a b", b=d2), ("ws", slot)

    def wrel(self, key):
        self.wreleased[self.wcur.pop(key)] = True
        self.wissue()

    def prologue(self):
        nc = self.nc
        add = self.add
        for t in self.tiles:
            for l in range(self.nl):
                items = self.layer_items(l)
                if t == "halo" and l == self.nl - 1 and HALO_EARLY_EXIT:
                    items = [it_ for it_ in items if it_[0] in (("win", 0), ("win", 1), ("win", 2), ("win", 5))]
                self.wq.extend(items)
        self.wreleased = [False] * len(self.wq)
        add("sp", lambda: nc.sync.dma_start(out=self.pp[:, :], in_=self.pp_d), writes=["pp"], dma="c0")
        add("pool", lambda: nc.gpsimd.dma_start(out=self.biasT[:, :, :, :].rearrange("p a b c -> p (a b c)"), in_=self.biasT_d),
            writes=["biasT"], dma="c1")
        add("sp", lambda: nc.sync.dma_start(out=self.biasS[:, :, :].rearrange("p a b -> p (a b)"), in_=self.biasS_d),
            writes=["biasS"], dma="c6")
        add("sp", lambda: nc.sync.dma_start(out=self.tokmask[:, :], in_=self.tokmask_d), writes=["tokmask"], dma="c2")
        add("sp", lambda: nc.sync.dma_start(out=self.kmask[:, :], in_=self.kmask_d), writes=["kmask"], dma="c3")
        add("sp", lambda: nc.sync.dma_start(out=self.invc[:, :, :].rearrange("p a b -> p (a b)"), in_=self.invc_d),
            writes=["invc"], dma="c4")
        add("pool", lambda: nc.gpsimd.memset(self.onesb[:, :], 1.0), writes=["onesb"])
        add("pool", lambda: nc.gpsimd.memset(self.onesz[:, :, :].rearrange("p a b -> p (a b)"), 0.0), writes=["onesz"])
        add("pool", lambda: nc.gpsimd.memset(self.onesz[:, 0, 0:64], 1.0), writes=["onesz"])
        add("pool", lambda: nc.gpsimd.memset(self.onesz[:, 1, 64:128], 1.0), writes=["onesz"])
        add("pool", lambda: nc.gpsimd.memset(self.kTz[:, :, :, :].rearrange("p a b c -> p (a b c)"), 0.0),
            writes=[("kT", 0), ("kT", 1)])
        add("pool", lambda: nc.gpsimd.memset(self.vz[:, :, :, :].rearrange("p a b c -> p (a b c)"), 0.0),
            writes=[("v", i) for i in range(5)])
        add("sp", lambda: nc.sync.dma_start(out=self.ident[:, :], in_=self.ident_d), writes=["ident"], dma="c5")
        add("pool", lambda: nc.gpsimd.memset(self.aext[:, :, :].rearrange("p a b -> p (a b)"), 0.0),
            writes=[("aext", g) for g in range(4)])
        add("pool", lambda: nc.gpsimd.memset(self.gext[:, :, :].rearrange("p a b -> p (a b)"), 0.0),
            writes=[("gext", g) for g in range(4)])
        add("pool", lambda: nc.gpsimd.memset(self.pscr[:, :, :].rearrange("p a b -> p (a b)"), 0.0),
            writes=[("pscr", 0), ("pscr", 1)])
        add("dve", lambda: nc.vector.tensor_copy(out=self.identb[:, :], in_=self.ident[:, :]),
            reads=["ident"], writes=["identb"])
        add("pool", lambda: nc.gpsimd.memset(self.stA[:, :, :, :].rearrange("p a b c -> p (a b c)"), 0.0), writes=["stA"])
        add("pool", lambda: nc.gpsimd.memset(self.stG[:, :, :, :].rearrange("p a b c -> p (a b c)"), 0.0), writes=["stG"])
        add("pool", lambda: nc.gpsimd.memset(self.stb[:, :], 0.0), writes=["stK", "stV"])

    def run_tile(self, tname):
        if tname == "samp":
            N, kind = NS, "samp"
            xsrc, ydst = self.xs, self.ys
        elif tname == "halo":
            N, kind = TT, "halo"
            xsrc, ydst = self.xh, None
        else:
            i = int(tname[1])
            N, kind = TT, "main"
            xsrc, ydst = self.xm[i * TT:(i + 1) * TT, :], self.ym[i * TT:(i + 1) * TT, :]
        self.N = N
        self.kind = kind
        self.tname = tname
        self.last_main = (tname == "m3")
        self.c0 = 0
        if getattr(self, "stats_ready", False):
            self.P.reserved.discard(self.sbank)
            self.stats_ready = False
        self.load_x(xsrc, N)
        for l in range(self.nl):
            if kind == "halo" and TRI_HALO:
                self.c0 = 128 * l
                self.N = TT - 128 * l
            self.layer(l)
        if self.dbg:
            nc = self.nc
            self.add("sp", lambda: nc.sync.dma_start(out=self.dbg_d, in_=self.h[:, :, :].rearrange("p a b -> p (a b)")),
                     reads=[("h", k) for k in range(8)], dma="dbg")
        if ydst is not None:
            self.final_norm(ydst, N)

    def load_x(self, xsrc, N):
        nc = self.nc
        add = self.add
        for tb in range(N // 128):
            xin = self.cy[:, 2 * (tb % 2):2 * (tb % 2) + 2, :].rearrange("p a b -> p (a b)")
            tk = [("cy", 2 * (tb % 2)), ("cy", 2 * (tb % 2) + 1)]
            add("sp", lambda xin=xin, tb=tb: nc.sync.dma_start(out=xin, in_=xsrc[tb * 128:(tb + 1) * 128, :]),
                writes=tk, dma=("xin", tb % 2))
            for hh in range(2):
                b = self.P.bank()
                for kk in range(4):
                    k = hh * 4 + kk
                    add("pe", lambda b=b, kk=kk, k=k, xin=xin: nc.tensor.transpose(
                        self.ps[b][:, kk * 128:(kk + 1) * 128], xin[:, k * 128:(k + 1) * 128], self.ident[:, :]),
                        reads=tk + ["ident"], writes=[("ps", b)])
                add("act", lambda b=b, hh=hh, tb=tb: nc.scalar.copy(
                    out=self.h[:, hh * 4:(hh + 1) * 4, tb * 128:(tb + 1) * 128],
                    in_=self.ps[b][:, :].rearrange("p (a b) -> p a b", b=128)),
                    reads=[("ps", b)], writes=[("h", hh * 4 + kk) for kk in range(4)])

    def stat_begin(self):
        self.sbank = self.P.bank()
        self.P.reserved.add(self.sbank)
        self.stat_n = 0
        self.stat_q = []
        self.stat_N = self.N

    def stat_chunk(self, k):
        nc = self.nc
        N = self.N
        i = self.stat_n
        self.stat_n += 1
        hk = self.h[:, k, self.c0:self.c0 + N]
        b = self.sbank
        self.add("act", lambda: nc.scalar.activation(out=self.sq[:, i % 2, 0:N], in_=hk, func=AF.Square),
                 reads=[("h", k)], writes=[("sq", i % 2)])
        self.stat_q.append(lambda: self.add(
            "pe", lambda: nc.tensor.matmul(self.ps[b][:, 0:N], self.onesb[:, :], self.sq[:, i % 2, 0:N],
                                           start=(i == 0), stop=(i == 7)),
            reads=[("sq", i % 2), "onesb"], writes=[("ps", b)]))

    def stat_pe(self, all_=False):
        while self.stat_q:
            self.stat_q.pop(0)()
            if not all_:
                break

    def stat_end(self):
        self.stat_pe(all_=True)
        self.stats_ready = True

    def rmsnorm_stats(self, N):
        nc = self.nc
        add = self.add
        if getattr(self, "stats_ready", False):
            self.stats_ready = False
            b = self.sbank
            off = self.stat_N - N
        else:
            off = 0
            b = self.P.bank()
            for k in range(8):
                hk = self.h[:, k, self.c0:self.c0 + N]
                add("act", lambda k=k, hk=hk: nc.scalar.activation(out=self.sq[:, k % 2, 0:N], in_=hk, func=AF.Square),
                    reads=[("h", k)], writes=[("sq", k % 2)])
                add("pe", lambda k=k, b=b: nc.tensor.matmul(self.ps[b][:, 0:N], self.onesb[:, :], self.sq[:, k % 2, 0:N],
                                                            start=(k == 0), stop=(k == 7)),
                    reads=[("sq", k % 2), "onesb"], writes=[("ps", b)])
        add("act", lambda b=b: nc.scalar.activation(out=self.rs[:, 0:N], in_=self.ps[b][:, off:off + N], func=AF.Ln,
                                                    scale=1.0 / D, bias=EPS),
            reads=[("ps", b)], writes=["rs"])
        add("act", lambda: nc.scalar.activation(out=self.rs[:, 0:N], in_=self.rs[:, 0:N], func=AF.Exp, scale=-0.5),
            reads=["rs"], writes=["rs"])
        self.P.reserved.discard(b)

    def rmsnorm(self, N, gcol):
        nc = self.nc
        self.rmsnorm_stats(N)
        for k in range(8):
            hk = self.h[:, k, self.c0:self.c0 + N]
            self.add("dve", lambda k=k, hk=hk: nc.vector.scalar_tensor_tensor(
                out=self.u[:, k, 0:N], in0=hk, scalar=self.pp[:, gcol + k:gcol + k + 1],
                in1=self.rs[:, 0:N], op0=ALU.mult, op1=ALU.mult),
                reads=[("h", k), "rs", "pp"], writes=[("u", k)])

    def final_norm(self, ydst, N):
        nc = self.nc
        add = self.add
        self.rmsnorm_stats(N)
        for k in range(8):
            add("dve", lambda k=k: nc.vector.scalar_tensor_tensor(
                out=self.h[:, k, 0:N], in0=self.h[:, k, 0:N], scalar=self.pp[:, PB_NF + k:PB_NF + k + 1],
                in1=self.rs[:, 0:N], op0=ALU.mult, op1=ALU.mult),
                reads=[("h", k), "rs", "pp"], writes=[("h", k)])
        for tb in range(N // 128):
            yo = self.cy[:, 2 * (tb % 2):2 * (tb % 2) + 2, :].rearrange("p a b -> p (a b)")
            tk = [("cy", 2 * (tb % 2)), ("cy", 2 * (tb % 2) + 1)]
            for hh in range(2):
                b = self.P.bank()
                for kk in range(4):
                    k = hh * 4 + kk
                    add("pe", lambda b=b, kk=kk, k=k, tb=tb: nc.tensor.transpose(
                        self.ps[b][:, kk * 128:(kk + 1) * 128], self.h[:, k, tb * 128:(tb + 1) * 128], self.ident[:, :]),
                        reads=[("h", k), "ident"], writes=[("ps", b)])
                add("act", lambda b=b, hh=hh, yo=yo: nc.scalar.copy(out=yo[:, hh * 512:(hh + 1) * 512], in_=self.ps[b][:, :]),
                    reads=[("ps", b)], writes=[tk[hh]])
            add("sp", lambda yo=yo, tb=tb: nc.sync.dma_start(out=ydst[tb * 128:(tb + 1) * 128, :], in_=yo),
                reads=tk, dma=("yout", tb % 2))

    def a_tok(self, ap2d):
        if self.kind == "samp":
            return ap2d[:, 0:NSEQ * 24].rearrange("p (b i) -> p b i", i=24)[:, :, 16:24]
        return ap2d[:, 16:16 + self.N]

    def g_tok(self, ap2d, off):
        if self.kind == "samp":
            return ap2d[:, 0:NSEQ * 40].rearrange("p (b i) -> p b i", i=40)[:, :, off:off + 8]
        return ap2d[:, off:off + self.N]

    def tokv(self, ap2d):
        if self.kind == "samp":
            return ap2d[:, 0:NS].rearrange("p (b t) -> p b t", t=8)
        return ap2d[:, 0:self.N]

    def layer(self, l):
        nc = self.nc
        add = self.add
        N = self.N
        kind = self.kind
        pb = l * PL
        samp = kind == "samp"
        c0 = self.c0
        tmask = self.tokmask[:, c0:c0 + N]
        WA = NSEQ * 24 if samp else 16 + N
        WG = NSEQ * 40 if samp else 32 + N

        def a_tok(ap2d):
            if samp:
                return ap2d[:, 0:NSEQ * 24].rearrange("p (b i) -> p b i", i=24)[:, :, 16:24]
            return ap2d[:, 16:16 + N]

        def g_tok(ap2d, off):
            if samp:
                return ap2d[:, 0:NSEQ * 40].rearrange("p (b i) -> p b i", i=40)[:, :, off:off + 8]
            return ap2d[:, off:off + N]

        def tokv(ap2d):
            if samp:
                return ap2d[:, 0:NS].rearrange("p (b t) -> p b t", t=8)
            return ap2d[:, 0:N]

        add("sp", lambda: nc.sync.dma_start(out=self.bkv[:, :], in_=self.bkv_d[l]), writes=["bkv"], dma="bkv")
        add("pool", lambda: nc.gpsimd.dma_start(out=self.wpl[:, :, :], in_=self.w_pool[l].rearrange("g c d -> c g d")),
            writes=["wpl"], dma="wpl")
        add("dve", lambda: nc.vector.tensor_scalar(self.bq8[:, :], self.pp[:, pb + PB_IN + 12:pb + PB_IN + 20], 0.125, None, ALU.mult),
            reads=["pp"], writes=["bq8"])
        add("act", lambda: nc.scalar.activation(out=self.sinkE[:, :], in_=self.pp[:, pb + PB_SINK:pb + PB_SINK + 8], func=AF.Exp),
            reads=["pp"], writes=["sinkE"])
        if not samp:
            for J in range(2):
                add("dve", lambda J=J: nc.vector.tensor_copy(
                    out=self.sinkB[:, J, :].rearrange("p (g q) -> p g q", q=128),
                    in_=self.sinkE[:, J * 4:(J + 1) * 4].unsqueeze(2).broadcast_to([128, 4, 128])),
                    reads=["sinkE"], writes=[("sinkB", J)])
            add("pool", lambda: nc.gpsimd.tensor_copy(out=self.aext[:, :, 0:16], in_=self.stA[:, l, :, :]),
                reads=["stA"], writes=[("aext", g) for g in range(4)])
            add("pool", lambda: nc.gpsimd.tensor_copy(out=self.gext[:, :, 0:32], in_=self.stG[:, l, :, :]),
                reads=["stG"], writes=[("gext", c) for c in range(4)])
            for hf in range(2):
                hs = slice(hf * 64, (hf + 1) * 64)
                add("pool", lambda hf=hf, hs=hs: nc.gpsimd.tensor_copy(out=self.kTz[hs, :, hf, 0:128], in_=self.stK[hs, l, :, :]),
                    reads=["stK"], writes=[("kT", 0), ("kT", 1)])
                add("pool", lambda hf=hf, hs=hs: nc.gpsimd.tensor_copy(
                    out=self.vz[:, 0, :, :].rearrange("p (j h) c -> p j h c", h=2)[:, :, hf, hs],
                    in_=self.stV[:, l, :].rearrange("p (j h d) -> p j h d", j=2, h=2)[:, :, hf, :]),
                    reads=["stV"], writes=[("v", 0)])
        elif "load" not in SKIP:
            self.samp_load_states(l)

        self.rmsnorm(N, pb + PB_NMIX)

        def proj_chunk(W, wtok, c0, evac):
            b = self.P.bank()
            for k in range(8):
                add("pe", lambda k=k, b=b: nc.tensor.matmul(self.ps[b][:, 0:N], W[:, k, c0:c0 + 128], self.u[:, k, 0:N],
                                                            start=(k == 0), stop=(k == 7)),
                    reads=[wtok, ("u", k)], writes=[("ps", b)])
            evac(b)

        W, wt = self.wget(("win", 0))
        for g in range(4):
            def ev(b, g=g):
                add("act", lambda: nc.scalar.activation(out=a_tok(self.aext[:, g, :]), in_=tokv(self.ps[b][:, :]),
                                                        func=AF.Identity, bias=self.pp[:, pb + PB_IN + g:pb + PB_IN + g + 1]),
                    reads=[("ps", b), "pp"], writes=[("aext", g)])
                if kind == "halo":
                    add("pool", lambda: nc.gpsimd.tensor_tensor(out=self.aext[:, g, 16:16 + N], in0=self.aext[:, g, 16:16 + N],
                                                                in1=tmask, op=ALU.mult),
                        reads=[("aext", g), "tokmask"], writes=[("aext", g)])
                if samp:
                    add("act", lambda: nc.scalar.activation(out=self.anew[:, g, 0:N], in_=self.ps[b][:, 0:N], func=AF.Identity,
                                                            bias=self.pp[:, pb + PB_IN + g:pb + PB_IN + g + 1]),
                        reads=[("ps", b), "pp"], writes=[("sinkB", 0)])
            proj_chunk(W, wt, g * 128, ev)
        self.wrel(("win", 0))
        Wv, wvt = self.wget(("win", 1))
        Wgt, wgtt = self.wget(("win", 2))
        for c in range(4):
            def evg(b, c=c):
                add("act", lambda: nc.scalar.activation(out=self.sig[:, c % 2, 0:N], in_=self.ps[b][:, 0:N], func=AF.Sigmoid,
                                                        bias=self.pp[:, pb + PB_IN + 8 + c:pb + PB_IN + 9 + c]),
                    reads=[("ps", b), "pp"], writes=[("sig", c % 2)])
            proj_chunk(Wgt, wgtt, c * 128, evg)

            def evv(b, c=c):
                add("dve", lambda: nc.vector.scalar_tensor_tensor(
                    out=g_tok(self.gext[:, c, :], 32), in0=tokv(self.ps[b][:, :]),
                    scalar=self.pp[:, pb + PB_IN + 4 + c:pb + PB_IN + 5 + c], in1=tokv(self.sig[:, c % 2, :]),
                    op0=ALU.add, op1=ALU.mult),
                    reads=[("ps", b), ("sig", c % 2), "pp"], writes=[("gext", c)])
                if kind == "halo":
                    add("pool", lambda: nc.gpsimd.tensor_tensor(out=self.gext[:, c, 32:32 + N], in0=self.gext[:, c, 32:32 + N],
                                                                in1=tmask, op=ALU.mult),
                        reads=[("gext", c), "tokmask"], writes=[("gext", c)])
                if samp:
                    add("pool", lambda: nc.gpsimd.tensor_copy(out=tokv(self.gnew[:, c, :]), in_=g_tok(self.gext[:, c, :], 32)),
                        reads=[("gext", c)], writes=[("sinkB", 1)])
                add("act", lambda: nc.scalar.copy(out=self.gbf[:, c, 0:WG], in_=self.gext[:, c, 0:WG]),
                    reads=[("gext", c)], writes=self.tgbf(c))
            proj_chunk(Wv, wvt, c * 128, evv)
        self.wrel(("win", 1))
        self.wrel(("win", 2))
        early = (kind == "halo" and l == self.nl - 1 and HALO_EARLY_EXIT)
        for hh in range(0 if early else 2):
            W, wt = self.wget(("win", 3 + hh))
            for jj in range(4):
                j = hh * 4 + jj

                def ev(b, j=j):
                    add("act", lambda: nc.scalar.activation(out=self.q[:, j, 0:N], in_=self.ps[b][:, 0:N], func=AF.Identity,
                                                            scale=0.125, bias=self.bq8[:, j:j + 1]),
                        reads=[("ps", b), "bq8"], writes=[self.tq(j)])
                proj_chunk(W, wt, jj * 128, ev)
            self.wrel(("win", 3 + hh))
        W, wt = self.wget(("win", 5))
        koff = 0 if samp else 128
        for J in range(2):
            def ev(b, J=J):
                for hf in range(2):
                    hs = slice(hf * 64, (hf + 1) * 64)
                    add("act", lambda hf=hf, hs=hs: nc.scalar.activation(
                        out=self.kTz[hs, J, hf, koff:koff + N], in_=self.ps[b][hs, 0:N], func=AF.Identity,
                        bias=self.pp[hs, pb + PB_IN + 20 + J:pb + PB_IN + 21 + J]),
                        reads=[("ps", b), "pp"], writes=[("kT", J)])
                if samp or self.last_main:
                    add("act", lambda: nc.scalar.activation(out=self.kf[:, J, :], in_=self.ps[b][:, N - 128:N], func=AF.Identity,
                                                            bias=self.pp[:, pb + PB_IN + 20 + J:pb + PB_IN + 21 + J]),
                        reads=[("ps", b), "pp"], writes=[("kf", J)])
            proj_chunk(W, wt, J * 128, ev)
        if not samp:
            for tb in range(N // 128):
                b = self.P.bank()
                for k in range(8):
                    add("pe", lambda k=k, b=b, tb=tb: nc.tensor.matmul(self.ps[b][:, 0:256], self.u[:, k, tb * 128:(tb + 1) * 128],
                                                                       W[:, k, 256:512], start=(k == 0), stop=(k == 7)),
                        reads=[wt, ("u", k)], writes=[("ps", b)])
                for hf in range(2):
                    hs = slice(hf * 64, (hf + 1) * 64)
                    add("dve", lambda b=b, tb=tb, hf=hf, hs=hs: nc.vector.tensor_tensor(
                        out=self.vz[:, 1 + tb, :, :].rearrange("p (j h) c -> p j h c", h=2)[:, :, hf, hs],
                        in0=self.ps[b][:, 0:256].rearrange("p (j h d) -> p j h d", j=2, h=2)[:, :, hf, :],
                        in1=self.bkv[:, 256:512].rearrange("p (j h d) -> p j h d", j=2, h=2)[:, :, hf, :], op=ALU.add),
                        reads=[("ps", b), "bkv"], writes=[("v", 1 + tb)])
                if self.last_main and tb == 3:
                    add("dve", lambda b=b: nc.vector.tensor_tensor(out=self.ost[:, 1, 0:256], in0=self.ps[b][:, 0:256],
                                                                   in1=self.bkv[:, 256:512], op=ALU.add),
                        reads=[("ps", b), "bkv"], writes=[("pscr", 1)])
                    add("sp", lambda: nc.sync.dma_start(out=self.v_p[l], in_=self.ost[:, 1, 0:256]),
                        reads=[("pscr", 1)], dma=("ostd", 1))
        elif "sv" not in SKIP:
            for bp in range(NSEQ // 2):
                b = self.P.bank()
                for bb in range(2):
                    sq_ = bp * 2 + bb
                    for k in range(8):
                        add("pe", lambda k=k, b=b, bb=bb, sq_=sq_: nc.tensor.matmul(
                            self.ps[b][0:8, bb * 256:(bb + 1) * 256], self.u[:, k, sq_ * 8:(sq_ + 1) * 8], W[:, k, 256:512],
                            start=(k == 0), stop=(k == 7)),
                            reads=[wt, ("u", k)], writes=[("ps", b)])
                add("dve", lambda b=b: nc.vector.tensor_tensor(
                    out=self.vnewf[:, :, :], in0=self.ps[b][0:8, :].rearrange("p (a f) -> p a f", f=256),
                    in1=self.bkv[0:8, 256:512].unsqueeze(1).broadcast_to([8, 2, 256]), op=ALU.add),
                    reads=[("ps", b), "bkv"], writes=[("rD", 1)])
                add("act", lambda bp=bp: nc.scalar.copy(out=self.vnew[:, bp * 2:bp * 2 + 2, :], in_=self.vnewf[:, :, :]),
                    reads=[("rD", 1)], writes=["vnew"])
                add("sp", lambda bp=bp: nc.sync.dma_start(
                    out=self.v_s[l, bp * 2:bp * 2 + 2, 120:128, :].rearrange("b t f -> t b f"), in_=self.vnewf[:, :, :]),
                    reads=[("rD", 1)], dma="vs")
        self.wrel(("win", 5))

        if not samp:
            add("pool", lambda: nc.gpsimd.tensor_copy(out=self.stA[:, l, :, :], in_=self.aext[:, :, N:N + 16]),
                reads=[("aext", g) for g in range(4)], writes=["stA"])
            add("pool", lambda: nc.gpsimd.tensor_copy(out=self.stG[:, l, :, :], in_=self.gext[:, :, N:N + 32]),
                reads=[("gext", c) for c in range(4)], writes=["stG"])
            if self.last_main:
                self.prompt_state_out(l)
        elif "out" not in SKIP:
            self.samp_state_out(l)

        if early:
            for hf in range(2):
                hs = slice(hf * 64, (hf + 1) * 64)
                add("pool", lambda hf=hf, hs=hs: nc.gpsimd.tensor_copy(out=self.stK[hs, l, :, :], in_=self.kTz[hs, :, hf, N:N + 128]),
                    reads=[("kT", 0), ("kT", 1)], writes=["stK"])
                add("pool", lambda hf=hf, hs=hs: nc.gpsimd.tensor_copy(
                    out=self.stV[:, l, :].rearrange("p (j h d) -> p j h d", j=2, h=2)[:, :, hf, :],
                    in_=self.vz[:, N // 128, :, :].rearrange("p (j h) c -> p j h c", h=2)[:, :, hf, hs]),
                    reads=[("v", N // 128)], writes=["stV"])
            return
        nd = [0]
        cb = []
        for c in range(4):
            b = self.P.bank()
            cb.append(b)
            for j in range(31):
                ds = nd[0] % NDIAG
                nd[0] += 1
                if j % 2 == 0:
                    add("pool", lambda ds=ds, c=c, j=j: nc.gpsimd.tensor_scalar(
                        self.diag[:, ds, :], self.identb[:, :], self.pp[:, pb + PB_CW + c * 31 + j:pb + PB_CW + c * 31 + j + 1],
                        1.0, ALU.mult, ALU.mult),
                        reads=["identb", "pp"], writes=[("diag", ds)])
                else:
                    add("act", lambda ds=ds, c=c, j=j: nc.scalar.activation(
                        out=self.diag[:, ds, :], in_=self.identb[:, :], func=AF.Identity,
                        scale=self.pp[:, pb + PB_CW + c * 31 + j:pb + PB_CW + c * 31 + j + 1]),
                        reads=["identb", "pp"], writes=[("diag", ds)])
                add("pe", lambda ds=ds, c=c, j=j, b=b: nc.tensor.matmul(
                    tokv(self.ps[b][:, :]), self.diag[:, ds, :], g_tok(self.gbf[:, c, :], 2 + j),
                    start=(j == 0), stop=(j == 30)),
                    reads=[("diag", ds)] + self.tgbf(c), writes=[("ps", b)])
        bm = self.P.bank()
        bv = self.P.bank()
        for c in range(4):
            b = cb[c]
            add("act", lambda b=b, c=c: nc.scalar.activation(out=self.cy[:, c, 0:N], in_=self.ps[b][:, 0:N], func=AF.Identity,
                                                             bias=self.pp[:, pb + PB_CB + c:pb + PB_CB + c + 1]),
                reads=[("ps", b), "pp"], writes=[("cy", c)])
            add("act", lambda c=c: nc.scalar.activation(out=self.sq[:, 0, 0:N], in_=self.cy[:, c, 0:N], func=AF.Square),
                reads=[("cy", c)], writes=[("sq", 0)])
            add("pool", lambda c=c: nc.gpsimd.tensor_copy(out=self.sq[:, 1, 0:N], in_=self.cy[:, c, 0:N]),
                reads=[("cy", c)], writes=[("sq", 1)])
            add("pe", lambda c=c: nc.tensor.matmul(self.ps[bm][:, 0:N], self.onesb[:, :], self.sq[:, 1, 0:N],
                                                   start=(c == 0), stop=(c == 3)),
                reads=[("sq", 1), "onesb"], writes=[("ps", bm)])
            add("pe", lambda c=c: nc.tensor.matmul(self.ps[bv][:, 0:N], self.onesb[:, :], self.sq[:, 0, 0:N],
                                                   start=(c == 0), stop=(c == 3)),
                reads=[("sq", 0), "onesb"], writes=[("ps", bv)])
        mu = self.mt[:, 0, 0:N]
        var = self.mt[:, 1, 0:N]
        add("dve", lambda: nc.vector.tensor_scalar(mu, self.ps[bm][:, 0:N], 1.0 / 512, None, ALU.mult),
            reads=[("ps", bm)], writes=[("mt", 0)])
        add("dve", lambda: nc.vector.tensor_tensor(out=self.rs[:, 0:N], in0=mu, in1=mu, op=ALU.mult),
            reads=[("mt", 0)], writes=["rs"])
        add("dve", lambda: nc.vector.scalar_tensor_tensor(out=var, in0=self.ps[bv][:, 0:N], scalar=1.0 / 512, in1=self.rs[:, 0:N],
                                                          op0=ALU.mult, op1=ALU.subtract),
            reads=[("ps", bv), "rs"], writes=[("mt", 1)])
        add("dve", lambda: nc.vector.tensor_scalar(var, var, 0.0, None, ALU.max), reads=[("mt", 1)], writes=[("mt", 1)])
        add("act", lambda: nc.scalar.activation(out=var, in_=var, func=AF.Ln, bias=EPS), reads=[("mt", 1)], writes=[("mt", 1)])
        add("act", lambda: nc.scalar.activation(out=var, in_=var, func=AF.Exp, scale=-0.5), reads=[("mt", 1)], writes=[("mt", 1)])
        for c in range(4):
            add("pool", lambda c=c: nc.gpsimd.tensor_tensor(out=self.cy[:, c, 0:N], in0=self.cy[:, c, 0:N], in1=mu, op=ALU.subtract),
                reads=[("cy", c), ("mt", 0)], writes=[("cy", c)])
            add("dve", lambda c=c: nc.vector.tensor_tensor(out=self.cy[:, c, 0:N], in0=self.cy[:, c, 0:N], in1=var, op=ALU.mult),
                reads=[("cy", c), ("mt", 1)], writes=[("cy", c)])
            add("act", lambda c=c: nc.scalar.activation(out=self.s[:, c, 0:N], in_=self.cy[:, c, 0:N], func=AF.Silu,
                                                        scale=self.pp[:, pb + PB_NG + c:pb + PB_NG + c + 1],
                                                        bias=self.pp[:, pb + PB_NB + c:pb + PB_NB + c + 1]),
                reads=[("cy", c), "pp"], writes=[self.ts_(c)])

        for g in range(4):
            ext = self.aext[:, g, :]
            src = ext
            src_tok = [("aext", g)]
            sh = 1
            i = 0
            while sh < WINS[g]:
                dst = self.pscr[:, i % 2, :]
                add("pool", lambda dst=dst, src=src, sh=sh: nc.gpsimd.tensor_tensor(
                    out=dst[:, sh:WA], in0=src[:, sh:WA], in1=src[:, 0:WA - sh], op=ALU.add),
                    reads=src_tok, writes=[("pscr", i % 2)])
                src = dst
                src_tok = [("pscr", i % 2)]
                sh *= 2
                i += 1
            add("dve", lambda src=src, ext=ext, g=g: nc.vector.scalar_tensor_tensor(
                out=tokv(self.r[:, g, :]), in0=a_tok(src), scalar=1.0 / WINS[g], in1=a_tok(ext),
                op0=ALU.mult, op1=ALU.subtract),
                reads=src_tok + [("aext", g)], writes=[self.tr(g)])
            if kind == "halo":
                add("dve", lambda src=src, g=g: nc.vector.tensor_tensor(out=self.rs[:, 0:16], in0=src[:, N:N + 16],
                                                                        in1=self.invc[:, g, :], op=ALU.mult),
                    reads=src_tok + ["invc"], writes=["rs"])
                add("dve", lambda ext=ext, g=g: nc.vector.tensor_tensor(out=self.r[:, g, N - 16:N], in0=self.rs[:, 0:16],
                                                                        in1=ext[:, N:N + 16], op=ALU.subtract),
                    reads=["rs", ("aext", g)], writes=[self.tr(g)])

        if samp:
            if "attn" not in SKIP:
                self.attn_sample(l)
        else:
            self.attn_prompt(l)
            for hf in range(2):
                hs = slice(hf * 64, (hf + 1) * 64)
                add("pool", lambda hf=hf, hs=hs: nc.gpsimd.tensor_copy(out=self.stK[hs, l, :, :], in_=self.kTz[hs, :, hf, N:N + 128]),
                    reads=[("kT", 0), ("kT", 1)], writes=["stK"])
                add("pool", lambda hf=hf, hs=hs: nc.gpsimd.tensor_copy(
                    out=self.stV[:, l, :].rearrange("p (j h d) -> p j h d", j=2, h=2)[:, :, hf, :],
                    in_=self.vz[:, N // 128, :, :].rearrange("p (j h) c -> p j h c", h=2)[:, :, hf, hs]),
                    reads=[("v", N // 128)], writes=["stV"])

        def gate_chunk(Wg, wgt, m, bi):
            bg = self.P.bank()
            for k in range(8):
                add("pe", lambda k=k, bg=bg: nc.tensor.matmul(self.ps[bg][:, 0:N], Wg[:, k, (m % 4) * 128:(m % 4 + 1) * 128],
                                                              self.u[:, k, 0:N], start=(k == 0), stop=(k == 7)),
                    reads=[wgt, ("u", k)], writes=[("ps", bg)])
            col = pb + PB_IN + 24 + bi * 8 + m
            add("act", lambda bg=bg: nc.scalar.activation(out=self.sig[:, m % 2, 0:N], in_=self.ps[bg][:, 0:N], func=AF.Sigmoid,
                                                          bias=self.pp[:, col:col + 1]),
                reads=[("ps", bg), "pp"], writes=[("sig", m % 2)])

        Wp, wpt = self.wpl, "wpl"
        for hh in range(2):
            Wg, wgt = self.wget(("win", 6 + hh))
            for mm in range(4):
                m = hh * 4 + mm
                gate_chunk(Wg, wgt, m, 0)
                by = self.P.bank()
                add("pe", lambda by=by, m=m: nc.tensor.matmul(self.ps[by][:, 0:N], Wp[:, m // 2, (m % 2) * 128:(m % 2 + 1) * 128],
                                                              self.r[:, m // 2, 0:N], start=True, stop=True),
                    reads=[wpt, self.tr(m // 2)], writes=[("ps", by)])
                ma, mtok = self.macc(m, N)
                add("dve", lambda by=by, m=m, ma=ma: nc.vector.scalar_tensor_tensor(
                    out=ma, in0=self.ps[by][:, 0:N], scalar=self.pp[:, pb + PB_PSC + m:pb + PB_PSC + m + 1],
                    in1=self.sig[:, m % 2, 0:N], op0=ALU.mult, op1=ALU.mult),
                    reads=[("ps", by), ("sig", m % 2), "pp"], writes=[mtok])
            self.wrel(("win", 6 + hh))
        for hh in range(2):
            Wc, wct = self.wget(("wco", hh))
            Wg, wgt = self.wget(("win", 8 + hh))
            for mm in range(4):
                m = hh * 4 + mm
                gate_chunk(Wg, wgt, m, 1)
                by = self.P.bank()
                for c in range(4):
                    add("pe", lambda by=by, mm=mm, c=c, Wc=Wc: nc.tensor.matmul(self.ps[by][:, 0:N], Wc[:, c, mm * 128:(mm + 1) * 128],
                                                                       self.s[:, c, 0:N], start=(c == 0), stop=(c == 3)),
                        reads=[wct, self.ts_(c)], writes=[("ps", by)])
                ma, mtok = self.macc(m, N)
                add("dve", lambda by=by, m=m: nc.vector.tensor_tensor(out=self.mt[:, m % 2, 0:N], in0=self.ps[by][:, 0:N],
                                                                      in1=self.sig[:, m % 2, 0:N], op=ALU.mult),
                    reads=[("ps", by), ("sig", m % 2)], writes=[("mt", m % 2)])
                add("pool", lambda m=m, ma=ma: nc.gpsimd.tensor_tensor(out=ma, in0=ma, in1=self.mt[:, m % 2, 0:N], op=ALU.add),
                    reads=[mtok, ("mt", m % 2)], writes=[mtok])
            self.wrel(("win", 8 + hh))
            self.wrel(("wco", hh))
        for hh in range(2):
            Wa, wat = self.wget(("wao", hh))
            Wg, wgt = self.wget(("win", 10 + hh))
            for mm in range(4):
                m = hh * 4 + mm
                gate_chunk(Wg, wgt, m, 2)
                by = self.P.bank()
                for k in range(8):
                    add("pe", lambda by=by, mm=mm, k=k, Wa=Wa: nc.tensor.matmul(self.ps[by][:, 0:N], Wa[:, k, mm * 128:(mm + 1) * 128],
                                                                         self.q[:, k, 0:N], start=(k == 0), stop=(k == 7)),
                        reads=[wat, self.tq(k)], writes=[("ps", by)])
                ma, mtok = self.macc(m, N)
                add("dve", lambda by=by, m=m: nc.vector.tensor_tensor(out=self.mt[:, m % 2, 0:N], in0=self.ps[by][:, 0:N],
                                                                      in1=self.sig[:, m % 2, 0:N], op=ALU.mult),
                    reads=[("ps", by), ("sig", m % 2)], writes=[("mt", m % 2)])
                add("pool", lambda m=m, ma=ma: nc.gpsimd.tensor_tensor(out=self.merged[:, m, 0:N], in0=ma, in1=self.mt[:, m % 2, 0:N],
                                                                       op=ALU.add),
                    reads=[mtok, ("mt", m % 2)], writes=[self.tmg(m)])
            self.wrel(("wao", hh))
            self.wrel(("win", 10 + hh))
        self.stat_begin()
        for hh in range(2):
            Wo, wot = self.wget(("wo", hh))
            for mm in range(4):
                m = hh * 4 + mm
                b = self.P.bank()
                for k in range(8):
                    add("pe", lambda b=b, mm=mm, k=k, Wo=Wo: nc.tensor.matmul(self.ps[b][:, 0:N], Wo[:, k, mm * 128:(mm + 1) * 128],
                                                                       self.merged[:, k, 0:N], start=(k == 0), stop=(k == 7)),
                        reads=[wot, self.tmg(k)], writes=[("ps", b)])
                self.stat_pe()
                hm = self.h[:, m, self.c0:self.c0 + N]
                add("dve", lambda b=b, hm=hm: nc.vector.tensor_tensor(out=hm, in0=self.ps[b][:, 0:N], in1=hm, op=ALU.add),
                    reads=[("ps", b), ("h", m)], writes=[("h", m)])
                self.stat_chunk(m)
            self.wrel(("wo", hh))
        self.stat_end()
        self.rmsnorm(N, pb + PB_NMLP)
        for j in range(8):
            Wu, wut = self.wget(("wup", j))
            for ff in range(4):
                f = j * 4 + ff
                b = self.P.bank()
                for k in range(8):
                    add("pe", lambda b=b, ff=ff, k=k, Wu=Wu: nc.tensor.matmul(self.ps[b][:, 0:N], Wu[:, k, ff * 128:(ff + 1) * 128],
                                                                       self.u[:, k, 0:N], start=(k == 0), stop=(k == 7)),
                        reads=[wut, ("u", k)], writes=[("ps", b)])
                add("act", lambda b=b, f=f: nc.scalar.activation(out=self.relu[:, f % 2, 0:N], in_=self.ps[b][:, 0:N], func=AF.Relu),
                    reads=[("ps", b)], writes=[("relu", f % 2)])
                add("dve", lambda b=b, f=f: nc.vector.tensor_tensor(out=self.hid[:, f, 0:N], in0=self.ps[b][:, 0:N],
                                                                    in1=self.relu[:, f % 2, 0:N], op=ALU.mult),
                    reads=[("ps", b), ("relu", f % 2)], writes=[self.thid(f)])
            self.wrel(("wup", j))
        self.stat_begin()
        for m in range(8):
            Wd, wdt = self.wget(("wdn", m))
            b = self.P.bank()
            for f in range(32):
                add("pe", lambda b=b, f=f, Wd=Wd: nc.tensor.matmul(self.ps[b][:, 0:N], Wd[:, f, :], self.hid[:, f, 0:N],
                                                            start=(f == 0), stop=(f == 31)),
                    reads=[wdt, self.thid(f)], writes=[("ps", b)])
            self.stat_pe()
            hm = self.h[:, m, self.c0:self.c0 + N]
            add("dve", lambda b=b, hm=hm: nc.vector.tensor_tensor(out=hm, in0=self.ps[b][:, 0:N], in1=hm, op=ALU.add),
                reads=[("ps", b), ("h", m)], writes=[("h", m)])
            self.stat_chunk(m)
            self.wrel(("wdn", m))
        self.stat_end()

    def attn_prompt(self, l):
        nc = self.nc
        add = self.add
        N = self.N
        LA = 3
        halo_off = self.c0 // 128
        units = [(qb, J, hf, c) for qb in range(N // 128) for J in range(2) for hf in range(2) for c in range(2)]
        info = {}
        grp = {}
        for gi, (qb, J) in enumerate([(qb, J) for qb in range(N // 128) for J in range(2)]):
            grp[(qb, J)] = (4, 5) if gi % 2 == 0 else (6, 7)

        def emit_qk(i):
            qb, J, hf, c = units[i]
            kv = 2 * J + hf
            qs = slice(qb * 128, (qb + 1) * 128)
            ps_ = slice(hf * 64, (hf + 1) * 64)
            kb = qb + c
            bl = i % 4
            add("pe", lambda: nc.tensor.matmul(
                self.ps[bl][:, :], self.identb[:, :],
                self.biasT[:, c, kv * 4:(kv + 1) * 4, :].rearrange("p g q -> p (g q)"), start=True, stop=False),
                reads=["identb", "biasT"], writes=[("ps", bl)])
            add("pe", lambda: nc.tensor.matmul(
                self.ps[bl][:, :].rearrange("p (g q) -> p g q", q=128),
                self.kTz[:, J, hf, kb * 128:(kb + 1) * 128], self.q[:, J * 4:(J + 1) * 4, qs], start=False, stop=True),
                reads=[("kT", J)] + [self.tq(J * 4 + g) for g in range(4)], writes=[("ps", bl)])
            info[i] = (bl, kb)

        def emit_soft(i):
            bl, kb = info[i]
            ai = i % 4
            mcol = None
            if self.kind == "halo":
                mcol = 4 if kb == 0 else kb - 1 + halo_off
            elif self.tname == "m0" and kb == 0:
                mcol = 3
            if mcol is None:
                add("act", lambda: nc.scalar.activation(out=self.pT[:, ai, :], in_=self.ps[bl][:, :], func=AF.Exp),
                    reads=[("ps", bl)], writes=[("pT", ai)])
            else:
                add("act", lambda: nc.scalar.activation(
                    out=self.pT[:, ai, :], in_=self.ps[bl][:, :], func=AF.Exp, bias=self.kmask[:, mcol:mcol + 1]),
                    reads=[("ps", bl), "kmask"], writes=[("pT", ai)])

        def emit_pv(i):
            qb, J, hf, c = units[i]
            kv = 2 * J + hf
            ps_ = slice(hf * 64, (hf + 1) * 64)
            bo, bd = grp[(qb, J)]
            bl, kb = info[i]
            ai = i % 4
            first = (hf == 0 and c == 0)
            last = (hf == 1 and c == 1)
            add("pe", lambda: nc.tensor.matmul(
                self.ps[bo][:, :], self.vz[:, kb, kv, :], self.pT[:, ai, :], start=first, stop=last),
                reads=[("v", kb), ("pT", ai)], writes=[("ps", bo)])
            add("pe", lambda: nc.tensor.matmul(
                self.ps[bd][:, :], self.onesz[:, hf, :], self.pT[:, ai, :], start=first, stop=last),
                reads=["onesz", ("pT", ai)], writes=[("ps", bd)])

        def emit_norm(qb, J):
            bo, bd = grp[(qb, J)]
            qs = slice(qb * 128, (qb + 1) * 128)
            add("dve", lambda: nc.vector.tensor_tensor(out=self.rD[:, J, :], in0=self.ps[bd][:, :], in1=self.sinkB[:, J, :], op=ALU.add),
                reads=[("ps", bd), ("sinkB", J)], writes=[("rD", J)])
            add("act", lambda: nc.scalar.activation(out=self.rD[:, J, :], in_=self.rD[:, J, :], func=AF.Ln),
                reads=[("rD", J)], writes=[("rD", J)])
            add("act", lambda: nc.scalar.activation(out=self.rD[:, J, :], in_=self.rD[:, J, :], func=AF.Exp, scale=-1.0),
                reads=[("rD", J)], writes=[("rD", J)])
            add("dve", lambda: nc.vector.tensor_tensor(
                out=self.q[:, J * 4:(J + 1) * 4, qs], in0=self.ps[bo][:, :].rearrange("p (g q) -> p g q", q=128),
                in1=self.rD[:, J, :].rearrange("p (g q) -> p g q", q=128), op=ALU.mult),
                reads=[("ps", bo), ("rD", J)], writes=[self.tq(J * 4 + g) for g in range(4)])

        n = len(units)
        NDEF = 3
        pending = []
        for i in range(min(LA, n)):
            emit_qk(i)
        for i in range(n):
            if i + LA < n:
                emit_qk(i + LA)
            emit_soft(i)
            while pending and pending[0][0] <= i:
                _, pq, pj = pending.pop(0)
                emit_norm(pq, pj)
            emit_pv(i)
            qb, J, hf, c = units[i]
            if hf == 1 and c == 1:
                pending.append((i + NDEF, qb, J))
        for _, pq, pj in pending:
            emit_norm(pq, pj)

    def prompt_state_out(self, l):
        nc = self.nc
        add = self.add
        N = self.N
        for (src, tokname, lo, n, dst, slot) in ((self.aext, "aext", N + 1, 15, self.pool_p, 0),
                                                 (self.gext, "gext", N + 2, 30, self.conv_p, 0)):
            b = self.P.bank()
            for g in range(4):
                add("pe", lambda b=b, g=g, src=src, lo=lo, n=n: nc.tensor.transpose(
                    self.ps[b][0:n, g * 128:(g + 1) * 128], src[:, g, lo:lo + n], self.ident[:, :]),
                    reads=[(tokname, g), "ident"], writes=[("ps", b)])
            add("act", lambda b=b, n=n: nc.scalar.copy(out=self.ost[0:n, 0, :], in_=self.ps[b][0:n, :]),
                reads=[("ps", b)], writes=[("pscr", 0)])
            add("sp", lambda dst=dst, n=n: nc.sync.dma_start(out=dst[l], in_=self.ost[0:n, 0, :]),
                reads=[("pscr", 0)], dma=("ostd", 0))
        b = self.P.bank()
        for J in range(2):
            add("pe", lambda b=b, J=J: nc.tensor.transpose(self.ps[b][:, J * 128:(J + 1) * 128], self.kf[:, J, :], self.ident[:, :]),
                reads=[("kf", J), "ident"], writes=[("ps", b)])
        add("act", lambda b=b: nc.scalar.copy(out=self.ost[:, 0, 0:256], in_=self.ps[b][:, 0:256]),
            reads=[("ps", b)], writes=[("pscr", 0)])
        add("sp", lambda: nc.sync.dma_start(out=self.k_p[l], in_=self.ost[:, 0, 0:256]), reads=[("pscr", 0)], dma=("ostd", 0))

    def samp_load_states(self, l):
        nc = self.nc
        add = self.add
        for rb in range(2):
            stg = self.ost[0:120, rb, :]
            add("sp", lambda rb=rb, stg=stg: nc.sync.dma_start(
                out=stg, in_=self.spool[l, rb * 8:(rb + 1) * 8].rearrange("b i f -> (b i) f")),
                writes=[("pscr", rb)], dma=("ostd", rb))
            b = self.P.bank()
            for g in range(4):
                add("pe", lambda b=b, g=g, stg=stg: nc.tensor.transpose(
                    self.ps[b][:, g * 128:g * 128 + 120], stg[:, g * 128:(g + 1) * 128], self.ident[0:120, 0:120]),
                    reads=[("pscr", rb), "ident"], writes=[("ps", b)])
            add("act", lambda b=b, rb=rb: nc.scalar.copy(
                out=self.aext[:, :, rb * 8 * 24:(rb + 1) * 8 * 24].rearrange("p g (b i) -> p g b i", i=24)[:, :, :, 1:16],
                in_=self.ps[b][:, :].rearrange("p (g x) -> p g x", x=128)[:, :, 0:120].rearrange("p g (b i) -> p g b i", i=15)),
                reads=[("ps", b)], writes=[("aext", g) for g in range(4)])
        for rb in range(4):
            stg = self.ost[0:120, rb % 2, :]
            add("sp", lambda rb=rb, stg=stg: nc.sync.dma_start(
                out=stg, in_=self.sconv[l, rb * 4:(rb + 1) * 4].rearrange("b i f -> (b i) f")),
                writes=[("pscr", rb % 2)], dma=("ostd", rb % 2))
            b = self.P.bank()
            for g in range(4):
                add("pe", lambda b=b, g=g, stg=stg: nc.tensor.transpose(
                    self.ps[b][:, g * 128:g * 128 + 120], stg[:, g * 128:(g + 1) * 128], self.ident[0:120, 0:120]),
                    reads=[("pscr", rb % 2), "ident"], writes=[("ps", b)])
            add("act", lambda b=b, rb=rb: nc.scalar.copy(
                out=self.gext[:, :, rb * 4 * 40:(rb + 1) * 4 * 40].rearrange("p g (b i) -> p g b i", i=40)[:, :, :, 2:32],
                in_=self.ps[b][:, :].rearrange("p (g x) -> p g x", x=128)[:, :, 0:120].rearrange("p g (b i) -> p g b i", i=30)),
                reads=[("ps", b)], writes=[("gext", g) for g in range(4)])

    def samp_state_out(self, l):
        nc = self.nc
        add = self.add
        add("sp", lambda: nc.sync.dma_start(out=self.pool_s[l, :, 0:7, :], in_=self.spool[l, :, 8:15, :]), dma="h2h0")
        add("sp", lambda: nc.sync.dma_start(out=self.conv_s[l, :, 0:22, :], in_=self.sconv[l, :, 8:30, :]), dma="h2h1")
        add("sp", lambda: nc.sync.dma_start(out=self.k_s[l, :, 0:120, :], in_=self.ck[l, :, 8:128, :]), dma="h2h2")
        add("sp", lambda: nc.sync.dma_start(out=self.v_s[l, :, 0:120, :], in_=self.cv[l, :, 8:128, :]), dma="h2h3")
        for (src, tokname, dst, r0, sidx) in ((self.anew, ("sinkB", 0), self.pool_s, 7, 0), (self.gnew, ("sinkB", 1), self.conv_s, 22, 1)):
            b = self.P.bank()
            for g in range(4):
                add("pe", lambda b=b, g=g, src=src: nc.tensor.transpose(self.ps[b][:, g * 128:(g + 1) * 128], src[:, g, :], self.ident[:, :]),
                    reads=[tokname, "ident"], writes=[("ps", b)])
            add("act", lambda b=b, sidx=sidx: nc.scalar.copy(out=self.ost[:, sidx, :], in_=self.ps[b][:, :]),
                reads=[("ps", b)], writes=[("pscr", sidx)])
            for sq_ in range(NSEQ):
                add("sp", lambda dst=dst, r0=r0, sq_=sq_, sidx=sidx: nc.sync.dma_start(
                    out=dst[l, sq_, r0:r0 + 8, :], in_=self.ost[sq_ * 8:(sq_ + 1) * 8, sidx, :]),
                    reads=[("pscr", sidx)], dma=("osts", sidx * 4 + sq_ % 4))
        b = self.P.bank()
        for J in range(2):
            add("pe", lambda b=b, J=J: nc.tensor.transpose(self.ps[b][:, J * 128:(J + 1) * 128], self.kf[:, J, :], self.ident[:, :]),
                reads=[("kf", J), "ident"], writes=[("ps", b)])
        add("act", lambda b=b: nc.scalar.copy(out=self.rs[:, 0:256], in_=self.ps[b][:, 0:256]), reads=[("ps", b)], writes=["rs"])
        for sq_ in range(NSEQ):
            add("sp", lambda sq_=sq_: nc.sync.dma_start(out=self.k_s[l, sq_, 120:128, :], in_=self.rs[sq_ * 8:(sq_ + 1) * 8, 0:256]),
                reads=["rs"], dma=("osts", 8 + sq_ % 4))

    def attn_sample(self, l):
        nc = self.nc
        add = self.add
        for grp in range(4):
            buf = grp % 2
            s0 = grp * 4
            add("sp", lambda s0=s0: nc.sync.dma_start(out=self.kst[:, :, :], in_=self.ck[l, s0:s0 + 4].rearrange("b s f -> s b f")),
                writes=[("mt", 0), ("mt", 1)], dma="kst")
            for J in range(2):
                b = self.P.bank()
                for bb in range(4):
                    add("pe", lambda b=b, bb=bb, J=J: nc.tensor.transpose(
                        self.ps[b][:, bb * 128:(bb + 1) * 128], self.kst[:, bb, J * 128:(J + 1) * 128], self.ident[:, :]),
                        reads=[("mt", 0), ("mt", 1), "ident"], writes=[("ps", b)])
                add("act", lambda b=b, J=J: nc.scalar.copy(out=self.kcT[:, J, :, :].rearrange("p b s -> p (b s)"),
                                                          in_=self.ps[b][:, :]),
                    reads=[("ps", b)], writes=["stK"])
            add("pool", lambda s0=s0: nc.gpsimd.dma_start(out=self.vc[:, :, :],
                                                          in_=self.cv[l, s0:s0 + 4].rearrange("b s f -> s b f")),
                writes=["stV"], dma="vc")
            blc = self.P.bank()
            blo = self.P.bank()
            for bb in range(4):
                sq_ = s0 + bb
                cs = slice(sq_ * 8, (sq_ + 1) * 8)
                for kv in range(4):
                    J, hf = kv // 2, kv % 2
                    ps_ = slice(hf * 64, (hf + 1) * 64)
                    oc = self.ps[blc][:, kv * 128:(kv + 1) * 128].rearrange("p (g b q) -> p g b q", g=4, b=4)[:, :, bb, :]
                    add("pe", lambda oc=oc, J=J, ps_=ps_, cs=cs, buf=buf, bb=bb: nc.tensor.matmul(
                        oc, self.kcT[ps_, J, bb, :], self.q[ps_, J * 4:(J + 1) * 4, cs], start=True, stop=True),
                        reads=["stK"] + [self.tq(J * 4 + g) for g in range(4)], writes=[("ps", blc)])
                    oo = self.ps[blo][0:8, kv * 128:(kv + 1) * 128].rearrange("p (g b q) -> p g b q", g=4, b=4)[:, :, bb, :]
                    add("pe", lambda oo=oo, J=J, ps_=ps_, cs=cs, hf=hf: nc.tensor.matmul(
                        oo, self.kTz[ps_, J, hf, cs], self.q[ps_, J * 4:(J + 1) * 4, cs], start=True, stop=True),
                        reads=[("kT", J)] + [self.tq(J * 4 + g) for g in range(4)], writes=[("ps", blo)])
            add("dve", lambda blc=blc: nc.vector.tensor_tensor(
                out=self.atmp[:, 0, :].rearrange("p (h b q) -> p h b q", h=16, b=4), in0=self.ps[blc][:, :].rearrange("p (h b q) -> p h b q", h=16, b=4),
                in1=self.biasS[:, 0, :].rearrange("p (h q) -> p h q", q=8).unsqueeze(2).broadcast_to([128, 16, 4, 8]), op=ALU.add),
                reads=[("ps", blc), "biasS"], writes=[("atmp", 0)])
            add("dve", lambda blo=blo: nc.vector.tensor_tensor(
                out=self.atmp[0:8, 1, :].rearrange("p (h b q) -> p h b q", h=16, b=4), in0=self.ps[blo][0:8, :].rearrange("p (h b q) -> p h b q", h=16, b=4),
                in1=self.biasS[0:8, 1, :].rearrange("p (h q) -> p h q", q=8).unsqueeze(2).broadcast_to([8, 16, 4, 8]), op=ALU.add),
                reads=[("ps", blo), "biasS"], writes=[("atmp", 1)])
            add("act", lambda: nc.scalar.activation(out=self.pT[:, 0, :], in_=self.atmp[:, 0, :], func=AF.Exp),
                reads=[("atmp", 0)], writes=[("pT", 0)])
            add("act", lambda: nc.scalar.activation(out=self.pT[0:8, 1, :], in_=self.atmp[0:8, 1, :], func=AF.Exp),
                reads=[("atmp", 1)], writes=[("pT", 1)])
            bo = self.P.bank()
            bd = self.P.bank()
            for bb in range(4):
                sq_ = s0 + bb
                for kv in range(4):
                    J, hf = kv // 2, kv % 2
                    ps_ = slice(hf * 64, (hf + 1) * 64)
                    pc = self.pT[:, 0, kv * 128:(kv + 1) * 128].rearrange("p (g b q) -> p g b q", g=4, b=4)[:, :, bb, :]
                    po = self.pT[0:8, 1, kv * 128:(kv + 1) * 128].rearrange("p (g b q) -> p g b q", g=4, b=4)[:, :, bb, :]
                    for (bk, wa, wb) in ((bo, self.vc[:, bb, kv * 64:(kv + 1) * 64], self.vnew[0:8, sq_, kv * 64:(kv + 1) * 64]),
                                         (bd, self.onesb[:, 0:64], self.onesb[0:8, 0:64])):
                        oo = self.ps[bk][ps_, J * 128:(J + 1) * 128].rearrange("p (g b q) -> p g b q", g=4, b=4)[:, :, bb, :]
                        add("pe", lambda oo=oo, wa=wa, pc=pc: nc.tensor.matmul(oo, wa, pc, start=True, stop=False),
                            reads=["stV", ("pT", 0), "onesb"], writes=[("ps", bk)])
                        add("pe", lambda oo=oo, wb=wb, po=po: nc.tensor.matmul(oo, wb, po, start=False, stop=True),
                            reads=["vnew", ("pT", 1), "onesb"], writes=[("ps", bk)])
            add("dve", lambda bd=bd: nc.vector.tensor_tensor(
                out=self.rD[:, 0, 0:256].rearrange("p (h x) -> p h x", x=32), in0=self.ps[bd][:, 0:256].rearrange("p (h x) -> p h x", x=32),
                in1=self.sinkE[:, :].unsqueeze(2).broadcast_to([128, 8, 32]), op=ALU.add),
                reads=[("ps", bd), "sinkE"], writes=[("rD", 0)])
            add("act", lambda: nc.scalar.activation(out=self.rD[:, 0, 0:256], in_=self.rD[:, 0, 0:256], func=AF.Ln),
                reads=[("rD", 0)], writes=[("rD", 0)])
            add("act", lambda: nc.scalar.activation(out=self.rD[:, 0, 0:256], in_=self.rD[:, 0, 0:256], func=AF.Exp, scale=-1.0),
                reads=[("rD", 0)], writes=[("rD", 0)])
            add("dve", lambda bo=bo, s0=s0: nc.vector.tensor_tensor(
                out=self.q[:, 0:8, s0 * 8:(s0 + 4) * 8], in0=self.ps[bo][:, 0:256].rearrange("p (h x) -> p h x", x=32),
                in1=self.rD[:, 0, 0:256].rearrange("p (h x) -> p h x", x=32), op=ALU.mult),
                reads=[("ps", bo), ("rD", 0)], writes=[self.tq(j) for j in range(8)])


def _t5_bucket(dist):
    n = np.maximum(dist, 0)
    exact = 16
    large = exact + (np.log(np.maximum(n, 1) / exact) / np.log(128 / exact) * (32 - exact)).astype(np.int32)
    large = np.minimum(large, 31)
    return np.where(n < exact, n, large).astype(np.int32)


def _qperm():
    idx = []
    for j in range(8):
        for hf in range(2):
            kv = 2 * (j // 4) + hf
            g = j % 4
            head = kv * 4 + g
            idx.extend(range(head * 64, head * 64 + 64))
    return np.array(idx)


def _fm(vec, nch):
    return np.ascontiguousarray(np.asarray(vec, np.float32).reshape(nch, 128).T)


_NC_CACHE = {}


def prepare(inputs):
    f = lambda k: np.asarray(inputs[k], np.float32)
    x_prompt, x_sample = f("x_prompt"), f("x_sample")
    qp = _qperm()
    cols = np.concatenate([np.arange(0, 1536), 1536 + qp, np.arange(2560, 6144)])
    w_in = np.ascontiguousarray(f("w_in")[:, :, cols])
    b_in = f("b_in")[:, cols]
    w_ao = np.ascontiguousarray(f("w_attn_out")[:, qp, :])
    pp = np.zeros((128, NPP), np.float32)
    sinks = f("attn_sinks")
    for l in range(NL):
        o = l * PL
        pp[:, o + PB_IN:o + PB_IN + 48] = _fm(b_in[l], 48)
        pp[:, o + PB_NMIX:o + PB_NMIX + 8] = _fm(f("norm_mix")[l], 8)
        pp[:, o + PB_PSC:o + PB_PSC + 8] = _fm(f("pool_scale")[l], 8)
        pp[:, o + PB_NMLP:o + PB_NMLP + 8] = _fm(f("norm_mlp")[l], 8)
        cw = f("conv_w")[l]
        pp[:, o + PB_CW:o + PB_CW + 124] = cw.T.reshape(4, 128, 31).transpose(1, 0, 2).reshape(128, 124)
        pp[:, o + PB_CB:o + PB_CB + 4] = _fm(f("conv_b")[l], 4)
        pp[:, o + PB_NG:o + PB_NG + 4] = _fm(f("conv_norm_g")[l], 4)
        pp[:, o + PB_NB:o + PB_NB + 4] = _fm(f("conv_norm_b")[l], 4)
        for J in range(2):
            for g in range(4):
                for hf in range(2):
                    pp[hf * 64:(hf + 1) * 64, o + PB_SINK + J * 4 + g] = sinks[l, (2 * J + hf) * 4 + g]
    pp[:, PB_NF:PB_NF + 8] = _fm(f("norm_final"), 8)
    bkv = np.ascontiguousarray(np.broadcast_to(b_in[:, None, 2560:3072], (NL, 128, 512)))
    ext = np.concatenate([f("rel_bias"), np.full((1, 16), NEG, np.float32)], axis=0)
    s = np.arange(128)[:, None]
    q = np.arange(128)[None, :]
    d_prev = 128 + q - s
    d_own = q - s
    tabs = []
    for dmat in (d_prev, d_own):
        idx = np.where((dmat >= 0) & (dmat <= 128), _t5_bucket(dmat), 32)
        tabs.append(ext[idx])
    biasT = np.stack(tabs, axis=1).transpose(0, 1, 3, 2)
    biasT = np.ascontiguousarray(biasT).reshape(128, 2 * 16 * 128)
    biasS = np.ascontiguousarray(biasT.reshape(128, 2, 16, 128)[:, :, :, 0:8]).reshape(128, 256)
    common = dict(identm=np.eye(128, dtype=np.float32), biasS=biasS, w_in=w_in, w_pool=f("pool_w"), w_co=f("w_conv_out"), w_ao=w_ao, w_o=f("w_out"),
                  w_up=f("w_up"), w_dn=f("w_down"), pp=pp, bkv=bkv, biasT=biasT)
    meta = f("meta_tokens")
    in_maps = []
    for c in range(8):
        b, cc = c // 4, c % 4
        m = dict(common)
        m["xm"] = np.ascontiguousarray(x_prompt[b, cc * 2048:(cc + 1) * 2048])
        tokmask = np.ones((128, 512), np.float32)
        kmask = np.zeros((128, 8), np.float32)
        kmask[:, 4] = NEG
        invc = np.zeros((128, 4, 16), np.float32)
        for g, w in enumerate(WINS):
            invc[:, g, :] = 1.0 / w
        if cc == 0:
            xh = np.zeros((512, D), np.float32)
            xh[496:] = meta
            tokmask[:, :496] = 0.0
            kmask[:, 0:3] = NEG
            kmask[:112, 3] = NEG
            for g, w in enumerate(WINS):
                invc[:, g, :] = 1.0 / np.minimum(np.arange(16) + 1, w)
        else:
            xh = np.ascontiguousarray(x_prompt[b, cc * 2048 - 512:cc * 2048])
        m["xh"] = xh
        m["tokmask"] = tokmask
        m["kmask"] = kmask
        m["invc"] = invc.reshape(128, 64)
        sl = slice(c * NSEQ, (c + 1) * NSEQ)
        m["xs"] = np.ascontiguousarray(x_sample[sl].reshape(NS, D))
        m["spool"] = np.ascontiguousarray(f("state_pool")[:, sl])
        m["sconv"] = np.ascontiguousarray(f("state_conv")[:, sl])
        m["ck"] = np.ascontiguousarray(f("cache_k")[:, sl].reshape(NL, NSEQ, 128, 256))
        m["cv"] = np.ascontiguousarray(f("cache_v")[:, sl].reshape(NL, NSEQ, 128, 256))
        in_maps.append(m)
    return in_maps


def assemble(res):
    R = res.results
    y_prompt = np.zeros((2, 8192, D), np.float32)
    for c in range(8):
        y_prompt[c // 4, (c % 4) * 2048:(c % 4 + 1) * 2048] = R[c]["ym"]
    y_sample = np.concatenate([R[c]["ys"].reshape(NSEQ, DT, D) for c in range(8)], axis=0)
    pool_p = np.stack([R[3]["pool_p"], R[7]["pool_p"]], axis=1)
    conv_p = np.stack([R[3]["conv_p"], R[7]["conv_p"]], axis=1)
    k_p = np.stack([R[3]["k_p"], R[7]["k_p"]], axis=1).reshape(NL, 2, 128, 4, 64)
    v_p = np.stack([R[3]["v_p"], R[7]["v_p"]], axis=1).reshape(NL, 2, 128, 4, 64)
    pool_s = np.concatenate([R[c]["pool_s"] for c in range(8)], axis=1)
    conv_s = np.concatenate([R[c]["conv_s"] for c in range(8)], axis=1)
    k_s = np.concatenate([R[c]["k_s"] for c in range(8)], axis=1).reshape(NL, 128, 128, 4, 64)
    v_s = np.concatenate([R[c]["v_s"] for c in range(8)], axis=1).reshape(NL, 128, 128, 4, 64)
    outs = (y_prompt, y_sample, pool_p, pool_s, conv_p, conv_s, k_p, k_s, v_p, v_s)
    return tuple(np.ascontiguousarray(o, dtype=np.float32) for o in outs)


def kernel(**inputs):
    in_maps = prepare(inputs)
    if "nc" not in _NC_CACHE:
        _NC_CACHE["nc"] = Builder().build()
    nc = _NC_CACHE["nc"]
    res = run_bass_kernel_spmd(nc, in_maps, core_ids=list(range(8)))
    return assemble(res)
```

```python
import contextlib
import numpy as np
import concourse.bass as bass
import concourse.mybir as mybir
from concourse.bass_utils import run_bass_kernel_spmd

F32 = mybir.dt.float32
BF16 = mybir.dt.bfloat16
AF = mybir.ActivationFunctionType
ALU = mybir.AluOpType

D = 1024
NL = 4
DIN = 6144
TT = 512
NSEQ = 16
DT = 8
NS = NSEQ * DT
NEG = -30000.0
EPS = 1e-6
WINS = (2, 4, 8, 16)

PB_IN = 0
PB_NMIX = 48
PB_PSC = 56
PB_NMLP = 64
PB_CW = 72
PB_CB = 196
PB_NG = 200
PB_NB = 204
PB_SINK = 208
PL = 216
PB_NF = NL * PL
NPP = PB_NF + 8

SAME_ENG_SYNC = True
NWS = 5
NDIAG = 16
TRI_HALO = True
HALO_EARLY_EXIT = True
SKIP = set()


class Op:
    __slots__ = ("eng", "fn", "deps", "sig", "sigval", "dma")

    def __init__(self, eng, fn, dma):
        self.eng = eng
        self.fn = fn
        self.deps = []
        self.sig = False
        self.sigval = None
        self.dma = dma


class Prog:
    ENGS = ("pe", "act", "dve", "pool", "sp")

    def __init__(self, nc):
        self.nc = nc
        self.ops = []
        self.writer = {}
        self.readers = {}
        self.dma_count = {}
        self.nbank = 0
        self.reserved = set()

    def eng_obj(self, eng):
        nc = self.nc
        return {"pe": nc.tensor, "act": nc.scalar, "dve": nc.vector,
                "pool": nc.gpsimd, "sp": nc.sync}[eng]

    def add(self, eng, fn, reads=(), writes=(), dma=None):
        o = Op(eng, fn, dma)
        deps = {}
        for t in reads:
            w = self.writer.get(t)
            if w is not None:
                deps[id(w)] = w
        for t in writes:
            w = self.writer.get(t)
            if w is not None:
                deps[id(w)] = w
            last = {}
            for r in self.readers.get(t, ()):
                if r.dma is not None:
                    deps[id(r)] = r
                else:
                    last[r.eng] = r
            for r in last.values():
                deps[id(r)] = r
        o.deps = list(deps.values())
        for t in reads:
            self.readers.setdefault(t, []).append(o)
        for t in writes:
            self.writer[t] = o
            self.readers[t] = []
        if dma is not None:
            c = self.dma_count.get(dma, 0) + 1
            self.dma_count[dma] = c
            o.sigval = 16 * c
        self.ops.append(o)
        return o

    def bank(self):
        while True:
            b = self.nbank % 8
            self.nbank += 1
            if b not in self.reserved:
                return b

    @staticmethod
    def _needs(o, d):
        if d.dma is not None:
            return True
        if o.dma is not None:
            return True
        if d.eng != o.eng:
            return True
        return SAME_ENG_SYNC and d.eng != "pe"

    def emit(self):
        nc = self.nc
        for o in self.ops:
            for d in o.deps:
                if d.dma is None and self._needs(o, d):
                    d.sig = True
        cnt = {e: 0 for e in self.ENGS}
        for o in self.ops:
            if o.dma is None and o.sig:
                cnt[o.eng] += 1
                o.sigval = cnt[o.eng]
        self.sig_counts = dict(cnt)
        per_eng = {e: [o for o in self.ops if o.eng == e] for e in self.ENGS}
        with contextlib.ExitStack() as st:
            esem = {e: st.enter_context(nc.semaphore("c_" + e)) for e in self.ENGS}
            dsem = {n: st.enter_context(nc.semaphore("d_%d" % i))
                    for i, n in enumerate(self.dma_count)}
            block = st.enter_context(nc.Block())

            def run(eng):
                E = self.eng_obj(eng)
                seen = {}
                for o in per_eng[eng]:
                    waits = {}
                    for d in o.deps:
                        if not self._needs(o, d):
                            continue
                        s = dsem[d.dma] if d.dma is not None else esem[d.eng]
                        k = id(s)
                        if seen.get(k, 0) >= d.sigval:
                            continue
                        if k not in waits or waits[k][1] < d.sigval:
                            waits[k] = (s, d.sigval)
                    for k, (s, v) in waits.items():
                        E.wait_ge(s, v)
                        seen[k] = v
                    ins = o.fn()
                    if o.dma is not None:
                        ins.then_inc(dsem[o.dma], 16)
                    elif o.sig:
                        ins.then_inc(esem[o.eng], 1)
                if eng == "sp":
                    for n, c in self.dma_count.items():
                        E.wait_ge(dsem[n], 16 * c)

            block.tensor(lambda e: run("pe"))
            block.scalar(lambda e: run("act"))
            block.vector(lambda e: run("dve"))
            block.gpsimd(lambda e: run("pool"))
            block.sync(lambda e: run("sp"))


class Builder:
    def __init__(self, tiles=("halo", "m0", "m1", "m2", "m3", "samp"), nl=NL, dbg=False):
        self.tiles = tiles
        self.nl = nl
        self.dbg = dbg
        self.nc = bass.Bass("TRN2", target_bir_lowering=False)
        self.P = Prog(self.nc)
        self.st = contextlib.ExitStack()

    def din(self, name, shape):
        return self.nc.dram_tensor(name, list(shape), F32, kind="ExternalInput").ap()

    def dout(self, name, shape):
        return self.nc.dram_tensor(name, list(shape), F32, kind="ExternalOutput").ap()

    def sb(self, name, shape, dt):
        return self.st.enter_context(self.nc.sbuf_tensor(name, list(shape), dt))

    def add(self, *a, **k):
        return self.P.add(*a, **k)

    def build(self):
        nc = self.nc
        with self.st:
            self.declare()
            self.prologue()
            for tname in self.tiles:
                self.run_tile(tname)
            self.P.emit()
        return nc

    def declare(self):
        nc = self.nc
        sb = self.sb
        self.xh = self.din("xh", [512, D])
        self.xm = self.din("xm", [2048, D])
        self.xs = self.din("xs", [NS, D])
        self.tokmask_d = self.din("tokmask", [128, 512])
        self.kmask_d = self.din("kmask", [128, 8])
        self.invc_d = self.din("invc", [128, 64])
        self.spool = self.din("spool", [NL, NSEQ, 15, 512])
        self.sconv = self.din("sconv", [NL, NSEQ, 30, 512])
        self.ck = self.din("ck", [NL, NSEQ, 128, 256])
        self.cv = self.din("cv", [NL, NSEQ, 128, 256])
        self.w_in = self.din("w_in", [NL, D, DIN])
        self.w_pool = self.din("w_pool", [NL, 4, 128, 256])
        self.w_co = self.din("w_co", [NL, 512, D])
        self.w_ao = self.din("w_ao", [NL, D, D])
        self.w_o = self.din("w_o", [NL, D, D])
        self.w_up = self.din("w_up", [NL, D, 4096])
        self.w_dn = self.din("w_dn", [NL, 4096, D])
        self.pp_d = self.din("pp", [128, NPP])
        self.bkv_d = self.din("bkv", [NL, 128, 512])
        self.biasT_d = self.din("biasT", [128, 2 * 16 * 128])
        self.ident_d = self.din("identm", [128, 128])
        self.biasS_d = self.din("biasS", [128, 256])
        self.ym = self.dout("ym", [2048, D])
        self.ys = self.dout("ys", [NS, D])
        self.pool_p = self.dout("pool_p", [NL, 15, 512])
        self.conv_p = self.dout("conv_p", [NL, 30, 512])
        self.k_p = self.dout("k_p", [NL, 128, 256])
        self.v_p = self.dout("v_p", [NL, 128, 256])
        self.pool_s = self.dout("pool_s", [NL, NSEQ, 15, 512])
        self.conv_s = self.dout("conv_s", [NL, NSEQ, 30, 512])
        self.k_s = self.dout("k_s", [NL, NSEQ, 128, 256])
        self.v_s = self.dout("v_s", [NL, NSEQ, 128, 256])
        if self.dbg:
            self.dbg_d = self.dout("dbg", [128, 8 * TT])
        self.h = sb("h", [128, 8, TT], F32)
        self.u = sb("u", [128, 8, TT], BF16)
        self.sq = sb("sq", [128, 2, TT], BF16)
        self.rs = sb("rs", [128, TT], F32)
        self.aext = sb("aext", [128, 4, 16 + TT], F32)
        self.pscr = sb("pscr", [128, 2, 16 + TT], F32)
        self.gext = sb("gext", [128, 4, 640], F32)
        self.sig = sb("sig", [128, 2, TT], F32)
        self.cy = sb("cy", [128, 4, TT], F32)
        self.big = sb("big", [128, 32 * TT], BF16)
        self.kTz = sb("kTz", [128, 2, 2, 128 + TT], BF16)
        self.vz = sb("vz", [128, 5, 4, 128], BF16)
        self.onesz = sb("onesz", [128, 2, 128], BF16)
        self.biasT = sb("biasT_s", [128, 2, 16, 128], BF16)
        self.atmp = sb("atmp", [128, 2, TT], F32)
        self.pT = sb("pT", [128, 4, TT], BF16)
        self.rD = sb("rD", [128, 2, TT], F32)
        self.sinkB = sb("sinkB", [128, 2, TT], F32)
        self.sinkE = sb("sinkE", [128, 8], F32)
        self.bq8 = sb("bq8", [128, 8], F32)
        self.mt = sb("mt", [128, 2, TT], F32)
        self.relu = sb("relu", [128, 2, TT], BF16)
        self.diag = sb("diag", [128, NDIAG, 128], BF16)
        self.ws = [sb("ws%d" % i, [128, 4096], BF16) for i in range(NWS)]
        self.wpl = sb("wpl", [128, 4, 256], BF16)
        self.stA = sb("stA", [128, NL, 4, 16], F32)
        self.stG = sb("stG", [128, NL, 4, 32], F32)
        self.stb = sb("stb", [128, 2048], BF16)
        self.stK = self.stb[:, 0:1024].rearrange("p (l j s) -> p l j s", l=NL, j=2)
        self.stV = self.stb[:, 1024:2048].rearrange("p (l f) -> p l f", l=NL)
        self.kcT = self.stb[:, 0:1024].rearrange("p (j b s) -> p j b s", j=2, b=4)
        self.vc = self.stb[:, 1024:2048].rearrange("p (b f) -> p b f", b=4)
        self.ident = sb("ident", [128, 128], F32)
        self.identb = sb("identb", [128, 128], BF16)
        self.onesb = sb("onesb", [128, 128], BF16)
        self.pp = sb("pp_s", [128, NPP], F32)
        self.bkv = sb("bkv_s", [128, 512], F32)
        self.tokmask = sb("tokmask_s", [128, 512], F32)
        self.kmask = sb("kmask_s", [128, 8], F32)
        self.invc = sb("invc_s", [128, 4, 16], F32)
        self.kf = sb("kf", [128, 2, 128], F32)
        self.vnew = sb("vnew", [8, NSEQ, 256], BF16)
        self.biasS = sb("biasS_s", [128, 2, 128], F32)
        self.ps = [self.st.enter_context(nc.psum_tensor("ps%d" % i, [128, 512], F32))
                   for i in range(8)]
        self.ost = self.pscr[:, :, 0:512]
        self.kst = self.mt[:, :, :].rearrange("p a b -> p (a b)").rearrange("p (b f) -> p b f", f=256)
        self.anew = self.sinkB[:, 0, :].rearrange("p (g t) -> p g t", t=128)
        self.gnew = self.sinkB[:, 1, :].rearrange("p (g t) -> p g t", t=128)
        self.vnewf = self.rD[0:8, 1, :].rearrange("p (a f) -> p a f", f=256)
        bigv = self.big[:, :].rearrange("p (c t) -> p c t", t=TT)
        self.hid = bigv
        self.q = bigv[:, 0:8, :]
        self.merged = bigv[:, 8:16, :]
        self.gbf = self.big[:, 16 * TT:16 * TT + 4 * 640].rearrange("p (c t) -> p c t", t=640)
        self.r = bigv[:, 21:25, :]
        self.s = bigv[:, 25:29, :]
        self.wq = []
        self.wi = 0
        self.wissued = 0
        self.wcur = {}

    def tq(self, j):
        return ("big", j)

    def tmg(self, m):
        return ("big", 8 + m)

    def tgbf(self, c):
        return [("big", 16 + c), ("big", 17 + c)]

    def tr(self, g):
        return ("big", 21 + g)

    def ts_(self, c):
        return ("big", 25 + c)

    def thid(self, f):
        return ("big", f)

    def macc(self, m, N):
        if m < 4:
            return self.cy[:, m, 0:N], ("cy", m)
        return self.aext[:, m - 4, 0:N], ("aext", m - 4)

    def layer_items(self, l):
        it = []
        for j in range(6):
            it.append((("win", j), self.w_in[l, :, j * 512:(j + 1) * 512].rearrange("(k p) c -> p k c", p=128), 8, 512))
        for j in (6, 7):
            it.append((("win", j), self.w_in[l, :, j * 512:(j + 1) * 512].rearrange("(k p) c -> p k c", p=128), 8, 512))
        for hh in range(2):
            it.append((("wco", hh), self.w_co[l, :, hh * 512:(hh + 1) * 512].rearrange("(k p) c -> p k c", p=128), 4, 512))
            it.append((("win", 8 + hh), self.w_in[l, :, (8 + hh) * 512:(9 + hh) * 512].rearrange("(k p) c -> p k c", p=128), 8, 512))
        for hh in range(2):
            it.append((("wao", hh), self.w_ao[l, :, hh * 512:(hh + 1) * 512].rearrange("(k p) c -> p k c", p=128), 8, 512))
            it.append((("win", 10 + hh), self.w_in[l, :, (10 + hh) * 512:(11 + hh) * 512].rearrange("(k p) c -> p k c", p=128), 8, 512))
        for hh in range(2):
            it.append((("wo", hh), self.w_o[l, :, hh * 512:(hh + 1) * 512].rearrange("(k p) c -> p k c", p=128), 8, 512))
        for j in range(8):
            it.append((("wup", j), self.w_up[l, :, j * 512:(j + 1) * 512].rearrange("(k p) c -> p k c", p=128), 8, 512))
        for m in range(8):
            it.append((("wdn", m), self.w_dn[l, :, m * 128:(m + 1) * 128].rearrange("(k p) c -> p k c", p=128), 32, 128))
        return it

    def wissue(self):
        nc = self.nc
        while self.wissued < len(self.wq):
            i = self.wissued
            if i >= NWS and not self.wreleased[i - NWS]:
                break
            key, src, d1, d2 = self.wq[i]
            slot = i % NWS
            dst = self.ws[slot][:, 0:d1 * d2].rearrange("p (a b) -> p a b", b=d2)
            self.add("pool", lambda dst=dst, src=src: nc.gpsimd.dma_start(out=dst, in_=src),
                     writes=[("ws", slot)], dma=("ws", slot))
            self.wissued += 1

    def wget(self, key):
        i = self.wi
        k, src, d1, d2 = self.wq[i]
        assert k == key, (k, key)
        self.wissue()
        assert self.wissued > i, ("weight stream stuck", i, key)
        self.wi += 1
        slot = i % NWS
        self.wcur[key] = i
        return self.ws[slot][:, 0:d1 * d2].rearrange("p (a b) -> p a b", b=d2), ("ws", slot)

    def wrel(self, key):
        self.wreleased[self.wcur.pop(key)] = True
        self.wissue()

    def prologue(self):
        nc = self.nc
        add = self.add
        for t in self.tiles:
            for l in range(self.nl):
                items = self.layer_items(l)
                if t == "halo" and l == self.nl - 1 and HALO_EARLY_EXIT:
                    items = [it_ for it_ in items if it_[0] in (("win", 0), ("win", 1), ("win", 2), ("win", 5))]
                self.wq.extend(items)
        self.wreleased = [False] * len(self.wq)
        add("sp", lambda: nc.sync.dma_start(out=self.pp[:, :], in_=self.pp_d), writes=["pp"], dma="c0")
        add("pool", lambda: nc.gpsimd.dma_start(out=self.biasT[:, :, :, :].rearrange("p a b c -> p (a b c)"), in_=self.biasT_d),
            writes=["biasT"], dma="c1")
        add("sp", lambda: nc.sync.dma_start(out=self.biasS[:, :, :].rearrange("p a b -> p (a b)"), in_=self.biasS_d),
            writes=["biasS"], dma="c6")
        add("sp", lambda: nc.sync.dma_start(out=self.tokmask[:, :], in_=self.tokmask_d), writes=["tokmask"], dma="c2")
        add("sp", lambda: nc.sync.dma_start(out=self.kmask[:, :], in_=self.kmask_d), writes=["kmask"], dma="c3")
        add("sp", lambda: nc.sync.dma_start(out=self.invc[:, :, :].rearrange("p a b -> p (a b)"), in_=self.invc_d),
            writes=["invc"], dma="c4")
        add("pool", lambda: nc.gpsimd.memset(self.onesb[:, :], 1.0), writes=["onesb"])
        add("pool", lambda: nc.gpsimd.memset(self.onesz[:, :, :].rearrange("p a b -> p (a b)"), 0.0), writes=["onesz"])
        add("pool", lambda: nc.gpsimd.memset(self.onesz[:, 0, 0:64], 1.0), writes=["onesz"])
        add("pool", lambda: nc.gpsimd.memset(self.onesz[:, 1, 64:128], 1.0), writes=["onesz"])
        add("pool", lambda: nc.gpsimd.memset(self.kTz[:, :, :, :].rearrange("p a b c -> p (a b c)"), 0.0),
            writes=[("kT", 0), ("kT", 1)])
        add("pool", lambda: nc.gpsimd.memset(self.vz[:, :, :, :].rearrange("p a b c -> p (a b c)"), 0.0),
            writes=[("v", i) for i in range(5)])
        add("sp", lambda: nc.sync.dma_start(out=self.ident[:, :], in_=self.ident_d), writes=["ident"], dma="c5")
        add("pool", lambda: nc.gpsimd.memset(self.aext[:, :, :].rearrange("p a b -> p (a b)"), 0.0),
            writes=[("aext", g) for g in range(4)])
        add("pool", lambda: nc.gpsimd.memset(self.gext[:, :, :].rearrange("p a b -> p (a b)"), 0.0),
            writes=[("gext", g) for g in range(4)])
        add("pool", lambda: nc.gpsimd.memset(self.pscr[:, :, :].rearrange("p a b -> p (a b)"), 0.0),
            writes=[("pscr", 0), ("pscr", 1)])
        add("dve", lambda: nc.vector.tensor_copy(out=self.identb[:, :], in_=self.ident[:, :]),
            reads=["ident"], writes=["identb"])
        add("pool", lambda: nc.gpsimd.memset(self.stA[:, :, :, :].rearrange("p a b c -> p (a b c)"), 0.0), writes=["stA"])
        add("pool", lambda: nc.gpsimd.memset(self.stG[:, :, :, :].rearrange("p a b c -> p (a b c)"), 0.0), writes=["stG"])
        add("pool", lambda: nc.gpsimd.memset(self.stb[:, :], 0.0), writes=["stK", "stV"])

    def run_tile(self, tname):
        if tname == "samp":
            N, kind = NS, "samp"
            xsrc, ydst = self.xs, self.ys
        elif tname == "halo":
            N, kind = TT, "halo"
            xsrc, ydst = self.xh, None
        else:
            i = int(tname[1])
            N, kind = TT, "main"
            xsrc, ydst = self.xm[i * TT:(i + 1) * TT, :], self.ym[i * TT:(i + 1) * TT, :]
        self.N = N
        self.kind = kind
        self.tname = tname
        self.last_main = (tname == "m3")
        self.c0 = 0
        if getattr(self, "stats_ready", False):
            self.P.reserved.discard(self.sbank)
            self.stats_ready = False
        self.load_x(xsrc, N)
        for l in range(self.nl):
            if kind == "halo" and TRI_HALO:
                self.c0 = 128 * l
                self.N = TT - 128 * l
            self.layer(l)
        if self.dbg:
            nc = self.nc
            self.add("sp", lambda: nc.sync.dma_start(out=self.dbg_d, in_=self.h[:, :, :].rearrange("p a b -> p (a b)")),
                     reads=[("h", k) for k in range(8)], dma="dbg")
        if ydst is not None:
            self.final_norm(ydst, N)

    def load_x(self, xsrc, N):
        nc = self.nc
        add = self.add
        for tb in range(N // 128):
            xin = self.cy[:, 2 * (tb % 2):2 * (tb % 2) + 2, :].rearrange("p a b -> p (a b)")
            tk = [("cy", 2 * (tb % 2)), ("cy", 2 * (tb % 2) + 1)]
            add("sp", lambda xin=xin, tb=tb: nc.sync.dma_start(out=xin, in_=xsrc[tb * 128:(tb + 1) * 128, :]),
                writes=tk, dma=("xin", tb % 2))
            for hh in range(2):
                b = self.P.bank()
                for kk in range(4):
                    k = hh * 4 + kk
                    add("pe", lambda b=b, kk=kk, k=k, xin=xin: nc.tensor.transpose(
                        self.ps[b][:, kk * 128:(kk + 1) * 128], xin[:, k * 128:(k + 1) * 128], self.ident[:, :]),
                        reads=tk + ["ident"], writes=[("ps", b)])
                add("act", lambda b=b, hh=hh, tb=tb: nc.scalar.copy(
                    out=self.h[:, hh * 4:(hh + 1) * 4, tb * 128:(tb + 1) * 128],
                    in_=self.ps[b][:, :].rearrange("p (a b) -> p a b", b=128)),
                    reads=[("ps", b)], writes=[("h", hh * 4 + kk) for kk in range(4)])

    def stat_begin(self):
        self.sbank = self.P.bank()
        self.P.reserved.add(self.sbank)
        self.stat_n = 0
        self.stat_q = []
        self.stat_N = self.N

    def stat_chunk(self, k):
        nc = self.nc
        N = self.N
        i = self.stat_n
        self.stat_n += 1
        hk = self.h[:, k, self.c0:self.c0 + N]
        b = self.sbank
        self.add("act", lambda: nc.scalar.activation(out=self.sq[:, i % 2, 0:N], in_=hk, func=AF.Square),
                 reads=[("h", k)], writes=[("sq", i % 2)])
        self.stat_q.append(lambda: self.add(
            "pe", lambda: nc.tensor.matmul(self.ps[b][:, 0:N], self.onesb[:, :], self.sq[:, i % 2, 0:N],
                                           start=(i == 0), stop=(i == 7)),
            reads=[("sq", i % 2), "onesb"], writes=[("ps", b)]))

    def stat_pe(self, all_=False):
        while self.stat_q:
            self.stat_q.pop(0)()
            if not all_:
                break

    def stat_end(self):
        self.stat_pe(all_=True)
        self.stats_ready = True

    def rmsnorm_stats(self, N):
        nc = self.nc
        add = self.add
        if getattr(self, "stats_ready", False):
            self.stats_ready = False
            b = self.sbank
            off = self.stat_N - N
        else:
            off = 0
            b = self.P.bank()
            for k in range(8):
                hk = self.h[:, k, self.c0:self.c0 + N]
                add("act", lambda k=k, hk=hk: nc.scalar.activation(out=self.sq[:, k % 2, 0:N], in_=hk, func=AF.Square),
                    reads=[("h", k)], writes=[("sq", k % 2)])
                add("pe", lambda k=k, b=b: nc.tensor.matmul(self.ps[b][:, 0:N], self.onesb[:, :], self.sq[:, k % 2, 0:N],
                                                            start=(k == 0), stop=(k == 7)),
                    reads=[("sq", k % 2), "onesb"], writes=[("ps", b)])
        add("act", lambda b=b: nc.scalar.activation(out=self.rs[:, 0:N], in_=self.ps[b][:, off:off + N], func=AF.Ln,
                                                    scale=1.0 / D, bias=EPS),
            reads=[("ps", b)], writes=["rs"])
        add("act", lambda: nc.scalar.activation(out=self.rs[:, 0:N], in_=self.rs[:, 0:N], func=AF.Exp, scale=-0.5),
            reads=["rs"], writes=["rs"])
        self.P.reserved.discard(b)

    def rmsnorm(self, N, gcol):
        nc = self.nc
        self.rmsnorm_stats(N)
        for k in range(8):
            hk = self.h[:, k, self.c0:self.c0 + N]
            self.add("dve", lambda k=k, hk=hk: nc.vector.scalar_tensor_tensor(
                out=self.u[:, k, 0:N], in0=hk, scalar=self.pp[:, gcol + k:gcol + k + 1],
                in1=self.rs[:, 0:N], op0=ALU.mult, op1=ALU.mult),
                reads=[("h", k), "rs", "pp"], writes=[("u", k)])

    def final_norm(self, ydst, N):
        nc = self.nc
        add = self.add
        self.rmsnorm_stats(N)
        for k in range(8):
            add("dve", lambda k=k: nc.vector.scalar_tensor_tensor(
                out=self.h[:, k, 0:N], in0=self.h[:, k, 0:N], scalar=self.pp[:, PB_NF + k:PB_NF + k + 1],
                in1=self.rs[:, 0:N], op0=ALU.mult, op1=ALU.mult),
                reads=[("h", k), "rs", "pp"], writes=[("h", k)])
        for tb in range(N // 128):
            yo = self.cy[:, 2 * (tb % 2):2 * (tb % 2) + 2, :].rearrange("p a b -> p (a b)")
            tk = [("cy", 2 * (tb % 2)), ("cy", 2 * (tb % 2) + 1)]
            for hh in range(2):
                b = self.P.bank()
                for kk in range(4):
                    k = hh * 4 + kk
                    add("pe", lambda b=b, kk=kk, k=k, tb=tb: nc.tensor.transpose(
                        self.ps[b][:, kk * 128:(kk + 1) * 128], self.h[:, k, tb * 128:(tb + 1) * 128], self.ident[:, :]),
                        reads=[("h", k), "ident"], writes=[("ps", b)])
                add("act", lambda b=b, hh=hh, yo=yo: nc.scalar.copy(out=yo[:, hh * 512:(hh + 1) * 512], in_=self.ps[b][:, :]),
                    reads=[("ps", b)], writes=[tk[hh]])
            add("sp", lambda yo=yo, tb=tb: nc.sync.dma_start(out=ydst[tb * 128:(tb + 1) * 128, :], in_=yo),
                reads=tk, dma=("yout", tb % 2))

    def a_tok(self, ap2d):
        if self.kind == "samp":
            return ap2d[:, 0:NSEQ * 24].rearrange("p (b i) -> p b i", i=24)[:, :, 16:24]
        return ap2d[:, 16:16 + self.N]

    def g_tok(self, ap2d, off):
        if self.kind == "samp":
            return ap2d[:, 0:NSEQ * 40].rearrange("p (b i) -> p b i", i=40)[:, :, off:off + 8]
        return ap2d[:, off:off + self.N]

    def tokv(self, ap2d):
        if self.kind == "samp":
            return ap2d[:, 0:NS].rearrange("p (b t) -> p b t", t=8)
        return ap2d[:, 0:self.N]

    def layer(self, l):
        nc = self.nc
        add = self.add
        N = self.N
        kind = self.kind
        pb = l * PL
        samp = kind == "samp"
        c0 = self.c0
        tmask = self.tokmask[:, c0:c0 + N]
        WA = NSEQ * 24 if samp else 16 + N
        WG = NSEQ * 40 if samp else 32 + N

        def a_tok(ap2d):
            if samp:
                return ap2d[:, 0:NSEQ * 24].rearrange("p (b i) -> p b i", i=24)[:, :, 16:24]
            return ap2d[:, 16:16 + N]

        def g_tok(ap2d, off):
            if samp:
                return ap2d[:, 0:NSEQ * 40].rearrange("p (b i) -> p b i", i=40)[:, :, off:off + 8]
            return ap2d[:, off:off + N]

        def tokv(ap2d):
            if samp:
                return ap2d[:, 0:NS].rearrange("p (b t) -> p b t", t=8)
            return ap2d[:, 0:N]

        add("sp", lambda: nc.sync.dma_start(out=self.bkv[:, :], in_=self.bkv_d[l]), writes=["bkv"], dma="bkv")
        add("pool", lambda: nc.gpsimd.dma_start(out=self.wpl[:, :, :], in_=self.w_pool[l].rearrange("g c d -> c g d")),
            writes=["wpl"], dma="wpl")
        add("dve", lambda: nc.vector.tensor_scalar(self.bq8[:, :], self.pp[:, pb + PB_IN + 12:pb + PB_IN + 20], 0.125, None, ALU.mult),
            reads=["pp"], writes=["bq8"])
        add("act", lambda: nc.scalar.activation(out=self.sinkE[:, :], in_=self.pp[:, pb + PB_SINK:pb + PB_SINK + 8], func=AF.Exp),
            reads=["pp"], writes=["sinkE"])
        if not samp:
            for J in range(2):
                add("dve", lambda J=J: nc.vector.tensor_copy(
                    out=self.sinkB[:, J, :].rearrange("p (g q) -> p g q", q=128),
                    in_=self.sinkE[:, J * 4:(J + 1) * 4].unsqueeze(2).broadcast_to([128, 4, 128])),
                    reads=["sinkE"], writes=[("sinkB", J)])
            add("pool", lambda: nc.gpsimd.tensor_copy(out=self.aext[:, :, 0:16], in_=self.stA[:, l, :, :]),
                reads=["stA"], writes=[("aext", g) for g in range(4)])
            add("pool", lambda: nc.gpsimd.tensor_copy(out=self.gext[:, :, 0:32], in_=self.stG[:, l, :, :]),
                reads=["stG"], writes=[("gext", c) for c in range(4)])
            for hf in range(2):
                hs = slice(hf * 64, (hf + 1) * 64)
                add("pool", lambda hf=hf, hs=hs: nc.gpsimd.tensor_copy(out=self.kTz[hs, :, hf, 0:128], in_=self.stK[hs, l, :, :]),
                    reads=["stK"], writes=[("kT", 0), ("kT", 1)])
                add("pool", lambda hf=hf, hs=hs: nc.gpsimd.tensor_copy(
                    out=self.vz[:, 0, :, :].rearrange("p (j h) c -> p j h c", h=2)[:, :, hf, hs],
                    in_=self.stV[:, l, :].rearrange("p (j h d) -> p j h d", j=2, h=2)[:, :, hf, :]),
                    reads=["stV"], writes=[("v", 0)])
        elif "load" not in SKIP:
            self.samp_load_states(l)

        self.rmsnorm(N, pb + PB_NMIX)

        def proj_chunk(W, wtok, c0, evac):
            b = self.P.bank()
            for k in range(8):
                add("pe", lambda k=k, b=b: nc.tensor.matmul(self.ps[b][:, 0:N], W[:, k, c0:c0 + 128], self.u[:, k, 0:N],
                                                            start=(k == 0), stop=(k == 7)),
                    reads=[wtok, ("u", k)], writes=[("ps", b)])
            evac(b)

        W, wt = self.wget(("win", 0))
        for g in range(4):
            def ev(b, g=g):
                add("act", lambda: nc.scalar.activation(out=a_tok(self.aext[:, g, :]), in_=tokv(self.ps[b][:, :]),
                                                        func=AF.Identity, bias=self.pp[:, pb + PB_IN + g:pb + PB_IN + g + 1]),
                    reads=[("ps", b), "pp"], writes=[("aext", g)])
                if kind == "halo":
                    add("pool", lambda: nc.gpsimd.tensor_tensor(out=self.aext[:, g, 16:16 + N], in0=self.aext[:, g, 16:16 + N],
                                                                in1=tmask, op=ALU.mult),
                        reads=[("aext", g), "tokmask"], writes=[("aext", g)])
                if samp:
                    add("act", lambda: nc.scalar.activation(out=self.anew[:, g, 0:N], in_=self.ps[b][:, 0:N], func=AF.Identity,
                                                            bias=self.pp[:, pb + PB_IN + g:pb + PB_IN + g + 1]),
                        reads=[("ps", b), "pp"], writes=[("sinkB", 0)])
            proj_chunk(W, wt, g * 128, ev)
        self.wrel(("win", 0))
        Wv, wvt = self.wget(("win", 1))
        Wgt, wgtt = self.wget(("win", 2))
        for c in range(4):
            def evg(b, c=c):
                add("act", lambda: nc.scalar.activation(out=self.sig[:, c % 2, 0:N], in_=self.ps[b][:, 0:N], func=AF.Sigmoid,
                                                        bias=self.pp[:, pb + PB_IN + 8 + c:pb + PB_IN + 9 + c]),
                    reads=[("ps", b), "pp"], writes=[("sig", c % 2)])
            proj_chunk(Wgt, wgtt, c * 128, evg)

            def evv(b, c=c):
                add("dve", lambda: nc.vector.scalar_tensor_tensor(
                    out=g_tok(self.gext[:, c, :], 32), in0=tokv(self.ps[b][:, :]),
                    scalar=self.pp[:, pb + PB_IN + 4 + c:pb + PB_IN + 5 + c], in1=tokv(self.sig[:, c % 2, :]),
                    op0=ALU.add, op1=ALU.mult),
                    reads=[("ps", b), ("sig", c % 2), "pp"], writes=[("gext", c)])
                if kind == "halo":
                    add("pool", lambda: nc.gpsimd.tensor_tensor(out=self.gext[:, c, 32:32 + N], in0=self.gext[:, c, 32:32 + N],
                                                                in1=tmask, op=ALU.mult),
                        reads=[("gext", c), "tokmask"], writes=[("gext", c)])
                if samp:
                    add("pool", lambda: nc.gpsimd.tensor_copy(out=tokv(self.gnew[:, c, :]), in_=g_tok(self.gext[:, c, :], 32)),
                        reads=[("gext", c)], writes=[("sinkB", 1)])
                add("act", lambda: nc.scalar.copy(out=self.gbf[:, c, 0:WG], in_=self.gext[:, c, 0:WG]),
                    reads=[("gext", c)], writes=self.tgbf(c))
            proj_chunk(Wv, wvt, c * 128, evv)
        self.wrel(("win", 1))
        self.wrel(("win", 2))
        early = (kind == "halo" and l == self.nl - 1 and HALO_EARLY_EXIT)
        for hh in range(0 if early else 2):
            W, wt = self.wget(("win", 3 + hh))
            for jj in range(4):
                j = hh * 4 + jj

                def ev(b, j=j):
                    add("act", lambda: nc.scalar.activation(out=self.q[:, j, 0:N], in_=self.ps[b][:, 0:N], func=AF.Identity,
                                                            scale=0.125, bias=self.bq8[:, j:j + 1]),
                        reads=[("ps", b), "bq8"], writes=[self.tq(j)])
                proj_chunk(W, wt, jj * 128, ev)
            self.wrel(("win", 3 + hh))
        W, wt = self.wget(("win", 5))
        koff = 0 if samp else 128
        for J in range(2):
            def ev(b, J=J):
                for hf in range(2):
                    hs = slice(hf * 64, (hf + 1) * 64)
                    add("act", lambda hf=hf, hs=hs: nc.scalar.activation(
                        out=self.kTz[hs, J, hf, koff:koff + N], in_=self.ps[b][hs, 0:N], func=AF.Identity,
                        bias=self.pp[hs, pb + PB_IN + 20 + J:pb + PB_IN + 21 + J]),
                        reads=[("ps", b), "pp"], writes=[("kT", J)])
                if samp or self.last_main:
                    add("act", lambda: nc.scalar.activation(out=self.kf[:, J, :], in_=self.ps[b][:, N - 128:N], func=AF.Identity,
                                                            bias=self.pp[:, pb + PB_IN + 20 + J:pb + PB_IN + 21 + J]),
                        reads=[("ps", b), "pp"], writes=[("kf", J)])
            proj_chunk(W, wt, J * 128, ev)
        if not samp:
            for tb in range(N // 128):
                b = self.P.bank()
                for k in range(8):
                    add("pe", lambda k=k, b=b, tb=tb: nc.tensor.matmul(self.ps[b][:, 0:256], self.u[:, k, tb * 128:(tb + 1) * 128],
                                                                       W[:, k, 256:512], start=(k == 0), stop=(k == 7)),
                        reads=[wt, ("u", k)], writes=[("ps", b)])
                for hf in range(2):
                    hs = slice(hf * 64, (hf + 1) * 64)
                    add("dve", lambda b=b, tb=tb, hf=hf, hs=hs: nc.vector.tensor_tensor(
                        out=self.vz[:, 1 + tb, :, :].rearrange("p (j h) c -> p j h c", h=2)[:, :, hf, hs],
                        in0=self.ps[b][:, 0:256].rearrange("p (j h d) -> p j h d", j=2, h=2)[:, :, hf, :],
                        in1=self.bkv[:, 256:512].rearrange("p (j h d) -> p j h d", j=2, h=2)[:, :, hf, :], op=ALU.add),
                        reads=[("ps", b), "bkv"], writes=[("v", 1 + tb)])
                if self.last_main and tb == 3:
                    add("dve", lambda b=b: nc.vector.tensor_tensor(out=self.ost[:, 1, 0:256], in0=self.ps[b][:, 0:256],
                                                                   in1=self.bkv[:, 256:512], op=ALU.add),
                        reads=[("ps", b), "bkv"], writes=[("pscr", 1)])
                    add("sp", lambda: nc.sync.dma_start(out=self.v_p[l], in_=self.ost[:, 1, 0:256]),
                        reads=[("pscr", 1)], dma=("ostd", 1))
        elif "sv" not in SKIP:
            for bp in range(NSEQ // 2):
                b = self.P.bank()
                for bb in range(2):
                    sq_ = bp * 2 + bb
                    for k in range(8):
                        add("pe", lambda k=k, b=b, bb=bb, sq_=sq_: nc.tensor.matmul(
                            self.ps[b][0:8, bb * 256:(bb + 1) * 256], self.u[:, k, sq_ * 8:(sq_ + 1) * 8], W[:, k, 256:512],
                            start=(k == 0), stop=(k == 7)),
                            reads=[wt, ("u", k)], writes=[("ps", b)])
                add("dve", lambda b=b: nc.vector.tensor_tensor(
                    out=self.vnewf[:, :, :], in0=self.ps[b][0:8, :].rearrange("p (a f) -> p a f", f=256),
                    in1=self.bkv[0:8, 256:512].unsqueeze(1).broadcast_to([8, 2, 256]), op=ALU.add),
                    reads=[("ps", b), "bkv"], writes=[("rD", 1)])
                add("act", lambda bp=bp: nc.scalar.copy(out=self.vnew[:, bp * 2:bp * 2 + 2, :], in_=self.vnewf[:, :, :]),
                    reads=[("rD", 1)], writes=["vnew"])
                add("sp", lambda bp=bp: nc.sync.dma_start(
                    out=self.v_s[l, bp * 2:bp * 2 + 2, 120:128, :].rearrange("b t f -> t b f"), in_=self.vnewf[:, :, :]),
                    reads=[("rD", 1)], dma="vs")
        self.wrel(("win", 5))

        if not samp:
            add("pool", lambda: nc.gpsimd.tensor_copy(out=self.stA[:, l, :, :], in_=self.aext[:, :, N:N + 16]),
                reads=[("aext", g) for g in range(4)], writes=["stA"])
            add("pool", lambda: nc.gpsimd.tensor_copy(out=self.stG[:, l, :, :], in_=self.gext[:, :, N:N + 32]),
                reads=[("gext", c) for c in range(4)], writes=["stG"])
            if self.last_main:
                self.prompt_state_out(l)
        elif "out" not in SKIP:
            self.samp_state_out(l)

        if early:
            for hf in range(2):
                hs = slice(hf * 64, (hf + 1) * 64)
                add("pool", lambda hf=hf, hs=hs: nc.gpsimd.tensor_copy(out=self.stK[hs, l, :, :], in_=self.kTz[hs, :, hf, N:N + 128]),
                    reads=[("kT", 0), ("kT", 1)], writes=["stK"])
                add("pool", lambda hf=hf, hs=hs: nc.gpsimd.tensor_copy(
                    out=self.stV[:, l, :].rearrange("p (j h d) -> p j h d", j=2, h=2)[:, :, hf, :],
                    in_=self.vz[:, N // 128, :, :].rearrange("p (j h) c -> p j h c", h=2)[:, :, hf, hs]),
                    reads=[("v", N // 128)], writes=["stV"])
            return
        nd = [0]
        cb = []
        for c in range(4):
            b = self.P.bank()
            cb.append(b)
            for j in range(31):
                ds = nd[0] % NDIAG
                nd[0] += 1
                if j % 2 == 0:
                    add("pool", lambda ds=ds, c=c, j=j: nc.gpsimd.tensor_scalar(
                        self.diag[:, ds, :], self.identb[:, :], self.pp[:, pb + PB_CW + c * 31 + j:pb + PB_CW + c * 31 + j + 1],
                        1.0, ALU.mult, ALU.mult),
                        reads=["identb", "pp"], writes=[("diag", ds)])
                else:
                    add("act", lambda ds=ds, c=c, j=j: nc.scalar.activation(
                        out=self.diag[:, ds, :], in_=self.identb[:, :], func=AF.Identity,
                        scale=self.pp[:, pb + PB_CW + c * 31 + j:pb + PB_CW + c * 31 + j + 1]),
                        reads=["identb", "pp"], writes=[("diag", ds)])
                add("pe", lambda ds=ds, c=c, j=j, b=b: nc.tensor.matmul(
                    tokv(self.ps[b][:, :]), self.diag[:, ds, :], g_tok(self.gbf[:, c, :], 2 + j),
                    start=(j == 0), stop=(j == 30)),
                    reads=[("diag", ds)] + self.tgbf(c), writes=[("ps", b)])
        bm = self.P.bank()
        bv = self.P.bank()
        for c in range(4):
            b = cb[c]
            add("act", lambda b=b, c=c: nc.scalar.activation(out=self.cy[:, c, 0:N], in_=self.ps[b][:, 0:N], func=AF.Identity,
                                                             bias=self.pp[:, pb + PB_CB + c:pb + PB_CB + c + 1]),
                reads=[("ps", b), "pp"], writes=[("cy", c)])
            add("act", lambda c=c: nc.scalar.activation(out=self.sq[:, 0, 0:N], in_=self.cy[:, c, 0:N], func=AF.Square),
                reads=[("cy", c)], writes=[("sq", 0)])
            add("pool", lambda c=c: nc.gpsimd.tensor_copy(out=self.sq[:, 1, 0:N], in_=self.cy[:, c, 0:N]),
                reads=[("cy", c)], writes=[("sq", 1)])
            add("pe", lambda c=c: nc.tensor.matmul(self.ps[bm][:, 0:N], self.onesb[:, :], self.sq[:, 1, 0:N],
                                                   start=(c == 0), stop=(c == 3)),
                reads=[("sq", 1), "onesb"], writes=[("ps", bm)])
            add("pe", lambda c=c: nc.tensor.matmul(self.ps[bv][:, 0:N], self.onesb[:, :], self.sq[:, 0, 0:N],
                                                   start=(c == 0), stop=(c == 3)),
                reads=[("sq", 0), "onesb"], writes=[("ps", bv)])
        mu = self.mt[:, 0, 0:N]
        var = self.mt[:, 1, 0:N]
        add("dve", lambda: nc.vector.tensor_scalar(mu, self.ps[bm][:, 0:N], 1.0 / 512, None, ALU.mult),
            reads=[("ps", bm)], writes=[("mt", 0)])
        add("dve", lambda: nc.vector.tensor_tensor(out=self.rs[:, 0:N], in0=mu, in1=mu, op=ALU.mult),
            reads=[("mt", 0)], writes=["rs"])
        add("dve", lambda: nc.vector.scalar_tensor_tensor(out=var, in0=self.ps[bv][:, 0:N], scalar=1.0 / 512, in1=self.rs[:, 0:N],
                                                          op0=ALU.mult, op1=ALU.subtract),
            reads=[("ps", bv), "rs"], writes=[("mt", 1)])
        add("dve", lambda: nc.vector.tensor_scalar(var, var, 0.0, None, ALU.max), reads=[("mt", 1)], writes=[("mt", 1)])
        add("act", lambda: nc.scalar.activation(out=var, in_=var, func=AF.Ln, bias=EPS), reads=[("mt", 1)], writes=[("mt", 1)])
        add("act", lambda: nc.scalar.activation(out=var, in_=var, func=AF.Exp, scale=-0.5), reads=[("mt", 1)], writes=[("mt", 1)])
        for c in range(4):
            add("pool", lambda c=c: nc.gpsimd.tensor_tensor(out=self.cy[:, c, 0:N], in0=self.cy[:, c, 0:N], in1=mu, op=ALU.subtract),
                reads=[("cy", c), ("mt", 0)], writes=[("cy", c)])
            add("dve", lambda c=c: nc.vector.tensor_tensor(out=self.cy[:, c, 0:N], in0=self.cy[:, c, 0:N], in1=var, op=ALU.mult),
                reads=[("cy", c), ("mt", 1)], writes=[("cy", c)])
            add("act", lambda c=c: nc.scalar.activation(out=self.s[:, c, 0:N], in_=self.cy[:, c, 0:N], func=AF.Silu,
                                                        scale=self.pp[:, pb + PB_NG + c:pb + PB_NG + c + 1],
                                                        bias=self.pp[:, pb + PB_NB + c:pb + PB_NB + c + 1]),
                reads=[("cy", c), "pp"], writes=[self.ts_(c)])

        for g in range(4):
            ext = self.aext[:, g, :]
            src = ext
            src_tok = [("aext", g)]
            sh = 1
            i = 0
            while sh < WINS[g]:
                dst = self.pscr[:, i % 2, :]
                add("pool", lambda dst=dst, src=src, sh=sh: nc.gpsimd.tensor_tensor(
                    out=dst[:, sh:WA], in0=src[:, sh:WA], in1=src[:, 0:WA - sh], op=ALU.add),
                    reads=src_tok, writes=[("pscr", i % 2)])
                src = dst
                src_tok = [("pscr", i % 2)]
                sh *= 2
                i += 1
            add("dve", lambda src=src, ext=ext, g=g: nc.vector.scalar_tensor_tensor(
                out=tokv(self.r[:, g, :]), in0=a_tok(src), scalar=1.0 / WINS[g], in1=a_tok(ext),
                op0=ALU.mult, op1=ALU.subtract),
                reads=src_tok + [("aext", g)], writes=[self.tr(g)])
            if kind == "halo":
                add("dve", lambda src=src, g=g: nc.vector.tensor_tensor(out=self.rs[:, 0:16], in0=src[:, N:N + 16],
                                                                        in1=self.invc[:, g, :], op=ALU.mult),
                    reads=src_tok + ["invc"], writes=["rs"])
                add("dve", lambda ext=ext, g=g: nc.vector.tensor_tensor(out=self.r[:, g, N - 16:N], in0=self.rs[:, 0:16],
                                                                        in1=ext[:, N:N + 16], op=ALU.subtract),
                    reads=["rs", ("aext", g)], writes=[self.tr(g)])

        if samp:
            if "attn" not in SKIP:
                self.attn_sample(l)
        else:
            self.attn_prompt(l)
            for hf in range(2):
                hs = slice(hf * 64, (hf + 1) * 64)
                add("pool", lambda hf=hf, hs=hs: nc.gpsimd.tensor_copy(out=self.stK[hs, l, :, :], in_=self.kTz[hs, :, hf, N:N + 128]),
                    reads=[("kT", 0), ("kT", 1)], writes=["stK"])
                add("pool", lambda hf=hf, hs=hs: nc.gpsimd.tensor_copy(
                    out=self.stV[:, l, :].rearrange("p (j h d) -> p j h d", j=2, h=2)[:, :, hf, :],
                    in_=self.vz[:, N // 128, :, :].rearrange("p (j h) c -> p j h c", h=2)[:, :, hf, hs]),
                    reads=[("v", N // 128)], writes=["stV"])

        def gate_chunk(Wg, wgt, m, bi):
            bg = self.P.bank()
            for k in range(8):
                add("pe", lambda k=k, bg=bg: nc.tensor.matmul(self.ps[bg][:, 0:N], Wg[:, k, (m % 4) * 128:(m % 4 + 1) * 128],
                                                              self.u[:, k, 0:N], start=(k == 0), stop=(k == 7)),
                    reads=[wgt, ("u", k)], writes=[("ps", bg)])
            col = pb + PB_IN + 24 + bi * 8 + m
            add("act", lambda bg=bg: nc.scalar.activation(out=self.sig[:, m % 2, 0:N], in_=self.ps[bg][:, 0:N], func=AF.Sigmoid,
                                                          bias=self.pp[:, col:col + 1]),
                reads=[("ps", bg), "pp"], writes=[("sig", m % 2)])

        Wp, wpt = self.wpl, "wpl"
        for hh in range(2):
            Wg, wgt = self.wget(("win", 6 + hh))
            for mm in range(4):
                m = hh * 4 + mm
                gate_chunk(Wg, wgt, m, 0)
                by = self.P.bank()
                add("pe", lambda by=by, m=m: nc.tensor.matmul(self.ps[by][:, 0:N], Wp[:, m // 2, (m % 2) * 128:(m % 2 + 1) * 128],
                                                              self.r[:, m // 2, 0:N], start=True, stop=True),
                    reads=[wpt, self.tr(m // 2)], writes=[("ps", by)])
                ma, mtok = self.macc(m, N)
                add("dve", lambda by=by, m=m, ma=ma: nc.vector.scalar_tensor_tensor(
                    out=ma, in0=self.ps[by][:, 0:N], scalar=self.pp[:, pb + PB_PSC + m:pb + PB_PSC + m + 1],
                    in1=self.sig[:, m % 2, 0:N], op0=ALU.mult, op1=ALU.mult),
                    reads=[("ps", by), ("sig", m % 2), "pp"], writes=[mtok])
            self.wrel(("win", 6 + hh))
        for hh in range(2):
            Wc, wct = self.wget(("wco", hh))
            Wg, wgt = self.wget(("win", 8 + hh))
            for mm in range(4):
                m = hh * 4 + mm
                gate_chunk(Wg, wgt, m, 1)
                by = self.P.bank()
                for c in range(4):
                    add("pe", lambda by=by, mm=mm, c=c, Wc=Wc: nc.tensor.matmul(self.ps[by][:, 0:N], Wc[:, c, mm * 128:(mm + 1) * 128],
                                                                       self.s[:, c, 0:N], start=(c == 0), stop=(c == 3)),
                        reads=[wct, self.ts_(c)], writes=[("ps", by)])
                ma, mtok = self.macc(m, N)
                add("dve", lambda by=by, m=m: nc.vector.tensor_tensor(out=self.mt[:, m % 2, 0:N], in0=self.ps[by][:, 0:N],
                                                                      in1=self.sig[:, m % 2, 0:N], op=ALU.mult),
                    reads=[("ps", by), ("sig", m % 2)], writes=[("mt", m % 2)])
                add("pool", lambda m=m, ma=ma: nc.gpsimd.tensor_tensor(out=ma, in0=ma, in1=self.mt[:, m % 2, 0:N], op=ALU.add),
                    reads=[mtok, ("mt", m % 2)], writes=[mtok])
            self.wrel(("win", 8 + hh))
            self.wrel(("wco", hh))
        for hh in range(2):
            Wa, wat = self.wget(("wao", hh))
            Wg, wgt = self.wget(("win", 10 + hh))
            for mm in range(4):
                m = hh * 4 + mm
                gate_chunk(Wg, wgt, m, 2)
                by = self.P.bank()
                for k in range(8):
                    add("pe", lambda by=by, mm=mm, k=k, Wa=Wa: nc.tensor.matmul(self.ps[by][:, 0:N], Wa[:, k, mm * 128:(mm + 1) * 128],
                                                                         self.q[:, k, 0:N], start=(k == 0), stop=(k == 7)),
                        reads=[wat, self.tq(k)], writes=[("ps", by)])
                ma, mtok = self.macc(m, N)
                add("dve", lambda by=by, m=m: nc.vector.tensor_tensor(out=self.mt[:, m % 2, 0:N], in0=self.ps[by][:, 0:N],
                                                                      in1=self.sig[:, m % 2, 0:N], op=ALU.mult),
                    reads=[("ps", by), ("sig", m % 2)], writes=[("mt", m % 2)])
                add("pool", lambda m=m, ma=ma: nc.gpsimd.tensor_tensor(out=self.merged[:, m, 0:N], in0=ma, in1=self.mt[:, m % 2, 0:N],
                                                                       op=ALU.add),
                    reads=[mtok, ("mt", m % 2)], writes=[self.tmg(m)])
            self.wrel(("wao", hh))
            self.wrel(("win", 10 + hh))
        self.stat_begin()
        for hh in range(2):
            Wo, wot = self.wget(("wo", hh))
            for mm in range(4):
                m = hh * 4 + mm
                b = self.P.bank()
                for k in range(8):
                    add("pe", lambda b=b, mm=mm, k=k, Wo=Wo: nc.tensor.matmul(self.ps[b][:, 0:N], Wo[:, k, mm * 128:(mm + 1) * 128],
                                                                       self.merged[:, k, 0:N], start=(k == 0), stop=(k == 7)),
                        reads=[wot, self.tmg(k)], writes=[("ps", b)])
                self.stat_pe()
                hm = self.h[:, m, self.c0:self.c0 + N]
                add("dve", lambda b=b, hm=hm: nc.vector.tensor_tensor(out=hm, in0=self.ps[b][:, 0:N], in1=hm, op=ALU.add),
                    reads=[("ps", b), ("h", m)], writes=[("h", m)])
                self.stat_chunk(m)
            self.wrel(("wo", hh))
        self.stat_end()
        self.rmsnorm(N, pb + PB_NMLP)
        for j in range(8):
            Wu, wut = self.wget(("wup", j))
            for ff in range(4):
                f = j * 4 + ff
                b = self.P.bank()
                for k in range(8):
                    add("pe", lambda b=b, ff=ff, k=k, Wu=Wu: nc.tensor.matmul(self.ps[b][:, 0:N], Wu[:, k, ff * 128:(ff + 1) * 128],
                                                                       self.u[:, k, 0:N], start=(k == 0), stop=(k == 7)),
                        reads=[wut, ("u", k)], writes=[("ps", b)])
                add("act", lambda b=b, f=f: nc.scalar.activation(out=self.relu[:, f % 2, 0:N], in_=self.ps[b][:, 0:N], func=AF.Relu),
                    reads=[("ps", b)], writes=[("relu", f % 2)])
                add("dve", lambda b=b, f=f: nc.vector.tensor_tensor(out=self.hid[:, f, 0:N], in0=self.ps[b][:, 0:N],
                                                                    in1=self.relu[:, f % 2, 0:N], op=ALU.mult),
                    reads=[("ps", b), ("relu", f % 2)], writes=[self.thid(f)])
            self.wrel(("wup", j))
        self.stat_begin()
        for m in range(8):
            Wd, wdt = self.wget(("wdn", m))
            b = self.P.bank()
            for f in range(32):
                add("pe", lambda b=b, f=f, Wd=Wd: nc.tensor.matmul(self.ps[b][:, 0:N], Wd[:, f, :], self.hid[:, f, 0:N],
                                                            start=(f == 0), stop=(f == 31)),
                    reads=[wdt, self.thid(f)], writes=[("ps", b)])
            self.stat_pe()
            hm = self.h[:, m, self.c0:self.c0 + N]
            add("dve", lambda b=b, hm=hm: nc.vector.tensor_tensor(out=hm, in0=self.ps[b][:, 0:N], in1=hm, op=ALU.add),
                reads=[("ps", b), ("h", m)], writes=[("h", m)])
            self.stat_chunk(m)
            self.wrel(("wdn", m))
        self.stat_end()

    def attn_prompt(self, l):
        nc = self.nc
        add = self.add
        N = self.N
        LA = 3
        halo_off = self.c0 // 128
        units = [(qb, J, hf, c) for qb in range(N // 128) for J in range(2) for hf in range(2) for c in range(2)]
        info = {}
        grp = {}
        for gi, (qb, J) in enumerate([(qb, J) for qb in range(N // 128) for J in range(2)]):
            grp[(qb, J)] = (4, 5) if gi % 2 == 0 else (6, 7)

        def emit_qk(i):
            qb, J, hf, c = units[i]
            kv = 2 * J + hf
            qs = slice(qb * 128, (qb + 1) * 128)
            ps_ = slice(hf * 64, (hf + 1) * 64)
            kb = qb + c
            bl = i % 4
            add("pe", lambda: nc.tensor.matmul(
                self.ps[bl][:, :], self.identb[:, :],
                self.biasT[:, c, kv * 4:(kv + 1) * 4, :].rearrange("p g q -> p (g q)"), start=True, stop=False),
                reads=["identb", "biasT"], writes=[("ps", bl)])
            add("pe", lambda: nc.tensor.matmul(
                self.ps[bl][:, :].rearrange("p (g q) -> p g q", q=128),
                self.kTz[:, J, hf, kb * 128:(kb + 1) * 128], self.q[:, J * 4:(J + 1) * 4, qs], start=False, stop=True),
                reads=[("kT", J)] + [self.tq(J * 4 + g) for g in range(4)], writes=[("ps", bl)])
            info[i] = (bl, kb)

        def emit_soft(i):
            bl, kb = info[i]
            ai = i % 4
            mcol = None
            if self.kind == "halo":
                mcol = 4 if kb == 0 else kb - 1 + halo_off
            elif self.tname == "m0" and kb == 0:
                mcol = 3
            if mcol is None:
                add("act", lambda: nc.scalar.activation(out=self.pT[:, ai, :], in_=self.ps[bl][:, :], func=AF.Exp),
                    reads=[("ps", bl)], writes=[("pT", ai)])
            else:
                add("act", lambda: nc.scalar.activation(
                    out=self.pT[:, ai, :], in_=self.ps[bl][:, :], func=AF.Exp, bias=self.kmask[:, mcol:mcol + 1]),
                    reads=[("ps", bl), "kmask"], writes=[("pT", ai)])

        def emit_pv(i):
            qb, J, hf, c = units[i]
            kv = 2 * J + hf
            ps_ = slice(hf * 64, (hf + 1) * 64)
            bo, bd = grp[(qb, J)]
            bl, kb = info[i]
            ai = i % 4
            first = (hf == 0 and c == 0)
            last = (hf == 1 and c == 1)
            add("pe", lambda: nc.tensor.matmul(
                self.ps[bo][:, :], self.vz[:, kb, kv, :], self.pT[:, ai, :], start=first, stop=last),
                reads=[("v", kb), ("pT", ai)], writes=[("ps", bo)])
            add("pe", lambda: nc.tensor.matmul(
                self.ps[bd][:, :], self.onesz[:, hf, :], self.pT[:, ai, :], start=first, stop=last),
                reads=["onesz", ("pT", ai)], writes=[("ps", bd)])

        def emit_norm(qb, J):
            bo, bd = grp[(qb, J)]
            qs = slice(qb * 128, (qb + 1) * 128)
            add("dve", lambda: nc.vector.tensor_tensor(out=self.rD[:, J, :], in0=self.ps[bd][:, :], in1=self.sinkB[:, J, :], op=ALU.add),
                reads=[("ps", bd), ("sinkB", J)], writes=[("rD", J)])
            add("act", lambda: nc.scalar.activation(out=self.rD[:, J, :], in_=self.rD[:, J, :], func=AF.Ln),
                reads=[("rD", J)], writes=[("rD", J)])
            add("act", lambda: nc.scalar.activation(out=self.rD[:, J, :], in_=self.rD[:, J, :], func=AF.Exp, scale=-1.0),
                reads=[("rD", J)], writes=[("rD", J)])
            add("dve", lambda: nc.vector.tensor_tensor(
                out=self.q[:, J * 4:(J + 1) * 4, qs], in0=self.ps[bo][:, :].rearrange("p (g q) -> p g q", q=128),
                in1=self.rD[:, J, :].rearrange("p (g q) -> p g q", q=128), op=ALU.mult),
                reads=[("ps", bo), ("rD", J)], writes=[self.tq(J * 4 + g) for g in range(4)])

        n = len(units)
        NDEF = 3
        pending = []
        for i in range(min(LA, n)):
            emit_qk(i)
        for i in range(n):
            if i + LA < n:
                emit_qk(i + LA)
            emit_soft(i)
            while pending and pending[0][0] <= i:
                _, pq, pj = pending.pop(0)
                emit_norm(pq, pj)
            emit_pv(i)
            qb, J, hf, c = units[i]
            if hf == 1 and c == 1:
                pending.append((i + NDEF, qb, J))
        for _, pq, pj in pending:
            emit_norm(pq, pj)

    def prompt_state_out(self, l):
        nc = self.nc
        add = self.add
        N = self.N
        for (src, tokname, lo, n, dst, slot) in ((self.aext, "aext", N + 1, 15, self.pool_p, 0),
                                                 (self.gext, "gext", N + 2, 30, self.conv_p, 0)):
            b = self.P.bank()
            for g in range(4):
                add("pe", lambda b=b, g=g, src=src, lo=lo, n=n: nc.tensor.transpose(
                    self.ps[b][0:n, g * 128:(g + 1) * 128], src[:, g, lo:lo + n], self.ident[:, :]),
                    reads=[(tokname, g), "ident"], writes=[("ps", b)])
            add("act", lambda b=b, n=n: nc.scalar.copy(out=self.ost[0:n, 0, :], in_=self.ps[b][0:n, :]),
                reads=[("ps", b)], writes=[("pscr", 0)])
            add("sp", lambda dst=dst, n=n: nc.sync.dma_start(out=dst[l], in_=self.ost[0:n, 0, :]),
                reads=[("pscr", 0)], dma=("ostd", 0))
        b = self.P.bank()
        for J in range(2):
            add("pe", lambda b=b, J=J: nc.tensor.transpose(self.ps[b][:, J * 128:(J + 1) * 128], self.kf[:, J, :], self.ident[:, :]),
                reads=[("kf", J), "ident"], writes=[("ps", b)])
        add("act", lambda b=b: nc.scalar.copy(out=self.ost[:, 0, 0:256], in_=self.ps[b][:, 0:256]),
            reads=[("ps", b)], writes=[("pscr", 0)])
        add("sp", lambda: nc.sync.dma_start(out=self.k_p[l], in_=self.ost[:, 0, 0:256]), reads=[("pscr", 0)], dma=("ostd", 0))

    def samp_load_states(self, l):
        nc = self.nc
        add = self.add
        for rb in range(2):
            stg = self.ost[0:120, rb, :]
            add("sp", lambda rb=rb, stg=stg: nc.sync.dma_start(
                out=stg, in_=self.spool[l, rb * 8:(rb + 1) * 8].rearrange("b i f -> (b i) f")),
                writes=[("pscr", rb)], dma=("ostd", rb))
            b = self.P.bank()
            for g in range(4):
                add("pe", lambda b=b, g=g, stg=stg: nc.tensor.transpose(
                    self.ps[b][:, g * 128:g * 128 + 120], stg[:, g * 128:(g + 1) * 128], self.ident[0:120, 0:120]),
                    reads=[("pscr", rb), "ident"], writes=[("ps", b)])
            add("act", lambda b=b, rb=rb: nc.scalar.copy(
                out=self.aext[:, :, rb * 8 * 24:(rb + 1) * 8 * 24].rearrange("p g (b i) -> p g b i", i=24)[:, :, :, 1:16],
                in_=self.ps[b][:, :].rearrange("p (g x) -> p g x", x=128)[:, :, 0:120].rearrange("p g (b i) -> p g b i", i=15)),
                reads=[("ps", b)], writes=[("aext", g) for g in range(4)])
        for rb in range(4):
            stg = self.ost[0:120, rb % 2, :]
            add("sp", lambda rb=rb, stg=stg: nc.sync.dma_start(
                out=stg, in_=self.sconv[l, rb * 4:(rb + 1) * 4].rearrange("b i f -> (b i) f")),
                writes=[("pscr", rb % 2)], dma=("ostd", rb % 2))
            b = self.P.bank()
            for g in range(4):
                add("pe", lambda b=b, g=g, stg=stg: nc.tensor.transpose(
                    self.ps[b][:, g * 128:g * 128 + 120], stg[:, g * 128:(g + 1) * 128], self.ident[0:120, 0:120]),
                    reads=[("pscr", rb % 2), "ident"], writes=[("ps", b)])
            add("act", lambda b=b, rb=rb: nc.scalar.copy(
                out=self.gext[:, :, rb * 4 * 40:(rb + 1) * 4 * 40].rearrange("p g (b i) -> p g b i", i=40)[:, :, :, 2:32],
                in_=self.ps[b][:, :].rearrange("p (g x) -> p g x", x=128)[:, :, 0:120].rearrange("p g (b i) -> p g b i", i=30)),
                reads=[("ps", b)], writes=[("gext", g) for g in range(4)])

    def samp_state_out(self, l):
        nc = self.nc
        add = self.add
        add("sp", lambda: nc.sync.dma_start(out=self.pool_s[l, :, 0:7, :], in_=self.spool[l, :, 8:15, :]), dma="h2h0")
        add("sp", lambda: nc.sync.dma_start(out=self.conv_s[l, :, 0:22, :], in_=self.sconv[l, :, 8:30, :]), dma="h2h1")
        add("sp", lambda: nc.sync.dma_start(out=self.k_s[l, :, 0:120, :], in_=self.ck[l, :, 8:128, :]), dma="h2h2")
        add("sp", lambda: nc.sync.dma_start(out=self.v_s[l, :, 0:120, :], in_=self.cv[l, :, 8:128, :]), dma="h2h3")
        for (src, tokname, dst, r0, sidx) in ((self.anew, ("sinkB", 0), self.pool_s, 7, 0), (self.gnew, ("sinkB", 1), self.conv_s, 22, 1)):
            b = self.P.bank()
            for g in range(4):
                add("pe", lambda b=b, g=g, src=src: nc.tensor.transpose(self.ps[b][:, g * 128:(g + 1) * 128], src[:, g, :], self.ident[:, :]),
                    reads=[tokname, "ident"], writes=[("ps", b)])
            add("act", lambda b=b, sidx=sidx: nc.scalar.copy(out=self.ost[:, sidx, :], in_=self.ps[b][:, :]),
                reads=[("ps", b)], writes=[("pscr", sidx)])
            for sq_ in range(NSEQ):
                add("sp", lambda dst=dst, r0=r0, sq_=sq_, sidx=sidx: nc.sync.dma_start(
                    out=dst[l, sq_, r0:r0 + 8, :], in_=self.ost[sq_ * 8:(sq_ + 1) * 8, sidx, :]),
                    reads=[("pscr", sidx)], dma=("osts", sidx * 4 + sq_ % 4))
        b = self.P.bank()
        for J in range(2):
            add("pe", lambda b=b, J=J: nc.tensor.transpose(self.ps[b][:, J * 128:(J + 1) * 128], self.kf[:, J, :], self.ident[:, :]),
                reads=[("kf", J), "ident"], writes=[("ps", b)])
        add("act", lambda b=b: nc.scalar.copy(out=self.rs[:, 0:256], in_=self.ps[b][:, 0:256]), reads=[("ps", b)], writes=["rs"])
        for sq_ in range(NSEQ):
            add("sp", lambda sq_=sq_: nc.sync.dma_start(out=self.k_s[l, sq_, 120:128, :], in_=self.rs[sq_ * 8:(sq_ + 1) * 8, 0:256]),
                reads=["rs"], dma=("osts", 8 + sq_ % 4))

    def attn_sample(self, l):
        nc = self.nc
        add = self.add
        for grp in range(4):
            buf = grp % 2
            s0 = grp * 4
            add("sp", lambda s0=s0: nc.sync.dma_start(out=self.kst[:, :, :], in_=self.ck[l, s0:s0 + 4].rearrange("b s f -> s b f")),
                writes=[("mt", 0), ("mt", 1)], dma="kst")
            for J in range(2):
                b = self.P.bank()
                for bb in range(4):
                    add("pe", lambda b=b, bb=bb, J=J: nc.tensor.transpose(
                        self.ps[b][:, bb * 128:(bb + 1) * 128], self.kst[:, bb, J * 128:(J + 1) * 128], self.ident[:, :]),
                        reads=[("mt", 0), ("mt", 1), "ident"], writes=[("ps", b)])
                add("act", lambda b=b, J=J: nc.scalar.copy(out=self.kcT[:, J, :, :].rearrange("p b s -> p (b s)"),
                                                          in_=self.ps[b][:, :]),
                    reads=[("ps", b)], writes=["stK"])
            add("pool", lambda s0=s0: nc.gpsimd.dma_start(out=self.vc[:, :, :],
                                                          in_=self.cv[l, s0:s0 + 4].rearrange("b s f -> s b f")),
                writes=["stV"], dma="vc")
            blc = self.P.bank()
            blo = self.P.bank()
            for bb in range(4):
                sq_ = s0 + bb
                cs = slice(sq_ * 8, (sq_ + 1) * 8)
                for kv in range(4):
                    J, hf = kv // 2, kv % 2
                    ps_ = slice(hf * 64, (hf + 1) * 64)
                    oc = self.ps[blc][:, kv * 128:(kv + 1) * 128].rearrange("p (g b q) -> p g b q", g=4, b=4)[:, :, bb, :]
                    add("pe", lambda oc=oc, J=J, ps_=ps_, cs=cs, buf=buf, bb=bb: nc.tensor.matmul(
                        oc, self.kcT[ps_, J, bb, :], self.q[ps_, J * 4:(J + 1) * 4, cs], start=True, stop=True),
                        reads=["stK"] + [self.tq(J * 4 + g) for g in range(4)], writes=[("ps", blc)])
                    oo = self.ps[blo][0:8, kv * 128:(kv + 1) * 128].rearrange("p (g b q) -> p g b q", g=4, b=4)[:, :, bb, :]
                    add("pe", lambda oo=oo, J=J, ps_=ps_, cs=cs, hf=hf: nc.tensor.matmul(
                        oo, self.kTz[ps_, J, hf, cs], self.q[ps_, J * 4:(J + 1) * 4, cs], start=True, stop=True),
                        reads=[("kT", J)] + [self.tq(J * 4 + g) for g in range(4)], writes=[("ps", blo)])
            add("dve", lambda blc=blc: nc.vector.tensor_tensor(
                out=self.atmp[:, 0, :].rearrange("p (h b q) -> p h b q", h=16, b=4), in0=self.ps[blc][:, :].rearrange("p (h b q) -> p h b q", h=16, b=4),
                in1=self.biasS[:, 0, :].rearrange("p (h q) -> p h q", q=8).unsqueeze(2).broadcast_to([128, 16, 4, 8]), op=ALU.add),
                reads=[("ps", blc), "biasS"], writes=[("atmp", 0)])
            add("dve", lambda blo=blo: nc.vector.tensor_tensor(
                out=self.atmp[0:8, 1, :].rearrange("p (h b q) -> p h b q", h=16, b=4), in0=self.ps[blo][0:8, :].rearrange("p (h b q) -> p h b q", h=16, b=4),
                in1=self.biasS[0:8, 1, :].rearrange("p (h q) -> p h q", q=8).unsqueeze(2).broadcast_to([8, 16, 4, 8]), op=ALU.add),
                reads=[("ps", blo), "biasS"], writes=[("atmp", 1)])
            add("act", lambda: nc.scalar.activation(out=self.pT[:, 0, :], in_=self.atmp[:, 0, :], func=AF.Exp),
                reads=[("atmp", 0)], writes=[("pT", 0)])
            add("act", lambda: nc.scalar.activation(out=self.pT[0:8, 1, :], in_=self.atmp[0:8, 1, :], func=AF.Exp),
                reads=[("atmp", 1)], writes=[("pT", 1)])
            bo = self.P.bank()
            bd = self.P.bank()
            for bb in range(4):
                sq_ = s0 + bb
                for kv in range(4):
                    J, hf = kv // 2, kv % 2
                    ps_ = slice(hf * 64, (hf + 1) * 64)
                    pc = self.pT[:, 0, kv * 128:(kv + 1) * 128].rearrange("p (g b q) -> p g b q", g=4, b=4)[:, :, bb, :]
                    po = self.pT[0:8, 1, kv * 128:(kv + 1) * 128].rearrange("p (g b q) -> p g b q", g=4, b=4)[:, :, bb, :]
                    for (bk, wa, wb) in ((bo, self.vc[:, bb, kv * 64:(kv + 1) * 64], self.vnew[0:8, sq_, kv * 64:(kv + 1) * 64]),
                                         (bd, self.onesb[:, 0:64], self.onesb[0:8, 0:64])):
                        oo = self.ps[bk][ps_, J * 128:(J + 1) * 128].rearrange("p (g b q) -> p g b q", g=4, b=4)[:, :, bb, :]
                        add("pe", lambda oo=oo, wa=wa, pc=pc: nc.tensor.matmul(oo, wa, pc, start=True, stop=False),
                            reads=["stV", ("pT", 0), "onesb"], writes=[("ps", bk)])
                        add("pe", lambda oo=oo, wb=wb, po=po: nc.tensor.matmul(oo, wb, po, start=False, stop=True),
                            reads=["vnew", ("pT", 1), "onesb"], writes=[("ps", bk)])
            add("dve", lambda bd=bd: nc.vector.tensor_tensor(
                out=self.rD[:, 0, 0:256].rearrange("p (h x) -> p h x", x=32), in0=self.ps[bd][:, 0:256].rearrange("p (h x) -> p h x", x=32),
                in1=self.sinkE[:, :].unsqueeze(2).broadcast_to([128, 8, 32]), op=ALU.add),
                reads=[("ps", bd), "sinkE"], writes=[("rD", 0)])
            add("act", lambda: nc.scalar.activation(out=self.rD[:, 0, 0:256], in_=self.rD[:, 0, 0:256], func=AF.Ln),
                reads=[("rD", 0)], writes=[("rD", 0)])
            add("act", lambda: nc.scalar.activation(out=self.rD[:, 0, 0:256], in_=self.rD[:, 0, 0:256], func=AF.Exp, scale=-1.0),
                reads=[("rD", 0)], writes=[("rD", 0)])
            add("dve", lambda bo=bo, s0=s0: nc.vector.tensor_tensor(
                out=self.q[:, 0:8, s0 * 8:(s0 + 4) * 8], in0=self.ps[bo][:, 0:256].rearrange("p (h x) -> p h x", x=32),
                in1=self.rD[:, 0, 0:256].rearrange("p (h x) -> p h x", x=32), op=ALU.mult),
                reads=[("ps", bo), ("rD", 0)], writes=[self.tq(j) for j in range(8)])


def _t5_bucket(dist):
    n = np.maximum(dist, 0)
    exact = 16
    large = exact + (np.log(np.maximum(n, 1) / exact) / np.log(128 / exact) * (32 - exact)).astype(np.int32)
    large = np.minimum(large, 31)
    return np.where(n < exact, n, large).astype(np.int32)


def _qperm():
    idx = []
    for j in range(8):
        for hf in range(2):
            kv = 2 * (j // 4) + hf
            g = j % 4
            head = kv * 4 + g
            idx.extend(range(head * 64, head * 64 + 64))
    return np.array(idx)


def _fm(vec, nch):
    return np.ascontiguousarray(np.asarray(vec, np.float32).reshape(nch, 128).T)


_NC_CACHE = {}


def prepare(inputs):
    f = lambda k: np.asarray(inputs[k], np.float32)
    x_prompt, x_sample = f("x_prompt"), f("x_sample")
    qp = _qperm()
    cols = np.concatenate([np.arange(0, 1536), 1536 + qp, np.arange(2560, 6144)])
    w_in = np.ascontiguousarray(f("w_in")[:, :, cols])
    b_in = f("b_in")[:, cols]
    w_ao = np.ascontiguousarray(f("w_attn_out")[:, qp, :])
    pp = np.zeros((128, NPP), np.float32)
    sinks = f("attn_sinks")
    for l in range(NL):
        o = l * PL
        pp[:, o + PB_IN:o + PB_IN + 48] = _fm(b_in[l], 48)
        pp[:, o + PB_NMIX:o + PB_NMIX + 8] = _fm(f("norm_mix")[l], 8)
        pp[:, o + PB_PSC:o + PB_PSC + 8] = _fm(f("pool_scale")[l], 8)
        pp[:, o + PB_NMLP:o + PB_NMLP + 8] = _fm(f("norm_mlp")[l], 8)
        cw = f("conv_w")[l]
        pp[:, o + PB_CW:o + PB_CW + 124] = cw.T.reshape(4, 128, 31).transpose(1, 0, 2).reshape(128, 124)
        pp[:, o + PB_CB:o + PB_CB + 4] = _fm(f("conv_b")[l], 4)
        pp[:, o + PB_NG:o + PB_NG + 4] = _fm(f("conv_norm_g")[l], 4)
        pp[:, o + PB_NB:o + PB_NB + 4] = _fm(f("conv_norm_b")[l], 4)
        for J in range(2):
            for g in range(4):
                for hf in range(2):
                    pp[hf * 64:(hf + 1) * 64, o + PB_SINK + J * 4 + g] = sinks[l, (2 * J + hf) * 4 + g]
    pp[:, PB_NF:PB_NF + 8] = _fm(f("norm_final"), 8)
    bkv = np.ascontiguousarray(np.broadcast_to(b_in[:, None, 2560:3072], (NL, 128, 512)))
    ext = np.concatenate([f("rel_bias"), np.full((1, 16), NEG, np.float32)], axis=0)
    s = np.arange(128)[:, None]
    q = np.arange(128)[None, :]
    d_prev = 128 + q - s
    d_own = q - s
    tabs = []
    for dmat in (d_prev, d_own):
        idx = np.where((dmat >= 0) & (dmat <= 128), _t5_bucket(dmat), 32)
        tabs.append(ext[idx])
    biasT = np.stack(tabs, axis=1).transpose(0, 1, 3, 2)
    biasT = np.ascontiguousarray(biasT).reshape(128, 2 * 16 * 128)
    biasS = np.ascontiguousarray(biasT.reshape(128, 2, 16, 128)[:, :, :, 0:8]).reshape(128, 256)
    common = dict(identm=np.eye(128, dtype=np.float32), biasS=biasS, w_in=w_in, w_pool=f("pool_w"), w_co=f("w_conv_out"), w_ao=w_ao, w_o=f("w_out"),
                  w_up=f("w_up"), w_dn=f("w_down"), pp=pp, bkv=bkv, biasT=biasT)
    meta = f("meta_tokens")
    in_maps = []
    for c in range(8):
        b, cc = c // 4, c % 4
        m = dict(common)
        m["xm"] = np.ascontiguousarray(x_prompt[b, cc * 2048:(cc + 1) * 2048])
        tokmask = np.ones((128, 512), np.float32)
        kmask = np.zeros((128, 8), np.float32)
        kmask[:, 4] = NEG
        invc = np.zeros((128, 4, 16), np.float32)
        for g, w in enumerate(WINS):
            invc[:, g, :] = 1.0 / w
        if cc == 0:
            xh = np.zeros((512, D), np.float32)
            xh[496:] = meta
            tokmask[:, :496] = 0.0
            kmask[:, 0:3] = NEG
            kmask[:112, 3] = NEG
            for g, w in enumerate(WINS):
                invc[:, g, :] = 1.0 / np.minimum(np.arange(16) + 1, w)
        else:
            xh = np.ascontiguousarray(x_prompt[b, cc * 2048 - 512:cc * 2048])
        m["xh"] = xh
        m["tokmask"] = tokmask
        m["kmask"] = kmask
        m["invc"] = invc.reshape(128, 64)
        sl = slice(c * NSEQ, (c + 1) * NSEQ)
        m["xs"] = np.ascontiguousarray(x_sample[sl].reshape(NS, D))
        m["spool"] = np.ascontiguousarray(f("state_pool")[:, sl])
        m["sconv"] = np.ascontiguousarray(f("state_conv")[:, sl])
        m["ck"] = np.ascontiguousarray(f("cache_k")[:, sl].reshape(NL, NSEQ, 128, 256))
        m["cv"] = np.ascontiguousarray(f("cache_v")[:, sl].reshape(NL, NSEQ, 128, 256))
        in_maps.append(m)
    return in_maps


def assemble(res):
    R = res.results
    y_prompt = np.zeros((2, 8192, D), np.float32)
    for c in range(8):
        y_prompt[c // 4, (c % 4) * 2048:(c % 4 + 1) * 2048] = R[c]["ym"]
    y_sample = np.concatenate([R[c]["ys"].reshape(NSEQ, DT, D) for c in range(8)], axis=0)
    pool_p = np.stack([R[3]["pool_p"], R[7]["pool_p"]], axis=1)
    conv_p = np.stack([R[3]["conv_p"], R[7]["conv_p"]], axis=1)
    k_p = np.stack([R[3]["k_p"], R[7]["k_p"]], axis=1).reshape(NL, 2, 128, 4, 64)
    v_p = np.stack([R[3]["v_p"], R[7]["v_p"]], axis=1).reshape(NL, 2, 128, 4, 64)
    pool_s = np.concatenate([R[c]["pool_s"] for c in range(8)], axis=1)
    conv_s = np.concatenate([R[c]["conv_s"] for c in range(8)], axis=1)
    k_s = np.concatenate([R[c]["k_s"] for c in range(8)], axis=1).reshape(NL, 128, 128, 4, 64)
    v_s = np.concatenate([R[c]["v_s"] for c in range(8)], axis=1).reshape(NL, 128, 128, 4, 64)
    outs = (y_prompt, y_sample, pool_p, pool_s, conv_p, conv_s, k_p, k_s, v_p, v_s)
    return tuple(np.ascontiguousarray(o, dtype=np.float32) for o in outs)


def kernel(**inputs):
    in_maps = prepare(inputs)
    if "nc" not in _NC_CACHE:
        _NC_CACHE["nc"] = Builder().build()
    nc = _NC_CACHE["nc"]
    res = run_bass_kernel_spmd(nc, in_maps, core_ids=list(range(8)))
    return assemble(res)
```
